# Optimizing a Trainium2 kernel written in Bass

```python
import jax, jax.numpy as jnp
from jax import lax
import numpy as np

D_MODEL = 1024
BATCH = 8
SEQ = 2048
DEPTH = 4
DEC_BATCH = 128
DEC_SEQ = 8
PAST_LEN = 16384
PAGE_SIZE = 128

MIX_WIDTH = D_MODEL // 2
HEAD_SIZE = 64
N_HEADS = MIX_WIDTH // HEAD_SIZE
D_DECAY_LORA = 64
D_AAA_LORA = 64
D_MV_LORA = 32
D_GATE_LORA = 128
POOL_WIDTH = D_MODEL // 2
POOL_WINDOWS = (2, 4, 8, 16)
N_POOL_GROUPS = len(POOL_WINDOWS)
POOL_GROUP = POOL_WIDTH // N_POOL_GROUPS
POOL_BUF = max(POOL_WINDOWS) - 1
D_FF = 4 * D_MODEL
N_BRANCHES = 2
RWKV_COLS = 3 * MIX_WIDTH + D_DECAY_LORA + D_AAA_LORA + D_GATE_LORA
IN_COLS = RWKV_COLS + POOL_WIDTH + N_BRANCHES * D_MODEL
RWKV_SPLITS = (MIX_WIDTH, 2 * MIX_WIDTH, 3 * MIX_WIDTH,
               3 * MIX_WIDTH + D_DECAY_LORA, 3 * MIX_WIDTH + D_DECAY_LORA + D_AAA_LORA)
NORM_EPS = 1e-6
GN_EPS = 64e-5

kernel_name = "rwkv7_pool_gated_hybrid_step"

V_NAMES = ("v0", "v1", "v2")


def rms_norm(x, g):
    xf = x.astype(jnp.float32)
    y = xf * lax.rsqrt(jnp.mean(xf * xf, axis=-1, keepdims=True) + NORM_EPS)
    return (y * g.astype(jnp.float32)).astype(x.dtype)


def ada_mod(c, w, b):
    m = jax.nn.silu(c) @ w + b
    shift, scale, gate = jnp.split(m, 3, axis=-1)
    return shift[:, None], scale[:, None], gate[:, None]


def wkv7_scan(state0, r, decay, k, v, kk, a):
    def step(S, inp):
        r_t, w_t, k_t, v_t, kk_t, a_t = inp
        sa = jnp.einsum("bhvk,bhk->bhv", S, -kk_t)
        S = (S * w_t[:, :, None, :]
             + sa[..., None] * (kk_t * a_t)[:, :, None, :]
             + v_t[..., None] * k_t[:, :, None, :])
        y = jnp.einsum("bhvk,bhk->bhv", S, r_t)
        return S, y
    xs = tuple(jnp.moveaxis(t, 1, 0) for t in (r, decay, k, v, kk, a))
    S, ys = lax.scan(step, state0, xs)
    return jnp.moveaxis(ys, 0, 1), S


def rwkv7_branch(pr, shift_prev, wkv_prev, v_first, lp):
    B, T, _ = pr.shape
    f32 = jnp.float32
    pr_prev = jnp.concatenate([shift_prev[:, None, :].astype(pr.dtype), pr[:, :-1]], axis=1)
    xl = pr + (pr_prev - pr) * lp["mu_shift"]
    r, k, v, xw, xa, xg = jnp.split(xl, RWKV_SPLITS, axis=-1)
    w_logit = (lp["w0"] + jnp.tanh(xw) @ lp["w2"]).astype(f32)
    decay = jnp.exp(-jnp.exp(-jax.nn.softplus(-w_logit) - 0.5))
    a = jax.nn.sigmoid(lp["a0"] + xa @ lp["a2"])
    g = jax.nn.sigmoid(xg) @ lp["g2"]
    if v_first is None:
        v_first = v
    else:
        v = v + (v_first - v) * jax.nn.sigmoid(lp["v0"] + (v @ lp["v1"]) @ lp["v2"])
    heads = lambda t: t.reshape(B, T, N_HEADS, HEAD_SIZE).astype(f32)
    kk = heads(k * lp["k_k"])
    kk = kk * lax.rsqrt(jnp.sum(kk * kk, axis=-1, keepdims=True) + 1e-12)
    k = k * (1 + (a - 1) * lp["k_a"])
    rh, kh, vh, ah = heads(r), heads(k), heads(v), heads(a)
    y, wkv_new = wkv7_scan(wkv_prev.astype(f32), rh, heads(decay), kh, vh, kk, ah)
    mean = jnp.mean(y, axis=-1, keepdims=True)
    var = jnp.mean(jnp.square(y - mean), axis=-1, keepdims=True)
    y = ((y - mean) * lax.rsqrt(var + GN_EPS)).reshape(B, T, MIX_WIDTH)
    y = y * lp["ln_w"].astype(f32) + lp["ln_b"].astype(f32)
    bonus = jnp.sum(rh * kh * lp["r_k"].astype(f32), axis=-1, keepdims=True) * vh
    y = (y + bonus.reshape(B, T, MIX_WIDTH)).astype(pr.dtype) * g
    return y, pr[:, -1], wkv_new, v_first


def pool_branch(pp, pool_prev, pos0, lp):
    B, T, _ = pp.shape
    buf = jnp.concatenate([pool_prev.astype(pp.dtype), pp], axis=1)
    cs = jnp.pad(jnp.cumsum(buf.astype(jnp.float32), axis=1), ((0, 0), (1, 0), (0, 0)))
    pos = pos0 + jnp.arange(T, dtype=jnp.int32)
    hi = cs[:, POOL_BUF + 1:]
    means = []
    for gi, win in enumerate(POOL_WINDOWS):
        ch = slice(gi * POOL_GROUP, (gi + 1) * POOL_GROUP)
        lo = cs[:, POOL_BUF + 1 - win: POOL_BUF + 1 - win + T, ch]
        cnt = jnp.minimum(pos + 1, win).astype(jnp.float32)[None, :, None]
        means.append((hi[..., ch] - lo) / cnt)
    pooled = jnp.concatenate(means, axis=-1).astype(pp.dtype) - pp
    z = jnp.einsum("btgc,gcd->btgd", pooled.reshape(B, T, N_POOL_GROUPS, POOL_GROUP), lp["pool_w"])
    return z.reshape(B, T, POOL_WIDTH) * lp["pool_scale"], buf[:, -POOL_BUF:]


def trunk(x, c, shift0, pool0, wkv0, pos0, W):
    shifts, pools, wkvs = [], [], []
    v_first = None
    for l in range(DEPTH):
        lp = {name: arr[l] for name, arr in W.items() if name not in V_NAMES}
        if l > 0:
            lp.update({name: W[name][l - 1] for name in V_NAMES})
        sh, sc, gt = ada_mod(c, lp["w_ada_mix"], lp["b_ada_mix"])
        h = rms_norm(x, lp["norm_mix"]) * (1 + sc) + sh
        P = h @ lp["w_in"]
        pr, pp, pg = jnp.split(P, [RWKV_COLS, RWKV_COLS + POOL_WIDTH], axis=-1)
        o_r, s_shift, s_wkv, v_first = rwkv7_branch(pr, shift0[l], wkv0[l], v_first, lp)
        o_p, s_pool = pool_branch(pp, pool0[l], pos0, lp)
        g_r, g_p = jnp.split(jax.nn.sigmoid(pg), 2, axis=-1)
        merged = g_r * (o_r @ lp["w_br_rwkv"]) + g_p * (o_p @ lp["w_br_pool"])
        x = x + gt * (merged @ lp["w_out"])
        sh, sc, gt = ada_mod(c, lp["w_ada_mlp"], lp["b_ada_mlp"])
        h = rms_norm(x, lp["norm_mlp"]) * (1 + sc) + sh
        x = x + gt * (jnp.square(jax.nn.relu(h @ lp["w_ff1"])) @ lp["w_ff2"])
        shifts.append(s_shift)
        pools.append(s_pool)
        wkvs.append(s_wkv)
    y = rms_norm(x, W["norm_final"])
    return y, jnp.stack(shifts), jnp.stack(pools), jnp.stack(wkvs)


def setup_inputs(seed: int = 0) -> dict:
    key = jax.random.key(seed)
    ks = iter(jax.random.split(key, 48))
    nrm = lambda shape, s: jax.random.normal(next(ks), shape, jnp.float32) * s
    uni = lambda shape, lo, hi: jax.random.uniform(next(ks), shape, jnp.float32, lo, hi)
    L, D, M = DEPTH, D_MODEL, MIX_WIDTH
    return {
        "x_prompt": nrm((BATCH, SEQ, D), 1.0),
        "x_sample": nrm((DEC_BATCH, DEC_SEQ, D), 1.0),
        "state_shift": nrm((L, DEC_BATCH, RWKV_COLS), 1.0),
        "state_pool": nrm((L, DEC_BATCH, POOL_BUF, POOL_WIDTH), 1.0),
        "state_wkv": nrm((L, DEC_BATCH, N_HEADS, HEAD_SIZE, HEAD_SIZE), 0.3),
        "c_prompt": nrm((BATCH, D), 1.0),
        "c_sample": nrm((DEC_BATCH, D), 1.0),
        "w_ada_mix": nrm((L, D, 3 * D), 0.5 * D ** -0.5),
        "b_ada_mix": nrm((L, 3 * D), 0.02),
        "norm_mix": 1.0 + nrm((L, D), 0.1),
        "w_in": nrm((L, D, IN_COLS), D ** -0.5),
        "mu_shift": uni((L, RWKV_COLS), 0.0, 1.0),
        "w0": uni((L, M), -6.0, 0.0),
        "w2": nrm((L, D_DECAY_LORA, M), 0.1),
        "a0": nrm((L, M), 0.1),
        "a2": nrm((L, D_AAA_LORA, M), D_AAA_LORA ** -0.5),
        "g2": nrm((L, D_GATE_LORA, M), D_GATE_LORA ** -0.5),
        "v0": nrm((L - 1, M), 0.1),
        "v1": nrm((L - 1, M, D_MV_LORA), M ** -0.5),
        "v2": nrm((L - 1, D_MV_LORA, M), D_MV_LORA ** -0.5),
        "k_k": 0.85 + nrm((L, M), 0.05),
        "k_a": 1.0 + nrm((L, M), 0.1),
        "r_k": nrm((L, N_HEADS, HEAD_SIZE), 0.1),
        "ln_w": 1.0 + nrm((L, M), 0.1),
        "ln_b": nrm((L, M), 0.01),
        "pool_w": nrm((L, N_POOL_GROUPS, POOL_GROUP, POOL_GROUP), POOL_GROUP ** -0.5),
        "pool_scale": 1.0 + nrm((L, POOL_WIDTH), 0.1),
        "w_br_rwkv": nrm((L, M, D), M ** -0.5),
        "w_br_pool": nrm((L, POOL_WIDTH, D), POOL_WIDTH ** -0.5),
        "w_out": nrm((L, D, D), D ** -0.5),
        "w_ada_mlp": nrm((L, D, 3 * D), 0.5 * D ** -0.5),
        "b_ada_mlp": nrm((L, 3 * D), 0.02),
        "norm_mlp": 1.0 + nrm((L, D), 0.1),
        "w_ff1": nrm((L, D, D_FF), D ** -0.5),
        "w_ff2": nrm((L, D_FF, D), D_FF ** -0.5),
        "norm_final": 1.0 + nrm((D,), 0.1),
    }


def reference(x_prompt, x_sample, state_shift, state_pool, state_wkv, c_prompt, c_sample,
              w_ada_mix, b_ada_mix, norm_mix, w_in, mu_shift, w0, w2, a0, a2, g2, v0, v1, v2,
              k_k, k_a, r_k, ln_w, ln_b, pool_w, pool_scale, w_br_rwkv, w_br_pool, w_out,
              w_ada_mlp, b_ada_mlp, norm_mlp, w_ff1, w_ff2, norm_final):
    W = {
        "w_ada_mix": w_ada_mix, "b_ada_mix": b_ada_mix, "norm_mix": norm_mix, "w_in": w_in,
        "mu_shift": mu_shift, "w0": w0, "w2": w2, "a0": a0, "a2": a2, "g2": g2,
        "v0": v0, "v1": v1, "v2": v2, "k_k": k_k, "k_a": k_a, "r_k": r_k,
        "ln_w": ln_w, "ln_b": ln_b, "pool_w": pool_w, "pool_scale": pool_scale,
        "w_br_rwkv": w_br_rwkv, "w_br_pool": w_br_pool, "w_out": w_out,
        "w_ada_mlp": w_ada_mlp, "b_ada_mlp": b_ada_mlp, "norm_mlp": norm_mlp,
        "w_ff1": w_ff1, "w_ff2": w_ff2, "norm_final": norm_final,
    }
    Bp = x_prompt.shape[0]
    shift0 = jnp.zeros((DEPTH, Bp, RWKV_COLS), x_prompt.dtype)
    pool0 = jnp.zeros((DEPTH, Bp, POOL_BUF, POOL_WIDTH), x_prompt.dtype)
    wkv0 = jnp.zeros((DEPTH, Bp, N_HEADS, HEAD_SIZE, HEAD_SIZE), jnp.float32)
    y_prompt, shift_p, pool_p, wkv_p = trunk(x_prompt, c_prompt, shift0, pool0, wkv0, 0, W)
    y_sample, shift_s, pool_s, wkv_s = trunk(x_sample, c_sample, state_shift, state_pool,
                                             state_wkv, PAST_LEN, W)
    return (y_prompt, y_sample, shift_p, pool_p, wkv_p, shift_s, pool_s, wkv_s)
```

```python
import numpy as np
from contextlib import ExitStack
import concourse.bass as bass
import concourse.mybir as mybir
from concourse.bass_utils import run_bass_kernel_spmd

F32 = mybir.dt.float32
BF16 = mybir.dt.bfloat16
AF = mybir.ActivationFunctionType
ALU = mybir.AluOpType

D = 1024
NKC = 8
MIX = 512
RW = 1792
INC = 4352
DFF = 4096
EPS = 1e-6
GN_EPS = 64e-5
DEC_C = -float(np.exp(-0.5))
WINS = (2, 4, 8, 16)


class Sched:
    STREAMS = ('pe', 'act', 'dve', 'pool', 'sp')

    def __init__(self):
        self.ops = {s: [] for s in self.STREAMS}
        self.cnt = {}
        self.known = {s: {} for s in self.STREAMS}
        self.lastw = {}
        self.readers = {}
        self.dma_i = {}

    def op(self, stream, fn, r=(), w=(), sem=None, inc=1, nsem=1):
        sem = sem or stream
        if nsem > 1:
            i = self.dma_i.get(sem, 0)
            self.dma_i[sem] = i + 1
            sem = "%s%d" % (sem, i % nsem)
        need = {}
        if nsem > 1 and self.cnt.get(sem, 0):
            need[sem] = self.cnt[sem]

        def add(s, v):
            if need.get(s, 0) < v:
                need[s] = v
        for b in r:
            if b in self.lastw:
                add(*self.lastw[b])
        for b in w:
            if b in self.lastw:
                add(*self.lastw[b])
            for s, v in self.readers.get(b, {}).items():
                add(s, v)
        waits = []
        kn = self.known[stream]
        for s, v in need.items():
            if stream == 'pe' and s == 'pe':
                continue
            if kn.get(s, 0) < v:
                waits.append((s, v))
                kn[s] = v
        self.cnt[sem] = self.cnt.get(sem, 0) + inc
        val = self.cnt[sem]
        self.ops[stream].append((waits, fn, sem, inc))
        for b in r:
            d = self.readers.setdefault(b, {})
            if d.get(sem, 0) < val:
                d[sem] = val
        for b in w:
            self.lastw[b] = (sem, val)
            self.readers[b] = {}

    def barrier(self, streams=('pe', 'act', 'dve', 'sp')):
        for s in streams:
            waits = []
            for sem in list(self.cnt):
                if sem.startswith('pq'):
                    continue
                v = self.cnt.get(sem, 0)
                if s == 'pe' and sem == 'pe':
                    continue
                if v and self.known[s].get(sem, 0) < v:
                    waits.append((sem, v))
                    self.known[s][sem] = v
            if waits:
                self.ops[s].append((waits, None, None, 0))


def build(TP=2048, L=4):
    NTOK = TP + 128
    NCH = NTOK // 128
    NPC = TP // 128
    nc = bass.Bass("TRN2", target_bir_lowering=False)
    S = Sched()
    st = ExitStack()

    def din(name, shape, dt=F32):
        return nc.dram_tensor(name, list(shape), dt, kind="ExternalInput").ap()

    def dout(name, shape, dt=F32):
        return nc.dram_tensor(name, list(shape), dt, kind="ExternalOutput").ap()

    xp = din("xp", [TP, D]); xs = din("xs", [128, D]); cc = din("cc", [17, D])
    sshift = din("sshift", [L, 16, RW]); spool = din("spool", [L, 16, 15, MIX])
    swkv = din("swkv", [L, 16, 8, 64, 64])
    W = {}
    LV = max(L - 1, 1)
    for nm, shp in [("w_ada_mix", [L, D, 3 * D]), ("b_ada_mix", [L, 3 * D]), ("norm_mix", [L, D]),
                    ("w_in", [L, D, INC]), ("mu_shift", [L, RW]), ("w0", [L, MIX]), ("w2", [L, 64, MIX]),
                    ("a0", [L, MIX]), ("a2", [L, 64, MIX]), ("g2", [L, 128, MIX]), ("v0", [LV, MIX]),
                    ("v1", [LV, MIX, 32]), ("v2", [LV, 32, MIX]), ("k_k", [L, MIX]),
                    ("k_a", [L, MIX]), ("r_k", [L, MIX]), ("ln_w", [L, MIX]), ("ln_b", [L, MIX]),
                    ("pool_w", [L, 4, 128, 128]), ("pool_scale", [L, MIX]), ("w_br_rwkv", [L, MIX, D]),
                    ("w_br_pool", [L, MIX, D]), ("w_out", [L, D, D]), ("w_ada_mlp", [L, D, 3 * D]),
                    ("b_ada_mlp", [L, 3 * D]), ("norm_mlp", [L, D]), ("w_ff1", [L, D, DFF]),
                    ("w_ff2", [L, DFF, D]), ("norm_final", [1, D])]:
        W[nm] = din(nm, shp)
    cst = din("cst", [128, CF_COLS + CB_COLS])
    yp = dout("yp", [TP, D]); ys = dout("ys", [128, D])
    nshp = dout("nshp", [L, RW]); npoolp = dout("npoolp", [L, 15, MIX]); nwkvp = dout("nwkvp", [L, 8, 64, 64])
    nshs = dout("nshs", [L, 16, RW]); npools = dout("npools", [L, 16 * 15, MIX])
    nwkvs = dout("nwkvs", [L, 16, 8, 64, 64])
    vfd = nc.dram_tensor("vfirst_scr", [128, 4, NTOK], BF16, kind="Internal").ap()
    orpd = nc.dram_tensor("orp_scr", [128, 8, NTOK], BF16, kind="Internal").ap()

    NW = 53200
    big = st.enter_context(nc.sbuf_tensor("big", [128, NW], F32))
    ptr = [0]

    def alloc(shape, dt=F32):
        n = int(np.prod(shape))
        words = n if dt == F32 else (n + 1) // 2
        words = (words + 7) // 8 * 8
        o = ptr[0]
        ptr[0] += words
        assert ptr[0] <= NW, ("SBUF arena overflow", ptr[0], NW)
        v = big[:, o:o + words]
        if dt != F32:
            v = v.bitcast(dt)
        v = v[:, 0:n]
        if len(shape) == 1:
            return v
        names = " ".join("d%d" % i for i in range(len(shape)))
        kw = {"d%d" % i: int(shape[i]) for i in range(len(shape) - 1)}
        return v.rearrange("p (%s) -> p %s" % (names, names), **kw)

    SD = F32
    NPI = 2
    HP = 2 * NPI
    xT = alloc([NKC, NTOK])
    ring = alloc([6, 4096], BF16)
    cF = alloc([CF_COLS]); cB = alloc([CB_COLS], BF16)
    pcol = alloc([L, 128])
    omm = alloc([14]); omka = alloc([4])
    siluT = alloc([NKC, 17], BF16)
    modv = alloc([24, 17]); G32 = alloc([NKC, 17]); G32f = alloc([NKC, 1])
    lw_small = alloc([2176], BF16)
    H0f = alloc([4, 128]); H0b = H0f
    prcarry = alloc([14, 1])
    UBASE = ptr[0]

    PB = [st.enter_context(nc.psum_tensor("pb%d" % i, [128, 512], F32)) for i in range(8)]
    NROT = 6
    pbi = [0]
    pbt_i = [0]

    def pbank():
        i = pbi[0] % NROT
        pbi[0] += 1
        return PB[i], ('pb', i)
    ZB, ZK = PB[6], ('pb', 6)
    YB, YK = PB[7], ('pb', 7)

    def PE(fn, r, w): S.op('pe', fn, r, w)
    def ACT(fn, r, w): S.op('act', fn, r, w)
    def DVE(fn, r, w): S.op('dve', fn, r, w)
    def SPD(fn, r, w): S.op('sp', fn, r, w, sem='sp', inc=16, nsem=16)
    def PQD(fn, r, w): S.op('pool', fn, r, w, sem='pq', inc=16, nsem=8)

    def mm(out, lhsT, rhs, start, stop, r, w):
        PE(lambda e: e.matmul(out, lhsT, rhs, start=start, stop=stop, skip_group_check=True), r, w)

    def tr(out, in_, ident, r, w):
        PE(lambda e: e.transpose(out, in_, ident), r, w)

    def act(out, in_, func, r, w, bias=0.0, scale=1.0):
        ACT(lambda e: e.activation(out, in_, func, bias=bias, scale=scale), r, w)

    def acopy(out, in_, r, w):
        ACT(lambda e: e.copy(out, in_), r, w)

    def vcopy(out, in_, r, w):
        DVE(lambda e: e.tensor_copy(out, in_), r, w)

    def tt(out, a, b, op, r, w):
        DVE(lambda e: e.tensor_tensor(out, a, b, op), r, w)

    def ts(out, a, s1, s2, op0, op1, r, w):
        DVE(lambda e: e.tensor_scalar(out, a, s1, s2, op0, op1), r, w)

    def stt(out, a, s, b, op0, op1, r, w):
        DVE(lambda e: e.scalar_tensor_tensor(out, a, s, b, op0, op1), r, w)

    def dma(out, in_, r, w):
        SPD(lambda e: e.dma_start(out=out, in_=in_), r, w)

    def rsq(out, in_, mulc, addc, r, w):
        ts(out, in_, mulc, addc, ALU.mult, ALU.add, r, w)
        act(out, out, AF.Ln, w, w)
        act(out, out, AF.Exp, w, w, scale=-0.5)

    def xk(ci): return [('x', ci)]
    f2 = lambda t: t.rearrange("p a b -> p (a b)")
    v4 = lambda t: t.rearrange("p (a b) -> p a b", a=4)

    def cf(name, lo=0, hi=None):
        o, n = CSTF[name]
        return cF[:, o + lo:o + (n if hi is None else hi)]

    def cb(name, lo=0, hi=None):
        o, n = CSTB[name]
        return cB[:, o + lo:o + (n if hi is None else hi)]
    SD = F32
    identF = cf('ident'); identB = cb('ident'); identS = identF if SD == F32 else identB; onesB = cb('ones'); bonesF = cf('bones'); onesF = cf('ones')

    ptr[0] = UBASE
    pstage = alloc([L, 128]); cst17 = alloc([D]); xin = alloc([2, D])
    dma(cF[:], cst[:, 0:CF_COLS], [], ['cF'])
    PQD(lambda e: e.dma_start(out=cB[:], in_=cst[:, CF_COLS:CF_COLS + CB_COLS]), [], ['cB'])
    DVE(lambda e: e.memset(pstage[:], 0.0), [], ['pstage'])
    PROW = {}
    ro = 0
    for nm, nchk in [("norm_mix", 8), ("norm_mlp", 8), ("mu_shift", 14), ("w0", 4), ("a0", 4), ("v0", 4),
                     ("k_k", 4), ("k_a", 4), ("r_k", 4), ("ln_w", 4), ("ln_b", 4), ("pool_scale", 4),
                     ("b_ada_mix", 24), ("b_ada_mlp", 24)]:
        PROW[nm] = ro
        src = W[nm]
        if nm == "v0":
            if L > 1:
                dma(pstage[ro:ro + nchk, 1:L, :], src[0:L - 1, :].rearrange("l (c p) -> c l p", p=128), ['pstage'], ['pstage'])
        else:
            dma(pstage[ro:ro + nchk, 0:L, :], src.rearrange("l (c p) -> c l p", p=128), ['pstage'], ['pstage'])
        ro += nchk
    assert ro <= 120
    dma(pstage[120:128, 0, :], W["norm_final"].rearrange("o (c p) -> (o c) p", p=128), ['pstage'], ['pstage'])
    for l in range(L):
        pb, pk = pbank()
        tr(pb[:, 0:128], pstage[:, l, :], identF, ['pstage', 'cF'], [pk])
        acopy(pcol[:, l, :], pb[:, 0:128], [pk], ['pcol'])

    def pc(l, nm, c): return pcol[:, l, PROW[nm] + c:PROW[nm] + c + 1]
    def pcs(l, nm, n): return pcol[:, l, PROW[nm]:PROW[nm] + n]

    dma(cst17[0:17, :], cc[:, :], [], ['cst17'])
    pb, pk = pbank()
    for kc in range(NKC):
        tr(pb[:, kc * 17:(kc + 1) * 17], cst17[0:17, kc * 128:(kc + 1) * 128], identF[0:17, 0:17], ['cst17', 'cF'], [pk])
    act(f2(siluT[:]), pb[:, 0:NKC * 17], AF.Silu, [pk], ['siluT'])

    for ci in range(NCH):
        src = xp[ci * 128:(ci + 1) * 128, :] if ci < NPC else xs[:, :]
        xb_ = xin[:, ci % 2, :]
        dma(xb_, src, [], [('xin', ci % 2)])
        for half in range(2):
            pb, pk = pbank()
            for q in range(4):
                c = half * 4 + q
                tr(pb[:, q * 128:(q + 1) * 128], xb_[:, c * 128:(c + 1) * 128], identF, [('xin', ci % 2), 'cF'], [pk])
            acopy(xT[:, half * 4:half * 4 + 4, ci * 128:(ci + 1) * 128], v4(pb[:, 0:512]), [pk], xk(ci))

    ring_i = [0]

    def wload(src3, a, b):
        i = ring_i[0] % 6
        ring_i[0] += 1
        dst = ring[:, i, 0:a * b].rearrange("p (a b) -> p a b", a=a)
        PQD(lambda e: e.dma_start(out=dst, in_=src3), [], [('ring', i)])
        return dst, ('ring', i)

    def wl_kc(src2, ncol):
        return wload(src2.rearrange("(kc p) n -> p kc n", p=128), src2.shape[0] // 128, ncol)

    def ada(l, which):
        wsrc = W["w_ada_" + which]
        pbm, pkm = pbank()
        for j in range(6):
            wt, wkey = wl_kc(wsrc[l, :, j * 512:(j + 1) * 512], 512)
            for q in range(4):
                ch = j * 4 + q
                for kc in range(NKC):
                    mm(pbm[:, ch * 17:(ch + 1) * 17], wt[:, kc, q * 128:(q + 1) * 128], siluT[:, kc, :],
                       kc == 0, kc == NKC - 1, [wkey, 'siluT'], [pkm])
        tt(modv[:], pbm[:, 0:408].rearrange("p (a b) -> p a b", b=17),
           pcs(l, "b_ada_" + which, 24).unsqueeze(2).to_broadcast([128, 24, 17]), ALU.add, [pkm, 'pcol'], ['mod'])
        ts(G32[:], modv[:, 8:16, :], 1.0, 32.0, ALU.add, ALU.mult, ['mod'], ['mod'])
        tt(G32[:], G32[:], pcs(l, "norm_" + which, 8).unsqueeze(2).to_broadcast([128, 8, 17]), ALU.mult, ['mod', 'pcol'], ['mod'])

    WK = {}

    def xks(t0, n): return [('x', c) for c in range(t0 // 128, (t0 + n) // 128)]

    def tiles(size):
        out = []
        t = 0
        while t < TP:
            n = min(size, TP - t)
            out.append((t, n, False))
            t += n
        out.append((TP, 128, True))
        return out

    def norm_mod_t(t0, n, samp, hdst, hkeys):
        sqr, rstd, tmpn = WK['sq'], WK['rstd'], WK['tmpn']
        pb, pk = pbank()
        for c in range(NKC):
            act(sqr[:, c % 2, 0:n], xT[:, c, t0:t0 + n], AF.Square, xks(t0, n), [('sq', c % 2)])
            mm(pb[:, 0:n], onesB, sqr[:, c % 2, 0:n], c == 0, c == NKC - 1, [('sq', c % 2), 'cB'], [pk])
        rsq(rstd[:, 0:n], pb[:, 0:n], 1.0, D * EPS, [pk], ['rstd'])
        for c in range(NKC):
            tb = tmpn[:, c % 2, 0:n]
            tk = ('tmpn', c % 2)
            if not samp:
                stt(tb, xT[:, c, t0:t0 + n], G32[:, c, 0:1], rstd[:, 0:n], ALU.mult, ALU.mult, xks(t0, n) + ['mod', 'rstd'], [tk])
                act(hdst[:, c, 0:n], tb, AF.Identity, [tk, 'mod'], hkeys, bias=modv[:, c, 0:1])
            else:
                tt(tb, xT[:, c, t0:t0 + n], rstd[:, 0:n], ALU.mult, xks(t0, n) + ['rstd'], [tk])
                t3 = tb.rearrange("p (b t) -> p b t", t=8)
                tt(t3, t3, G32[:, c, 1:17].unsqueeze(2).to_broadcast([128, 16, 8]), ALU.mult, [tk, 'mod'], [tk])
                tt(hdst[:, c, 0:n].rearrange("p (b t) -> p b t", t=8), t3,
                   modv[:, c, 1:17].unsqueeze(2).to_broadcast([128, 16, 8]), ALU.add, [tk, 'mod'], hkeys)

    def norm_mod(ci, samp, hdst, hkeys):
        norm_mod_t(ci * 128, 128, samp, hdst, hkeys)

    def resid_update_t(t0, n, samp, c, pb, pk):
        xv = xT[:, c, t0:t0 + n]
        mg = WK['mg']
        if not samp:
            stt(xv, pb[:, 0:n], modv[:, 16 + c, 0:1], xv, ALU.mult, ALU.add, [pk, 'mod'] + xks(t0, n), xks(t0, n))
        else:
            tt(mg[:, 0:128].rearrange("p (b t) -> p b t", t=8), pb[:, 0:128].rearrange("p (b t) -> p b t", t=8),
               modv[:, 16 + c, 1:17].unsqueeze(2).to_broadcast([128, 16, 8]), ALU.mult, [pk, 'mod'], ['mg'])
            tt(xv, xv, mg[:, 0:128], ALU.add, ['mg'] + xks(t0, n), xks(t0, n))

    for l in range(L):
        S.barrier()
        ptr[0] = UBASE
        WK['sq'] = alloc([2, 128], BF16); WK['rstd'] = alloc([128]); WK['tmpn'] = alloc([2, 128]); WK['mg'] = alloc([128])
        hTc = alloc([NKC, 129], BF16); hT = hTc[:, :, 1:129]
        rkv = alloc([12, 128])
        twa = alloc([128], BF16); sgg = alloc([128], BF16); t1b = alloc([128], BF16)
        gbf = alloc([4, 128], BF16); vbf = alloc([4, 128], BF16); vfb = alloc([4, 128], BF16)
        FMB = [alloc([4, 128]) for _ in range(8)]
        KR = alloc([4, 256], SD)
        Bt = alloc([4, 128], SD); Kt = alloc([4, 128], SD); BWf = alloc([4, 128], SD); KWf = alloc([4, 128], SD)
        Vtok = alloc([4, 128], SD); BWtok = alloc([4, 128], SD); KWtok = alloc([4, 128], SD)
        AT = alloc([HP, 384], SD); MA = alloc([HP, 256], SD); MB = alloc([HP, 256], SD)
        PT = alloc([HP, 128], SD)
        Zn = alloc([NPI, 128], SD); Ut = alloc([NPI, 128], SD)
        wcs = alloc([4, 16])
        pz = vbf; orp = alloc([8, 128], BF16)
        shiftT = alloc([14, 16]); ppbS = alloc([4, 16, 23])
        ppb = ppbS.rearrange("p a b c -> p (a b c)")[:, 0:576].rearrange("p (a b) -> p a b", a=4)
        A_, LW_, X1, X2, X3, X4, X5, X6 = FMB
        DIAG = X1; WcBC = X2; SCo = X4[:, :, 0:64]
        S0g = H0f; HSb = A_; BKK = LW_; BRR = X3; BIGB = BWf; BIGK = KWf
        tsh = f2(X1[:])[:, 0:272].rearrange("p (a b) -> p a b", a=2)
        xl12 = X2[:, 0:2, :]
        pq2 = f2(X1[:])[:, 0:144]; pq4 = f2(X2[:])[:, 0:144]

        ts(omm[:], pcs(l, "mu_shift", 14), -1.0, 1.0, ALU.mult, ALU.add, ['pcol'], ['omm'])
        ts(omka[:], pcs(l, "k_a", 4), -1.0, 1.0, ALU.mult, ALU.add, ['pcol'], ['omm'])
        PQD(lambda e, l=l: e.dma_start(out=lw_small[0:64, 0:512], in_=W["w2"][l]), [], ['lws'])
        PQD(lambda e, l=l: e.dma_start(out=lw_small[64:128, 0:512], in_=W["a2"][l]), [], ['lws'])
        PQD(lambda e, l=l: e.dma_start(out=lw_small[:, 512:1024], in_=W["g2"][l]), [], ['lws'])
        if l > 0:
            PQD(lambda e, l=l: e.dma_start(out=lw_small[:, 1024:1152].rearrange("p (a b) -> p a b", a=4),
                                           in_=W["v1"][l - 1].rearrange("(kc p) n -> p kc n", p=128)), [], ['lws'])
            PQD(lambda e, l=l: e.dma_start(out=lw_small[0:32, 1152:1664], in_=W["v2"][l - 1]), [], ['lws'])
        PQD(lambda e, l=l: e.dma_start(out=lw_small[:, 1664:2176].rearrange("p (g d) -> p g d", g=4),
                                       in_=W["pool_w"][l].rearrange("g c d -> c g d")), [], ['lws'])
        w2a2 = lw_small[:, 0:512]; g2w = lw_small[:, 512:1024]
        v1w = lw_small[:, 1024:1152].rearrange("p (a b) -> p a b", a=4); v2w = lw_small[0:32, 1152:1664]
        poolw = lw_small[:, 1664:2176].rearrange("p (g d) -> p g d", g=4)

        ada(l, "mix")
        DVE(lambda e: e.memset(ppb[:, :, 0:15], 0.0), [], ['ppb'])
        DVE(lambda e: e.memset(hTc[:, :, 0:1], 0.0), ['hT'], ['hT'])

        W1 = []
        for j in range(5):
            ncol = 512 if j < 4 else 256
            W1.append(wl_kc(W["w_in"][l, :, j * 512:j * 512 + ncol], ncol))

        for ci in range(NCH):
            samp = ci >= NPC
            t0 = ci * 128
            first_chunk = ci == 0
            last_chunk = ci == NPC - 1
            nb = 16 if samp else 1
            blk = 128 // nb
            mset = 'S' if samp else 'P'
            norm_mod(ci, samp, hT, ['hT'])
            if samp:
                pbs_, pks_ = pbank()
                for g0 in range(0, 14, 4):
                    gn = min(4, 14 - g0)
                    dma(X6[0:16, :, :].rearrange("p a b -> p (a b)")[:, 0:gn * 128], sshift[l, :, g0 * 128:(g0 + gn) * 128], [], ['X6'])
                    for c in range(g0, g0 + gn):
                        tr(pbs_[:, c * 16:(c + 1) * 16], f2(X6[0:16, :, :])[:, (c - g0) * 128:(c - g0 + 1) * 128],
                           identF[0:16, 0:16], ['X6', 'cF'], [pks_])
                acopy(f2(shiftT[:]), pbs_[:, 0:224], [pks_], ['shiftT'])
            for cidx in range(18):
                wt, wkey = W1[cidx // 4]
                q = cidx % 4
                pb, pk = pbank()
                for kc in range(NKC):
                    if samp:
                        mm(pb[:, 0:128], wt[:, kc, q * 128:(q + 1) * 128], hT[:, kc, :], kc == 0, kc == NKC - 1, [wkey, 'hT'], [pk])
                    else:
                        mm(pb[:, 0:129], wt[:, kc, q * 128:(q + 1) * 128], hTc[:, kc, 0:129], kc == 0, kc == NKC - 1, [wkey, 'hT'], [pk])
                if cidx >= 14:
                    g = cidx - 14
                    if not samp:
                        acopy(ppb[:, g, 15:143], pb[:, 1:129], [pk], ['ppb'])
                    else:
                        acopy(ppbS[:, g, :, 15:23], pb[:, 0:128].rearrange("p (b t) -> p b t", t=8), [pk], ['ppbS', 'ppb'])
                    continue
                c = cidx
                tb = tsh[:, c % 2, :]
                tk = 'X1'
                mu = pc(l, "mu_shift", c)
                dst = rkv[:, c, :] if c < 12 else xl12[:, c - 12, :]
                dkey = 'rkv' if c < 12 else 'X2'
                if not samp:
                    act(tb[:, 0:129], pb[:, 0:129], AF.Identity, [pk, 'pcol'], [tk], scale=mu)
                    if last_chunk:
                        acopy(prcarry[:, c, :], pb[:, 128:129], [pk], ['prcarry'])
                    stt(dst, pb[:, 1:129], omm[:, c:c + 1], tb[:, 0:128], ALU.mult, ALU.add, [pk, 'omm', tk], [dkey])
                else:
                    p3 = pb[:, 0:128].rearrange("p (b t) -> p b t", t=8)
                    t3 = tb[:, 0:128].rearrange("p (b t) -> p b t", t=8)
                    act(t3[:, :, 1:8], p3[:, :, 0:7], AF.Identity, [pk, 'pcol'], [tk], scale=mu)
                    act(t3[:, :, 0:1], shiftT[:, c, :].unsqueeze(2), AF.Identity, ['shiftT', 'pcol'], [tk], scale=mu)
                    acopy(f2(X5[:])[:, c * 16:(c + 1) * 16].unsqueeze(2), p3[:, :, 7:8], [pk], ['X5'])
                    stt(dst, pb[:, 0:128], omm[:, c:c + 1], tb[:, 0:128], ALU.mult, ALU.add, [pk, 'omm', tk], [dkey])
            if not samp and not last_chunk:
                acopy(hTc[:, :, 0:1], hTc[:, :, 128:129], ['hT'], ['hT'])
            if last_chunk:
                pb, pk = pbank()
                tr(pb[0:14, 0:128], f2(prcarry[:]), identF, ['prcarry', 'cF'], [pk])
                acopy(f2(X6[0:14, :, :])[:, 0:128], pb[0:14, 0:128], [pk], ['X6'])
                dma(nshp[l].rearrange("(c p) -> c p", p=128), f2(X6[0:14, :, :])[:, 0:128], ['X6'], [])
            if samp:
                for g0 in range(0, 14, 4):
                    gn = min(4, 14 - g0)
                    pbx, pkx = pbank()
                    for c in range(g0, g0 + gn):
                        tr(pbx[0:16, (c - g0) * 128:(c - g0 + 1) * 128], f2(X5[:])[:, c * 16:(c + 1) * 16], identF, ['X5', 'cF'], [pkx])
                    ob = [X3, X4][(g0 // 4) % 2]
                    okey = ['X3', 'X4'][(g0 // 4) % 2]
                    acopy(f2(ob[0:16, :, :])[:, 0:gn * 128], pbx[0:16, 0:gn * 128], [pkx], [okey])
                    dma(nshs[l, :, g0 * 128:(g0 + gn) * 128], f2(ob[0:16, :, :])[:, 0:gn * 128], [okey], [])
            act(twa[0:64, :], xl12[0:64, 0, :], AF.Tanh, ['X2'], ['twa'])
            acopy(twa[64:128, :], xl12[64:128, 0, :], ['X2'], ['twa'])
            act(sgg[:], xl12[:, 1, :], AF.Sigmoid, ['X2'], ['sgg'])
            r_c = rkv[:, 0:4, :]; k_c = rkv[:, 4:8, :]; v_c = rkv[:, 8:12, :]
            for p in range(4):
                pb, pk = pbank()
                mm(pb[:, 0:128], w2a2[0:64, p * 128:(p + 1) * 128], twa[0:64, :], True, True, ['lws', 'twa'], [pk])
                act(LW_[:, p, :], pb[:, 0:128], AF.Sigmoid, [pk, 'pcol'], ['LW'], bias=pc(l, "w0", p))
                pb, pk = pbank()
                mm(pb[:, 0:128], w2a2[64:128, p * 128:(p + 1) * 128], twa[64:128, :], True, True, ['lws', 'twa'], [pk])
                act(A_[:, p, :], pb[:, 0:128], AF.Sigmoid, [pk, 'pcol'], ['A'], bias=pc(l, "a0", p))
                pb, pk = pbank()
                mm(pb[:, 0:128], g2w[:, p * 128:(p + 1) * 128], sgg[:], True, True, ['lws', 'sgg'], [pk])
                acopy(gbf[:, p, :], pb[:, 0:128], [pk], ['gbf'])
            ts(LW_[:], LW_[:], DEC_C, None, ALU.mult, ALU.bypass, ['LW'], ['LW'])
            bc4 = lambda col: col.unsqueeze(2).to_broadcast([128, 4, 128])
            d0 = cf('rmS') if samp else onesF
            for p in range(4):
                DVE(lambda e, p=p, d0=d0: e.tensor_tensor_scan(X2[:, p, :], d0, LW_[:, p, :], 0.0, ALU.mult, ALU.add),
                    ['LW', 'cF'], ['X2'])
            tt(X1[:], X2[:], LW_[:], ALU.subtract, ['X2', 'LW'], ['X1'])
            act(X1[:], X1[:], AF.Exp, ['X1'], ['X1'])
            act(LW_[:], X2[:], AF.Exp, ['X2'], ['LW'])
            ein4 = LW_[:].rearrange("p a (b t) -> p a b t", t=blk)
            vcopy(wcs[:, :, 0:nb].unsqueeze(3), ein4[:, :, :, blk - 1:blk], ['LW'], ['wcs'])
            act(X2[:], X2[:], AF.Exp, ['X2'], ['X2'], scale=-1.0)
            if l == 0:
                acopy(vbf[:], v_c, ['rkv'], ['vbf'])
                dma(vfd[:, :, t0:t0 + 128], vbf[:], ['vbf'], [('vfd', ci)])
            else:
                dma(vfb[:], vfd[:, :, t0:t0 + 128], [('vfd', ci)], ['vfb'])
                acopy(vbf[:], v_c, ['rkv'], ['vbf'])
                pb, pk = pbank()
                for p in range(4):
                    mm(pb[0:32, 0:128], v1w[:, p, :], vbf[:, p, :], p == 0, p == 3, ['lws', 'vbf'], [pk])
                acopy(t1b[0:32, :], pb[0:32, 0:128], [pk], ['t1b'])
                for p in range(4):
                    pb, pk = pbank()
                    mm(pb[:, 0:128], v2w[:, p * 128:(p + 1) * 128], t1b[0:32, :], True, True, ['lws', 't1b'], [pk])
                    act(X3[:, p, :], pb[:, 0:128], AF.Sigmoid, [pk, 'pcol'], ['X3'], bias=pc(l, "v0", p))
                tt(X4[:], vfb[:], v_c, ALU.subtract, ['vfb', 'rkv'], ['X4'])
                tt(X4[:], X4[:], X3[:], ALU.mult, ['X4', 'X3'], ['X4'])
                tt(v_c, v_c, X4[:], ALU.add, ['rkv', 'X4'], ['rkv'])

            tt(BWf[:], k_c, bc4(pcs(l, "k_k", 4)), ALU.mult, ['rkv', 'pcol'], ['BWf'])
            tt(KWf[:], BWf[:], BWf[:], ALU.mult, ['BWf'], ['KWf'])
            pb, pk = pbank()
            mm(pb[:, 0:512], bonesF, f2(KWf[:]), True, True, ['KWf', 'cF'], [pk])
            rsq(KWf[:], v4(pb[:, 0:512]), 1.0, 1e-12, [pk], ['KWf'])
            tt(X3[:], BWf[:], KWf[:], ALU.mult, ['BWf', 'KWf'], ['X3'])
            tt(X4[:], X3[:], A_[:], ALU.mult, ['X3', 'A'], ['X4'])
            tt(BWf[:], A_[:], bc4(pcs(l, "k_a", 4)), ALU.mult, ['A', 'pcol'], ['BWf'])
            tt(BWf[:], BWf[:], bc4(omka[:, 0:4]), ALU.add, ['BWf', 'omm'], ['BWf'])
            tt(X5[:], k_c, BWf[:], ALU.mult, ['rkv', 'BWf'], ['X5'])
            tt(BWf[:], r_c, X5[:], ALU.mult, ['rkv', 'X5'], ['BWf'])
            tt(BWf[:], BWf[:], bc4(pcs(l, "r_k", 4)), ALU.mult, ['BWf', 'pcol'], ['BWf'])
            pb, pk = pbank()
            mm(pb[:, 0:512], bonesF, f2(BWf[:]), True, True, ['BWf', 'cF'], [pk])
            tt(X6[:], v4(pb[:, 0:512]), v_c, ALU.mult, [pk, 'rkv'], ['X6'])
            tt(KR[:, :, 0:128], X3[:], X1[:], ALU.mult, ['X3', 'X1'], ['KR'])
            tt(KR[:, :, 128:256], r_c, LW_[:], ALU.mult, ['rkv', 'LW'], ['KR'])
            wcb = wcs[:, :, 0:nb].unsqueeze(3).to_broadcast([128, 4, nb, blk])
            b4 = lambda t: t.rearrange("p a (b t) -> p a b t", t=blk)
            tt(X3[:], X4[:], X2[:], ALU.mult, ['X4', 'X2'], ['X3'])
            acopy(Bt[:], X3[:], ['X3'], ['Bt'])
            tt(b4(BWf[:]), b4(X3[:]), wcb, ALU.mult, ['X3', 'wcs'], ['BWf'])
            tt(X4[:], X5[:], X2[:], ALU.mult, ['X5', 'X2'], ['X4'])
            acopy(Kt[:], X4[:], ['X4'], ['Kt'])
            tt(b4(KWf[:]), b4(X4[:]), wcb, ALU.mult, ['X4', 'wcs'], ['KWf'])
            for (srcb, dstb, sk, dk) in ((v_c, Vtok, 'rkv', 'Vtok'), (BWf, BWtok, 'BWf', 'BWtok'), (KWf, KWtok, 'KWf', 'KWtok')):
                pbb, pkb = pbank()
                for p in range(4):
                    tr(pbb[:, p * 128:(p + 1) * 128], srcb[:, p, :], identS, [sk, 'cF', 'cB'], [pkb])
                acopy(f2(dstb[:]), pbb[:, 0:512], [pkb], [dk])

            if samp:
                DVE(lambda e: e.memset(S0g[:], 0.0), ['H0f'], ['H0f'])
                DVE(lambda e: e.memset(BKK[:], 0.0), ['LW'], ['LW'])
                DVE(lambda e: e.memset(BRR[:], 0.0), ['X3'], ['X3'])
            for hf in range(4 // NPI):
                for q in range(HP):
                    hd = hf * HP + q
                    p, hh = hd // 2, hd % 2
                    hs = slice(hh * 64, hh * 64 + 64)
                    pb, pk = pbank()
                    mm(pb[:, 0:256], Bt[hs, p, :], KR[hs, p, :], True, True, ['Bt', 'KR'], [pk])
                    mm(pb[:, 256:512], Kt[hs, p, :], KR[hs, p, :], True, True, ['Kt', 'KR'], [pk])
                    tt(MB[:, q, 128:256], pb[:, 0:128], cb('mA' + mset, 0, 128), ALU.mult, [pk, 'cB'], ['MB'])
                    tt(AT[:, q, :], pb[:, 128:512], cb('mA' + mset, 128, 512), ALU.mult, [pk, 'cB'], ['AT'])
                pbA, pkA = pbank()
                pbB, pkB = pbank()
                for q in range(HP):
                    hd = hf * HP + q
                    p, hh = hd // 2, hd % 2
                    hs = slice(hh * 64, hh * 64 + 64)
                    pbx, pkx = (pbA, pkA) if hh == 0 else (pbB, pkB)
                    mm(pbx[:, (q // 2) * 128:(q // 2 + 1) * 128], KR[hs, p, 0:128], Bt[hs, p, :], True, True, ['KR', 'Bt'], [pkx])
                for hh, (pbx, pkx) in enumerate(((pbA, pkA), (pbB, pkB))):
                    tt(MB[:, hh:HP:2, 0:128], pbx[:, 0:NPI * 128].rearrange("p (a b) -> p a b", a=NPI),
                       cb('mL' + mset).unsqueeze(1).to_broadcast([128, NPI, 128]), ALU.mult, [pkx, 'cB'], ['MB'])
                tt(PT[:], MB[:, :, 128:256], identF.unsqueeze(1).to_broadcast([128, HP, 128]), ALU.add, ['MB', 'cF'], ['PT'])
                if hf == 0 and not samp:
                    PA = bass.AP(tensor=X1.tensor, offset=X1.offset, ap=[list(X1.ap[0]), [144, 4], [1, 144]])
                    PBs = bass.AP(tensor=A_.tensor, offset=A_.offset, ap=[list(A_.ap[0]), [144, 4], [1, 144]])
                    ka, kb = ['X1', 'X2'], ['A', 'LW']
                    tt(PA[:, :, 1:143], ppb[:, :, 1:143], ppb[:, :, 0:142], ALU.add, ['ppb'], ka)
                    tt(PBs[:, 1:4, 3:143], PA[:, 1:4, 3:143], PA[:, 1:4, 1:141], ALU.add, ka, kb)
                    tt(PA[:, 2:4, 7:143], PBs[:, 2:4, 7:143], PBs[:, 2:4, 3:139], ALU.add, kb + ka, ka)
                    tt(PBs[:, 3:4, 15:143], PA[:, 3:4, 15:143], PA[:, 3:4, 7:135], ALU.add, ka + kb, kb)
                    psrc = [PA[:, 0, :], PBs[:, 1, :], PA[:, 2, :], PBs[:, 3, :]]
                    for g in range(4):
                        if first_chunk:
                            mg = WK['mg']
                            stt(mg[:], psrc[g][:, 15:143], 1.0 / WINS[g], ppb[:, g, 15:143], ALU.mult, ALU.subtract, ka + kb + ['ppb'], ['mg'])
                            tt(mg[:, 0:16], psrc[g][:, 15:31], cf('pcorr')[:, g * 16:(g + 1) * 16], ALU.mult, ka + kb + ['cF', 'mg'], ['mg'])
                            tt(mg[:, 0:16], mg[:, 0:16], ppb[:, g, 15:31], ALU.subtract, ['mg', 'ppb'], ['mg'])
                            vcopy(pz[:, g, :], mg[:], ['mg'], ['vbf'])
                        else:
                            stt(pz[:, g, :], psrc[g][:, 15:143], 1.0 / WINS[g], ppb[:, g, 15:143], ALU.mult, ALU.subtract, ka + kb + ['ppb'], ['vbf'])
                    if not last_chunk:
                        vcopy(PA[:, :, 0:15], ppb[:, :, 128:143], ['ppb'] + ka, ka)
                        vcopy(ppb[:, :, 0:15], PA[:, :, 0:15], ka + ['ppb'], ['ppb'])
                nlev = 2 if samp else 6
                curM = lambda q: MB[:, q, 0:128]
                curMT = lambda q: MB[:, q, 128:256]
                curk = ['MB']
                for lev in range(nlev):
                    nxt, nk = (MA, 'MA') if lev % 2 == 0 else (MB, 'MB')
                    lastlev = lev == nlev - 1
                    for h2 in range(NPI):
                        pb, pk = pbank()
                        for qq in range(2):
                            q = h2 * 2 + qq
                            mm(pb[:, qq * 256:qq * 256 + 128], curMT(q), curM(q), True, True, curk, [pk])
                            if not lastlev:
                                mm(pb[:, qq * 256 + 128:qq * 256 + 256], curM(q), curMT(q), True, True, curk, [pk])
                        acopy(nxt[:, h2 * 2:h2 * 2 + 2, :], pb[:, 0:512].rearrange("p (a b) -> p a b", a=2), [pk], [nk])
                    pb, pk = pbank()
                    for q in range(HP):
                        mm(pb[:, q * 128:(q + 1) * 128], nxt[:, q, 0:128], PT[:, q, :], True, True, [nk, 'PT'], [pk])
                    tt(PT[:], PT[:], pb[:, 0:HP * 128].rearrange("p (a b) -> p a b", a=HP), ALU.add, [pk, 'PT'], ['PT'])
                    curM = lambda q, nxt=nxt: nxt[:, q, 0:128]
                    curMT = lambda q, nxt=nxt: nxt[:, q, 128:256]
                    curk = [nk]

                if hf == 0 and not samp:
                    for g in range(4):
                        pb, pk = pbank()
                        mm(pb[:, 0:128], poolw[:, g, :], pz[:, g, :], True, True, ['lws', 'vbf'], [pk])
                        act(orp[:, 4 + g, :], pb[:, 0:128], AF.Identity, [pk, 'pcol'], ['orp'], scale=pc(l, "pool_scale", g))
                if not samp:
                    for pp_ in range(NPI):
                        p = hf * NPI + pp_
                        if not first_chunk:
                            mm(ZB[:, pp_ * 128:(pp_ + 1) * 128], KR[:, p, 0:128], H0b[:, p, :], True, False, ['KR', 'H0f'], [ZK])
                        for hh in range(2):
                            q = pp_ * 2 + hh
                            mm(ZB[:, pp_ * 128 + hh * 64:pp_ * 128 + hh * 64 + 64], AT[:, q, 128:256],
                               Vtok[:, p, hh * 64:hh * 64 + 64], first_chunk, True, ['AT', 'Vtok'], [ZK])
                    act(f2(Zn[:]), ZB[:, 0:NPI * 128], AF.Copy, [ZK], ['Zn'], scale=-1.0)
                    pbu, pku = pbank()
                    for q in range(HP):
                        pp_, hh = q // 2, q % 2
                        mm(pbu[:, q * 64:(q + 1) * 64], PT[:, q, :], Zn[:, pp_, hh * 64:hh * 64 + 64], True, True, ['PT', 'Zn'], [pku])
                    acopy(f2(Ut[:]), pbu[:, 0:NPI * 128], [pku], ['Ut'])
                    for pp_ in range(NPI):
                        p = hf * NPI + pp_
                        if not first_chunk:
                            mm(YB[:, pp_ * 128:(pp_ + 1) * 128], H0b[:, p, :], KR[:, p, 128:256], True, False, ['H0f', 'KR'], [YK])
                        for hh in range(2):
                            q = pp_ * 2 + hh
                            hs = slice(hh * 64, hh * 64 + 64)
                            mm(YB[hs, pp_ * 128:(pp_ + 1) * 128], Ut[:, pp_, hh * 64:hh * 64 + 64], AT[:, q, 0:128],
                               first_chunk, False, ['Ut', 'AT'], [YK])
                            mm(YB[hs, pp_ * 128:(pp_ + 1) * 128], Vtok[:, p, hh * 64:hh * 64 + 64], AT[:, q, 256:384],
                               False, True, ['Vtok', 'AT'], [YK])
                    acopy(f2(X5[:, hf * NPI:(hf + 1) * NPI, :]), YB[:, 0:NPI * 128], [YK], ['X5'])
                    pbh, pkh = pbank()
                    for pp_ in range(NPI):
                        p = hf * NPI + pp_
                        mm(pbh[:, pp_ * 128:(pp_ + 1) * 128], BWtok[:, p, :], Ut[:, pp_, :], True, False, ['BWtok', 'Ut'], [pkh])
                        mm(pbh[:, pp_ * 128:(pp_ + 1) * 128], KWtok[:, p, :], Vtok[:, p, :], False, True, ['KWtok', 'Vtok'], [pkh])
                    hsl = slice(hf * NPI, (hf + 1) * NPI)
                    tt(X3[:, 0:NPI, :], pbh[:, 0:NPI * 128].rearrange("p (a b) -> p a b", a=NPI),
                       bonesF.unsqueeze(1).to_broadcast([128, NPI, 128]), ALU.mult, [pkh, 'cF'], ['X3'])
                    if first_chunk:
                        vcopy(H0f[:, hsl, :], X3[:, 0:NPI, :], ['X3'], ['H0f'])
                    else:
                        tt(H0f[:, hsl, :], H0f[:, hsl, :], wcs[:, hsl, 0:1].to_broadcast([128, NPI, 128]), ALU.mult, ['H0f', 'wcs'], ['H0f'])
                        tt(H0f[:, hsl, :], H0f[:, hsl, :], X3[:, 0:NPI, :], ALU.add, ['H0f', 'X3'], ['H0f'])
                    if last_chunk:
                        pbs, pks = pbank()
                        for pp_ in range(NPI):
                            tr(pbs[:, pp_ * 128:(pp_ + 1) * 128], H0f[:, hf * NPI + pp_, :], identF, ['H0f', 'cF'], [pks])
                        for hh in range(2):
                            hs = slice(hh * 64, hh * 64 + 64)
                            acopy(SCo[hs, 0:NPI, :], pbs[hs, 0:NPI * 128].rearrange("p (a b) -> p a b", a=NPI)[:, :, hh * 64:hh * 64 + 64], [pks], ['X4'])
                        dma(nwkvp[l, hf * HP:hf * HP + HP].rearrange("(p hh) v n -> (hh v) p n", hh=2), SCo[:, 0:NPI, :], ['X4'], ['X4'])
                else:
                    for pp_ in range(NPI):
                        p = hf * NPI + pp_
                        for g in range(4):
                            for hh in range(2):
                                hs = slice(hh * 64, hh * 64 + 64)
                                dma(S0g[hs, :, hh * 64:hh * 64 + 64],
                                    swkv[l, g * 4:g * 4 + 4, 2 * p + hh, :, :].rearrange("b v k -> v b k"), ['H0f'], ['H0f'])
                            pb, pk = pbank()
                            for j in range(4):
                                tr(pb[:, j * 128:(j + 1) * 128], S0g[:, j, :], identF, ['H0f', 'cF'], [pk])
                            acopy(f2(HSb[:]), pb[:, 0:512], [pk], ['A'])
                            bkk_diag = bass.AP(tensor=BKK.tensor, offset=BKK.offset + 32 * g,
                                               ap=[list(BKK.ap[0]), [136, 4], [1, 8]])
                            vcopy(bkk_diag, KR[:, p, g * 32:g * 32 + 32].rearrange("q (b t) -> q b t", t=8), ['KR', 'LW'], ['LW'])
                            brr_diag = bass.AP(tensor=BRR.tensor, offset=BRR.offset + 32 * g,
                                               ap=[list(BRR.ap[0]), [136, 4], [1, 8]])
                            vcopy(brr_diag, KR[:, p, 128 + g * 32:128 + g * 32 + 32].rearrange("q (b t) -> q b t", t=8), ['KR', 'X3'], ['X3'])
                            for j in range(4):
                                b_ = g * 4 + j
                                mm(ZB[:, 0:128], BKK[:, j, :], HSb[:, j, :], b_ == 0, False, ['LW', 'A'], [ZK])
                                mm(YB[:, 0:128], HSb[:, j, :], BRR[:, j, :], b_ == 0, False, ['A', 'X3'], [YK])
                            DVE(lambda e, bkk_diag=bkk_diag: e.memset(bkk_diag, 0.0), ['LW'], ['LW'])
                            DVE(lambda e, brr_diag=brr_diag: e.memset(brr_diag, 0.0), ['X3'], ['X3'])
                        for hh in range(2):
                            q = pp_ * 2 + hh
                            mm(ZB[:, hh * 64:hh * 64 + 64], AT[:, q, 128:256], Vtok[:, p, hh * 64:hh * 64 + 64], False, True, ['AT', 'Vtok'], [ZK])
                        act(Zn[:, 0, :], ZB[:, 0:128], AF.Copy, [ZK], ['Zn'], scale=-1.0)
                        pbu, pku = pbank()
                        for hh in range(2):
                            q = pp_ * 2 + hh
                            mm(pbu[:, hh * 64:hh * 64 + 64], PT[:, q, :], Zn[:, 0, hh * 64:hh * 64 + 64], True, True, ['PT', 'Zn'], [pku])
                        acopy(Ut[:, 0, :], pbu[:, 0:128], [pku], ['Ut'])
                        for hh in range(2):
                            q = pp_ * 2 + hh
                            hs = slice(hh * 64, hh * 64 + 64)
                            mm(YB[hs, 0:128], Ut[:, 0, hh * 64:hh * 64 + 64], AT[:, q, 0:128], False, False, ['Ut', 'AT'], [YK])
                            mm(YB[hs, 0:128], Vtok[:, p, hh * 64:hh * 64 + 64], AT[:, q, 256:384], False, True, ['Vtok', 'AT'], [YK])
                        acopy(X5[:, p, :], YB[:, 0:128], [YK], ['X5'])
                        smk = cb('seqm')
                        for g in range(4):
                            for hh in range(2):
                                hs = slice(hh * 64, hh * 64 + 64)
                                dma(S0g[hs, :, hh * 64:hh * 64 + 64],
                                    swkv[l, g * 4:g * 4 + 4, 2 * p + hh, :, :].rearrange("b v k -> v b k"), ['H0f'], ['H0f'])
                            sm4 = smk[:, g * 4:g * 4 + 4].unsqueeze(2).to_broadcast([128, 4, 128])
                            tt(BIGB[:], BWtok[:, p, :].unsqueeze(1).to_broadcast([128, 4, 128]), sm4, ALU.mult, ['BWtok', 'cB'], ['BWf'])
                            tt(BIGK[:], KWtok[:, p, :].unsqueeze(1).to_broadcast([128, 4, 128]), sm4, ALU.mult, ['KWtok', 'cB'], ['KWf'])
                            tt(DIAG[:], identF.unsqueeze(1).to_broadcast([128, 4, 128]),
                               wcs[:, p, g * 4:g * 4 + 4].unsqueeze(2).to_broadcast([128, 4, 128]), ALU.mult, ['cF', 'wcs'], ['X1'])
                            pb, pk = pbank()
                            mm(pb[:, 0:512], onesF, f2(DIAG[:]), True, True, ['X1', 'cF'], [pk])
                            acopy(f2(WcBC[:]), pb[:, 0:512], [pk], ['X2'])
                            tt(WcBC[:], WcBC[:], S0g[:], ALU.mult, ['X2', 'H0f'], ['X2'])
                            pbs, pks = pbank()
                            mm(pbs[:, 0:512], Ut[:, 0, :], f2(BIGB[:]), True, False, ['Ut', 'BWf'], [pks])
                            mm(pbs[:, 0:512], Vtok[:, p, :], f2(BIGK[:]), False, True, ['Vtok', 'KWf'], [pks])
                            tt(WcBC[:], WcBC[:], v4(pbs[:, 0:512]), ALU.add, ['X2', pks], ['X2'])
                            for hh in range(2):
                                hs = slice(hh * 64, hh * 64 + 64)
                                vcopy(SCo[hs, :, :], WcBC[hs, :, hh * 64:hh * 64 + 64], ['X2', 'X4'], ['X4'])
                            dma(nwkvs[l, g * 4:g * 4 + 4, 2 * p:2 * p + 2, :, :].rearrange("b hh v n -> (hh v) b n"), SCo[:], ['X4'], ['X4'])

            pb, pk = pbank()
            mm(pb[:, 0:512], bonesF, f2(X5[:]), True, True, ['X5', 'cF'], [pk])
            ts(X1[:], v4(pb[:, 0:512]), 1.0 / 64, None, ALU.mult, ALU.bypass, [pk], ['X1'])
            tt(X5[:], X5[:], X1[:], ALU.subtract, ['X5', 'X1'], ['X5'])
            tt(X2[:], X5[:], X5[:], ALU.mult, ['X5'], ['X2'])
            pb, pk = pbank()
            mm(pb[:, 0:512], bonesF, f2(X2[:]), True, True, ['X2', 'cF'], [pk])
            rsq(X1[:], v4(pb[:, 0:512]), 1.0 / 64, GN_EPS, [pk], ['X1'])
            tt(X5[:], X5[:], X1[:], ALU.mult, ['X5', 'X1'], ['X5'])
            tt(X5[:], X5[:], bc4(pcs(l, "ln_w", 4)), ALU.mult, ['X5', 'pcol'], ['X5'])
            tt(X5[:], X5[:], bc4(pcs(l, "ln_b", 4)), ALU.add, ['X5', 'pcol'], ['X5'])
            tt(X5[:], X5[:], X6[:], ALU.add, ['X5', 'X6'], ['X5'])
            tt(orp[:, 0:4, :], X5[:], gbf[:], ALU.mult, ['X5', 'gbf'], ['orp'])

            if not samp:
                if last_chunk:
                    pb, pk = pbank()
                    for g in range(4):
                        tr(pb[0:15, g * 128:(g + 1) * 128], ppb[:, g, 128:143], identF, ['ppb', 'cF'], [pk])
                    acopy(f2(X1[0:15, :, :]), pb[0:15, 0:512], [pk], ['X1'])
                    dma(npoolp[l], f2(X1[0:15, :, :]), ['X1'], ['X1'])
                pass
            else:
                spv = spool[l].rearrange("b r c -> (b r) c")
                for half in range(2):
                    r0, rn = (0, 128) if half == 0 else (128, 112)
                    stg = [X1, X2][half]
                    dma(f2(stg[0:rn, :, :]), spv[r0:r0 + rn, :], [], [['X1', 'X2'][half]])
                    pb, pk = pbank()
                    for g in range(4):
                        tr(pb[:, g * 128:g * 128 + rn], f2(stg[0:rn, :, :])[:, g * 128:(g + 1) * 128], identF[0:rn, 0:rn],
                           [['X1', 'X2'][half], 'cF'], [pk])
                    acopy(f2(X3[:]) if half == 0 else f2(X4[:]), pb[:, 0:512], [pk], [['X3', 'X4'][half]])
                for g in range(4):
                    vcopy(ppbS[:, g, 0:8, 0:15], X3[:, g, 0:120].rearrange("p (b r) -> p b r", r=15), ['X3'], ['ppbS', 'ppb'])
                    vcopy(ppbS[:, g, 8, 0:8], X3[:, g, 120:128], ['X3'], ['ppbS', 'ppb'])
                    vcopy(ppbS[:, g, 8, 8:15], X4[:, g, 0:7], ['X4'], ['ppbS', 'ppb'])
                    vcopy(ppbS[:, g, 9:16, 0:15], X4[:, g, 7:112].rearrange("p (b r) -> p b r", r=15), ['X4'], ['ppbS', 'ppb'])
                for half in range(2):
                    r0, rn = (0, 128) if half == 0 else (128, 112)
                    stg = [X3, X4][half]
                    for g in range(4):
                        if half == 0:
                            vcopy(stg[:, g, 0:120].rearrange("p (b r) -> p b r", r=15), ppbS[:, g, 0:8, 8:23], ['ppbS'], [['X3', 'X4'][half]])
                            vcopy(stg[:, g, 120:128], ppbS[:, g, 8, 8:16], ['ppbS'], [['X3', 'X4'][half]])
                        else:
                            vcopy(stg[:, g, 0:7], ppbS[:, g, 8, 16:23], ['ppbS'], [['X3', 'X4'][half]])
                            vcopy(stg[:, g, 7:112].rearrange("p (b r) -> p b r", r=15), ppbS[:, g, 9:16, 8:23], ['ppbS'], [['X3', 'X4'][half]])
                    pb, pk = pbank()
                    for g in range(4):
                        tr(pb[0:rn, g * 128:(g + 1) * 128], stg[:, g, 0:rn], identF, [['X3', 'X4'][half], 'cF'], [pk])
                    ob = [X1, X2][half]
                    acopy(f2(ob[0:rn, :, :]), pb[0:rn, 0:512], [pk], [['X1', 'X2'][half]])
                    dma(npools[l, r0:r0 + rn, :], f2(ob[0:rn, :, :]), [['X1', 'X2'][half]], [['X1', 'X2'][half]])
                for g in range(4):
                    A0 = ppbS[:, g, :, :]
                    cur = X3[:].rearrange("p a b -> p (a b)")[:, 0:368].rearrange("p (b r) -> p b r", r=23)
                    oth = X4[:].rearrange("p a b -> p (a b)")[:, 0:368].rearrange("p (b r) -> p b r", r=23)
                    tt(cur[:, :, 1:23], A0[:, :, 1:23], A0[:, :, 0:22], ALU.add, ['ppbS', 'X3', 'X4'], ['X3', 'X4'])
                    span = 2
                    while span < WINS[g]:
                        lo = 2 * span - 1
                        tt(oth[:, :, lo:23], cur[:, :, lo:23], cur[:, :, lo - span:23 - span], ALU.add, ['X3', 'X4'], ['X3', 'X4'])
                        cur, oth = oth, cur
                        span *= 2
                    mg = WK['mg']
                    stt(mg[:].rearrange("p (b t) -> p b t", t=8), cur[:, :, 15:23], 1.0 / WINS[g], ppbS[:, g, :, 15:23],
                        ALU.mult, ALU.subtract, ['X3', 'X4', 'ppbS'], ['mg'])
                    acopy(pz[:, g, :], mg[:], ['mg'], ['vbf'])
            if samp:
                for g in range(4):
                    pb, pk = pbank()
                    mm(pb[:, 0:128], poolw[:, g, :], pz[:, g, :], True, True, ['lws', 'vbf'], [pk])
                    act(orp[:, 4 + g, :], pb[:, 0:128], AF.Identity, [pk, 'pcol'], ['orp'], scale=pc(l, "pool_scale", g))
            dma(orpd[:, :, t0:t0 + 128], orp[:], ['orp'], [('orpd', ci)])

        S.barrier()
        ptr[0] = UBASE
        T2 = 512
        WK['sq'] = alloc([2, T2], BF16); WK['rstd'] = alloc([T2]); WK['tmpn'] = alloc([2, T2]); WK['mg'] = alloc([128])
        hT2 = alloc([NKC, T2], BF16)
        mrg = alloc([NKC, NTOK], BF16)
        orp2 = alloc([1, 8, T2], BF16)
        g16 = alloc([16, T2], BF16)
        tmpb = WK['tmpn'][:, 0:1, :]
        W2 = [wl_kc(W["w_in"][l, :, 2304 + j * 512:2304 + (j + 1) * 512], 512) for j in range(4)]
        wbr, wbrk = wl_kc(W["w_br_rwkv"][l], 1024)
        wbp, wbpk = wl_kc(W["w_br_pool"][l], 1024)
        for ti, (t0, n, samp) in enumerate(tiles(T2)):
            ob = orp2[:, 0, :, 0:n]
            ok = ('orp2', 0)
            dma(ob, orpd[:, :, t0:t0 + n], [('orpd', c) for c in range(t0 // 128, (t0 + n) // 128)], [ok])
            norm_mod_t(t0, n, samp, hT2, ['hT2'])
            for cg in range(16):
                wt, wkey = W2[cg // 4]
                q = cg % 4
                pb, pk = pbank()
                for kc in range(NKC):
                    mm(pb[:, 0:n], wt[:, kc, q * 128:(q + 1) * 128], hT2[:, kc, 0:n], kc == 0, kc == NKC - 1, [wkey, 'hT2'], [pk])
                act(g16[:, cg, 0:n], pb[:, 0:n], AF.Sigmoid, [pk], ['g16'])
            for c in range(8):
                pb, pk = pbank()
                for kc in range(4):
                    mm(pb[:, 0:n], wbr[:, kc, c * 128:(c + 1) * 128], ob[:, kc, :], kc == 0, kc == 3, [wbrk, ok], [pk])
                tt(tmpb[:, 0, 0:n], pb[:, 0:n], g16[:, c, 0:n], ALU.mult, [pk, 'g16'], [('tmpn', 0)])
                pb2, pk2 = pbank()
                for kc in range(4):
                    mm(pb2[:, 0:n], wbp[:, kc, c * 128:(c + 1) * 128], ob[:, 4 + kc, :], kc == 0, kc == 3, [wbpk, ok], [pk2])
                tt(mrg[:, c, t0:t0 + n], pb2[:, 0:n], g16[:, 8 + c, 0:n], ALU.mult, [pk2, 'g16'], [('mrg', t0)])
                tt(mrg[:, c, t0:t0 + n], mrg[:, c, t0:t0 + n], tmpb[:, 0, 0:n], ALU.add, [('tmpn', 0), ('mrg', t0)], [('mrg', t0)])
        wo = [wl_kc(W["w_out"][l, :, j * 512:(j + 1) * 512], 512) for j in range(2)]
        for (t0, n, samp) in tiles(T2):
            for c in range(8):
                wt, wkey = wo[c // 4]
                q = c % 4
                pb, pk = pbank()
                for kc in range(NKC):
                    mm(pb[:, 0:n], wt[:, kc, q * 128:(q + 1) * 128], mrg[:, kc, t0:t0 + n], kc == 0, kc == NKC - 1, [wkey, ('mrg', t0)], [pk])
                resid_update_t(t0, n, samp, c, pb, pk)

        S.barrier()
        ptr[0] = UBASE
        T3 = 512
        WK['sq'] = alloc([2, T3], BF16); WK['rstd'] = alloc([T3]); WK['tmpn'] = alloc([2, T3]); WK['mg'] = alloc([128])
        hTm = alloc([NKC, NTOK], BF16)
        rl = alloc([2, T3]); r2 = alloc([8, T3], BF16)
        ada(l, "mlp")
        for (t0, n, samp) in tiles(T3):
            norm_mod_t(t0, n, samp, hTm[:, :, t0:t0 + n], [('hTm', t0)])
        for qd in range(4):
            w1 = [wl_kc(W["w_ff1"][l, :, qd * 1024 + j * 512:qd * 1024 + (j + 1) * 512], 512) for j in range(2)]
            w2 = [wl_kc(W["w_ff2"][l, qd * 1024:(qd + 1) * 1024, j * 512:(j + 1) * 512], 512) for j in range(2)]
            for (t0, n, samp) in tiles(T3):
                for c in range(8):
                    wt, wkey = w1[c // 4]
                    q = c % 4
                    pb, pk = pbank()
                    for kc in range(NKC):
                        mm(pb[:, 0:n], wt[:, kc, q * 128:(q + 1) * 128], hTm[:, kc, t0:t0 + n], kc == 0, kc == NKC - 1, [wkey, ('hTm', t0)], [pk])
                    act(rl[:, c % 2, 0:n], pb[:, 0:n], AF.Relu, [pk], [('rl', c % 2)])
                    tt(r2[:, c, 0:n], rl[:, c % 2, 0:n], rl[:, c % 2, 0:n], ALU.mult, [('rl', c % 2)], [('r2', c)])
                for c in range(8):
                    wt, wkey = w2[c // 4]
                    q = c % 4
                    pb, pk = pbank()
                    for kc in range(NKC):
                        mm(pb[:, 0:n], wt[:, kc, q * 128:(q + 1) * 128], r2[:, kc, 0:n], kc == 0, kc == NKC - 1, [wkey, ('r2', kc)], [pk])
                    resid_update_t(t0, n, samp, c, pb, pk)

    S.barrier()
    ptr[0] = UBASE
    sqr = alloc([2, 128], BF16); rstd = alloc([128]); tmpo = alloc([NKC, 128]); yout = alloc([2, D])
    ts(G32f[:], pcol[:, 0, 120:128].unsqueeze(2), 32.0, None, ALU.mult, ALU.bypass, ['pcol'], ['G32f'])
    for ci in range(NCH):
        t0 = ci * 128
        pb, pk = pbank()
        for c in range(NKC):
            act(sqr[:, c % 2, :], xT[:, c, t0:t0 + 128], AF.Square, xk(ci), [('sq', c % 2)])
            mm(pb[:, 0:128], onesB, sqr[:, c % 2, :], c == 0, c == NKC - 1, [('sq', c % 2), 'cB'], [pk])
        rsq(rstd[:], pb[:, 0:128], 1.0, D * EPS, [pk], ['rstd'])
        for c in range(NKC):
            stt(tmpo[:, c, :], xT[:, c, t0:t0 + 128], G32f[:, c, 0:1], rstd[:], ALU.mult, ALU.mult, xk(ci) + ['G32f', 'rstd'], ['tmpo'])
        yo = yout[:, ci % 2, :]
        for half in range(2):
            pb, pk = pbank()
            for q in range(4):
                c = half * 4 + q
                tr(pb[:, q * 128:(q + 1) * 128], tmpo[:, c, :], identF, ['tmpo', 'cF'], [pk])
            acopy(yo[:, half * 512:(half + 1) * 512], pb[:, 0:512], [pk], [('yout', ci % 2)])
        dst = yp[t0:t0 + 128, :] if ci < NPC else ys[:, :]
        dma(dst, yo, [('yout', ci % 2)], [('yout', ci % 2)])
    return nc, S, st


CSTF = {}
CSTB = {}
CF_COLS = 0
CB_COLS = 0


def _layout_consts():
    global CF_COLS, CB_COLS
    o = 0
    for nm, n in [('ident', 128), ('ones', 128), ('bones', 128), ('rmS', 128), ('pcorr', 64)]:
        CSTF[nm] = (o, n)
        o += n
    CF_COLS = o
    o = 0
    for nm, n in [('ident', 128), ('ones', 128), ('mAP', 512), ('mAS', 512), ('mLP', 128), ('mLS', 128), ('seqm', 16)]:
        CSTB[nm] = (o, n)
        o += n
    CB_COLS = o


_layout_consts()


def make_consts():
    c = np.zeros((128, CF_COLS + CB_COLS), np.float32)

    def putf(nm, a):
        o, n = CSTF[nm]
        c[:, o:o + n] = a

    def putb(nm, a):
        o, n = CSTB[nm]
        c[:, CF_COLS + o:CF_COLS + o + n] = a
    i = np.arange(128)
    putf('ident', np.eye(128)); putb('ident', np.eye(128))
    putf('ones', np.ones((128, 128))); putb('ones', np.ones((128, 128)))
    putf('bones', (i[:, None] // 64 == i[None, :] // 64).astype(np.float32))
    s, t = i[:, None], i[None, :]
    for tag, same in (('P', np.ones((128, 128), bool)), ('S', (s // 8) == (t // 8))):
        lt = ((s < t) & same).astype(np.float32)
        le = ((s <= t) & same).astype(np.float32)
        gtm = ((s > t) & same).astype(np.float32)
        putb('mA' + tag, np.concatenate([-lt, le, lt, le], axis=1))
        putb('mL' + tag, -gtm)
    rmS = np.ones((128, 128), np.float32)
    rmS[:, ::8] = 0
    putf('rmS', rmS)
    putb('seqm', (i[:, None] // 8 == np.arange(16)[None, :]).astype(np.float32))
    pc_ = np.zeros((128, 64), np.float32)
    for g, w in enumerate(WINS):
        tt_ = np.arange(16)
        pc_[:, g * 16:(g + 1) * 16] = (1.0 / np.minimum(tt_ + 1, w))[None, :]
    putf('pcorr', pc_)
    return c


def emit(nc, S, st):
    sems = {name: st.enter_context(nc.semaphore(name)) for name in S.cnt}
    block = st.enter_context(nc.Block())

    def run(stream, eng):
        for waits, fn, sem, inc in S.ops[stream]:
            for (s, v) in waits:
                eng.wait_ge(sems[s], v)
            if fn is not None:
                fn(eng).then_inc(sems[sem], inc)

    @block.sync
    def _(e):
        run('sp', e)
        for nm in S.cnt:
            if nm.startswith('sp'):
                e.wait_ge(sems[nm], S.cnt[nm])

    @block.gpsimd
    def _(e):
        run('pool', e)

    @block.tensor
    def _(e):
        run('pe', e)

    @block.vector
    def _(e):
        run('dve', e)

    @block.scalar
    def _(e):
        run('act', e)
    st.close()
    return nc


_WNAMES = ["w_ada_mix", "b_ada_mix", "norm_mix", "w_in", "mu_shift", "w0", "w2", "a0", "a2", "g2", "v0", "v1", "v2",
           "k_k", "k_a", "r_k", "ln_w", "ln_b", "pool_w", "pool_scale", "w_br_rwkv", "w_br_pool", "w_out",
           "w_ada_mlp", "b_ada_mlp", "norm_mlp", "w_ff1", "w_ff2", "norm_final"]


def make_in_maps(inputs, ncores, L):
    consts = make_consts()
    f = lambda a: np.ascontiguousarray(np.asarray(a, dtype=np.float32))
    shared = {}
    for nm in _WNAMES:
        a = f(inputs[nm])
        if nm == "r_k":
            a = a.reshape(L, MIX)
        if nm == "norm_final":
            a = a.reshape(1, D)
        shared[nm] = a
    shared["cst"] = consts
    maps = []
    for i in range(ncores):
        m = dict(shared)
        m["xp"] = f(inputs["x_prompt"][i])
        m["xs"] = f(inputs["x_sample"][16 * i:16 * i + 16]).reshape(128, D)
        m["cc"] = f(np.concatenate([np.asarray(inputs["c_prompt"])[i:i + 1], np.asarray(inputs["c_sample"])[16 * i:16 * i + 16]], axis=0))
        m["sshift"] = f(np.asarray(inputs["state_shift"])[:, 16 * i:16 * i + 16])
        m["spool"] = f(np.asarray(inputs["state_pool"])[:, 16 * i:16 * i + 16])
        m["swkv"] = f(np.asarray(inputs["state_wkv"])[:, 16 * i:16 * i + 16])
        maps.append(m)
    return maps


def gather(R, ncores, L):
    y_p = np.stack([R[i]["yp"] for i in range(ncores)], 0)
    y_s = np.concatenate([R[i]["ys"].reshape(16, 8, D) for i in range(ncores)], 0)
    sh_p = np.stack([R[i]["nshp"] for i in range(ncores)], 1)
    pool_p = np.stack([R[i]["npoolp"] for i in range(ncores)], 1)
    wkv_p = np.stack([R[i]["nwkvp"] for i in range(ncores)], 1)
    sh_s = np.concatenate([R[i]["nshs"] for i in range(ncores)], 1)
    pool_s = np.concatenate([R[i]["npools"].reshape(L, 16, 15, MIX) for i in range(ncores)], 1)
    wkv_s = np.concatenate([R[i]["nwkvs"] for i in range(ncores)], 1)
    return tuple(np.ascontiguousarray(a, dtype=np.float32) for a in (y_p, y_s, sh_p, pool_p, wkv_p, sh_s, pool_s, wkv_s))


def kernel(**inputs):
    ncores = 8
    L = 4
    nc, S, st = build(TP=2048, L=L)
    emit(nc, S, st)
    maps = make_in_maps(inputs, ncores, L)
    res = run_bass_kernel_spmd(nc, maps, core_ids=list(range(ncores)))
    return gather(res.results, ncores, L)
```

```python
import numpy as np
from contextlib import ExitStack
import concourse.bass as bass
import concourse.mybir as mybir
from concourse.bass_utils import run_bass_kernel_spmd

F32 = mybir.dt.float32
BF16 = mybir.dt.bfloat16
AF = mybir.ActivationFunctionType
ALU = mybir.AluOpType

D = 1024
NKC = 8
MIX = 512
RW = 1792
INC = 4352
DFF = 4096
EPS = 1e-6
GN_EPS = 64e-5
DEC_C = -float(np.exp(-0.5))
WINS = (2, 4, 8, 16)


class Sched:
    STREAMS = ('pe', 'act', 'dve', 'pool', 'sp')

    def __init__(self):
        self.ops = {s: [] for s in self.STREAMS}
        self.cnt = {}
        self.known = {s: {} for s in self.STREAMS}
        self.lastw = {}
        self.readers = {}
        self.dma_i = {}

    def op(self, stream, fn, r=(), w=(), sem=None, inc=1, nsem=1):
        sem = sem or stream
        if nsem > 1:
            i = self.dma_i.get(sem, 0)
            self.dma_i[sem] = i + 1
            sem = "%s%d" % (sem, i % nsem)
        need = {}
        if nsem > 1 and self.cnt.get(sem, 0):
            need[sem] = self.cnt[sem]

        def add(s, v):
            if need.get(s, 0) < v:
                need[s] = v
        for b in r:
            if b in self.lastw:
                add(*self.lastw[b])
        for b in w:
            if b in self.lastw:
                add(*self.lastw[b])
            for s, v in self.readers.get(b, {}).items():
                add(s, v)
        waits = []
        kn = self.known[stream]
        for s, v in need.items():
            if stream == 'pe' and s == 'pe':
                continue
            if kn.get(s, 0) < v:
                waits.append((s, v))
                kn[s] = v
        self.cnt[sem] = self.cnt.get(sem, 0) + inc
        val = self.cnt[sem]
        self.ops[stream].append((waits, fn, sem, inc))
        for b in r:
            d = self.readers.setdefault(b, {})
            if d.get(sem, 0) < val:
                d[sem] = val
        for b in w:
            self.lastw[b] = (sem, val)
            self.readers[b] = {}

    def barrier(self, streams=('pe', 'act', 'dve', 'sp')):
        for s in streams:
            waits = []
            for sem in list(self.cnt):
                if sem.startswith('pq'):
                    continue
                v = self.cnt.get(sem, 0)
                if s == 'pe' and sem == 'pe':
                    continue
                if v and self.known[s].get(sem, 0) < v:
                    waits.append((sem, v))
                    self.known[s][sem] = v
            if waits:
                self.ops[s].append((waits, None, None, 0))


def build(TP=2048, L=4):
    NTOK = TP + 128
    NCH = NTOK // 128
    NPC = TP // 128
    nc = bass.Bass("TRN2", target_bir_lowering=False)
    S = Sched()
    st = ExitStack()

    def din(name, shape, dt=F32):
        return nc.dram_tensor(name, list(shape), dt, kind="ExternalInput").ap()

    def dout(name, shape, dt=F32):
        return nc.dram_tensor(name, list(shape), dt, kind="ExternalOutput").ap()

    xp = din("xp", [TP, D]); xs = din("xs", [128, D]); cc = din("cc", [17, D])
    sshift = din("sshift", [L, 16, RW]); spool = din("spool", [L, 16, 15, MIX])
    swkv = din("swkv", [L, 16, 8, 64, 64])
    W = {}
    LV = max(L - 1, 1)
    for nm, shp in [("w_ada_mix", [L, D, 3 * D]), ("b_ada_mix", [L, 3 * D]), ("norm_mix", [L, D]),
                    ("w_in", [L, D, INC]), ("mu_shift", [L, RW]), ("w0", [L, MIX]), ("w2", [L, 64, MIX]),
                    ("a0", [L, MIX]), ("a2", [L, 64, MIX]), ("g2", [L, 128, MIX]), ("v0", [LV, MIX]),
                    ("v1", [LV, MIX, 32]), ("v2", [LV, 32, MIX]), ("k_k", [L, MIX]),
                    ("k_a", [L, MIX]), ("r_k", [L, MIX]), ("ln_w", [L, MIX]), ("ln_b", [L, MIX]),
                    ("pool_w", [L, 4, 128, 128]), ("pool_scale", [L, MIX]), ("w_br_rwkv", [L, MIX, D]),
                    ("w_br_pool", [L, MIX, D]), ("w_out", [L, D, D]), ("w_ada_mlp", [L, D, 3 * D]),
                    ("b_ada_mlp", [L, 3 * D]), ("norm_mlp", [L, D]), ("w_ff1", [L, D, DFF]),
                    ("w_ff2", [L, DFF, D]), ("norm_final", [1, D])]:
        W[nm] = din(nm, shp)
    cst = din("cst", [128, CF_COLS + CB_COLS])
    yp = dout("yp", [TP, D]); ys = dout("ys", [128, D])
    nshp = dout("nshp", [L, RW]); npoolp = dout("npoolp", [L, 15, MIX]); nwkvp = dout("nwkvp", [L, 8, 64, 64])
    nshs = dout("nshs", [L, 16, RW]); npools = dout("npools", [L, 16 * 15, MIX])
    nwkvs = dout("nwkvs", [L, 16, 8, 64, 64])
    vfd = nc.dram_tensor("vfirst_scr", [128, 4, NTOK], BF16, kind="Internal").ap()
    orpd = nc.dram_tensor("orp_scr", [128, 8, NTOK], BF16, kind="Internal").ap()

    NW = 53200
    big = st.enter_context(nc.sbuf_tensor("big", [128, NW], F32))
    ptr = [0]

    def alloc(shape, dt=F32):
        n = int(np.prod(shape))
        words = n if dt == F32 else (n + 1) // 2
        words = (words + 7) // 8 * 8
        o = ptr[0]
        ptr[0] += words
        assert ptr[0] <= NW, ("SBUF arena overflow", ptr[0], NW)
        v = big[:, o:o + words]
        if dt != F32:
            v = v.bitcast(dt)
        v = v[:, 0:n]
        if len(shape) == 1:
            return v
        names = " ".join("d%d" % i for i in range(len(shape)))
        kw = {"d%d" % i: int(shape[i]) for i in range(len(shape) - 1)}
        return v.rearrange("p (%s) -> p %s" % (names, names), **kw)

    SD = F32
    NPI = 2
    HP = 2 * NPI
    xT = alloc([NKC, NTOK])
    ring = alloc([6, 4096], BF16)
    cF = alloc([CF_COLS]); cB = alloc([CB_COLS], BF16)
    pcol = alloc([L, 128])
    omm = alloc([14]); omka = alloc([4])
    siluT = alloc([NKC, 17], BF16)
    modv = alloc([24, 17]); G32 = alloc([NKC, 17]); G32f = alloc([NKC, 1])
    lw_small = alloc([2176], BF16)
    H0f = alloc([4, 128]); H0b = H0f
    prcarry = alloc([14, 1])
    UBASE = ptr[0]

    PB = [st.enter_context(nc.psum_tensor("pb%d" % i, [128, 512], F32)) for i in range(8)]
    NROT = 6
    pbi = [0]
    pbt_i = [0]

    def pbank():
        i = pbi[0] % NROT
        pbi[0] += 1
        return PB[i], ('pb', i)
    ZB, ZK = PB[6], ('pb', 6)
    YB, YK = PB[7], ('pb', 7)

    def PE(fn, r, w): S.op('pe', fn, r, w)
    def ACT(fn, r, w): S.op('act', fn, r, w)
    def DVE(fn, r, w): S.op('dve', fn, r, w)
    def SPD(fn, r, w): S.op('sp', fn, r, w, sem='sp', inc=16, nsem=16)
    def PQD(fn, r, w): S.op('pool', fn, r, w, sem='pq', inc=16, nsem=8)

    def mm(out, lhsT, rhs, start, stop, r, w):
        PE(lambda e: e.matmul(out, lhsT, rhs, start=start, stop=stop, skip_group_check=True), r, w)

    def tr(out, in_, ident, r, w):
        PE(lambda e: e.transpose(out, in_, ident), r, w)

    def act(out, in_, func, r, w, bias=0.0, scale=1.0):
        ACT(lambda e: e.activation(out, in_, func, bias=bias, scale=scale), r, w)

    def acopy(out, in_, r, w):
        ACT(lambda e: e.copy(out, in_), r, w)

    def vcopy(out, in_, r, w):
        DVE(lambda e: e.tensor_copy(out, in_), r, w)

    def tt(out, a, b, op, r, w):
        DVE(lambda e: e.tensor_tensor(out, a, b, op), r, w)

    def ts(out, a, s1, s2, op0, op1, r, w):
        DVE(lambda e: e.tensor_scalar(out, a, s1, s2, op0, op1), r, w)

    def stt(out, a, s, b, op0, op1, r, w):
        DVE(lambda e: e.scalar_tensor_tensor(out, a, s, b, op0, op1), r, w)

    def dma(out, in_, r, w):
        SPD(lambda e: e.dma_start(out=out, in_=in_), r, w)

    def rsq(out, in_, mulc, addc, r, w):
        ts(out, in_, mulc, addc, ALU.mult, ALU.add, r, w)
        act(out, out, AF.Ln, w, w)
        act(out, out, AF.Exp, w, w, scale=-0.5)

    def xk(ci): return [('x', ci)]
    f2 = lambda t: t.rearrange("p a b -> p (a b)")
    v4 = lambda t: t.rearrange("p (a b) -> p a b", a=4)

    def cf(name, lo=0, hi=None):
        o, n = CSTF[name]
        return cF[:, o + lo:o + (n if hi is None else hi)]

    def cb(name, lo=0, hi=None):
        o, n = CSTB[name]
        return cB[:, o + lo:o + (n if hi is None else hi)]
    SD = F32
    identF = cf('ident'); identB = cb('ident'); identS = identF if SD == F32 else identB; onesB = cb('ones'); bonesF = cf('bones'); onesF = cf('ones')

    ptr[0] = UBASE
    pstage = alloc([L, 128]); cst17 = alloc([D]); xin = alloc([2, D])
    dma(cF[:], cst[:, 0:CF_COLS], [], ['cF'])
    PQD(lambda e: e.dma_start(out=cB[:], in_=cst[:, CF_COLS:CF_COLS + CB_COLS]), [], ['cB'])
    DVE(lambda e: e.memset(pstage[:], 0.0), [], ['pstage'])
    PROW = {}
    ro = 0
    for nm, nchk in [("norm_mix", 8), ("norm_mlp", 8), ("mu_shift", 14), ("w0", 4), ("a0", 4), ("v0", 4),
                     ("k_k", 4), ("k_a", 4), ("r_k", 4), ("ln_w", 4), ("ln_b", 4), ("pool_scale", 4),
                     ("b_ada_mix", 24), ("b_ada_mlp", 24)]:
        PROW[nm] = ro
        src = W[nm]
        if nm == "v0":
            if L > 1:
                dma(pstage[ro:ro + nchk, 1:L, :], src[0:L - 1, :].rearrange("l (c p) -> c l p", p=128), ['pstage'], ['pstage'])
        else:
            dma(pstage[ro:ro + nchk, 0:L, :], src.rearrange("l (c p) -> c l p", p=128), ['pstage'], ['pstage'])
        ro += nchk
    assert ro <= 120
    dma(pstage[120:128, 0, :], W["norm_final"].rearrange("o (c p) -> (o c) p", p=128), ['pstage'], ['pstage'])
    for l in range(L):
        pb, pk = pbank()
        tr(pb[:, 0:128], pstage[:, l, :], identF, ['pstage', 'cF'], [pk])
        acopy(pcol[:, l, :], pb[:, 0:128], [pk], ['pcol'])

    def pc(l, nm, c): return pcol[:, l, PROW[nm] + c:PROW[nm] + c + 1]
    def pcs(l, nm, n): return pcol[:, l, PROW[nm]:PROW[nm] + n]

    dma(cst17[0:17, :], cc[:, :], [], ['cst17'])
    pb, pk = pbank()
    for kc in range(NKC):
        tr(pb[:, kc * 17:(kc + 1) * 17], cst17[0:17, kc * 128:(kc + 1) * 128], identF[0:17, 0:17], ['cst17', 'cF'], [pk])
    act(f2(siluT[:]), pb[:, 0:NKC * 17], AF.Silu, [pk], ['siluT'])

    for ci in range(NCH):
        src = xp[ci * 128:(ci + 1) * 128, :] if ci < NPC else xs[:, :]
        xb_ = xin[:, ci % 2, :]
        dma(xb_, src, [], [('xin', ci % 2)])
        for half in range(2):
            pb, pk = pbank()
            for q in range(4):
                c = half * 4 + q
                tr(pb[:, q * 128:(q + 1) * 128], xb_[:, c * 128:(c + 1) * 128], identF, [('xin', ci % 2), 'cF'], [pk])
            acopy(xT[:, half * 4:half * 4 + 4, ci * 128:(ci + 1) * 128], v4(pb[:, 0:512]), [pk], xk(ci))

    ring_i = [0]

    def wload(src3, a, b):
        i = ring_i[0] % 6
        ring_i[0] += 1
        dst = ring[:, i, 0:a * b].rearrange("p (a b) -> p a b", a=a)
        PQD(lambda e: e.dma_start(out=dst, in_=src3), [], [('ring', i)])
        return dst, ('ring', i)

    def wl_kc(src2, ncol):
        return wload(src2.rearrange("(kc p) n -> p kc n", p=128), src2.shape[0] // 128, ncol)

    def ada(l, which):
        wsrc = W["w_ada_" + which]
        pbm, pkm = pbank()
        for j in range(6):
            wt, wkey = wl_kc(wsrc[l, :, j * 512:(j + 1) * 512], 512)
            for q in range(4):
                ch = j * 4 + q
                for kc in range(NKC):
                    mm(pbm[:, ch * 17:(ch + 1) * 17], wt[:, kc, q * 128:(q + 1) * 128], siluT[:, kc, :],
                       kc == 0, kc == NKC - 1, [wkey, 'siluT'], [pkm])
        tt(modv[:], pbm[:, 0:408].rearrange("p (a b) -> p a b", b=17),
           pcs(l, "b_ada_" + which, 24).unsqueeze(2).to_broadcast([128, 24, 17]), ALU.add, [pkm, 'pcol'], ['mod'])
        ts(G32[:], modv[:, 8:16, :], 1.0, 32.0, ALU.add, ALU.mult, ['mod'], ['mod'])
        tt(G32[:], G32[:], pcs(l, "norm_" + which, 8).unsqueeze(2).to_broadcast([128, 8, 17]), ALU.mult, ['mod', 'pcol'], ['mod'])

    WK = {}

    def xks(t0, n): return [('x', c) for c in range(t0 // 128, (t0 + n) // 128)]

    def tiles(size):
        out = []
        t = 0
        while t < TP:
            n = min(size, TP - t)
            out.append((t, n, False))
            t += n
        out.append((TP, 128, True))
        return out

    def norm_mod_t(t0, n, samp, hdst, hkeys):
        sqr, rstd, tmpn = WK['sq'], WK['rstd'], WK['tmpn']
        pb, pk = pbank()
        for c in range(NKC):
            act(sqr[:, c % 2, 0:n], xT[:, c, t0:t0 + n], AF.Square, xks(t0, n), [('sq', c % 2)])
            mm(pb[:, 0:n], onesB, sqr[:, c % 2, 0:n], c == 0, c == NKC - 1, [('sq', c % 2), 'cB'], [pk])
        rsq(rstd[:, 0:n], pb[:, 0:n], 1.0, D * EPS, [pk], ['rstd'])
        for c in range(NKC):
            tb = tmpn[:, c % 2, 0:n]
            tk = ('tmpn', c % 2)
            if not samp:
                stt(tb, xT[:, c, t0:t0 + n], G32[:, c, 0:1], rstd[:, 0:n], ALU.mult, ALU.mult, xks(t0, n) + ['mod', 'rstd'], [tk])
                act(hdst[:, c, 0:n], tb, AF.Identity, [tk, 'mod'], hkeys, bias=modv[:, c, 0:1])
            else:
                tt(tb, xT[:, c, t0:t0 + n], rstd[:, 0:n], ALU.mult, xks(t0, n) + ['rstd'], [tk])
                t3 = tb.rearrange("p (b t) -> p b t", t=8)
                tt(t3, t3, G32[:, c, 1:17].unsqueeze(2).to_broadcast([128, 16, 8]), ALU.mult, [tk, 'mod'], [tk])
                tt(hdst[:, c, 0:n].rearrange("p (b t) -> p b t", t=8), t3,
                   modv[:, c, 1:17].unsqueeze(2).to_broadcast([128, 16, 8]), ALU.add, [tk, 'mod'], hkeys)

    def norm_mod(ci, samp, hdst, hkeys):
        norm_mod_t(ci * 128, 128, samp, hdst, hkeys)

    def resid_update_t(t0, n, samp, c, pb, pk):
        xv = xT[:, c, t0:t0 + n]
        mg = WK['mg']
        if not samp:
            stt(xv, pb[:, 0:n], modv[:, 16 + c, 0:1], xv, ALU.mult, ALU.add, [pk, 'mod'] + xks(t0, n), xks(t0, n))
        else:
            tt(mg[:, 0:128].rearrange("p (b t) -> p b t", t=8), pb[:, 0:128].rearrange("p (b t) -> p b t", t=8),
               modv[:, 16 + c, 1:17].unsqueeze(2).to_broadcast([128, 16, 8]), ALU.mult, [pk, 'mod'], ['mg'])
            tt(xv, xv, mg[:, 0:128], ALU.add, ['mg'] + xks(t0, n), xks(t0, n))

    for l in range(L):
        S.barrier()
        ptr[0] = UBASE
        WK['sq'] = alloc([2, 128], BF16); WK['rstd'] = alloc([128]); WK['tmpn'] = alloc([2, 128]); WK['mg'] = alloc([128])
        hTc = alloc([NKC, 129], BF16); hT = hTc[:, :, 1:129]
        rkv = alloc([12, 128])
        twa = alloc([128], BF16); sgg = alloc([2, 128], BF16); t1b = alloc([128], BF16)
        vbf = alloc([4, 128], BF16); vfb = alloc([4, 128], BF16)
        FMB = [alloc([4, 128]) for _ in range(8)]
        KR = alloc([4, 256], SD)
        Bt = alloc([4, 128], SD); Kt = alloc([4, 128], SD); BWf = alloc([4, 128], SD); KWf = alloc([4, 128], SD)
        Vtok = alloc([4, 128], SD); BWtok = alloc([4, 128], SD); KWtok = alloc([4, 128], SD)
        AT = alloc([HP, 384], SD); MA = alloc([HP, 256], SD); MB = alloc([HP, 256], SD)
        PT = alloc([HP, 128], SD)
        Zn = alloc([NPI, 128], SD); Ut = alloc([NPI, 128], SD)
        wcs = alloc([4, 16])
        pz = vbf; orp = alloc([8, 128], BF16)
        shiftT = alloc([14, 16]); ppbS = alloc([4, 16, 23])
        ppb = ppbS.rearrange("p a b c -> p (a b c)")[:, 0:576].rearrange("p (a b) -> p a b", a=4)
        A_, LW_, X1, X2, X3, X4, X5, X6 = FMB
        DIAG = X1; WcBC = X2; SCo = X4[:, :, 0:64]
        S0g = H0f; HSb = A_; BKK = LW_; BRR = X3; BIGB = BWf; BIGK = KWf
        tsh = f2(X1[:])[:, 0:272].rearrange("p (a b) -> p a b", a=2)
        xl12 = X2[:, 0:2, :]
        pq2 = f2(X1[:])[:, 0:144]; pq4 = f2(X2[:])[:, 0:144]

        ts(omm[:], pcs(l, "mu_shift", 14), -1.0, 1.0, ALU.mult, ALU.add, ['pcol'], ['omm'])
        ts(omka[:], pcs(l, "k_a", 4), -1.0, 1.0, ALU.mult, ALU.add, ['pcol'], ['omm'])
        PQD(lambda e, l=l: e.dma_start(out=lw_small[0:64, 0:512], in_=W["w2"][l]), [], ['lws'])
        PQD(lambda e, l=l: e.dma_start(out=lw_small[64:128, 0:512], in_=W["a2"][l]), [], ['lws'])
        PQD(lambda e, l=l: e.dma_start(out=lw_small[:, 512:1024], in_=W["g2"][l]), [], ['lws'])
        if l > 0:
            PQD(lambda e, l=l: e.dma_start(out=lw_small[:, 1024:1152].rearrange("p (a b) -> p a b", a=4),
                                           in_=W["v1"][l - 1].rearrange("(kc p) n -> p kc n", p=128)), [], ['lws'])
            PQD(lambda e, l=l: e.dma_start(out=lw_small[0:32, 1152:1664], in_=W["v2"][l - 1]), [], ['lws'])
        PQD(lambda e, l=l: e.dma_start(out=lw_small[:, 1664:2176].rearrange("p (g d) -> p g d", g=4),
                                       in_=W["pool_w"][l].rearrange("g c d -> c g d")), [], ['lws'])
        w2a2 = lw_small[:, 0:512]; g2w = lw_small[:, 512:1024]
        v1w = lw_small[:, 1024:1152].rearrange("p (a b) -> p a b", a=4); v2w = lw_small[0:32, 1152:1664]
        poolw = lw_small[:, 1664:2176].rearrange("p (g d) -> p g d", g=4)

        ada(l, "mix")
        DVE(lambda e: e.memset(ppb[:, :, 0:15], 0.0), [], ['ppb'])
        DVE(lambda e: e.memset(hTc[:, :, 0:1], 0.0), ['hT'], ['hT'])

        W1 = []
        for j in range(5):
            ncol = 512 if j < 4 else 256
            W1.append(wl_kc(W["w_in"][l, :, j * 512:j * 512 + ncol], ncol))

        def fe_norm(ci):
            samp = ci >= NPC
            norm_mod(ci, samp, hT, ['hT'])

        def fe_proj(ci):
            samp = ci >= NPC
            last_chunk = ci == NPC - 1
            if samp:
                pbs_, pks_ = pbank()
                for g0 in range(0, 14, 4):
                    gn = min(4, 14 - g0)
                    dma(X6[0:16, :, :].rearrange("p a b -> p (a b)")[:, 0:gn * 128], sshift[l, :, g0 * 128:(g0 + gn) * 128], [], ['X6'])
                    for c in range(g0, g0 + gn):
                        tr(pbs_[:, c * 16:(c + 1) * 16], f2(X6[0:16, :, :])[:, (c - g0) * 128:(c - g0 + 1) * 128],
                           identF[0:16, 0:16], ['X6', 'cF'], [pks_])
                acopy(f2(shiftT[:]), pbs_[:, 0:224], [pks_], ['shiftT'])
            for cidx in range(18):
                wt, wkey = W1[cidx // 4]
                q = cidx % 4
                pb, pk = pbank()
                for kc in range(NKC):
                    if samp:
                        mm(pb[:, 0:128], wt[:, kc, q * 128:(q + 1) * 128], hT[:, kc, :], kc == 0, kc == NKC - 1, [wkey, 'hT'], [pk])
                    else:
                        mm(pb[:, 0:129], wt[:, kc, q * 128:(q + 1) * 128], hTc[:, kc, 0:129], kc == 0, kc == NKC - 1, [wkey, 'hT'], [pk])
                if cidx >= 14:
                    g = cidx - 14
                    if not samp:
                        acopy(ppb[:, g, 15:143], pb[:, 1:129], [pk], ['ppb'])
                    else:
                        acopy(ppbS[:, g, :, 15:23], pb[:, 0:128].rearrange("p (b t) -> p b t", t=8), [pk], ['ppbS', 'ppb'])
                    continue
                c = cidx
                tb = tsh[:, c % 2, :]
                tk = 'X1'
                mu = pc(l, "mu_shift", c)
                dst = rkv[:, c, :] if c < 12 else xl12[:, c - 12, :]
                dkey = 'rkv' if c < 12 else 'X2'
                if not samp:
                    act(tb[:, 0:129], pb[:, 0:129], AF.Identity, [pk, 'pcol'], [tk], scale=mu)
                    if last_chunk:
                        acopy(prcarry[:, c, :], pb[:, 128:129], [pk], ['prcarry'])
                    stt(dst, pb[:, 1:129], omm[:, c:c + 1], tb[:, 0:128], ALU.mult, ALU.add, [pk, 'omm', tk], [dkey])
                else:
                    p3 = pb[:, 0:128].rearrange("p (b t) -> p b t", t=8)
                    t3 = tb[:, 0:128].rearrange("p (b t) -> p b t", t=8)
                    act(t3[:, :, 1:8], p3[:, :, 0:7], AF.Identity, [pk, 'pcol'], [tk], scale=mu)
                    act(t3[:, :, 0:1], shiftT[:, c, :].unsqueeze(2), AF.Identity, ['shiftT', 'pcol'], [tk], scale=mu)
                    acopy(f2(X5[:])[:, c * 16:(c + 1) * 16].unsqueeze(2), p3[:, :, 7:8], [pk], ['X5'])
                    stt(dst, pb[:, 0:128], omm[:, c:c + 1], tb[:, 0:128], ALU.mult, ALU.add, [pk, 'omm', tk], [dkey])
            if not samp and not last_chunk:
                acopy(hTc[:, :, 0:1], hTc[:, :, 128:129], ['hT'], ['hT'])
            if last_chunk:
                pb, pk = pbank()
                tr(pb[0:14, 0:128], f2(prcarry[:]), identF, ['prcarry', 'cF'], [pk])
                acopy(f2(X6[0:14, :, :])[:, 0:128], pb[0:14, 0:128], [pk], ['X6'])
                dma(nshp[l].rearrange("(c p) -> c p", p=128), f2(X6[0:14, :, :])[:, 0:128], ['X6'], [])
            if samp:
                for g0 in range(0, 14, 4):
                    gn = min(4, 14 - g0)
                    pbx, pkx = pbank()
                    for c in range(g0, g0 + gn):
                        tr(pbx[0:16, (c - g0) * 128:(c - g0 + 1) * 128], f2(X5[:])[:, c * 16:(c + 1) * 16], identF, ['X5', 'cF'], [pkx])
                    ob = [X3, X4][(g0 // 4) % 2]
                    okey = ['X3', 'X4'][(g0 // 4) % 2]
                    acopy(f2(ob[0:16, :, :])[:, 0:gn * 128], pbx[0:16, 0:gn * 128], [pkx], [okey])
                    dma(nshs[l, :, g0 * 128:(g0 + gn) * 128], f2(ob[0:16, :, :])[:, 0:gn * 128], [okey], [])
            act(twa[0:64, :], xl12[0:64, 0, :], AF.Tanh, ['X2'], ['twa'])
            acopy(twa[64:128, :], xl12[64:128, 0, :], ['X2'], ['twa'])
            act(sgg[:, ci % 2, :], xl12[:, 1, :], AF.Sigmoid, ['X2'], [('sgg', ci % 2)])
            for p in range(4):
                pb, pk = pbank()
                mm(pb[:, 0:128], w2a2[0:64, p * 128:(p + 1) * 128], twa[0:64, :], True, True, ['lws', 'twa'], [pk])
                act(LW_[:, p, :], pb[:, 0:128], AF.Sigmoid, [pk, 'pcol'], ['LW'], bias=pc(l, "w0", p))
                pb, pk = pbank()
                mm(pb[:, 0:128], w2a2[64:128, p * 128:(p + 1) * 128], twa[64:128, :], True, True, ['lws', 'twa'], [pk])
                act(A_[:, p, :], pb[:, 0:128], AF.Sigmoid, [pk, 'pcol'], ['A'], bias=pc(l, "a0", p))
            ts(LW_[:], LW_[:], DEC_C, None, ALU.mult, ALU.bypass, ['LW'], ['LW'])

        pre_done = set()
        for ci in range(NCH):
            samp = ci >= NPC
            t0 = ci * 128
            first_chunk = ci == 0
            last_chunk = ci == NPC - 1
            nb = 16 if samp else 1
            blk = 128 // nb
            mset = 'S' if samp else 'P'
            if ci not in pre_done:
                fe_norm(ci)
                fe_proj(ci)
            r_c = rkv[:, 0:4, :]; k_c = rkv[:, 4:8, :]; v_c = rkv[:, 8:12, :]
            bc4 = lambda col: col.unsqueeze(2).to_broadcast([128, 4, 128])
            d0 = cf('rmS') if samp else onesF
            for p in range(4):
                DVE(lambda e, p=p, d0=d0: e.tensor_tensor_scan(X2[:, p, :], d0, LW_[:, p, :], 0.0, ALU.mult, ALU.add),
                    ['LW', 'cF'], ['X2'])
            tt(X1[:], X2[:], LW_[:], ALU.subtract, ['X2', 'LW'], ['X1'])
            act(X1[:], X1[:], AF.Exp, ['X1'], ['X1'])
            act(LW_[:], X2[:], AF.Exp, ['X2'], ['LW'])
            ein4 = LW_[:].rearrange("p a (b t) -> p a b t", t=blk)
            vcopy(wcs[:, :, 0:nb].unsqueeze(3), ein4[:, :, :, blk - 1:blk], ['LW'], ['wcs'])
            act(X2[:], X2[:], AF.Exp, ['X2'], ['X2'], scale=-1.0)
            if l == 0:
                acopy(vbf[:], v_c, ['rkv'], ['vbf'])
                dma(vfd[:, :, t0:t0 + 128], vbf[:], ['vbf'], [('vfd', ci)])
            else:
                dma(vfb[:], vfd[:, :, t0:t0 + 128], [('vfd', ci)], ['vfb'])
                acopy(vbf[:], v_c, ['rkv'], ['vbf'])
                pb, pk = pbank()
                for p in range(4):
                    mm(pb[0:32, 0:128], v1w[:, p, :], vbf[:, p, :], p == 0, p == 3, ['lws', 'vbf'], [pk])
                acopy(t1b[0:32, :], pb[0:32, 0:128], [pk], ['t1b'])
                for p in range(4):
                    pb, pk = pbank()
                    mm(pb[:, 0:128], v2w[:, p * 128:(p + 1) * 128], t1b[0:32, :], True, True, ['lws', 't1b'], [pk])
                    act(X3[:, p, :], pb[:, 0:128], AF.Sigmoid, [pk, 'pcol'], ['X3'], bias=pc(l, "v0", p))
                tt(X4[:], vfb[:], v_c, ALU.subtract, ['vfb', 'rkv'], ['X4'])
                tt(X4[:], X4[:], X3[:], ALU.mult, ['X4', 'X3'], ['X4'])
                tt(v_c, v_c, X4[:], ALU.add, ['rkv', 'X4'], ['rkv'])

            tt(BWf[:], k_c, bc4(pcs(l, "k_k", 4)), ALU.mult, ['rkv', 'pcol'], ['BWf'])
            tt(KWf[:], BWf[:], BWf[:], ALU.mult, ['BWf'], ['KWf'])
            pb, pk = pbank()
            mm(pb[:, 0:512], bonesF, f2(KWf[:]), True, True, ['KWf', 'cF'], [pk])
            rsq(KWf[:], v4(pb[:, 0:512]), 1.0, 1e-12, [pk], ['KWf'])
            tt(X3[:], BWf[:], KWf[:], ALU.mult, ['BWf', 'KWf'], ['X3'])
            tt(X4[:], X3[:], A_[:], ALU.mult, ['X3', 'A'], ['X4'])
            tt(BWf[:], A_[:], bc4(pcs(l, "k_a", 4)), ALU.mult, ['A', 'pcol'], ['BWf'])
            tt(BWf[:], BWf[:], bc4(omka[:, 0:4]), ALU.add, ['BWf', 'omm'], ['BWf'])
            tt(X5[:], k_c, BWf[:], ALU.mult, ['rkv', 'BWf'], ['X5'])
            tt(BWf[:], r_c, X5[:], ALU.mult, ['rkv', 'X5'], ['BWf'])
            tt(BWf[:], BWf[:], bc4(pcs(l, "r_k", 4)), ALU.mult, ['BWf', 'pcol'], ['BWf'])
            pb, pk = pbank()
            mm(pb[:, 0:512], bonesF, f2(BWf[:]), True, True, ['BWf', 'cF'], [pk])
            tt(X6[:], v4(pb[:, 0:512]), v_c, ALU.mult, [pk, 'rkv'], ['X6'])
            tt(KR[:, :, 0:128], X3[:], X1[:], ALU.mult, ['X3', 'X1'], ['KR'])
            tt(KR[:, :, 128:256], r_c, LW_[:], ALU.mult, ['rkv', 'LW'], ['KR'])
            wcb = wcs[:, :, 0:nb].unsqueeze(3).to_broadcast([128, 4, nb, blk])
            b4 = lambda t: t.rearrange("p a (b t) -> p a b t", t=blk)
            tt(X3[:], X4[:], X2[:], ALU.mult, ['X4', 'X2'], ['X3'])
            acopy(Bt[:], X3[:], ['X3'], ['Bt'])
            tt(b4(BWf[:]), b4(X3[:]), wcb, ALU.mult, ['X3', 'wcs'], ['BWf'])
            tt(X4[:], X5[:], X2[:], ALU.mult, ['X5', 'X2'], ['X4'])
            acopy(Kt[:], X4[:], ['X4'], ['Kt'])
            tt(b4(KWf[:]), b4(X4[:]), wcb, ALU.mult, ['X4', 'wcs'], ['KWf'])
            for (srcb, dstb, sk, dk) in ((v_c, Vtok, 'rkv', 'Vtok'), (BWf, BWtok, 'BWf', 'BWtok'), (KWf, KWtok, 'KWf', 'KWtok')):
                pbb, pkb = pbank()
                for p in range(4):
                    tr(pbb[:, p * 128:(p + 1) * 128], srcb[:, p, :], identS, [sk, 'cF', 'cB'], [pkb])
                acopy(f2(dstb[:]), pbb[:, 0:512], [pkb], [dk])

            if samp:
                DVE(lambda e: e.memset(S0g[:], 0.0), ['H0f'], ['H0f'])
                DVE(lambda e: e.memset(BKK[:], 0.0), ['LW'], ['LW'])
                DVE(lambda e: e.memset(BRR[:], 0.0), ['X3'], ['X3'])
            prefetch_next = (not samp) and (ci + 1 < NPC - 1)
            for hf in range(4 // NPI):
                if hf == 1 and prefetch_next:
                    fe_proj(ci + 1)
                    pre_done.add(ci + 1)
                for q in range(HP):
                    hd = hf * HP + q
                    p, hh = hd // 2, hd % 2
                    hs = slice(hh * 64, hh * 64 + 64)
                    pb, pk = pbank()
                    mm(pb[:, 0:256], Bt[hs, p, :], KR[hs, p, :], True, True, ['Bt', 'KR'], [pk])
                    mm(pb[:, 256:512], Kt[hs, p, :], KR[hs, p, :], True, True, ['Kt', 'KR'], [pk])
                    tt(MB[:, q, 128:256], pb[:, 0:128], cb('mA' + mset, 0, 128), ALU.mult, [pk, 'cB'], ['MB'])
                    tt(AT[:, q, :], pb[:, 128:512], cb('mA' + mset, 128, 512), ALU.mult, [pk, 'cB'], ['AT'])
                pbA, pkA = pbank()
                pbB, pkB = pbank()
                for q in range(HP):
                    hd = hf * HP + q
                    p, hh = hd // 2, hd % 2
                    hs = slice(hh * 64, hh * 64 + 64)
                    pbx, pkx = (pbA, pkA) if hh == 0 else (pbB, pkB)
                    mm(pbx[:, (q // 2) * 128:(q // 2 + 1) * 128], KR[hs, p, 0:128], Bt[hs, p, :], True, True, ['KR', 'Bt'], [pkx])
                for hh, (pbx, pkx) in enumerate(((pbA, pkA), (pbB, pkB))):
                    tt(MB[:, hh:HP:2, 0:128], pbx[:, 0:NPI * 128].rearrange("p (a b) -> p a b", a=NPI),
                       cb('mL' + mset).unsqueeze(1).to_broadcast([128, NPI, 128]), ALU.mult, [pkx, 'cB'], ['MB'])
                tt(PT[:], MB[:, :, 128:256], identF.unsqueeze(1).to_broadcast([128, HP, 128]), ALU.add, ['MB', 'cF'], ['PT'])
                if hf == 0 and not samp:
                    PA = bass.AP(tensor=X1.tensor, offset=X1.offset, ap=[list(X1.ap[0]), [144, 4], [1, 144]])
                    PBs = bass.AP(tensor=A_.tensor, offset=A_.offset, ap=[list(A_.ap[0]), [144, 4], [1, 144]])
                    ka, kb = ['X1', 'X2'], ['A', 'LW']
                    tt(PA[:, :, 1:143], ppb[:, :, 1:143], ppb[:, :, 0:142], ALU.add, ['ppb'], ka)
                    tt(PBs[:, 1:4, 3:143], PA[:, 1:4, 3:143], PA[:, 1:4, 1:141], ALU.add, ka, kb)
                    tt(PA[:, 2:4, 7:143], PBs[:, 2:4, 7:143], PBs[:, 2:4, 3:139], ALU.add, kb + ka, ka)
                    tt(PBs[:, 3:4, 15:143], PA[:, 3:4, 15:143], PA[:, 3:4, 7:135], ALU.add, ka + kb, kb)
                    psrc = [PA[:, 0, :], PBs[:, 1, :], PA[:, 2, :], PBs[:, 3, :]]
                    for g in range(4):
                        if first_chunk:
                            mg = WK['mg']
                            stt(mg[:], psrc[g][:, 15:143], 1.0 / WINS[g], ppb[:, g, 15:143], ALU.mult, ALU.subtract, ka + kb + ['ppb'], ['mg'])
                            tt(mg[:, 0:16], psrc[g][:, 15:31], cf('pcorr')[:, g * 16:(g + 1) * 16], ALU.mult, ka + kb + ['cF', 'mg'], ['mg'])
                            tt(mg[:, 0:16], mg[:, 0:16], ppb[:, g, 15:31], ALU.subtract, ['mg', 'ppb'], ['mg'])
                            vcopy(pz[:, g, :], mg[:], ['mg'], ['vbf'])
                        else:
                            stt(pz[:, g, :], psrc[g][:, 15:143], 1.0 / WINS[g], ppb[:, g, 15:143], ALU.mult, ALU.subtract, ka + kb + ['ppb'], ['vbf'])
                    if not last_chunk:
                        vcopy(PA[:, :, 0:15], ppb[:, :, 128:143], ['ppb'] + ka, ka)
                        vcopy(ppb[:, :, 0:15], PA[:, :, 0:15], ka + ['ppb'], ['ppb'])
                if hf == 0 and prefetch_next:
                    fe_norm(ci + 1)
                nlev = 2 if samp else 6
                curM = lambda q: MB[:, q, 0:128]
                curMT = lambda q: MB[:, q, 128:256]
                curk = ['MB']
                for lev in range(nlev):
                    nxt, nk = (MA, 'MA') if lev % 2 == 0 else (MB, 'MB')
                    lastlev = lev == nlev - 1
                    for h2 in range(NPI):
                        pb, pk = pbank()
                        for qq in range(2):
                            q = h2 * 2 + qq
                            mm(pb[:, qq * 256:qq * 256 + 128], curMT(q), curM(q), True, True, curk, [pk])
                            if not lastlev:
                                mm(pb[:, qq * 256 + 128:qq * 256 + 256], curM(q), curMT(q), True, True, curk, [pk])
                        acopy(nxt[:, h2 * 2:h2 * 2 + 2, :], pb[:, 0:512].rearrange("p (a b) -> p a b", a=2), [pk], [nk])
                    pb, pk = pbank()
                    for q in range(HP):
                        mm(pb[:, q * 128:(q + 1) * 128], nxt[:, q, 0:128], PT[:, q, :], True, True, [nk, 'PT'], [pk])
                    tt(PT[:], PT[:], pb[:, 0:HP * 128].rearrange("p (a b) -> p a b", a=HP), ALU.add, [pk, 'PT'], ['PT'])
                    curM = lambda q, nxt=nxt: nxt[:, q, 0:128]
                    curMT = lambda q, nxt=nxt: nxt[:, q, 128:256]
                    curk = [nk]

                if hf == 0 and not samp:
                    for g in range(4):
                        pb, pk = pbank()
                        mm(pb[:, 0:128], poolw[:, g, :], pz[:, g, :], True, True, ['lws', 'vbf'], [pk])
                        act(orp[:, 4 + g, :], pb[:, 0:128], AF.Identity, [pk, 'pcol'], ['orp'], scale=pc(l, "pool_scale", g))
                if not samp:
                    for pp_ in range(NPI):
                        p = hf * NPI + pp_
                        if not first_chunk:
                            mm(ZB[:, pp_ * 128:(pp_ + 1) * 128], KR[:, p, 0:128], H0b[:, p, :], True, False, ['KR', 'H0f'], [ZK])
                        for hh in range(2):
                            q = pp_ * 2 + hh
                            mm(ZB[:, pp_ * 128 + hh * 64:pp_ * 128 + hh * 64 + 64], AT[:, q, 128:256],
                               Vtok[:, p, hh * 64:hh * 64 + 64], first_chunk, True, ['AT', 'Vtok'], [ZK])
                    act(f2(Zn[:]), ZB[:, 0:NPI * 128], AF.Copy, [ZK], ['Zn'], scale=-1.0)
                    pbu, pku = pbank()
                    for q in range(HP):
                        pp_, hh = q // 2, q % 2
                        mm(pbu[:, q * 64:(q + 1) * 64], PT[:, q, :], Zn[:, pp_, hh * 64:hh * 64 + 64], True, True, ['PT', 'Zn'], [pku])
                    acopy(f2(Ut[:]), pbu[:, 0:NPI * 128], [pku], ['Ut'])
                    for pp_ in range(NPI):
                        p = hf * NPI + pp_
                        if not first_chunk:
                            mm(YB[:, pp_ * 128:(pp_ + 1) * 128], H0b[:, p, :], KR[:, p, 128:256], True, False, ['H0f', 'KR'], [YK])
                        for hh in range(2):
                            q = pp_ * 2 + hh
                            hs = slice(hh * 64, hh * 64 + 64)
                            mm(YB[hs, pp_ * 128:(pp_ + 1) * 128], Ut[:, pp_, hh * 64:hh * 64 + 64], AT[:, q, 0:128],
                               first_chunk, False, ['Ut', 'AT'], [YK])
                            mm(YB[hs, pp_ * 128:(pp_ + 1) * 128], Vtok[:, p, hh * 64:hh * 64 + 64], AT[:, q, 256:384],
                               False, True, ['Vtok', 'AT'], [YK])
                    acopy(f2(X5[:, hf * NPI:(hf + 1) * NPI, :]), YB[:, 0:NPI * 128], [YK], ['X5'])
                    pbh, pkh = pbank()
                    for pp_ in range(NPI):
                        p = hf * NPI + pp_
                        mm(pbh[:, pp_ * 128:(pp_ + 1) * 128], BWtok[:, p, :], Ut[:, pp_, :], True, False, ['BWtok', 'Ut'], [pkh])
                        mm(pbh[:, pp_ * 128:(pp_ + 1) * 128], KWtok[:, p, :], Vtok[:, p, :], False, True, ['KWtok', 'Vtok'], [pkh])
                    hsl = slice(hf * NPI, (hf + 1) * NPI)
                    tt(X3[:, 0:NPI, :], pbh[:, 0:NPI * 128].rearrange("p (a b) -> p a b", a=NPI),
                       bonesF.unsqueeze(1).to_broadcast([128, NPI, 128]), ALU.mult, [pkh, 'cF'], ['X3'])
                    if first_chunk:
                        vcopy(H0f[:, hsl, :], X3[:, 0:NPI, :], ['X3'], ['H0f'])
                    else:
                        tt(H0f[:, hsl, :], H0f[:, hsl, :], wcs[:, hsl, 0:1].to_broadcast([128, NPI, 128]), ALU.mult, ['H0f', 'wcs'], ['H0f'])
                        tt(H0f[:, hsl, :], H0f[:, hsl, :], X3[:, 0:NPI, :], ALU.add, ['H0f', 'X3'], ['H0f'])
                    if last_chunk:
                        pbs, pks = pbank()
                        for pp_ in range(NPI):
                            tr(pbs[:, pp_ * 128:(pp_ + 1) * 128], H0f[:, hf * NPI + pp_, :], identF, ['H0f', 'cF'], [pks])
                        for hh in range(2):
                            hs = slice(hh * 64, hh * 64 + 64)
                            acopy(SCo[hs, 0:NPI, :], pbs[hs, 0:NPI * 128].rearrange("p (a b) -> p a b", a=NPI)[:, :, hh * 64:hh * 64 + 64], [pks], ['X4'])
                        dma(nwkvp[l, hf * HP:hf * HP + HP].rearrange("(p hh) v n -> (hh v) p n", hh=2), SCo[:, 0:NPI, :], ['X4'], ['X4'])
                else:
                    for pp_ in range(NPI):
                        p = hf * NPI + pp_
                        for g in range(4):
                            for hh in range(2):
                                hs = slice(hh * 64, hh * 64 + 64)
                                dma(S0g[hs, :, hh * 64:hh * 64 + 64],
                                    swkv[l, g * 4:g * 4 + 4, 2 * p + hh, :, :].rearrange("b v k -> v b k"), ['H0f'], ['H0f'])
                            pb, pk = pbank()
                            for j in range(4):
                                tr(pb[:, j * 128:(j + 1) * 128], S0g[:, j, :], identF, ['H0f', 'cF'], [pk])
                            acopy(f2(HSb[:]), pb[:, 0:512], [pk], ['A'])
                            bkk_diag = bass.AP(tensor=BKK.tensor, offset=BKK.offset + 32 * g,
                                               ap=[list(BKK.ap[0]), [136, 4], [1, 8]])
                            vcopy(bkk_diag, KR[:, p, g * 32:g * 32 + 32].rearrange("q (b t) -> q b t", t=8), ['KR', 'LW'], ['LW'])
                            brr_diag = bass.AP(tensor=BRR.tensor, offset=BRR.offset + 32 * g,
                                               ap=[list(BRR.ap[0]), [136, 4], [1, 8]])
                            vcopy(brr_diag, KR[:, p, 128 + g * 32:128 + g * 32 + 32].rearrange("q (b t) -> q b t", t=8), ['KR', 'X3'], ['X3'])
                            for j in range(4):
                                b_ = g * 4 + j
                                mm(ZB[:, 0:128], BKK[:, j, :], HSb[:, j, :], b_ == 0, False, ['LW', 'A'], [ZK])
                                mm(YB[:, 0:128], HSb[:, j, :], BRR[:, j, :], b_ == 0, False, ['A', 'X3'], [YK])
                            DVE(lambda e, bkk_diag=bkk_diag: e.memset(bkk_diag, 0.0), ['LW'], ['LW'])
                            DVE(lambda e, brr_diag=brr_diag: e.memset(brr_diag, 0.0), ['X3'], ['X3'])
                        for hh in range(2):
                            q = pp_ * 2 + hh
                            mm(ZB[:, hh * 64:hh * 64 + 64], AT[:, q, 128:256], Vtok[:, p, hh * 64:hh * 64 + 64], False, True, ['AT', 'Vtok'], [ZK])
                        act(Zn[:, 0, :], ZB[:, 0:128], AF.Copy, [ZK], ['Zn'], scale=-1.0)
                        pbu, pku = pbank()
                        for hh in range(2):
                            q = pp_ * 2 + hh
                            mm(pbu[:, hh * 64:hh * 64 + 64], PT[:, q, :], Zn[:, 0, hh * 64:hh * 64 + 64], True, True, ['PT', 'Zn'], [pku])
                        acopy(Ut[:, 0, :], pbu[:, 0:128], [pku], ['Ut'])
                        for hh in range(2):
                            q = pp_ * 2 + hh
                            hs = slice(hh * 64, hh * 64 + 64)
                            mm(YB[hs, 0:128], Ut[:, 0, hh * 64:hh * 64 + 64], AT[:, q, 0:128], False, False, ['Ut', 'AT'], [YK])
                            mm(YB[hs, 0:128], Vtok[:, p, hh * 64:hh * 64 + 64], AT[:, q, 256:384], False, True, ['Vtok', 'AT'], [YK])
                        acopy(X5[:, p, :], YB[:, 0:128], [YK], ['X5'])
                        smk = cb('seqm')
                        for g in range(4):
                            for hh in range(2):
                                hs = slice(hh * 64, hh * 64 + 64)
                                dma(S0g[hs, :, hh * 64:hh * 64 + 64],
                                    swkv[l, g * 4:g * 4 + 4, 2 * p + hh, :, :].rearrange("b v k -> v b k"), ['H0f'], ['H0f'])
                            sm4 = smk[:, g * 4:g * 4 + 4].unsqueeze(2).to_broadcast([128, 4, 128])
                            tt(BIGB[:], BWtok[:, p, :].unsqueeze(1).to_broadcast([128, 4, 128]), sm4, ALU.mult, ['BWtok', 'cB'], ['BWf'])
                            tt(BIGK[:], KWtok[:, p, :].unsqueeze(1).to_broadcast([128, 4, 128]), sm4, ALU.mult, ['KWtok', 'cB'], ['KWf'])
                            tt(DIAG[:], identF.unsqueeze(1).to_broadcast([128, 4, 128]),
                               wcs[:, p, g * 4:g * 4 + 4].unsqueeze(2).to_broadcast([128, 4, 128]), ALU.mult, ['cF', 'wcs'], ['X1'])
                            pb, pk = pbank()
                            mm(pb[:, 0:512], onesF, f2(DIAG[:]), True, True, ['X1', 'cF'], [pk])
                            acopy(f2(WcBC[:]), pb[:, 0:512], [pk], ['X2'])
                            tt(WcBC[:], WcBC[:], S0g[:], ALU.mult, ['X2', 'H0f'], ['X2'])
                            pbs, pks = pbank()
                            mm(pbs[:, 0:512], Ut[:, 0, :], f2(BIGB[:]), True, False, ['Ut', 'BWf'], [pks])
                            mm(pbs[:, 0:512], Vtok[:, p, :], f2(BIGK[:]), False, True, ['Vtok', 'KWf'], [pks])
                            tt(WcBC[:], WcBC[:], v4(pbs[:, 0:512]), ALU.add, ['X2', pks], ['X2'])
                            for hh in range(2):
                                hs = slice(hh * 64, hh * 64 + 64)
                                vcopy(SCo[hs, :, :], WcBC[hs, :, hh * 64:hh * 64 + 64], ['X2', 'X4'], ['X4'])
                            dma(nwkvs[l, g * 4:g * 4 + 4, 2 * p:2 * p + 2, :, :].rearrange("b hh v n -> (hh v) b n"), SCo[:], ['X4'], ['X4'])

            pb, pk = pbank()
            mm(pb[:, 0:512], bonesF, f2(X5[:]), True, True, ['X5', 'cF'], [pk])
            ts(X1[:], v4(pb[:, 0:512]), 1.0 / 64, None, ALU.mult, ALU.bypass, [pk], ['X1'])
            tt(X5[:], X5[:], X1[:], ALU.subtract, ['X5', 'X1'], ['X5'])
            tt(X2[:], X5[:], X5[:], ALU.mult, ['X5'], ['X2'])
            pb, pk = pbank()
            mm(pb[:, 0:512], bonesF, f2(X2[:]), True, True, ['X2', 'cF'], [pk])
            rsq(X1[:], v4(pb[:, 0:512]), 1.0 / 64, GN_EPS, [pk], ['X1'])
            tt(X5[:], X5[:], X1[:], ALU.mult, ['X5', 'X1'], ['X5'])
            tt(X5[:], X5[:], bc4(pcs(l, "ln_w", 4)), ALU.mult, ['X5', 'pcol'], ['X5'])
            tt(X5[:], X5[:], bc4(pcs(l, "ln_b", 4)), ALU.add, ['X5', 'pcol'], ['X5'])
            tt(X5[:], X5[:], X6[:], ALU.add, ['X5', 'X6'], ['X5'])
            pbg, pkg = pbank()
            for p in range(4):
                mm(pbg[:, p * 128:(p + 1) * 128], g2w[:, p * 128:(p + 1) * 128], sgg[:, ci % 2, :], True, True, ['lws', ('sgg', ci % 2)], [pkg])
            tt(orp[:, 0:4, :], X5[:], v4(pbg[:, 0:512]), ALU.mult, ['X5', pkg], ['orp'])

            if not samp:
                if last_chunk:
                    pb, pk = pbank()
                    for g in range(4):
                        tr(pb[0:15, g * 128:(g + 1) * 128], ppb[:, g, 128:143], identF, ['ppb', 'cF'], [pk])
                    acopy(f2(X1[0:15, :, :]), pb[0:15, 0:512], [pk], ['X1'])
                    dma(npoolp[l], f2(X1[0:15, :, :]), ['X1'], ['X1'])
                pass
            else:
                spv = spool[l].rearrange("b r c -> (b r) c")
                for half in range(2):
                    r0, rn = (0, 128) if half == 0 else (128, 112)
                    stg = [X1, X2][half]
                    dma(f2(stg[0:rn, :, :]), spv[r0:r0 + rn, :], [], [['X1', 'X2'][half]])
                    pb, pk = pbank()
                    for g in range(4):
                        tr(pb[:, g * 128:g * 128 + rn], f2(stg[0:rn, :, :])[:, g * 128:(g + 1) * 128], identF[0:rn, 0:rn],
                           [['X1', 'X2'][half], 'cF'], [pk])
                    acopy(f2(X3[:]) if half == 0 else f2(X4[:]), pb[:, 0:512], [pk], [['X3', 'X4'][half]])
                for g in range(4):
                    vcopy(ppbS[:, g, 0:8, 0:15], X3[:, g, 0:120].rearrange("p (b r) -> p b r", r=15), ['X3'], ['ppbS', 'ppb'])
                    vcopy(ppbS[:, g, 8, 0:8], X3[:, g, 120:128], ['X3'], ['ppbS', 'ppb'])
                    vcopy(ppbS[:, g, 8, 8:15], X4[:, g, 0:7], ['X4'], ['ppbS', 'ppb'])
                    vcopy(ppbS[:, g, 9:16, 0:15], X4[:, g, 7:112].rearrange("p (b r) -> p b r", r=15), ['X4'], ['ppbS', 'ppb'])
                for half in range(2):
                    r0, rn = (0, 128) if half == 0 else (128, 112)
                    stg = [X3, X4][half]
                    for g in range(4):
                        if half == 0:
                            vcopy(stg[:, g, 0:120].rearrange("p (b r) -> p b r", r=15), ppbS[:, g, 0:8, 8:23], ['ppbS'], [['X3', 'X4'][half]])
                            vcopy(stg[:, g, 120:128], ppbS[:, g, 8, 8:16], ['ppbS'], [['X3', 'X4'][half]])
                        else:
                            vcopy(stg[:, g, 0:7], ppbS[:, g, 8, 16:23], ['ppbS'], [['X3', 'X4'][half]])
                            vcopy(stg[:, g, 7:112].rearrange("p (b r) -> p b r", r=15), ppbS[:, g, 9:16, 8:23], ['ppbS'], [['X3', 'X4'][half]])
                    pb, pk = pbank()
                    for g in range(4):
                        tr(pb[0:rn, g * 128:(g + 1) * 128], stg[:, g, 0:rn], identF, [['X3', 'X4'][half], 'cF'], [pk])
                    ob = [X1, X2][half]
                    acopy(f2(ob[0:rn, :, :]), pb[0:rn, 0:512], [pk], [['X1', 'X2'][half]])
                    dma(npools[l, r0:r0 + rn, :], f2(ob[0:rn, :, :]), [['X1', 'X2'][half]], [['X1', 'X2'][half]])
                for g in range(4):
                    A0 = ppbS[:, g, :, :]
                    cur = X3[:].rearrange("p a b -> p (a b)")[:, 0:368].rearrange("p (b r) -> p b r", r=23)
                    oth = X4[:].rearrange("p a b -> p (a b)")[:, 0:368].rearrange("p (b r) -> p b r", r=23)
                    tt(cur[:, :, 1:23], A0[:, :, 1:23], A0[:, :, 0:22], ALU.add, ['ppbS', 'X3', 'X4'], ['X3', 'X4'])
                    span = 2
                    while span < WINS[g]:
                        lo = 2 * span - 1
                        tt(oth[:, :, lo:23], cur[:, :, lo:23], cur[:, :, lo - span:23 - span], ALU.add, ['X3', 'X4'], ['X3', 'X4'])
                        cur, oth = oth, cur
                        span *= 2
                    mg = WK['mg']
                    stt(mg[:].rearrange("p (b t) -> p b t", t=8), cur[:, :, 15:23], 1.0 / WINS[g], ppbS[:, g, :, 15:23],
                        ALU.mult, ALU.subtract, ['X3', 'X4', 'ppbS'], ['mg'])
                    acopy(pz[:, g, :], mg[:], ['mg'], ['vbf'])
            if samp:
                for g in range(4):
                    pb, pk = pbank()
                    mm(pb[:, 0:128], poolw[:, g, :], pz[:, g, :], True, True, ['lws', 'vbf'], [pk])
                    act(orp[:, 4 + g, :], pb[:, 0:128], AF.Identity, [pk, 'pcol'], ['orp'], scale=pc(l, "pool_scale", g))
            dma(orpd[:, :, t0:t0 + 128], orp[:], ['orp'], [('orpd', ci)])

        S.barrier()
        ptr[0] = UBASE
        T2 = 512
        WK['sq'] = alloc([2, T2], BF16); WK['rstd'] = alloc([T2]); WK['tmpn'] = alloc([2, T2]); WK['mg'] = alloc([128])
        hT2 = alloc([NKC, T2], BF16)
        mrg = alloc([NKC, NTOK], BF16)
        orp2 = alloc([1, 8, T2], BF16)
        g16 = alloc([16, T2], BF16)
        tmpb = WK['tmpn'][:, 0:1, :]
        W2 = [wl_kc(W["w_in"][l, :, 2304 + j * 512:2304 + (j + 1) * 512], 512) for j in range(4)]
        wbr, wbrk = wl_kc(W["w_br_rwkv"][l], 1024)
        wbp, wbpk = wl_kc(W["w_br_pool"][l], 1024)
        for ti, (t0, n, samp) in enumerate(tiles(T2)):
            ob = orp2[:, 0, :, 0:n]
            ok = ('orp2', 0)
            dma(ob, orpd[:, :, t0:t0 + n], [('orpd', c) for c in range(t0 // 128, (t0 + n) // 128)], [ok])
            norm_mod_t(t0, n, samp, hT2, ['hT2'])
            for cg in range(16):
                wt, wkey = W2[cg // 4]
                q = cg % 4
                pb, pk = pbank()
                for kc in range(NKC):
                    mm(pb[:, 0:n], wt[:, kc, q * 128:(q + 1) * 128], hT2[:, kc, 0:n], kc == 0, kc == NKC - 1, [wkey, 'hT2'], [pk])
                act(g16[:, cg, 0:n], pb[:, 0:n], AF.Sigmoid, [pk], ['g16'])
            for c in range(8):
                pb, pk = pbank()
                for kc in range(4):
                    mm(pb[:, 0:n], wbr[:, kc, c * 128:(c + 1) * 128], ob[:, kc, :], kc == 0, kc == 3, [wbrk, ok], [pk])
                tt(tmpb[:, 0, 0:n], pb[:, 0:n], g16[:, c, 0:n], ALU.mult, [pk, 'g16'], [('tmpn', 0)])
                pb2, pk2 = pbank()
                for kc in range(4):
                    mm(pb2[:, 0:n], wbp[:, kc, c * 128:(c + 1) * 128], ob[:, 4 + kc, :], kc == 0, kc == 3, [wbpk, ok], [pk2])
                tt(mrg[:, c, t0:t0 + n], pb2[:, 0:n], g16[:, 8 + c, 0:n], ALU.mult, [pk2, 'g16'], [('mrg', t0)])
                tt(mrg[:, c, t0:t0 + n], mrg[:, c, t0:t0 + n], tmpb[:, 0, 0:n], ALU.add, [('tmpn', 0), ('mrg', t0)], [('mrg', t0)])
        wo = [wl_kc(W["w_out"][l, :, j * 512:(j + 1) * 512], 512) for j in range(2)]
        for (t0, n, samp) in tiles(T2):
            for c in range(8):
                wt, wkey = wo[c // 4]
                q = c % 4
                pb, pk = pbank()
                for kc in range(NKC):
                    mm(pb[:, 0:n], wt[:, kc, q * 128:(q + 1) * 128], mrg[:, kc, t0:t0 + n], kc == 0, kc == NKC - 1, [wkey, ('mrg', t0)], [pk])
                resid_update_t(t0, n, samp, c, pb, pk)

        S.barrier()
        ptr[0] = UBASE
        T3 = 512
        WK['sq'] = alloc([2, T3], BF16); WK['rstd'] = alloc([T3]); WK['tmpn'] = alloc([2, T3]); WK['mg'] = alloc([128])
        hTm = alloc([NKC, NTOK], BF16)
        rl = alloc([2, T3]); r2 = alloc([8, T3], BF16)
        ada(l, "mlp")
        for (t0, n, samp) in tiles(T3):
            norm_mod_t(t0, n, samp, hTm[:, :, t0:t0 + n], [('hTm', t0)])
        for qd in range(4):
            w1 = [wl_kc(W["w_ff1"][l, :, qd * 1024 + j * 512:qd * 1024 + (j + 1) * 512], 512) for j in range(2)]
            w2 = [wl_kc(W["w_ff2"][l, qd * 1024:(qd + 1) * 1024, j * 512:(j + 1) * 512], 512) for j in range(2)]
            for (t0, n, samp) in tiles(T3):
                for c in range(8):
                    wt, wkey = w1[c // 4]
                    q = c % 4
                    pb, pk = pbank()
                    for kc in range(NKC):
                        mm(pb[:, 0:n], wt[:, kc, q * 128:(q + 1) * 128], hTm[:, kc, t0:t0 + n], kc == 0, kc == NKC - 1, [wkey, ('hTm', t0)], [pk])
                    act(rl[:, c % 2, 0:n], pb[:, 0:n], AF.Relu, [pk], [('rl', c % 2)])
                    tt(r2[:, c, 0:n], rl[:, c % 2, 0:n], rl[:, c % 2, 0:n], ALU.mult, [('rl', c % 2)], [('r2', c)])
                for c in range(8):
                    wt, wkey = w2[c // 4]
                    q = c % 4
                    pb, pk = pbank()
                    for kc in range(NKC):
                        mm(pb[:, 0:n], wt[:, kc, q * 128:(q + 1) * 128], r2[:, kc, 0:n], kc == 0, kc == NKC - 1, [wkey, ('r2', kc)], [pk])
                    resid_update_t(t0, n, samp, c, pb, pk)

    S.barrier()
    ptr[0] = UBASE
    sqr = alloc([2, 128], BF16); rstd = alloc([128]); tmpo = alloc([NKC, 128]); yout = alloc([2, D])
    ts(G32f[:], pcol[:, 0, 120:128].unsqueeze(2), 32.0, None, ALU.mult, ALU.bypass, ['pcol'], ['G32f'])
    for ci in range(NCH):
        t0 = ci * 128
        pb, pk = pbank()
        for c in range(NKC):
            act(sqr[:, c % 2, :], xT[:, c, t0:t0 + 128], AF.Square, xk(ci), [('sq', c % 2)])
            mm(pb[:, 0:128], onesB, sqr[:, c % 2, :], c == 0, c == NKC - 1, [('sq', c % 2), 'cB'], [pk])
        rsq(rstd[:], pb[:, 0:128], 1.0, D * EPS, [pk], ['rstd'])
        for c in range(NKC):
            stt(tmpo[:, c, :], xT[:, c, t0:t0 + 128], G32f[:, c, 0:1], rstd[:], ALU.mult, ALU.mult, xk(ci) + ['G32f', 'rstd'], ['tmpo'])
        yo = yout[:, ci % 2, :]
        for half in range(2):
            pb, pk = pbank()
            for q in range(4):
                c = half * 4 + q
                tr(pb[:, q * 128:(q + 1) * 128], tmpo[:, c, :], identF, ['tmpo', 'cF'], [pk])
            acopy(yo[:, half * 512:(half + 1) * 512], pb[:, 0:512], [pk], [('yout', ci % 2)])
        dst = yp[t0:t0 + 128, :] if ci < NPC else ys[:, :]
        dma(dst, yo, [('yout', ci % 2)], [('yout', ci % 2)])
    return nc, S, st


CSTF = {}
CSTB = {}
CF_COLS = 0
CB_COLS = 0


def _layout_consts():
    global CF_COLS, CB_COLS
    o = 0
    for nm, n in [('ident', 128), ('ones', 128), ('bones', 128), ('rmS', 128), ('pcorr', 64)]:
        CSTF[nm] = (o, n)
        o += n
    CF_COLS = o
    o = 0
    for nm, n in [('ident', 128), ('ones', 128), ('mAP', 512), ('mAS', 512), ('mLP', 128), ('mLS', 128), ('seqm', 16)]:
        CSTB[nm] = (o, n)
        o += n
    CB_COLS = o


_layout_consts()


def make_consts():
    c = np.zeros((128, CF_COLS + CB_COLS), np.float32)

    def putf(nm, a):
        o, n = CSTF[nm]
        c[:, o:o + n] = a

    def putb(nm, a):
        o, n = CSTB[nm]
        c[:, CF_COLS + o:CF_COLS + o + n] = a
    i = np.arange(128)
    putf('ident', np.eye(128)); putb('ident', np.eye(128))
    putf('ones', np.ones((128, 128))); putb('ones', np.ones((128, 128)))
    putf('bones', (i[:, None] // 64 == i[None, :] // 64).astype(np.float32))
    s, t = i[:, None], i[None, :]
    for tag, same in (('P', np.ones((128, 128), bool)), ('S', (s // 8) == (t // 8))):
        lt = ((s < t) & same).astype(np.float32)
        le = ((s <= t) & same).astype(np.float32)
        gtm = ((s > t) & same).astype(np.float32)
        putb('mA' + tag, np.concatenate([-lt, le, lt, le], axis=1))
        putb('mL' + tag, -gtm)
    rmS = np.ones((128, 128), np.float32)
    rmS[:, ::8] = 0
    putf('rmS', rmS)
    putb('seqm', (i[:, None] // 8 == np.arange(16)[None, :]).astype(np.float32))
    pc_ = np.zeros((128, 64), np.float32)
    for g, w in enumerate(WINS):
        tt_ = np.arange(16)
        pc_[:, g * 16:(g + 1) * 16] = (1.0 / np.minimum(tt_ + 1, w))[None, :]
    putf('pcorr', pc_)
    return c


def emit(nc, S, st):
    sems = {name: st.enter_context(nc.semaphore(name)) for name in S.cnt}
    block = st.enter_context(nc.Block())

    def run(stream, eng):
        for waits, fn, sem, inc in S.ops[stream]:
            for (s, v) in waits:
                eng.wait_ge(sems[s], v)
            if fn is not None:
                fn(eng).then_inc(sems[sem], inc)

    @block.sync
    def _(e):
        run('sp', e)
        for nm in S.cnt:
            if nm.startswith('sp'):
                e.wait_ge(sems[nm], S.cnt[nm])

    @block.gpsimd
    def _(e):
        run('pool', e)

    @block.tensor
    def _(e):
        run('pe', e)

    @block.vector
    def _(e):
        run('dve', e)

    @block.scalar
    def _(e):
        run('act', e)
    st.close()
    return nc


_WNAMES = ["w_ada_mix", "b_ada_mix", "norm_mix", "w_in", "mu_shift", "w0", "w2", "a0", "a2", "g2", "v0", "v1", "v2",
           "k_k", "k_a", "r_k", "ln_w", "ln_b", "pool_w", "pool_scale", "w_br_rwkv", "w_br_pool", "w_out",
           "w_ada_mlp", "b_ada_mlp", "norm_mlp", "w_ff1", "w_ff2", "norm_final"]


def make_in_maps(inputs, ncores, L):
    consts = make_consts()
    f = lambda a: np.ascontiguousarray(np.asarray(a, dtype=np.float32))
    shared = {}
    for nm in _WNAMES:
        a = f(inputs[nm])
        if nm == "r_k":
            a = a.reshape(L, MIX)
        if nm == "norm_final":
            a = a.reshape(1, D)
        shared[nm] = a
    shared["cst"] = consts
    maps = []
    for i in range(ncores):
        m = dict(shared)
        m["xp"] = f(inputs["x_prompt"][i])
        m["xs"] = f(inputs["x_sample"][16 * i:16 * i + 16]).reshape(128, D)
        m["cc"] = f(np.concatenate([np.asarray(inputs["c_prompt"])[i:i + 1], np.asarray(inputs["c_sample"])[16 * i:16 * i + 16]], axis=0))
        m["sshift"] = f(np.asarray(inputs["state_shift"])[:, 16 * i:16 * i + 16])
        m["spool"] = f(np.asarray(inputs["state_pool"])[:, 16 * i:16 * i + 16])
        m["swkv"] = f(np.asarray(inputs["state_wkv"])[:, 16 * i:16 * i + 16])
        maps.append(m)
    return maps


def gather(R, ncores, L):
    y_p = np.stack([R[i]["yp"] for i in range(ncores)], 0)
    y_s = np.concatenate([R[i]["ys"].reshape(16, 8, D) for i in range(ncores)], 0)
    sh_p = np.stack([R[i]["nshp"] for i in range(ncores)], 1)
    pool_p = np.stack([R[i]["npoolp"] for i in range(ncores)], 1)
    wkv_p = np.stack([R[i]["nwkvp"] for i in range(ncores)], 1)
    sh_s = np.concatenate([R[i]["nshs"] for i in range(ncores)], 1)
    pool_s = np.concatenate([R[i]["npools"].reshape(L, 16, 15, MIX) for i in range(ncores)], 1)
    wkv_s = np.concatenate([R[i]["nwkvs"] for i in range(ncores)], 1)
    return tuple(np.ascontiguousarray(a, dtype=np.float32) for a in (y_p, y_s, sh_p, pool_p, wkv_p, sh_s, pool_s, wkv_s))


def kernel(**inputs):
    ncores = 8
    L = 4
    nc, S, st = build(TP=2048, L=L)
    emit(nc, S, st)
    maps = make_in_maps(inputs, ncores, L)
    res = run_bass_kernel_spmd(nc, maps, core_ids=list(range(ncores)))
    return gather(res.results, ncores, L)
```

```python
import numpy as np
from contextlib import ExitStack
import concourse.bass as bass
import concourse.mybir as mybir
from concourse.bass_utils import run_bass_kernel_spmd

F32 = mybir.dt.float32
BF16 = mybir.dt.bfloat16
AF = mybir.ActivationFunctionType
ALU = mybir.AluOpType

D = 1024
NKC = 8
MIX = 512
RW = 1792
INC = 4352
DFF = 4096
EPS = 1e-6
GN_EPS = 64e-5
DEC_C = -float(np.exp(-0.5))
WINS = (2, 4, 8, 16)


class Sched:
    STREAMS = ('pe', 'act', 'dve', 'pool', 'sp')

    def __init__(self):
        self.ops = {s: [] for s in self.STREAMS}
        self.cnt = {}
        self.known = {s: {} for s in self.STREAMS}
        self.lastw = {}
        self.readers = {}
        self.dma_i = {}

    def op(self, stream, fn, r=(), w=(), sem=None, inc=1, nsem=1):
        sem = sem or stream
        if nsem > 1:
            i = self.dma_i.get(sem, 0)
            self.dma_i[sem] = i + 1
            sem = "%s%d" % (sem, i % nsem)
        need = {}
        if nsem > 1 and self.cnt.get(sem, 0):
            need[sem] = self.cnt[sem]

        def add(s, v):
            if need.get(s, 0) < v:
                need[s] = v
        for b in r:
            if b in self.lastw:
                add(*self.lastw[b])
        for b in w:
            if b in self.lastw:
                add(*self.lastw[b])
            for s, v in self.readers.get(b, {}).items():
                add(s, v)
        waits = []
        kn = self.known[stream]
        for s, v in need.items():
            if stream == 'pe' and s == 'pe':
                continue
            if kn.get(s, 0) < v:
                waits.append((s, v))
                kn[s] = v
        self.cnt[sem] = self.cnt.get(sem, 0) + inc
        val = self.cnt[sem]
        self.ops[stream].append((waits, fn, sem, inc))
        for b in r:
            d = self.readers.setdefault(b, {})
            if d.get(sem, 0) < val:
                d[sem] = val
        for b in w:
            self.lastw[b] = (sem, val)
            self.readers[b] = {}

    def barrier(self, streams=('pe', 'act', 'dve', 'sp')):
        for s in streams:
            waits = []
            for sem in list(self.cnt):
                if sem.startswith('pq'):
                    continue
                v = self.cnt.get(sem, 0)
                if s == 'pe' and sem == 'pe':
                    continue
                if v and self.known[s].get(sem, 0) < v:
                    waits.append((sem, v))
                    self.known[s][sem] = v
            if waits:
                self.ops[s].append((waits, None, None, 0))


def build(TP=2048, L=4):
    NTOK = TP + 128
    NCH = NTOK // 128
    NPC = TP // 128
    nc = bass.Bass("TRN2", target_bir_lowering=False)
    S = Sched()
    st = ExitStack()

    def din(name, shape, dt=F32):
        return nc.dram_tensor(name, list(shape), dt, kind="ExternalInput").ap()

    def dout(name, shape, dt=F32):
        return nc.dram_tensor(name, list(shape), dt, kind="ExternalOutput").ap()

    xp = din("xp", [TP, D]); xs = din("xs", [128, D]); cc = din("cc", [17, D])
    sshift = din("sshift", [L, 16, RW]); spool = din("spool", [L, 16, 15, MIX])
    swkv = din("swkv", [L, 16, 8, 64, 64])
    W = {}
    LV = max(L - 1, 1)
    for nm, shp in [("w_ada_mix", [L, D, 3 * D]), ("b_ada_mix", [L, 3 * D]), ("norm_mix", [L, D]),
                    ("w_in", [L, D, INC]), ("mu_shift", [L, RW]), ("w0", [L, MIX]), ("w2", [L, 64, MIX]),
                    ("a0", [L, MIX]), ("a2", [L, 64, MIX]), ("g2", [L, 128, MIX]), ("v0", [LV, MIX]),
                    ("v1", [LV, MIX, 32]), ("v2", [LV, 32, MIX]), ("k_k", [L, MIX]),
                    ("k_a", [L, MIX]), ("r_k", [L, MIX]), ("ln_w", [L, MIX]), ("ln_b", [L, MIX]),
                    ("pool_w", [L, 4, 128, 128]), ("pool_scale", [L, MIX]), ("w_br_rwkv", [L, MIX, D]),
                    ("w_br_pool", [L, MIX, D]), ("w_out", [L, D, D]), ("w_ada_mlp", [L, D, 3 * D]),
                    ("b_ada_mlp", [L, 3 * D]), ("norm_mlp", [L, D]), ("w_ff1", [L, D, DFF]),
                    ("w_ff2", [L, DFF, D]), ("norm_final", [1, D])]:
        W[nm] = din(nm, shp)
    cst = din("cst", [128, CF_COLS + CB_COLS])
    yp = dout("yp", [TP, D]); ys = dout("ys", [128, D])
    nshp = dout("nshp", [L, RW]); npoolp = dout("npoolp", [L, 15, MIX]); nwkvp = dout("nwkvp", [L, 8, 64, 64])
    nshs = dout("nshs", [L, 16, RW]); npools = dout("npools", [L, 16 * 15, MIX])
    nwkvs = dout("nwkvs", [L, 16, 8, 64, 64])
    vfd = nc.dram_tensor("vfirst_scr", [128, 4, NTOK], BF16, kind="Internal").ap()
    orpd = nc.dram_tensor("orp_scr", [128, 8, NTOK], BF16, kind="Internal").ap()

    NW = 53200
    big = st.enter_context(nc.sbuf_tensor("big", [128, NW], F32))
    ptr = [0]

    def alloc(shape, dt=F32):
        n = int(np.prod(shape))
        words = n if dt == F32 else (n + 1) // 2
        words = (words + 7) // 8 * 8
        o = ptr[0]
        ptr[0] += words
        assert ptr[0] <= NW, ("SBUF arena overflow", ptr[0], NW)
        v = big[:, o:o + words]
        if dt != F32:
            v = v.bitcast(dt)
        v = v[:, 0:n]
        if len(shape) == 1:
            return v
        names = " ".join("d%d" % i for i in range(len(shape)))
        kw = {"d%d" % i: int(shape[i]) for i in range(len(shape) - 1)}
        return v.rearrange("p (%s) -> p %s" % (names, names), **kw)

    SD = F32
    NPI = 2
    HP = 2 * NPI
    xT = alloc([NKC, NTOK])
    ring = alloc([6, 4096], BF16)
    cF = alloc([CF_COLS]); cB = alloc([CB_COLS], BF16)
    pcol = alloc([L, 128])
    omm = alloc([14]); omka = alloc([4])
    siluT = alloc([NKC, 17], BF16)
    modv = alloc([24, 17]); G32 = alloc([NKC, 17]); G32f = alloc([NKC, 1])
    lw_small = alloc([2176], BF16)
    H0f = alloc([4, 128]); H0b = H0f
    prcarry = alloc([14, 1])
    UBASE = ptr[0]

    PB = [st.enter_context(nc.psum_tensor("pb%d" % i, [128, 512], F32)) for i in range(8)]
    NROT = 6
    pbi = [0]
    pbt_i = [0]

    def pbank():
        i = pbi[0] % NROT
        pbi[0] += 1
        return PB[i], ('pb', i)
    ZB, ZK = PB[6], ('pb', 6)
    YB, YK = PB[7], ('pb', 7)

    def PE(fn, r, w): S.op('pe', fn, r, w)
    def ACT(fn, r, w): S.op('act', fn, r, w)
    def DVE(fn, r, w): S.op('dve', fn, r, w)
    def SPD(fn, r, w): S.op('sp', fn, r, w, sem='sp', inc=16, nsem=16)
    def PQD(fn, r, w): S.op('pool', fn, r, w, sem='pq', inc=16, nsem=8)

    def mm(out, lhsT, rhs, start, stop, r, w):
        PE(lambda e: e.matmul(out, lhsT, rhs, start=start, stop=stop, skip_group_check=True), r, w)

    def tr(out, in_, ident, r, w):
        PE(lambda e: e.transpose(out, in_, ident), r, w)

    def act(out, in_, func, r, w, bias=0.0, scale=1.0):
        ACT(lambda e: e.activation(out, in_, func, bias=bias, scale=scale), r, w)

    def acopy(out, in_, r, w):
        ACT(lambda e: e.copy(out, in_), r, w)

    def vcopy(out, in_, r, w):
        DVE(lambda e: e.tensor_copy(out, in_), r, w)

    def tt(out, a, b, op, r, w):
        DVE(lambda e: e.tensor_tensor(out, a, b, op), r, w)

    def ts(out, a, s1, s2, op0, op1, r, w):
        DVE(lambda e: e.tensor_scalar(out, a, s1, s2, op0, op1), r, w)

    def stt(out, a, s, b, op0, op1, r, w):
        DVE(lambda e: e.scalar_tensor_tensor(out, a, s, b, op0, op1), r, w)

    def dma(out, in_, r, w):
        SPD(lambda e: e.dma_start(out=out, in_=in_), r, w)

    def rsq(out, in_, mulc, addc, r, w):
        ts(out, in_, mulc, addc, ALU.mult, ALU.add, r, w)
        act(out, out, AF.Ln, w, w)
        act(out, out, AF.Exp, w, w, scale=-0.5)

    def xk(ci): return [('x', ci)]
    f2 = lambda t: t.rearrange("p a b -> p (a b)")
    v4 = lambda t: t.rearrange("p (a b) -> p a b", a=4)

    def cf(name, lo=0, hi=None):
        o, n = CSTF[name]
        return cF[:, o + lo:o + (n if hi is None else hi)]

    def cb(name, lo=0, hi=None):
        o, n = CSTB[name]
        return cB[:, o + lo:o + (n if hi is None else hi)]
    SD = F32
    identF = cf('ident'); identB = cb('ident'); identS = identF if SD == F32 else identB; onesB = cb('ones'); bonesF = cf('bones'); onesF = cf('ones')

    ptr[0] = UBASE
    pstage = alloc([L, 128]); cst17 = alloc([D]); xin = alloc([2, D])
    dma(cF[:], cst[:, 0:CF_COLS], [], ['cF'])
    PQD(lambda e: e.dma_start(out=cB[:], in_=cst[:, CF_COLS:CF_COLS + CB_COLS]), [], ['cB'])
    DVE(lambda e: e.memset(pstage[:], 0.0), [], ['pstage'])
    PROW = {}
    ro = 0
    for nm, nchk in [("norm_mix", 8), ("norm_mlp", 8), ("mu_shift", 14), ("w0", 4), ("a0", 4), ("v0", 4),
                     ("k_k", 4), ("k_a", 4), ("r_k", 4), ("ln_w", 4), ("ln_b", 4), ("pool_scale", 4),
                     ("b_ada_mix", 24), ("b_ada_mlp", 24)]:
        PROW[nm] = ro
        src = W[nm]
        if nm == "v0":
            if L > 1:
                dma(pstage[ro:ro + nchk, 1:L, :], src[0:L - 1, :].rearrange("l (c p) -> c l p", p=128), ['pstage'], ['pstage'])
        else:
            dma(pstage[ro:ro + nchk, 0:L, :], src.rearrange("l (c p) -> c l p", p=128), ['pstage'], ['pstage'])
        ro += nchk
    assert ro <= 120
    dma(pstage[120:128, 0, :], W["norm_final"].rearrange("o (c p) -> (o c) p", p=128), ['pstage'], ['pstage'])
    for l in range(L):
        pb, pk = pbank()
        tr(pb[:, 0:128], pstage[:, l, :], identF, ['pstage', 'cF'], [pk])
        acopy(pcol[:, l, :], pb[:, 0:128], [pk], ['pcol'])

    def pc(l, nm, c): return pcol[:, l, PROW[nm] + c:PROW[nm] + c + 1]
    def pcs(l, nm, n): return pcol[:, l, PROW[nm]:PROW[nm] + n]

    dma(cst17[0:17, :], cc[:, :], [], ['cst17'])
    pb, pk = pbank()
    for kc in range(NKC):
        tr(pb[:, kc * 17:(kc + 1) * 17], cst17[0:17, kc * 128:(kc + 1) * 128], identF[0:17, 0:17], ['cst17', 'cF'], [pk])
    act(f2(siluT[:]), pb[:, 0:NKC * 17], AF.Silu, [pk], ['siluT'])

    for ci in range(NCH):
        src = xp[ci * 128:(ci + 1) * 128, :] if ci < NPC else xs[:, :]
        xb_ = xin[:, ci % 2, :]
        dma(xb_, src, [], [('xin', ci % 2)])
        for half in range(2):
            pb, pk = pbank()
            for q in range(4):
                c = half * 4 + q
                tr(pb[:, q * 128:(q + 1) * 128], xb_[:, c * 128:(c + 1) * 128], identF, [('xin', ci % 2), 'cF'], [pk])
            acopy(xT[:, half * 4:half * 4 + 4, ci * 128:(ci + 1) * 128], v4(pb[:, 0:512]), [pk], xk(ci))

    ring_i = [0]

    def wload(src3, a, b):
        i = ring_i[0] % 6
        ring_i[0] += 1
        dst = ring[:, i, 0:a * b].rearrange("p (a b) -> p a b", a=a)
        PQD(lambda e: e.dma_start(out=dst, in_=src3), [], [('ring', i)])
        return dst, ('ring', i)

    def wl_kc(src2, ncol):
        return wload(src2.rearrange("(kc p) n -> p kc n", p=128), src2.shape[0] // 128, ncol)

    def ada(l, which):
        wsrc = W["w_ada_" + which]
        pbm, pkm = pbank()
        for j in range(6):
            wt, wkey = wl_kc(wsrc[l, :, j * 512:(j + 1) * 512], 512)
            for q in range(4):
                ch = j * 4 + q
                for kc in range(NKC):
                    mm(pbm[:, ch * 17:(ch + 1) * 17], wt[:, kc, q * 128:(q + 1) * 128], siluT[:, kc, :],
                       kc == 0, kc == NKC - 1, [wkey, 'siluT'], [pkm])
        tt(modv[:], pbm[:, 0:408].rearrange("p (a b) -> p a b", b=17),
           pcs(l, "b_ada_" + which, 24).unsqueeze(2).to_broadcast([128, 24, 17]), ALU.add, [pkm, 'pcol'], ['mod'])
        ts(G32[:], modv[:, 8:16, :], 1.0, 32.0, ALU.add, ALU.mult, ['mod'], ['mod'])
        tt(G32[:], G32[:], pcs(l, "norm_" + which, 8).unsqueeze(2).to_broadcast([128, 8, 17]), ALU.mult, ['mod', 'pcol'], ['mod'])

    WK = {}

    def xks(t0, n): return [('x', c) for c in range(t0 // 128, (t0 + n) // 128)]

    def tiles(size):
        out = []
        t = 0
        while t < TP:
            n = min(size, TP - t)
            out.append((t, n, False))
            t += n
        out.append((TP, 128, True))
        return out

    def norm_mod_t(t0, n, samp, hdst, hkeys):
        sqr, rstd, tmpn = WK['sq'], WK['rstd'], WK['tmpn']
        pb, pk = pbank()
        for c in range(NKC):
            act(sqr[:, c % 2, 0:n], xT[:, c, t0:t0 + n], AF.Square, xks(t0, n), [('sq', c % 2)])
            mm(pb[:, 0:n], onesB, sqr[:, c % 2, 0:n], c == 0, c == NKC - 1, [('sq', c % 2), 'cB'], [pk])
        rsq(rstd[:, 0:n], pb[:, 0:n], 1.0, D * EPS, [pk], ['rstd'])
        for c in range(NKC):
            tb = tmpn[:, c % 2, 0:n]
            tk = ('tmpn', c % 2)
            if not samp:
                stt(tb, xT[:, c, t0:t0 + n], G32[:, c, 0:1], rstd[:, 0:n], ALU.mult, ALU.mult, xks(t0, n) + ['mod', 'rstd'], [tk])
                act(hdst[:, c, 0:n], tb, AF.Identity, [tk, 'mod'], hkeys, bias=modv[:, c, 0:1])
            else:
                tt(tb, xT[:, c, t0:t0 + n], rstd[:, 0:n], ALU.mult, xks(t0, n) + ['rstd'], [tk])
                t3 = tb.rearrange("p (b t) -> p b t", t=8)
                tt(t3, t3, G32[:, c, 1:17].unsqueeze(2).to_broadcast([128, 16, 8]), ALU.mult, [tk, 'mod'], [tk])
                tt(hdst[:, c, 0:n].rearrange("p (b t) -> p b t", t=8), t3,
                   modv[:, c, 1:17].unsqueeze(2).to_broadcast([128, 16, 8]), ALU.add, [tk, 'mod'], hkeys)

    def norm_mod(ci, samp, hdst, hkeys):
        norm_mod_t(ci * 128, 128, samp, hdst, hkeys)

    def resid_update_t(t0, n, samp, c, pb, pk):
        xv = xT[:, c, t0:t0 + n]
        mg = WK['mg']
        if not samp:
            stt(xv, pb[:, 0:n], modv[:, 16 + c, 0:1], xv, ALU.mult, ALU.add, [pk, 'mod'] + xks(t0, n), xks(t0, n))
        else:
            tt(mg[:, 0:128].rearrange("p (b t) -> p b t", t=8), pb[:, 0:128].rearrange("p (b t) -> p b t", t=8),
               modv[:, 16 + c, 1:17].unsqueeze(2).to_broadcast([128, 16, 8]), ALU.mult, [pk, 'mod'], ['mg'])
            tt(xv, xv, mg[:, 0:128], ALU.add, ['mg'] + xks(t0, n), xks(t0, n))

    for l in range(L):
        S.barrier()
        ptr[0] = UBASE
        WK['sq'] = alloc([2, 128], BF16); WK['rstd'] = alloc([128]); WK['tmpn'] = alloc([2, 128]); WK['mg'] = alloc([128])
        hTc = alloc([NKC, 129], BF16); hT = hTc[:, :, 1:129]
        rkv = alloc([12, 128])
        twa = alloc([128], BF16); sgg = alloc([2, 128], BF16); t1b = alloc([128], BF16)
        vbf = alloc([4, 128], BF16); vfb = alloc([4, 128], BF16)
        FMB = [alloc([4, 128]) for _ in range(8)]
        KR = alloc([4, 256], SD)
        Bt = alloc([4, 128], SD); Kt = alloc([4, 128], SD); BWf = alloc([4, 128], SD); KWf = alloc([4, 128], SD)
        Vtok = alloc([4, 128], SD); BWtok = alloc([4, 128], SD); KWtok = alloc([4, 128], SD)
        AT = alloc([HP, 384], SD); MA = alloc([HP, 256], SD); MB = alloc([HP, 256], SD)
        PT = alloc([HP, 128], SD)
        Zn = alloc([NPI, 128], SD); Ut = alloc([NPI, 128], SD)
        wcs = alloc([4, 16])
        pz = vbf; orp = alloc([8, 128], BF16)
        shiftT = alloc([14, 16]); ppbS = alloc([4, 16, 23])
        ppb = ppbS.rearrange("p a b c -> p (a b c)")[:, 0:576].rearrange("p (a b) -> p a b", a=4)
        A_, LW_, X1, X2, X3, X4, X5, X6 = FMB
        DIAG = X1; WcBC = X2; SCo = X4[:, :, 0:64]
        S0g = H0f; HSb = A_; BKK = LW_; BRR = X3; BIGB = BWf; BIGK = KWf
        tsh = f2(X1[:])[:, 0:272].rearrange("p (a b) -> p a b", a=2)
        xl12 = X2[:, 0:2, :]
        pq2 = f2(X1[:])[:, 0:144]; pq4 = f2(X2[:])[:, 0:144]

        ts(omm[:], pcs(l, "mu_shift", 14), -1.0, 1.0, ALU.mult, ALU.add, ['pcol'], ['omm'])
        ts(omka[:], pcs(l, "k_a", 4), -1.0, 1.0, ALU.mult, ALU.add, ['pcol'], ['omm'])
        PQD(lambda e, l=l: e.dma_start(out=lw_small[0:64, 0:512], in_=W["w2"][l]), [], ['lws'])
        PQD(lambda e, l=l: e.dma_start(out=lw_small[64:128, 0:512], in_=W["a2"][l]), [], ['lws'])
        PQD(lambda e, l=l: e.dma_start(out=lw_small[:, 512:1024], in_=W["g2"][l]), [], ['lws'])
        if l > 0:
            PQD(lambda e, l=l: e.dma_start(out=lw_small[:, 1024:1152].rearrange("p (a b) -> p a b", a=4),
                                           in_=W["v1"][l - 1].rearrange("(kc p) n -> p kc n", p=128)), [], ['lws'])
            PQD(lambda e, l=l: e.dma_start(out=lw_small[0:32, 1152:1664], in_=W["v2"][l - 1]), [], ['lws'])
        PQD(lambda e, l=l: e.dma_start(out=lw_small[:, 1664:2176].rearrange("p (g d) -> p g d", g=4),
                                       in_=W["pool_w"][l].rearrange("g c d -> c g d")), [], ['lws'])
        w2a2 = lw_small[:, 0:512]; g2w = lw_small[:, 512:1024]
        v1w = lw_small[:, 1024:1152].rearrange("p (a b) -> p a b", a=4); v2w = lw_small[0:32, 1152:1664]
        poolw = lw_small[:, 1664:2176].rearrange("p (g d) -> p g d", g=4)

        ada(l, "mix")
        DVE(lambda e: e.memset(ppb[:, :, 0:15], 0.0), [], ['ppb'])
        DVE(lambda e: e.memset(hTc[:, :, 0:1], 0.0), ['hT'], ['hT'])

        W1 = []
        for j in range(5):
            ncol = 512 if j < 4 else 256
            W1.append(wl_kc(W["w_in"][l, :, j * 512:j * 512 + ncol], ncol))

        def fe_norm(ci):
            samp = ci >= NPC
            norm_mod(ci, samp, hT, ['hT'])

        def fe_proj_gen(ci):
            samp = ci >= NPC
            last_chunk = ci == NPC - 1
            if samp:
                pbs_, pks_ = pbank()
                for g0 in range(0, 14, 4):
                    gn = min(4, 14 - g0)
                    dma(X6[0:16, :, :].rearrange("p a b -> p (a b)")[:, 0:gn * 128], sshift[l, :, g0 * 128:(g0 + gn) * 128], [], ['X6'])
                    for c in range(g0, g0 + gn):
                        tr(pbs_[:, c * 16:(c + 1) * 16], f2(X6[0:16, :, :])[:, (c - g0) * 128:(c - g0 + 1) * 128],
                           identF[0:16, 0:16], ['X6', 'cF'], [pks_])
                acopy(f2(shiftT[:]), pbs_[:, 0:224], [pks_], ['shiftT'])
            for cidx in range(18):
                if cidx > 0:
                    yield
                wt, wkey = W1[cidx // 4]
                q = cidx % 4
                pb, pk = pbank()
                for kc in range(NKC):
                    if samp:
                        mm(pb[:, 0:128], wt[:, kc, q * 128:(q + 1) * 128], hT[:, kc, :], kc == 0, kc == NKC - 1, [wkey, 'hT'], [pk])
                    else:
                        mm(pb[:, 0:129], wt[:, kc, q * 128:(q + 1) * 128], hTc[:, kc, 0:129], kc == 0, kc == NKC - 1, [wkey, 'hT'], [pk])
                if cidx >= 14:
                    g = cidx - 14
                    if not samp:
                        acopy(ppb[:, g, 15:143], pb[:, 1:129], [pk], ['ppb'])
                    else:
                        acopy(ppbS[:, g, :, 15:23], pb[:, 0:128].rearrange("p (b t) -> p b t", t=8), [pk], ['ppbS', 'ppb'])
                    continue
                c = cidx
                tb = tsh[:, c % 2, :]
                tk = 'X1'
                mu = pc(l, "mu_shift", c)
                dst = rkv[:, c, :] if c < 12 else xl12[:, c - 12, :]
                dkey = 'rkv' if c < 12 else 'X2'
                if not samp:
                    act(tb[:, 0:129], pb[:, 0:129], AF.Identity, [pk, 'pcol'], [tk], scale=mu)
                    if last_chunk:
                        acopy(prcarry[:, c, :], pb[:, 128:129], [pk], ['prcarry'])
                    stt(dst, pb[:, 1:129], omm[:, c:c + 1], tb[:, 0:128], ALU.mult, ALU.add, [pk, 'omm', tk], [dkey])
                else:
                    p3 = pb[:, 0:128].rearrange("p (b t) -> p b t", t=8)
                    t3 = tb[:, 0:128].rearrange("p (b t) -> p b t", t=8)
                    act(t3[:, :, 1:8], p3[:, :, 0:7], AF.Identity, [pk, 'pcol'], [tk], scale=mu)
                    act(t3[:, :, 0:1], shiftT[:, c, :].unsqueeze(2), AF.Identity, ['shiftT', 'pcol'], [tk], scale=mu)
                    acopy(f2(X5[:])[:, c * 16:(c + 1) * 16].unsqueeze(2), p3[:, :, 7:8], [pk], ['X5'])
                    stt(dst, pb[:, 0:128], omm[:, c:c + 1], tb[:, 0:128], ALU.mult, ALU.add, [pk, 'omm', tk], [dkey])
            yield
            if not samp and not last_chunk:
                acopy(hTc[:, :, 0:1], hTc[:, :, 128:129], ['hT'], ['hT'])
            if last_chunk:
                pb, pk = pbank()
                tr(pb[0:14, 0:128], f2(prcarry[:]), identF, ['prcarry', 'cF'], [pk])
                acopy(f2(X6[0:14, :, :])[:, 0:128], pb[0:14, 0:128], [pk], ['X6'])
                dma(nshp[l].rearrange("(c p) -> c p", p=128), f2(X6[0:14, :, :])[:, 0:128], ['X6'], [])
            if samp:
                for g0 in range(0, 14, 4):
                    gn = min(4, 14 - g0)
                    pbx, pkx = pbank()
                    for c in range(g0, g0 + gn):
                        tr(pbx[0:16, (c - g0) * 128:(c - g0 + 1) * 128], f2(X5[:])[:, c * 16:(c + 1) * 16], identF, ['X5', 'cF'], [pkx])
                    ob = [X3, X4][(g0 // 4) % 2]
                    okey = ['X3', 'X4'][(g0 // 4) % 2]
                    acopy(f2(ob[0:16, :, :])[:, 0:gn * 128], pbx[0:16, 0:gn * 128], [pkx], [okey])
                    dma(nshs[l, :, g0 * 128:(g0 + gn) * 128], f2(ob[0:16, :, :])[:, 0:gn * 128], [okey], [])
            act(twa[0:64, :], xl12[0:64, 0, :], AF.Tanh, ['X2'], ['twa'])
            acopy(twa[64:128, :], xl12[64:128, 0, :], ['X2'], ['twa'])
            act(sgg[:, ci % 2, :], xl12[:, 1, :], AF.Sigmoid, ['X2'], [('sgg', ci % 2)])
            for p in range(4):
                pb, pk = pbank()
                mm(pb[:, 0:128], w2a2[0:64, p * 128:(p + 1) * 128], twa[0:64, :], True, True, ['lws', 'twa'], [pk])
                act(LW_[:, p, :], pb[:, 0:128], AF.Sigmoid, [pk, 'pcol'], ['LW'], bias=pc(l, "w0", p))
                pb, pk = pbank()
                mm(pb[:, 0:128], w2a2[64:128, p * 128:(p + 1) * 128], twa[64:128, :], True, True, ['lws', 'twa'], [pk])
                act(A_[:, p, :], pb[:, 0:128], AF.Sigmoid, [pk, 'pcol'], ['A'], bias=pc(l, "a0", p))
            ts(LW_[:], LW_[:], DEC_C, None, ALU.mult, ALU.bypass, ['LW'], ['LW'])

        def fe_proj(ci):
            for _ in fe_proj_gen(ci):
                pass

        pre_done = set()
        for ci in range(NCH):
            samp = ci >= NPC
            t0 = ci * 128
            first_chunk = ci == 0
            last_chunk = ci == NPC - 1
            nb = 16 if samp else 1
            blk = 128 // nb
            mset = 'S' if samp else 'P'
            if ci not in pre_done:
                fe_norm(ci)
                fe_proj(ci)
            r_c = rkv[:, 0:4, :]; k_c = rkv[:, 4:8, :]; v_c = rkv[:, 8:12, :]
            bc4 = lambda col: col.unsqueeze(2).to_broadcast([128, 4, 128])
            d0 = cf('rmS') if samp else onesF
            for p in range(4):
                DVE(lambda e, p=p, d0=d0: e.tensor_tensor_scan(X2[:, p, :], d0, LW_[:, p, :], 0.0, ALU.mult, ALU.add),
                    ['LW', 'cF'], ['X2'])
            tt(X1[:], X2[:], LW_[:], ALU.subtract, ['X2', 'LW'], ['X1'])
            act(X1[:], X1[:], AF.Exp, ['X1'], ['X1'])
            act(LW_[:], X2[:], AF.Exp, ['X2'], ['LW'])
            ein4 = LW_[:].rearrange("p a (b t) -> p a b t", t=blk)
            vcopy(wcs[:, :, 0:nb].unsqueeze(3), ein4[:, :, :, blk - 1:blk], ['LW'], ['wcs'])
            act(X2[:], X2[:], AF.Exp, ['X2'], ['X2'], scale=-1.0)
            if l == 0:
                acopy(vbf[:], v_c, ['rkv'], ['vbf'])
                dma(vfd[:, :, t0:t0 + 128], vbf[:], ['vbf'], [('vfd', ci)])
            else:
                dma(vfb[:], vfd[:, :, t0:t0 + 128], [('vfd', ci)], ['vfb'])
                acopy(vbf[:], v_c, ['rkv'], ['vbf'])
                pb, pk = pbank()
                for p in range(4):
                    mm(pb[0:32, 0:128], v1w[:, p, :], vbf[:, p, :], p == 0, p == 3, ['lws', 'vbf'], [pk])
                acopy(t1b[0:32, :], pb[0:32, 0:128], [pk], ['t1b'])
                for p in range(4):
                    pb, pk = pbank()
                    mm(pb[:, 0:128], v2w[:, p * 128:(p + 1) * 128], t1b[0:32, :], True, True, ['lws', 't1b'], [pk])
                    act(X3[:, p, :], pb[:, 0:128], AF.Sigmoid, [pk, 'pcol'], ['X3'], bias=pc(l, "v0", p))
                tt(X4[:], vfb[:], v_c, ALU.subtract, ['vfb', 'rkv'], ['X4'])
                tt(X4[:], X4[:], X3[:], ALU.mult, ['X4', 'X3'], ['X4'])
                tt(v_c, v_c, X4[:], ALU.add, ['rkv', 'X4'], ['rkv'])

            tt(BWf[:], k_c, bc4(pcs(l, "k_k", 4)), ALU.mult, ['rkv', 'pcol'], ['BWf'])
            tt(KWf[:], BWf[:], BWf[:], ALU.mult, ['BWf'], ['KWf'])
            pb, pk = pbank()
            mm(pb[:, 0:512], bonesF, f2(KWf[:]), True, True, ['KWf', 'cF'], [pk])
            rsq(KWf[:], v4(pb[:, 0:512]), 1.0, 1e-12, [pk], ['KWf'])
            tt(X3[:], BWf[:], KWf[:], ALU.mult, ['BWf', 'KWf'], ['X3'])
            tt(X4[:], X3[:], A_[:], ALU.mult, ['X3', 'A'], ['X4'])
            tt(BWf[:], A_[:], bc4(pcs(l, "k_a", 4)), ALU.mult, ['A', 'pcol'], ['BWf'])
            tt(BWf[:], BWf[:], bc4(omka[:, 0:4]), ALU.add, ['BWf', 'omm'], ['BWf'])
            tt(X5[:], k_c, BWf[:], ALU.mult, ['rkv', 'BWf'], ['X5'])
            tt(BWf[:], r_c, X5[:], ALU.mult, ['rkv', 'X5'], ['BWf'])
            tt(BWf[:], BWf[:], bc4(pcs(l, "r_k", 4)), ALU.mult, ['BWf', 'pcol'], ['BWf'])
            pb, pk = pbank()
            mm(pb[:, 0:512], bonesF, f2(BWf[:]), True, True, ['BWf', 'cF'], [pk])
            tt(X6[:], v4(pb[:, 0:512]), v_c, ALU.mult, [pk, 'rkv'], ['X6'])
            tt(KR[:, :, 0:128], X3[:], X1[:], ALU.mult, ['X3', 'X1'], ['KR'])
            tt(KR[:, :, 128:256], r_c, LW_[:], ALU.mult, ['rkv', 'LW'], ['KR'])
            wcb = wcs[:, :, 0:nb].unsqueeze(3).to_broadcast([128, 4, nb, blk])
            b4 = lambda t: t.rearrange("p a (b t) -> p a b t", t=blk)
            tt(X3[:], X4[:], X2[:], ALU.mult, ['X4', 'X2'], ['X3'])
            acopy(Bt[:], X3[:], ['X3'], ['Bt'])
            tt(b4(BWf[:]), b4(X3[:]), wcb, ALU.mult, ['X3', 'wcs'], ['BWf'])
            tt(X4[:], X5[:], X2[:], ALU.mult, ['X5', 'X2'], ['X4'])
            acopy(Kt[:], X4[:], ['X4'], ['Kt'])
            tt(b4(KWf[:]), b4(X4[:]), wcb, ALU.mult, ['X4', 'wcs'], ['KWf'])
            for (srcb, dstb, sk, dk) in ((v_c, Vtok, 'rkv', 'Vtok'), (BWf, BWtok, 'BWf', 'BWtok'), (KWf, KWtok, 'KWf', 'KWtok')):
                pbb, pkb = pbank()
                for p in range(4):
                    tr(pbb[:, p * 128:(p + 1) * 128], srcb[:, p, :], identS, [sk, 'cF', 'cB'], [pkb])
                acopy(f2(dstb[:]), pbb[:, 0:512], [pkb], [dk])

            if samp:
                DVE(lambda e: e.memset(S0g[:], 0.0), ['H0f'], ['H0f'])
                DVE(lambda e: e.memset(BKK[:], 0.0), ['LW'], ['LW'])
                DVE(lambda e: e.memset(BRR[:], 0.0), ['X3'], ['X3'])
            prefetch_next = (not samp) and (ci + 1 < NPC - 1)
            for hf in range(4 // NPI):
                for q in range(HP):
                    hd = hf * HP + q
                    p, hh = hd // 2, hd % 2
                    hs = slice(hh * 64, hh * 64 + 64)
                    pb, pk = pbank()
                    mm(pb[:, 0:256], Bt[hs, p, :], KR[hs, p, :], True, True, ['Bt', 'KR'], [pk])
                    mm(pb[:, 256:512], Kt[hs, p, :], KR[hs, p, :], True, True, ['Kt', 'KR'], [pk])
                    tt(MB[:, q, 128:256], pb[:, 0:128], cb('mA' + mset, 0, 128), ALU.mult, [pk, 'cB'], ['MB'])
                    tt(AT[:, q, :], pb[:, 128:512], cb('mA' + mset, 128, 512), ALU.mult, [pk, 'cB'], ['AT'])
                pbA, pkA = pbank()
                pbB, pkB = pbank()
                for q in range(HP):
                    hd = hf * HP + q
                    p, hh = hd // 2, hd % 2
                    hs = slice(hh * 64, hh * 64 + 64)
                    pbx, pkx = (pbA, pkA) if hh == 0 else (pbB, pkB)
                    mm(pbx[:, (q // 2) * 128:(q // 2 + 1) * 128], KR[hs, p, 0:128], Bt[hs, p, :], True, True, ['KR', 'Bt'], [pkx])
                for hh, (pbx, pkx) in enumerate(((pbA, pkA), (pbB, pkB))):
                    tt(MB[:, hh:HP:2, 0:128], pbx[:, 0:NPI * 128].rearrange("p (a b) -> p a b", a=NPI),
                       cb('mL' + mset).unsqueeze(1).to_broadcast([128, NPI, 128]), ALU.mult, [pkx, 'cB'], ['MB'])
                tt(PT[:], MB[:, :, 128:256], identF.unsqueeze(1).to_broadcast([128, HP, 128]), ALU.add, ['MB', 'cF'], ['PT'])
                if hf == 0 and not samp:
                    PA = bass.AP(tensor=X1.tensor, offset=X1.offset, ap=[list(X1.ap[0]), [144, 4], [1, 144]])
                    PBs = bass.AP(tensor=A_.tensor, offset=A_.offset, ap=[list(A_.ap[0]), [144, 4], [1, 144]])
                    ka, kb = ['X1', 'X2'], ['A', 'LW']
                    tt(PA[:, :, 1:143], ppb[:, :, 1:143], ppb[:, :, 0:142], ALU.add, ['ppb'], ka)
                    tt(PBs[:, 1:4, 3:143], PA[:, 1:4, 3:143], PA[:, 1:4, 1:141], ALU.add, ka, kb)
                    tt(PA[:, 2:4, 7:143], PBs[:, 2:4, 7:143], PBs[:, 2:4, 3:139], ALU.add, kb + ka, ka)
                    tt(PBs[:, 3:4, 15:143], PA[:, 3:4, 15:143], PA[:, 3:4, 7:135], ALU.add, ka + kb, kb)
                    psrc = [PA[:, 0, :], PBs[:, 1, :], PA[:, 2, :], PBs[:, 3, :]]
                    for g in range(4):
                        if first_chunk:
                            mg = WK['mg']
                            stt(mg[:], psrc[g][:, 15:143], 1.0 / WINS[g], ppb[:, g, 15:143], ALU.mult, ALU.subtract, ka + kb + ['ppb'], ['mg'])
                            tt(mg[:, 0:16], psrc[g][:, 15:31], cf('pcorr')[:, g * 16:(g + 1) * 16], ALU.mult, ka + kb + ['cF', 'mg'], ['mg'])
                            tt(mg[:, 0:16], mg[:, 0:16], ppb[:, g, 15:31], ALU.subtract, ['mg', 'ppb'], ['mg'])
                            vcopy(pz[:, g, :], mg[:], ['mg'], ['vbf'])
                        else:
                            stt(pz[:, g, :], psrc[g][:, 15:143], 1.0 / WINS[g], ppb[:, g, 15:143], ALU.mult, ALU.subtract, ka + kb + ['ppb'], ['vbf'])
                    if not last_chunk:
                        vcopy(PA[:, :, 0:15], ppb[:, :, 128:143], ['ppb'] + ka, ka)
                        vcopy(ppb[:, :, 0:15], PA[:, :, 0:15], ka + ['ppb'], ['ppb'])
                if hf == 0 and prefetch_next:
                    fe_norm(ci + 1)
                    fe_gen = fe_proj_gen(ci + 1)
                    pre_done.add(ci + 1)
                nlev = 2 if samp else 6
                curM = lambda q: MB[:, q, 0:128]
                curMT = lambda q: MB[:, q, 128:256]
                curk = ['MB']
                for lev in range(nlev):
                    if prefetch_next and (hf > 0 or lev >= 2):
                        for _ in range(2):
                            next(fe_gen, None)
                    nxt, nk = (MA, 'MA') if lev % 2 == 0 else (MB, 'MB')
                    lastlev = lev == nlev - 1
                    for h2 in range(NPI):
                        pb, pk = pbank()
                        for qq in range(2):
                            q = h2 * 2 + qq
                            mm(pb[:, qq * 256:qq * 256 + 128], curMT(q), curM(q), True, True, curk, [pk])
                            if not lastlev:
                                mm(pb[:, qq * 256 + 128:qq * 256 + 256], curM(q), curMT(q), True, True, curk, [pk])
                        acopy(nxt[:, h2 * 2:h2 * 2 + 2, :], pb[:, 0:512].rearrange("p (a b) -> p a b", a=2), [pk], [nk])
                    pb, pk = pbank()
                    for q in range(HP):
                        mm(pb[:, q * 128:(q + 1) * 128], nxt[:, q, 0:128], PT[:, q, :], True, True, [nk, 'PT'], [pk])
                    tt(PT[:], PT[:], pb[:, 0:HP * 128].rearrange("p (a b) -> p a b", a=HP), ALU.add, [pk, 'PT'], ['PT'])
                    curM = lambda q, nxt=nxt: nxt[:, q, 0:128]
                    curMT = lambda q, nxt=nxt: nxt[:, q, 128:256]
                    curk = [nk]

                if hf == 0 and not samp:
                    for g in range(4):
                        pb, pk = pbank()
                        mm(pb[:, 0:128], poolw[:, g, :], pz[:, g, :], True, True, ['lws', 'vbf'], [pk])
                        act(orp[:, 4 + g, :], pb[:, 0:128], AF.Identity, [pk, 'pcol'], ['orp'], scale=pc(l, "pool_scale", g))
                if prefetch_next and hf == 4 // NPI - 1:
                    for _ in fe_gen:
                        pass
                if not samp:
                    for pp_ in range(NPI):
                        p = hf * NPI + pp_
                        if not first_chunk:
                            mm(ZB[:, pp_ * 128:(pp_ + 1) * 128], KR[:, p, 0:128], H0b[:, p, :], True, False, ['KR', 'H0f'], [ZK])
                        for hh in range(2):
                            q = pp_ * 2 + hh
                            mm(ZB[:, pp_ * 128 + hh * 64:pp_ * 128 + hh * 64 + 64], AT[:, q, 128:256],
                               Vtok[:, p, hh * 64:hh * 64 + 64], first_chunk, True, ['AT', 'Vtok'], [ZK])
                    act(f2(Zn[:]), ZB[:, 0:NPI * 128], AF.Copy, [ZK], ['Zn'], scale=-1.0)
                    pbu, pku = pbank()
                    for q in range(HP):
                        pp_, hh = q // 2, q % 2
                        mm(pbu[:, q * 64:(q + 1) * 64], PT[:, q, :], Zn[:, pp_, hh * 64:hh * 64 + 64], True, True, ['PT', 'Zn'], [pku])
                    acopy(f2(Ut[:]), pbu[:, 0:NPI * 128], [pku], ['Ut'])
                    for pp_ in range(NPI):
                        p = hf * NPI + pp_
                        if not first_chunk:
                            mm(YB[:, pp_ * 128:(pp_ + 1) * 128], H0b[:, p, :], KR[:, p, 128:256], True, False, ['H0f', 'KR'], [YK])
                        for hh in range(2):
                            q = pp_ * 2 + hh
                            hs = slice(hh * 64, hh * 64 + 64)
                            mm(YB[hs, pp_ * 128:(pp_ + 1) * 128], Ut[:, pp_, hh * 64:hh * 64 + 64], AT[:, q, 0:128],
                               first_chunk, False, ['Ut', 'AT'], [YK])
                            mm(YB[hs, pp_ * 128:(pp_ + 1) * 128], Vtok[:, p, hh * 64:hh * 64 + 64], AT[:, q, 256:384],
                               False, True, ['Vtok', 'AT'], [YK])
                    acopy(f2(X5[:, hf * NPI:(hf + 1) * NPI, :]), YB[:, 0:NPI * 128], [YK], ['X5'])
                    pbh, pkh = pbank()
                    for pp_ in range(NPI):
                        p = hf * NPI + pp_
                        mm(pbh[:, pp_ * 128:(pp_ + 1) * 128], BWtok[:, p, :], Ut[:, pp_, :], True, False, ['BWtok', 'Ut'], [pkh])
                        mm(pbh[:, pp_ * 128:(pp_ + 1) * 128], KWtok[:, p, :], Vtok[:, p, :], False, True, ['KWtok', 'Vtok'], [pkh])
                    hsl = slice(hf * NPI, (hf + 1) * NPI)
                    tt(X3[:, 0:NPI, :], pbh[:, 0:NPI * 128].rearrange("p (a b) -> p a b", a=NPI),
                       bonesF.unsqueeze(1).to_broadcast([128, NPI, 128]), ALU.mult, [pkh, 'cF'], ['X3'])
                    if first_chunk:
                        vcopy(H0f[:, hsl, :], X3[:, 0:NPI, :], ['X3'], ['H0f'])
                    else:
                        tt(H0f[:, hsl, :], H0f[:, hsl, :], wcs[:, hsl, 0:1].to_broadcast([128, NPI, 128]), ALU.mult, ['H0f', 'wcs'], ['H0f'])
                        tt(H0f[:, hsl, :], H0f[:, hsl, :], X3[:, 0:NPI, :], ALU.add, ['H0f', 'X3'], ['H0f'])
                    if last_chunk:
                        pbs, pks = pbank()
                        for pp_ in range(NPI):
                            tr(pbs[:, pp_ * 128:(pp_ + 1) * 128], H0f[:, hf * NPI + pp_, :], identF, ['H0f', 'cF'], [pks])
                        for hh in range(2):
                            hs = slice(hh * 64, hh * 64 + 64)
                            acopy(SCo[hs, 0:NPI, :], pbs[hs, 0:NPI * 128].rearrange("p (a b) -> p a b", a=NPI)[:, :, hh * 64:hh * 64 + 64], [pks], ['X4'])
                        dma(nwkvp[l, hf * HP:hf * HP + HP].rearrange("(p hh) v n -> (hh v) p n", hh=2), SCo[:, 0:NPI, :], ['X4'], ['X4'])
                else:
                    for pp_ in range(NPI):
                        p = hf * NPI + pp_
                        for g in range(4):
                            for hh in range(2):
                                hs = slice(hh * 64, hh * 64 + 64)
                                dma(S0g[hs, :, hh * 64:hh * 64 + 64],
                                    swkv[l, g * 4:g * 4 + 4, 2 * p + hh, :, :].rearrange("b v k -> v b k"), ['H0f'], ['H0f'])
                            pb, pk = pbank()
                            for j in range(4):
                                tr(pb[:, j * 128:(j + 1) * 128], S0g[:, j, :], identF, ['H0f', 'cF'], [pk])
                            acopy(f2(HSb[:]), pb[:, 0:512], [pk], ['A'])
                            bkk_diag = bass.AP(tensor=BKK.tensor, offset=BKK.offset + 32 * g,
                                               ap=[list(BKK.ap[0]), [136, 4], [1, 8]])
                            vcopy(bkk_diag, KR[:, p, g * 32:g * 32 + 32].rearrange("q (b t) -> q b t", t=8), ['KR', 'LW'], ['LW'])
                            brr_diag = bass.AP(tensor=BRR.tensor, offset=BRR.offset + 32 * g,
                                               ap=[list(BRR.ap[0]), [136, 4], [1, 8]])
                            vcopy(brr_diag, KR[:, p, 128 + g * 32:128 + g * 32 + 32].rearrange("q (b t) -> q b t", t=8), ['KR', 'X3'], ['X3'])
                            for j in range(4):
                                b_ = g * 4 + j
                                mm(ZB[:, 0:128], BKK[:, j, :], HSb[:, j, :], b_ == 0, False, ['LW', 'A'], [ZK])
                                mm(YB[:, 0:128], HSb[:, j, :], BRR[:, j, :], b_ == 0, False, ['A', 'X3'], [YK])
                            DVE(lambda e, bkk_diag=bkk_diag: e.memset(bkk_diag, 0.0), ['LW'], ['LW'])
                            DVE(lambda e, brr_diag=brr_diag: e.memset(brr_diag, 0.0), ['X3'], ['X3'])
                        for hh in range(2):
                            q = pp_ * 2 + hh
                            mm(ZB[:, hh * 64:hh * 64 + 64], AT[:, q, 128:256], Vtok[:, p, hh * 64:hh * 64 + 64], False, True, ['AT', 'Vtok'], [ZK])
                        act(Zn[:, 0, :], ZB[:, 0:128], AF.Copy, [ZK], ['Zn'], scale=-1.0)
                        pbu, pku = pbank()
                        for hh in range(2):
                            q = pp_ * 2 + hh
                            mm(pbu[:, hh * 64:hh * 64 + 64], PT[:, q, :], Zn[:, 0, hh * 64:hh * 64 + 64], True, True, ['PT', 'Zn'], [pku])
                        acopy(Ut[:, 0, :], pbu[:, 0:128], [pku], ['Ut'])
                        for hh in range(2):
                            q = pp_ * 2 + hh
                            hs = slice(hh * 64, hh * 64 + 64)
                            mm(YB[hs, 0:128], Ut[:, 0, hh * 64:hh * 64 + 64], AT[:, q, 0:128], False, False, ['Ut', 'AT'], [YK])
                            mm(YB[hs, 0:128], Vtok[:, p, hh * 64:hh * 64 + 64], AT[:, q, 256:384], False, True, ['Vtok', 'AT'], [YK])
                        acopy(X5[:, p, :], YB[:, 0:128], [YK], ['X5'])
                        smk = cb('seqm')
                        for g in range(4):
                            for hh in range(2):
                                hs = slice(hh * 64, hh * 64 + 64)
                                dma(S0g[hs, :, hh * 64:hh * 64 + 64],
                                    swkv[l, g * 4:g * 4 + 4, 2 * p + hh, :, :].rearrange("b v k -> v b k"), ['H0f'], ['H0f'])
                            sm4 = smk[:, g * 4:g * 4 + 4].unsqueeze(2).to_broadcast([128, 4, 128])
                            tt(BIGB[:], BWtok[:, p, :].unsqueeze(1).to_broadcast([128, 4, 128]), sm4, ALU.mult, ['BWtok', 'cB'], ['BWf'])
                            tt(BIGK[:], KWtok[:, p, :].unsqueeze(1).to_broadcast([128, 4, 128]), sm4, ALU.mult, ['KWtok', 'cB'], ['KWf'])
                            tt(DIAG[:], identF.unsqueeze(1).to_broadcast([128, 4, 128]),
                               wcs[:, p, g * 4:g * 4 + 4].unsqueeze(2).to_broadcast([128, 4, 128]), ALU.mult, ['cF', 'wcs'], ['X1'])
                            pb, pk = pbank()
                            mm(pb[:, 0:512], onesF, f2(DIAG[:]), True, True, ['X1', 'cF'], [pk])
                            acopy(f2(WcBC[:]), pb[:, 0:512], [pk], ['X2'])
                            tt(WcBC[:], WcBC[:], S0g[:], ALU.mult, ['X2', 'H0f'], ['X2'])
                            pbs, pks = pbank()
                            mm(pbs[:, 0:512], Ut[:, 0, :], f2(BIGB[:]), True, False, ['Ut', 'BWf'], [pks])
                            mm(pbs[:, 0:512], Vtok[:, p, :], f2(BIGK[:]), False, True, ['Vtok', 'KWf'], [pks])
                            tt(WcBC[:], WcBC[:], v4(pbs[:, 0:512]), ALU.add, ['X2', pks], ['X2'])
                            for hh in range(2):
                                hs = slice(hh * 64, hh * 64 + 64)
                                vcopy(SCo[hs, :, :], WcBC[hs, :, hh * 64:hh * 64 + 64], ['X2', 'X4'], ['X4'])
                            dma(nwkvs[l, g * 4:g * 4 + 4, 2 * p:2 * p + 2, :, :].rearrange("b hh v n -> (hh v) b n"), SCo[:], ['X4'], ['X4'])

            pb, pk = pbank()
            mm(pb[:, 0:512], bonesF, f2(X5[:]), True, True, ['X5', 'cF'], [pk])
            ts(X1[:], v4(pb[:, 0:512]), 1.0 / 64, None, ALU.mult, ALU.bypass, [pk], ['X1'])
            tt(X5[:], X5[:], X1[:], ALU.subtract, ['X5', 'X1'], ['X5'])
            tt(X2[:], X5[:], X5[:], ALU.mult, ['X5'], ['X2'])
            pb, pk = pbank()
            mm(pb[:, 0:512], bonesF, f2(X2[:]), True, True, ['X2', 'cF'], [pk])
            rsq(X1[:], v4(pb[:, 0:512]), 1.0 / 64, GN_EPS, [pk], ['X1'])
            tt(X5[:], X5[:], X1[:], ALU.mult, ['X5', 'X1'], ['X5'])
            tt(X5[:], X5[:], bc4(pcs(l, "ln_w", 4)), ALU.mult, ['X5', 'pcol'], ['X5'])
            tt(X5[:], X5[:], bc4(pcs(l, "ln_b", 4)), ALU.add, ['X5', 'pcol'], ['X5'])
            tt(X5[:], X5[:], X6[:], ALU.add, ['X5', 'X6'], ['X5'])
            pbg, pkg = pbank()
            for p in range(4):
                mm(pbg[:, p * 128:(p + 1) * 128], g2w[:, p * 128:(p + 1) * 128], sgg[:, ci % 2, :], True, True, ['lws', ('sgg', ci % 2)], [pkg])
            tt(orp[:, 0:4, :], X5[:], v4(pbg[:, 0:512]), ALU.mult, ['X5', pkg], ['orp'])

            if not samp:
                if last_chunk:
                    pb, pk = pbank()
                    for g in range(4):
                        tr(pb[0:15, g * 128:(g + 1) * 128], ppb[:, g, 128:143], identF, ['ppb', 'cF'], [pk])
                    acopy(f2(X1[0:15, :, :]), pb[0:15, 0:512], [pk], ['X1'])
                    dma(npoolp[l], f2(X1[0:15, :, :]), ['X1'], ['X1'])
                pass
            else:
                spv = spool[l].rearrange("b r c -> (b r) c")
                for half in range(2):
                    r0, rn = (0, 128) if half == 0 else (128, 112)
                    stg = [X1, X2][half]
                    dma(f2(stg[0:rn, :, :]), spv[r0:r0 + rn, :], [], [['X1', 'X2'][half]])
                    pb, pk = pbank()
                    for g in range(4):
                        tr(pb[:, g * 128:g * 128 + rn], f2(stg[0:rn, :, :])[:, g * 128:(g + 1) * 128], identF[0:rn, 0:rn],
                           [['X1', 'X2'][half], 'cF'], [pk])
                    acopy(f2(X3[:]) if half == 0 else f2(X4[:]), pb[:, 0:512], [pk], [['X3', 'X4'][half]])
                for g in range(4):
                    vcopy(ppbS[:, g, 0:8, 0:15], X3[:, g, 0:120].rearrange("p (b r) -> p b r", r=15), ['X3'], ['ppbS', 'ppb'])
                    vcopy(ppbS[:, g, 8, 0:8], X3[:, g, 120:128], ['X3'], ['ppbS', 'ppb'])
                    vcopy(ppbS[:, g, 8, 8:15], X4[:, g, 0:7], ['X4'], ['ppbS', 'ppb'])
                    vcopy(ppbS[:, g, 9:16, 0:15], X4[:, g, 7:112].rearrange("p (b r) -> p b r", r=15), ['X4'], ['ppbS', 'ppb'])
                for half in range(2):
                    r0, rn = (0, 128) if half == 0 else (128, 112)
                    stg = [X3, X4][half]
                    for g in range(4):
                        if half == 0:
                            vcopy(stg[:, g, 0:120].rearrange("p (b r) -> p b r", r=15), ppbS[:, g, 0:8, 8:23], ['ppbS'], [['X3', 'X4'][half]])
                            vcopy(stg[:, g, 120:128], ppbS[:, g, 8, 8:16], ['ppbS'], [['X3', 'X4'][half]])
                        else:
                            vcopy(stg[:, g, 0:7], ppbS[:, g, 8, 16:23], ['ppbS'], [['X3', 'X4'][half]])
                            vcopy(stg[:, g, 7:112].rearrange("p (b r) -> p b r", r=15), ppbS[:, g, 9:16, 8:23], ['ppbS'], [['X3', 'X4'][half]])
                    pb, pk = pbank()
                    for g in range(4):
                        tr(pb[0:rn, g * 128:(g + 1) * 128], stg[:, g, 0:rn], identF, [['X3', 'X4'][half], 'cF'], [pk])
                    ob = [X1, X2][half]
                    acopy(f2(ob[0:rn, :, :]), pb[0:rn, 0:512], [pk], [['X1', 'X2'][half]])
                    dma(npools[l, r0:r0 + rn, :], f2(ob[0:rn, :, :]), [['X1', 'X2'][half]], [['X1', 'X2'][half]])
                for g in range(4):
                    A0 = ppbS[:, g, :, :]
                    cur = X3[:].rearrange("p a b -> p (a b)")[:, 0:368].rearrange("p (b r) -> p b r", r=23)
                    oth = X4[:].rearrange("p a b -> p (a b)")[:, 0:368].rearrange("p (b r) -> p b r", r=23)
                    tt(cur[:, :, 1:23], A0[:, :, 1:23], A0[:, :, 0:22], ALU.add, ['ppbS', 'X3', 'X4'], ['X3', 'X4'])
                    span = 2
                    while span < WINS[g]:
                        lo = 2 * span - 1
                        tt(oth[:, :, lo:23], cur[:, :, lo:23], cur[:, :, lo - span:23 - span], ALU.add, ['X3', 'X4'], ['X3', 'X4'])
                        cur, oth = oth, cur
                        span *= 2
                    mg = WK['mg']
                    stt(mg[:].rearrange("p (b t) -> p b t", t=8), cur[:, :, 15:23], 1.0 / WINS[g], ppbS[:, g, :, 15:23],
                        ALU.mult, ALU.subtract, ['X3', 'X4', 'ppbS'], ['mg'])
                    acopy(pz[:, g, :], mg[:], ['mg'], ['vbf'])
            if samp:
                for g in range(4):
                    pb, pk = pbank()
                    mm(pb[:, 0:128], poolw[:, g, :], pz[:, g, :], True, True, ['lws', 'vbf'], [pk])
                    act(orp[:, 4 + g, :], pb[:, 0:128], AF.Identity, [pk, 'pcol'], ['orp'], scale=pc(l, "pool_scale", g))
            dma(orpd[:, :, t0:t0 + 128], orp[:], ['orp'], [('orpd', ci)])

        S.barrier()
        ptr[0] = UBASE
        T2 = 512
        WK['sq'] = alloc([2, T2], BF16); WK['rstd'] = alloc([T2]); WK['tmpn'] = alloc([2, T2]); WK['mg'] = alloc([128])
        hT2 = alloc([NKC, T2], BF16)
        mrg = alloc([NKC, NTOK], BF16)
        orp2 = alloc([1, 8, T2], BF16)
        g16 = alloc([16, T2], BF16)
        tmpb = WK['tmpn'][:, 0:1, :]
        W2 = [wl_kc(W["w_in"][l, :, 2304 + j * 512:2304 + (j + 1) * 512], 512) for j in range(4)]
        wbr, wbrk = wl_kc(W["w_br_rwkv"][l], 1024)
        wbp, wbpk = wl_kc(W["w_br_pool"][l], 1024)
        for ti, (t0, n, samp) in enumerate(tiles(T2)):
            ob = orp2[:, 0, :, 0:n]
            ok = ('orp2', 0)
            dma(ob, orpd[:, :, t0:t0 + n], [('orpd', c) for c in range(t0 // 128, (t0 + n) // 128)], [ok])
            norm_mod_t(t0, n, samp, hT2, ['hT2'])
            for cg in range(16):
                wt, wkey = W2[cg // 4]
                q = cg % 4
                pb, pk = pbank()
                for kc in range(NKC):
                    mm(pb[:, 0:n], wt[:, kc, q * 128:(q + 1) * 128], hT2[:, kc, 0:n], kc == 0, kc == NKC - 1, [wkey, 'hT2'], [pk])
                act(g16[:, cg, 0:n], pb[:, 0:n], AF.Sigmoid, [pk], ['g16'])
            for c in range(8):
                pb, pk = pbank()
                for kc in range(4):
                    mm(pb[:, 0:n], wbr[:, kc, c * 128:(c + 1) * 128], ob[:, kc, :], kc == 0, kc == 3, [wbrk, ok], [pk])
                tt(tmpb[:, 0, 0:n], pb[:, 0:n], g16[:, c, 0:n], ALU.mult, [pk, 'g16'], [('tmpn', 0)])
                pb2, pk2 = pbank()
                for kc in range(4):
                    mm(pb2[:, 0:n], wbp[:, kc, c * 128:(c + 1) * 128], ob[:, 4 + kc, :], kc == 0, kc == 3, [wbpk, ok], [pk2])
                tt(mrg[:, c, t0:t0 + n], pb2[:, 0:n], g16[:, 8 + c, 0:n], ALU.mult, [pk2, 'g16'], [('mrg', t0)])
                tt(mrg[:, c, t0:t0 + n], mrg[:, c, t0:t0 + n], tmpb[:, 0, 0:n], ALU.add, [('tmpn', 0), ('mrg', t0)], [('mrg', t0)])
        wo = [wl_kc(W["w_out"][l, :, j * 512:(j + 1) * 512], 512) for j in range(2)]
        for (t0, n, samp) in tiles(T2):
            for c in range(8):
                wt, wkey = wo[c // 4]
                q = c % 4
                pb, pk = pbank()
                for kc in range(NKC):
                    mm(pb[:, 0:n], wt[:, kc, q * 128:(q + 1) * 128], mrg[:, kc, t0:t0 + n], kc == 0, kc == NKC - 1, [wkey, ('mrg', t0)], [pk])
                resid_update_t(t0, n, samp, c, pb, pk)

        S.barrier()
        ptr[0] = UBASE
        T3 = 512
        WK['sq'] = alloc([2, T3], BF16); WK['rstd'] = alloc([T3]); WK['tmpn'] = alloc([2, T3]); WK['mg'] = alloc([128])
        hTm = alloc([NKC, NTOK], BF16)
        rl = alloc([2, T3]); r2 = alloc([8, T3], BF16)
        ada(l, "mlp")
        for (t0, n, samp) in tiles(T3):
            norm_mod_t(t0, n, samp, hTm[:, :, t0:t0 + n], [('hTm', t0)])
        for qd in range(4):
            w1 = [wl_kc(W["w_ff1"][l, :, qd * 1024 + j * 512:qd * 1024 + (j + 1) * 512], 512) for j in range(2)]
            w2 = [wl_kc(W["w_ff2"][l, qd * 1024:(qd + 1) * 1024, j * 512:(j + 1) * 512], 512) for j in range(2)]
            for (t0, n, samp) in tiles(T3):
                for c in range(8):
                    wt, wkey = w1[c // 4]
                    q = c % 4
                    pb, pk = pbank()
                    for kc in range(NKC):
                        mm(pb[:, 0:n], wt[:, kc, q * 128:(q + 1) * 128], hTm[:, kc, t0:t0 + n], kc == 0, kc == NKC - 1, [wkey, ('hTm', t0)], [pk])
                    act(rl[:, c % 2, 0:n], pb[:, 0:n], AF.Relu, [pk], [('rl', c % 2)])
                    tt(r2[:, c, 0:n], rl[:, c % 2, 0:n], rl[:, c % 2, 0:n], ALU.mult, [('rl', c % 2)], [('r2', c)])
                for c in range(8):
                    wt, wkey = w2[c // 4]
                    q = c % 4
                    pb, pk = pbank()
                    for kc in range(NKC):
                        mm(pb[:, 0:n], wt[:, kc, q * 128:(q + 1) * 128], r2[:, kc, 0:n], kc == 0, kc == NKC - 1, [wkey, ('r2', kc)], [pk])
                    resid_update_t(t0, n, samp, c, pb, pk)

    S.barrier()
    ptr[0] = UBASE
    sqr = alloc([2, 128], BF16); rstd = alloc([128]); tmpo = alloc([NKC, 128]); yout = alloc([2, D])
    ts(G32f[:], pcol[:, 0, 120:128].unsqueeze(2), 32.0, None, ALU.mult, ALU.bypass, ['pcol'], ['G32f'])
    for ci in range(NCH):
        t0 = ci * 128
        pb, pk = pbank()
        for c in range(NKC):
            act(sqr[:, c % 2, :], xT[:, c, t0:t0 + 128], AF.Square, xk(ci), [('sq', c % 2)])
            mm(pb[:, 0:128], onesB, sqr[:, c % 2, :], c == 0, c == NKC - 1, [('sq', c % 2), 'cB'], [pk])
        rsq(rstd[:], pb[:, 0:128], 1.0, D * EPS, [pk], ['rstd'])
        for c in range(NKC):
            stt(tmpo[:, c, :], xT[:, c, t0:t0 + 128], G32f[:, c, 0:1], rstd[:], ALU.mult, ALU.mult, xk(ci) + ['G32f', 'rstd'], ['tmpo'])
        yo = yout[:, ci % 2, :]
        for half in range(2):
            pb, pk = pbank()
            for q in range(4):
                c = half * 4 + q
                tr(pb[:, q * 128:(q + 1) * 128], tmpo[:, c, :], identF, ['tmpo', 'cF'], [pk])
            acopy(yo[:, half * 512:(half + 1) * 512], pb[:, 0:512], [pk], [('yout', ci % 2)])
        dst = yp[t0:t0 + 128, :] if ci < NPC else ys[:, :]
        dma(dst, yo, [('yout', ci % 2)], [('yout', ci % 2)])
    return nc, S, st


CSTF = {}
CSTB = {}
CF_COLS = 0
CB_COLS = 0


def _layout_consts():
    global CF_COLS, CB_COLS
    o = 0
    for nm, n in [('ident', 128), ('ones', 128), ('bones', 128), ('rmS', 128), ('pcorr', 64)]:
        CSTF[nm] = (o, n)
        o += n
    CF_COLS = o
    o = 0
    for nm, n in [('ident', 128), ('ones', 128), ('mAP', 512), ('mAS', 512), ('mLP', 128), ('mLS', 128), ('seqm', 16)]:
        CSTB[nm] = (o, n)
        o += n
    CB_COLS = o


_layout_consts()


def make_consts():
    c = np.zeros((128, CF_COLS + CB_COLS), np.float32)

    def putf(nm, a):
        o, n = CSTF[nm]
        c[:, o:o + n] = a

    def putb(nm, a):
        o, n = CSTB[nm]
        c[:, CF_COLS + o:CF_COLS + o + n] = a
    i = np.arange(128)
    putf('ident', np.eye(128)); putb('ident', np.eye(128))
    putf('ones', np.ones((128, 128))); putb('ones', np.ones((128, 128)))
    putf('bones', (i[:, None] // 64 == i[None, :] // 64).astype(np.float32))
    s, t = i[:, None], i[None, :]
    for tag, same in (('P', np.ones((128, 128), bool)), ('S', (s // 8) == (t // 8))):
        lt = ((s < t) & same).astype(np.float32)
        le = ((s <= t) & same).astype(np.float32)
        gtm = ((s > t) & same).astype(np.float32)
        putb('mA' + tag, np.concatenate([-lt, le, lt, le], axis=1))
        putb('mL' + tag, -gtm)
    rmS = np.ones((128, 128), np.float32)
    rmS[:, ::8] = 0
    putf('rmS', rmS)
    putb('seqm', (i[:, None] // 8 == np.arange(16)[None, :]).astype(np.float32))
    pc_ = np.zeros((128, 64), np.float32)
    for g, w in enumerate(WINS):
        tt_ = np.arange(16)
        pc_[:, g * 16:(g + 1) * 16] = (1.0 / np.minimum(tt_ + 1, w))[None, :]
    putf('pcorr', pc_)
    return c


def emit(nc, S, st):
    sems = {name: st.enter_context(nc.semaphore(name)) for name in S.cnt}
    block = st.enter_context(nc.Block())

    def run(stream, eng):
        for waits, fn, sem, inc in S.ops[stream]:
            for (s, v) in waits:
                eng.wait_ge(sems[s], v)
            if fn is not None:
                fn(eng).then_inc(sems[sem], inc)

    @block.sync
    def _(e):
        run('sp', e)
        for nm in S.cnt:
            if nm.startswith('sp'):
                e.wait_ge(sems[nm], S.cnt[nm])

    @block.gpsimd
    def _(e):
        run('pool', e)

    @block.tensor
    def _(e):
        run('pe', e)

    @block.vector
    def _(e):
        run('dve', e)

    @block.scalar
    def _(e):
        run('act', e)
    st.close()
    return nc


_WNAMES = ["w_ada_mix", "b_ada_mix", "norm_mix", "w_in", "mu_shift", "w0", "w2", "a0", "a2", "g2", "v0", "v1", "v2",
           "k_k", "k_a", "r_k", "ln_w", "ln_b", "pool_w", "pool_scale", "w_br_rwkv", "w_br_pool", "w_out",
           "w_ada_mlp", "b_ada_mlp", "norm_mlp", "w_ff1", "w_ff2", "norm_final"]


def make_in_maps(inputs, ncores, L):
    consts = make_consts()
    f = lambda a: np.ascontiguousarray(np.asarray(a, dtype=np.float32))
    shared = {}
    for nm in _WNAMES:
        a = f(inputs[nm])
        if nm == "r_k":
            a = a.reshape(L, MIX)
        if nm == "norm_final":
            a = a.reshape(1, D)
        shared[nm] = a
    shared["cst"] = consts
    maps = []
    for i in range(ncores):
        m = dict(shared)
        m["xp"] = f(inputs["x_prompt"][i])
        m["xs"] = f(inputs["x_sample"][16 * i:16 * i + 16]).reshape(128, D)
        m["cc"] = f(np.concatenate([np.asarray(inputs["c_prompt"])[i:i + 1], np.asarray(inputs["c_sample"])[16 * i:16 * i + 16]], axis=0))
        m["sshift"] = f(np.asarray(inputs["state_shift"])[:, 16 * i:16 * i + 16])
        m["spool"] = f(np.asarray(inputs["state_pool"])[:, 16 * i:16 * i + 16])
        m["swkv"] = f(np.asarray(inputs["state_wkv"])[:, 16 * i:16 * i + 16])
        maps.append(m)
    return maps


def gather(R, ncores, L):
    y_p = np.stack([R[i]["yp"] for i in range(ncores)], 0)
    y_s = np.concatenate([R[i]["ys"].reshape(16, 8, D) for i in range(ncores)], 0)
    sh_p = np.stack([R[i]["nshp"] for i in range(ncores)], 1)
    pool_p = np.stack([R[i]["npoolp"] for i in range(ncores)], 1)
    wkv_p = np.stack([R[i]["nwkvp"] for i in range(ncores)], 1)
    sh_s = np.concatenate([R[i]["nshs"] for i in range(ncores)], 1)
    pool_s = np.concatenate([R[i]["npools"].reshape(L, 16, 15, MIX) for i in range(ncores)], 1)
    wkv_s = np.concatenate([R[i]["nwkvs"] for i in range(ncores)], 1)
    return tuple(np.ascontiguousarray(a, dtype=np.float32) for a in (y_p, y_s, sh_p, pool_p, wkv_p, sh_s, pool_s, wkv_s))


def kernel(**inputs):
    ncores = 8
    L = 4
    nc, S, st = build(TP=2048, L=L)
    emit(nc, S, st)
    maps = make_in_maps(inputs, ncores, L)
    res = run_bass_kernel_spmd(nc, maps, core_ids=list(range(ncores)))
    return gather(res.results, ncores, L)
```

```python
import numpy as np
from contextlib import ExitStack
import concourse.bass as bass
import concourse.mybir as mybir
from concourse.bass_utils import run_bass_kernel_spmd

F32 = mybir.dt.float32
BF16 = mybir.dt.bfloat16
AF = mybir.ActivationFunctionType
ALU = mybir.AluOpType

D = 1024
NKC = 8
MIX = 512
RW = 1792
INC = 4352
DFF = 4096
EPS = 1e-6
GN_EPS = 64e-5
DEC_C = -float(np.exp(-0.5))
WINS = (2, 4, 8, 16)


class Sched:
    STREAMS = ('pe', 'act', 'dve', 'pool', 'sp')

    def __init__(self):
        self.ops = {s: [] for s in self.STREAMS}
        self.cnt = {}
        self.known = {s: {} for s in self.STREAMS}
        self.lastw = {}
        self.readers = {}
        self.dma_i = {}

    def op(self, stream, fn, r=(), w=(), sem=None, inc=1, nsem=1):
        sem = sem or stream
        if nsem > 1:
            i = self.dma_i.get(sem, 0)
            self.dma_i[sem] = i + 1
            sem = "%s%d" % (sem, i % nsem)
        need = {}
        if nsem > 1 and self.cnt.get(sem, 0):
            need[sem] = self.cnt[sem]

        def add(s, v):
            if need.get(s, 0) < v:
                need[s] = v
        for b in r:
            if b in self.lastw:
                add(*self.lastw[b])
        for b in w:
            if b in self.lastw:
                add(*self.lastw[b])
            for s, v in self.readers.get(b, {}).items():
                add(s, v)
        waits = []
        kn = self.known[stream]
        for s, v in need.items():
            if stream == 'pe' and s == 'pe':
                continue
            if kn.get(s, 0) < v:
                waits.append((s, v))
                kn[s] = v
        self.cnt[sem] = self.cnt.get(sem, 0) + inc
        val = self.cnt[sem]
        self.ops[stream].append((waits, fn, sem, inc))
        for b in r:
            d = self.readers.setdefault(b, {})
            if d.get(sem, 0) < val:
                d[sem] = val
        for b in w:
            self.lastw[b] = (sem, val)
            self.readers[b] = {}

    def barrier(self, streams=('pe', 'act', 'dve', 'sp')):
        for s in streams:
            waits = []
            for sem in list(self.cnt):
                if sem.startswith('pq'):
                    continue
                v = self.cnt.get(sem, 0)
                if s == 'pe' and sem == 'pe':
                    continue
                if v and self.known[s].get(sem, 0) < v:
                    waits.append((sem, v))
                    self.known[s][sem] = v
            if waits:
                self.ops[s].append((waits, None, None, 0))


def build(TP=2048, L=4):
    NTOK = TP + 128
    NCH = NTOK // 128
    NPC = TP // 128
    nc = bass.Bass("TRN2", target_bir_lowering=False)
    S = Sched()
    st = ExitStack()

    def din(name, shape, dt=F32):
        return nc.dram_tensor(name, list(shape), dt, kind="ExternalInput").ap()

    def dout(name, shape, dt=F32):
        return nc.dram_tensor(name, list(shape), dt, kind="ExternalOutput").ap()

    xp = din("xp", [TP, D]); xs = din("xs", [128, D]); cc = din("cc", [17, D])
    sshift = din("sshift", [L, 16, RW]); spool = din("spool", [L, 16, 15, MIX])
    swkv = din("swkv", [L, 16, 8, 64, 64])
    W = {}
    LV = max(L - 1, 1)
    for nm, shp in [("w_ada_mix", [L, D, 3 * D]), ("b_ada_mix", [L, 3 * D]), ("norm_mix", [L, D]),
                    ("w_in", [L, D, INC]), ("mu_shift", [L, RW]), ("w0", [L, MIX]), ("w2", [L, 64, MIX]),
                    ("a0", [L, MIX]), ("a2", [L, 64, MIX]), ("g2", [L, 128, MIX]), ("v0", [LV, MIX]),
                    ("v1", [LV, MIX, 32]), ("v2", [LV, 32, MIX]), ("k_k", [L, MIX]),
                    ("k_a", [L, MIX]), ("r_k", [L, MIX]), ("ln_w", [L, MIX]), ("ln_b", [L, MIX]),
                    ("pool_w", [L, 4, 128, 128]), ("pool_scale", [L, MIX]), ("w_br_rwkv", [L, MIX, D]),
                    ("w_br_pool", [L, MIX, D]), ("w_out", [L, D, D]), ("w_ada_mlp", [L, D, 3 * D]),
                    ("b_ada_mlp", [L, 3 * D]), ("norm_mlp", [L, D]), ("w_ff1", [L, D, DFF]),
                    ("w_ff2", [L, DFF, D]), ("norm_final", [1, D])]:
        W[nm] = din(nm, shp)
    cst = din("cst", [128, CF_COLS + CB_COLS])
    yp = dout("yp", [TP, D]); ys = dout("ys", [128, D])
    nshp = dout("nshp", [L, RW]); npoolp = dout("npoolp", [L, 15, MIX]); nwkvp = dout("nwkvp", [L, 8, 64, 64])
    nshs = dout("nshs", [L, 16, RW]); npools = dout("npools", [L, 16 * 15, MIX])
    nwkvs = dout("nwkvs", [L, 16, 8, 64, 64])
    vfd = nc.dram_tensor("vfirst_scr", [128, 4, NTOK], BF16, kind="Internal").ap()
    orpd = nc.dram_tensor("orp_scr", [128, 8, NTOK], BF16, kind="Internal").ap()

    NW = 53200
    big = st.enter_context(nc.sbuf_tensor("big", [128, NW], F32))
    ptr = [0]

    def alloc(shape, dt=F32):
        n = int(np.prod(shape))
        words = n if dt == F32 else (n + 1) // 2
        words = (words + 7) // 8 * 8
        o = ptr[0]
        ptr[0] += words
        assert ptr[0] <= NW, ("SBUF arena overflow", ptr[0], NW)
        v = big[:, o:o + words]
        if dt != F32:
            v = v.bitcast(dt)
        v = v[:, 0:n]
        if len(shape) == 1:
            return v
        names = " ".join("d%d" % i for i in range(len(shape)))
        kw = {"d%d" % i: int(shape[i]) for i in range(len(shape) - 1)}
        return v.rearrange("p (%s) -> p %s" % (names, names), **kw)

    SD = F32
    NPI = 2
    HP = 2 * NPI
    xT = alloc([NKC, NTOK])
    ring = alloc([6, 4096], BF16)
    cF = alloc([CF_COLS]); cB = alloc([CB_COLS], BF16)
    pcol = alloc([L, 128])
    omm = alloc([14]); omka = alloc([4])
    siluT = alloc([NKC, 17], BF16)
    modv = alloc([24, 17]); G32 = alloc([NKC, 17]); G32f = alloc([NKC, 1])
    lw_small = alloc([2176], BF16)
    H0f = alloc([4, 128]); H0b = H0f
    prcarry = alloc([14, 1])
    UBASE = ptr[0]

    PB = [st.enter_context(nc.psum_tensor("pb%d" % i, [128, 512], F32)) for i in range(8)]
    NROT = 6
    pbi = [0]
    pbt_i = [0]

    def pbank():
        i = pbi[0] % NROT
        pbi[0] += 1
        return PB[i], ('pb', i)
    ZB, ZK = PB[6], ('pb', 6)
    YB, YK = PB[7], ('pb', 7)

    def PE(fn, r, w): S.op('pe', fn, r, w)
    def ACT(fn, r, w): S.op('act', fn, r, w)
    def DVE(fn, r, w): S.op('dve', fn, r, w)
    def SPD(fn, r, w): S.op('sp', fn, r, w, sem='sp', inc=16, nsem=16)
    def PQD(fn, r, w): S.op('pool', fn, r, w, sem='pq', inc=16, nsem=8)

    def mm(out, lhsT, rhs, start, stop, r, w):
        PE(lambda e: e.matmul(out, lhsT, rhs, start=start, stop=stop, skip_group_check=True), r, w)

    def tr(out, in_, ident, r, w):
        PE(lambda e: e.transpose(out, in_, ident), r, w)

    def act(out, in_, func, r, w, bias=0.0, scale=1.0):
        ACT(lambda e: e.activation(out, in_, func, bias=bias, scale=scale), r, w)

    def acopy(out, in_, r, w):
        ACT(lambda e: e.copy(out, in_), r, w)

    def vcopy(out, in_, r, w):
        DVE(lambda e: e.tensor_copy(out, in_), r, w)

    def tt(out, a, b, op, r, w):
        DVE(lambda e: e.tensor_tensor(out, a, b, op), r, w)

    def ts(out, a, s1, s2, op0, op1, r, w):
        DVE(lambda e: e.tensor_scalar(out, a, s1, s2, op0, op1), r, w)

    def stt(out, a, s, b, op0, op1, r, w):
        DVE(lambda e: e.scalar_tensor_tensor(out, a, s, b, op0, op1), r, w)

    def dma(out, in_, r, w):
        SPD(lambda e: e.dma_start(out=out, in_=in_), r, w)

    def rsq(out, in_, mulc, addc, r, w):
        ts(out, in_, mulc, addc, ALU.mult, ALU.add, r, w)
        act(out, out, AF.Ln, w, w)
        act(out, out, AF.Exp, w, w, scale=-0.5)

    def xk(ci): return [('x', ci)]
    f2 = lambda t: t.rearrange("p a b -> p (a b)")
    v4 = lambda t: t.rearrange("p (a b) -> p a b", a=4)

    def cf(name, lo=0, hi=None):
        o, n = CSTF[name]
        return cF[:, o + lo:o + (n if hi is None else hi)]

    def cb(name, lo=0, hi=None):
        o, n = CSTB[name]
        return cB[:, o + lo:o + (n if hi is None else hi)]
    SD = F32
    identF = cf('ident'); identB = cb('ident'); identS = identF if SD == F32 else identB; onesB = cb('ones'); bonesF = cf('bones'); onesF = cf('ones')

    ptr[0] = UBASE
    pstage = alloc([L, 128]); cst17 = alloc([D]); xin = alloc([2, D])
    dma(cF[:], cst[:, 0:CF_COLS], [], ['cF'])
    PQD(lambda e: e.dma_start(out=cB[:], in_=cst[:, CF_COLS:CF_COLS + CB_COLS]), [], ['cB'])
    DVE(lambda e: e.memset(pstage[:], 0.0), [], ['pstage'])
    PROW = {}
    ro = 0
    for nm, nchk in [("norm_mix", 8), ("norm_mlp", 8), ("mu_shift", 14), ("w0", 4), ("a0", 4), ("v0", 4),
                     ("k_k", 4), ("k_a", 4), ("r_k", 4), ("ln_w", 4), ("ln_b", 4), ("pool_scale", 4),
                     ("b_ada_mix", 24), ("b_ada_mlp", 24)]:
        PROW[nm] = ro
        src = W[nm]
        if nm == "v0":
            if L > 1:
                dma(pstage[ro:ro + nchk, 1:L, :], src[0:L - 1, :].rearrange("l (c p) -> c l p", p=128), ['pstage'], ['pstage'])
        else:
            dma(pstage[ro:ro + nchk, 0:L, :], src.rearrange("l (c p) -> c l p", p=128), ['pstage'], ['pstage'])
        ro += nchk
    assert ro <= 120
    dma(pstage[120:128, 0, :], W["norm_final"].rearrange("o (c p) -> (o c) p", p=128), ['pstage'], ['pstage'])
    for l in range(L):
        pb, pk = pbank()
        tr(pb[:, 0:128], pstage[:, l, :], identF, ['pstage', 'cF'], [pk])
        acopy(pcol[:, l, :], pb[:, 0:128], [pk], ['pcol'])

    def pc(l, nm, c): return pcol[:, l, PROW[nm] + c:PROW[nm] + c + 1]
    def pcs(l, nm, n): return pcol[:, l, PROW[nm]:PROW[nm] + n]

    dma(cst17[0:17, :], cc[:, :], [], ['cst17'])
    pb, pk = pbank()
    for kc in range(NKC):
        tr(pb[:, kc * 17:(kc + 1) * 17], cst17[0:17, kc * 128:(kc + 1) * 128], identF[0:17, 0:17], ['cst17', 'cF'], [pk])
    act(f2(siluT[:]), pb[:, 0:NKC * 17], AF.Silu, [pk], ['siluT'])

    for ci in range(NCH):
        src = xp[ci * 128:(ci + 1) * 128, :] if ci < NPC else xs[:, :]
        xb_ = xin[:, ci % 2, :]
        dma(xb_, src, [], [('xin', ci % 2)])
        for half in range(2):
            pb, pk = pbank()
            for q in range(4):
                c = half * 4 + q
                tr(pb[:, q * 128:(q + 1) * 128], xb_[:, c * 128:(c + 1) * 128], identF, [('xin', ci % 2), 'cF'], [pk])
            acopy(xT[:, half * 4:half * 4 + 4, ci * 128:(ci + 1) * 128], v4(pb[:, 0:512]), [pk], xk(ci))

    ring_i = [0]

    def wload(src3, a, b):
        i = ring_i[0] % 6
        ring_i[0] += 1
        dst = ring[:, i, 0:a * b].rearrange("p (a b) -> p a b", a=a)
        PQD(lambda e: e.dma_start(out=dst, in_=src3), [], [('ring', i)])
        return dst, ('ring', i)

    def wl_kc(src2, ncol):
        return wload(src2.rearrange("(kc p) n -> p kc n", p=128), src2.shape[0] // 128, ncol)

    def ada(l, which):
        wsrc = W["w_ada_" + which]
        pbm, pkm = pbank()
        for j in range(6):
            wt, wkey = wl_kc(wsrc[l, :, j * 512:(j + 1) * 512], 512)
            for q in range(4):
                ch = j * 4 + q
                for kc in range(NKC):
                    mm(pbm[:, ch * 17:(ch + 1) * 17], wt[:, kc, q * 128:(q + 1) * 128], siluT[:, kc, :],
                       kc == 0, kc == NKC - 1, [wkey, 'siluT'], [pkm])
        tt(modv[:], pbm[:, 0:408].rearrange("p (a b) -> p a b", b=17),
           pcs(l, "b_ada_" + which, 24).unsqueeze(2).to_broadcast([128, 24, 17]), ALU.add, [pkm, 'pcol'], ['mod'])
        ts(G32[:], modv[:, 8:16, :], 1.0, 32.0, ALU.add, ALU.mult, ['mod'], ['mod'])
        tt(G32[:], G32[:], pcs(l, "norm_" + which, 8).unsqueeze(2).to_broadcast([128, 8, 17]), ALU.mult, ['mod', 'pcol'], ['mod'])

    WK = {}

    def xks(t0, n): return [('x', c) for c in range(t0 // 128, (t0 + n) // 128)]

    def tiles(size):
        out = []
        t = 0
        while t < TP:
            n = min(size, TP - t)
            out.append((t, n, False))
            t += n
        out.append((TP, 128, True))
        return out

    def norm_mod_t(t0, n, samp, hdst, hkeys):
        sqr, rstd, tmpn = WK['sq'], WK['rstd'], WK['tmpn']
        pb, pk = pbank()
        for c in range(NKC):
            act(sqr[:, c % 2, 0:n], xT[:, c, t0:t0 + n], AF.Square, xks(t0, n), [('sq', c % 2)])
            mm(pb[:, 0:n], onesB, sqr[:, c % 2, 0:n], c == 0, c == NKC - 1, [('sq', c % 2), 'cB'], [pk])
        rsq(rstd[:, 0:n], pb[:, 0:n], 1.0, D * EPS, [pk], ['rstd'])
        for c in range(NKC):
            tb = tmpn[:, c % 2, 0:n]
            tk = ('tmpn', c % 2)
            if not samp:
                stt(tb, xT[:, c, t0:t0 + n], G32[:, c, 0:1], rstd[:, 0:n], ALU.mult, ALU.mult, xks(t0, n) + ['mod', 'rstd'], [tk])
                act(hdst[:, c, 0:n], tb, AF.Identity, [tk, 'mod'], hkeys, bias=modv[:, c, 0:1])
            else:
                tt(tb, xT[:, c, t0:t0 + n], rstd[:, 0:n], ALU.mult, xks(t0, n) + ['rstd'], [tk])
                t3 = tb.rearrange("p (b t) -> p b t", t=8)
                tt(t3, t3, G32[:, c, 1:17].unsqueeze(2).to_broadcast([128, 16, 8]), ALU.mult, [tk, 'mod'], [tk])
                tt(hdst[:, c, 0:n].rearrange("p (b t) -> p b t", t=8), t3,
                   modv[:, c, 1:17].unsqueeze(2).to_broadcast([128, 16, 8]), ALU.add, [tk, 'mod'], hkeys)

    def norm_mod(ci, samp, hdst, hkeys):
        norm_mod_t(ci * 128, 128, samp, hdst, hkeys)

    def resid_update_t(t0, n, samp, c, pb, pk):
        xv = xT[:, c, t0:t0 + n]
        mg = WK['mg']
        if not samp:
            stt(xv, pb[:, 0:n], modv[:, 16 + c, 0:1], xv, ALU.mult, ALU.add, [pk, 'mod'] + xks(t0, n), xks(t0, n))
        else:
            tt(mg[:, 0:128].rearrange("p (b t) -> p b t", t=8), pb[:, 0:128].rearrange("p (b t) -> p b t", t=8),
               modv[:, 16 + c, 1:17].unsqueeze(2).to_broadcast([128, 16, 8]), ALU.mult, [pk, 'mod'], ['mg'])
            tt(xv, xv, mg[:, 0:128], ALU.add, ['mg'] + xks(t0, n), xks(t0, n))

    for l in range(L):
        S.barrier()
        ptr[0] = UBASE
        WK['sq'] = alloc([2, 128], BF16); WK['rstd'] = alloc([128]); WK['tmpn'] = alloc([2, 128]); WK['mg'] = alloc([128])
        hTc = alloc([NKC, 129], BF16); hT = hTc[:, :, 1:129]
        rkv = alloc([12, 128])
        twa = alloc([128], BF16); sgg = alloc([128], BF16); t1b = alloc([128], BF16)
        gbf = alloc([4, 128], BF16); vbf = alloc([4, 128], BF16); vfb = alloc([4, 128], BF16)
        FMB = [alloc([4, 128]) for _ in range(8)]
        KR = alloc([4, 256], SD)
        Bt = alloc([4, 128], SD); Kt = alloc([4, 128], SD); BWf = alloc([4, 128], SD); KWf = alloc([4, 128], SD)
        Vtok = alloc([4, 128], SD); BWtok = alloc([4, 128], SD); KWtok = alloc([4, 128], SD)
        AT = alloc([HP, 384], SD); MA = alloc([HP, 256], SD); MB = alloc([HP, 256], SD)
        PT = alloc([HP, 128], SD)
        Zn = alloc([NPI, 128], SD); Ut = alloc([NPI, 128], SD)
        wcs = alloc([4, 16])
        pz = vbf; orp = alloc([8, 128], BF16)
        shiftT = alloc([14, 16]); ppbS = alloc([4, 16, 23])
        ppb = ppbS.rearrange("p a b c -> p (a b c)")[:, 0:576].rearrange("p (a b) -> p a b", a=4)
        A_, LW_, X1, X2, X3, X4, X5, X6 = FMB
        DIAG = X1; WcBC = X2; SCo = X4[:, :, 0:64]
        S0g = H0f; HSb = A_; BKK = LW_; BRR = X3; BIGB = BWf; BIGK = KWf
        tsh = f2(X1[:])[:, 0:272].rearrange("p (a b) -> p a b", a=2)
        xl12 = X2[:, 0:2, :]
        pq2 = f2(X1[:])[:, 0:144]; pq4 = f2(X2[:])[:, 0:144]

        ts(omm[:], pcs(l, "mu_shift", 14), -1.0, 1.0, ALU.mult, ALU.add, ['pcol'], ['omm'])
        ts(omka[:], pcs(l, "k_a", 4), -1.0, 1.0, ALU.mult, ALU.add, ['pcol'], ['omm'])
        PQD(lambda e, l=l: e.dma_start(out=lw_small[0:64, 0:512], in_=W["w2"][l]), [], ['lws'])
        PQD(lambda e, l=l: e.dma_start(out=lw_small[64:128, 0:512], in_=W["a2"][l]), [], ['lws'])
        PQD(lambda e, l=l: e.dma_start(out=lw_small[:, 512:1024], in_=W["g2"][l]), [], ['lws'])
        if l > 0:
            PQD(lambda e, l=l: e.dma_start(out=lw_small[:, 1024:1152].rearrange("p (a b) -> p a b", a=4),
                                           in_=W["v1"][l - 1].rearrange("(kc p) n -> p kc n", p=128)), [], ['lws'])
            PQD(lambda e, l=l: e.dma_start(out=lw_small[0:32, 1152:1664], in_=W["v2"][l - 1]), [], ['lws'])
        PQD(lambda e, l=l: e.dma_start(out=lw_small[:, 1664:2176].rearrange("p (g d) -> p g d", g=4),
                                       in_=W["pool_w"][l].rearrange("g c d -> c g d")), [], ['lws'])
        w2a2 = lw_small[:, 0:512]; g2w = lw_small[:, 512:1024]
        v1w = lw_small[:, 1024:1152].rearrange("p (a b) -> p a b", a=4); v2w = lw_small[0:32, 1152:1664]
        poolw = lw_small[:, 1664:2176].rearrange("p (g d) -> p g d", g=4)

        ada(l, "mix")
        DVE(lambda e: e.memset(ppb[:, :, 0:15], 0.0), [], ['ppb'])
        DVE(lambda e: e.memset(hTc[:, :, 0:1], 0.0), ['hT'], ['hT'])

        W1 = []
        for j in range(5):
            ncol = 512 if j < 4 else 256
            W1.append(wl_kc(W["w_in"][l, :, j * 512:j * 512 + ncol], ncol))

        for ci in range(NCH):
            samp = ci >= NPC
            t0 = ci * 128
            first_chunk = ci == 0
            last_chunk = ci == NPC - 1
            nb = 16 if samp else 1
            blk = 128 // nb
            mset = 'S' if samp else 'P'
            norm_mod(ci, samp, hT, ['hT'])
            if samp:
                pbs_, pks_ = pbank()
                for g0 in range(0, 14, 4):
                    gn = min(4, 14 - g0)
                    dma(X6[0:16, :, :].rearrange("p a b -> p (a b)")[:, 0:gn * 128], sshift[l, :, g0 * 128:(g0 + gn) * 128], [], ['X6'])
                    for c in range(g0, g0 + gn):
                        tr(pbs_[:, c * 16:(c + 1) * 16], f2(X6[0:16, :, :])[:, (c - g0) * 128:(c - g0 + 1) * 128],
                           identF[0:16, 0:16], ['X6', 'cF'], [pks_])
                acopy(f2(shiftT[:]), pbs_[:, 0:224], [pks_], ['shiftT'])
            for cidx in range(18):
                wt, wkey = W1[cidx // 4]
                q = cidx % 4
                pb, pk = pbank()
                for kc in range(NKC):
                    if samp:
                        mm(pb[:, 0:128], wt[:, kc, q * 128:(q + 1) * 128], hT[:, kc, :], kc == 0, kc == NKC - 1, [wkey, 'hT'], [pk])
                    else:
                        mm(pb[:, 0:129], wt[:, kc, q * 128:(q + 1) * 128], hTc[:, kc, 0:129], kc == 0, kc == NKC - 1, [wkey, 'hT'], [pk])
                if cidx >= 14:
                    g = cidx - 14
                    if not samp:
                        acopy(ppb[:, g, 15:143], pb[:, 1:129], [pk], ['ppb'])
                    else:
                        acopy(ppbS[:, g, :, 15:23], pb[:, 0:128].rearrange("p (b t) -> p b t", t=8), [pk], ['ppbS', 'ppb'])
                    continue
                c = cidx
                tb = tsh[:, c % 2, :]
                tk = 'X1'
                mu = pc(l, "mu_shift", c)
                dst = rkv[:, c, :] if c < 12 else xl12[:, c - 12, :]
                dkey = 'rkv' if c < 12 else 'X2'
                if not samp:
                    act(tb[:, 0:129], pb[:, 0:129], AF.Identity, [pk, 'pcol'], [tk], scale=mu)
                    if last_chunk:
                        acopy(prcarry[:, c, :], pb[:, 128:129], [pk], ['prcarry'])
                    stt(dst, pb[:, 1:129], omm[:, c:c + 1], tb[:, 0:128], ALU.mult, ALU.add, [pk, 'omm', tk], [dkey])
                else:
                    p3 = pb[:, 0:128].rearrange("p (b t) -> p b t", t=8)
                    t3 = tb[:, 0:128].rearrange("p (b t) -> p b t", t=8)
                    act(t3[:, :, 1:8], p3[:, :, 0:7], AF.Identity, [pk, 'pcol'], [tk], scale=mu)
                    act(t3[:, :, 0:1], shiftT[:, c, :].unsqueeze(2), AF.Identity, ['shiftT', 'pcol'], [tk], scale=mu)
                    acopy(f2(X5[:])[:, c * 16:(c + 1) * 16].unsqueeze(2), p3[:, :, 7:8], [pk], ['X5'])
                    stt(dst, pb[:, 0:128], omm[:, c:c + 1], tb[:, 0:128], ALU.mult, ALU.add, [pk, 'omm', tk], [dkey])
            if not samp and not last_chunk:
                acopy(hTc[:, :, 0:1], hTc[:, :, 128:129], ['hT'], ['hT'])
            if last_chunk:
                pb, pk = pbank()
                tr(pb[0:14, 0:128], f2(prcarry[:]), identF, ['prcarry', 'cF'], [pk])
                acopy(f2(X6[0:14, :, :])[:, 0:128], pb[0:14, 0:128], [pk], ['X6'])
                dma(nshp[l].rearrange("(c p) -> c p", p=128), f2(X6[0:14, :, :])[:, 0:128], ['X6'], [])
            if samp:
                for g0 in range(0, 14, 4):
                    gn = min(4, 14 - g0)
                    pbx, pkx = pbank()
                    for c in range(g0, g0 + gn):
                        tr(pbx[0:16, (c - g0) * 128:(c - g0 + 1) * 128], f2(X5[:])[:, c * 16:(c + 1) * 16], identF, ['X5', 'cF'], [pkx])
                    ob = [X3, X4][(g0 // 4) % 2]
                    okey = ['X3', 'X4'][(g0 // 4) % 2]
                    acopy(f2(ob[0:16, :, :])[:, 0:gn * 128], pbx[0:16, 0:gn * 128], [pkx], [okey])
                    dma(nshs[l, :, g0 * 128:(g0 + gn) * 128], f2(ob[0:16, :, :])[:, 0:gn * 128], [okey], [])
            act(twa[0:64, :], xl12[0:64, 0, :], AF.Tanh, ['X2'], ['twa'])
            acopy(twa[64:128, :], xl12[64:128, 0, :], ['X2'], ['twa'])
            act(sgg[:], xl12[:, 1, :], AF.Sigmoid, ['X2'], ['sgg'])
            r_c = rkv[:, 0:4, :]; k_c = rkv[:, 4:8, :]; v_c = rkv[:, 8:12, :]
            for p in range(4):
                pb, pk = pbank()
                mm(pb[:, 0:128], w2a2[0:64, p * 128:(p + 1) * 128], twa[0:64, :], True, True, ['lws', 'twa'], [pk])
                act(LW_[:, p, :], pb[:, 0:128], AF.Sigmoid, [pk, 'pcol'], ['LW'], bias=pc(l, "w0", p))
                pb, pk = pbank()
                mm(pb[:, 0:128], w2a2[64:128, p * 128:(p + 1) * 128], twa[64:128, :], True, True, ['lws', 'twa'], [pk])
                act(A_[:, p, :], pb[:, 0:128], AF.Sigmoid, [pk, 'pcol'], ['A'], bias=pc(l, "a0", p))
                pb, pk = pbank()
                mm(pb[:, 0:128], g2w[:, p * 128:(p + 1) * 128], sgg[:], True, True, ['lws', 'sgg'], [pk])
                acopy(gbf[:, p, :], pb[:, 0:128], [pk], ['gbf'])
            ts(LW_[:], LW_[:], DEC_C, None, ALU.mult, ALU.bypass, ['LW'], ['LW'])
            bc4 = lambda col: col.unsqueeze(2).to_broadcast([128, 4, 128])
            d0 = cf('rmS') if samp else onesF
            for p in range(4):
                DVE(lambda e, p=p, d0=d0: e.tensor_tensor_scan(X2[:, p, :], d0, LW_[:, p, :], 0.0, ALU.mult, ALU.add),
                    ['LW', 'cF'], ['X2'])
            tt(X1[:], X2[:], LW_[:], ALU.subtract, ['X2', 'LW'], ['X1'])
            act(X1[:], X1[:], AF.Exp, ['X1'], ['X1'])
            act(LW_[:], X2[:], AF.Exp, ['X2'], ['LW'])
            ein4 = LW_[:].rearrange("p a (b t) -> p a b t", t=blk)
            vcopy(wcs[:, :, 0:nb].unsqueeze(3), ein4[:, :, :, blk - 1:blk], ['LW'], ['wcs'])
            act(X2[:], X2[:], AF.Exp, ['X2'], ['X2'], scale=-1.0)
            if l == 0:
                acopy(vbf[:], v_c, ['rkv'], ['vbf'])
                dma(vfd[:, :, t0:t0 + 128], vbf[:], ['vbf'], [('vfd', ci)])
            else:
                dma(vfb[:], vfd[:, :, t0:t0 + 128], [('vfd', ci)], ['vfb'])
                acopy(vbf[:], v_c, ['rkv'], ['vbf'])
                pb, pk = pbank()
                for p in range(4):
                    mm(pb[0:32, 0:128], v1w[:, p, :], vbf[:, p, :], p == 0, p == 3, ['lws', 'vbf'], [pk])
                acopy(t1b[0:32, :], pb[0:32, 0:128], [pk], ['t1b'])
                for p in range(4):
                    pb, pk = pbank()
                    mm(pb[:, 0:128], v2w[:, p * 128:(p + 1) * 128], t1b[0:32, :], True, True, ['lws', 't1b'], [pk])
                    act(X3[:, p, :], pb[:, 0:128], AF.Sigmoid, [pk, 'pcol'], ['X3'], bias=pc(l, "v0", p))
                tt(X4[:], vfb[:], v_c, ALU.subtract, ['vfb', 'rkv'], ['X4'])
                tt(X4[:], X4[:], X3[:], ALU.mult, ['X4', 'X3'], ['X4'])
                tt(v_c, v_c, X4[:], ALU.add, ['rkv', 'X4'], ['rkv'])

            tt(BWf[:], k_c, bc4(pcs(l, "k_k", 4)), ALU.mult, ['rkv', 'pcol'], ['BWf'])
            tt(KWf[:], BWf[:], BWf[:], ALU.mult, ['BWf'], ['KWf'])
            pb, pk = pbank()
            mm(pb[:, 0:512], bonesF, f2(KWf[:]), True, True, ['KWf', 'cF'], [pk])
            rsq(KWf[:], v4(pb[:, 0:512]), 1.0, 1e-12, [pk], ['KWf'])
            tt(X3[:], BWf[:], KWf[:], ALU.mult, ['BWf', 'KWf'], ['X3'])
            tt(X4[:], X3[:], A_[:], ALU.mult, ['X3', 'A'], ['X4'])
            tt(BWf[:], A_[:], bc4(pcs(l, "k_a", 4)), ALU.mult, ['A', 'pcol'], ['BWf'])
            tt(BWf[:], BWf[:], bc4(omka[:, 0:4]), ALU.add, ['BWf', 'omm'], ['BWf'])
            tt(X5[:], k_c, BWf[:], ALU.mult, ['rkv', 'BWf'], ['X5'])
            tt(BWf[:], r_c, X5[:], ALU.mult, ['rkv', 'X5'], ['BWf'])
            tt(BWf[:], BWf[:], bc4(pcs(l, "r_k", 4)), ALU.mult, ['BWf', 'pcol'], ['BWf'])
            pb, pk = pbank()
            mm(pb[:, 0:512], bonesF, f2(BWf[:]), True, True, ['BWf', 'cF'], [pk])
            tt(X6[:], v4(pb[:, 0:512]), v_c, ALU.mult, [pk, 'rkv'], ['X6'])
            tt(KR[:, :, 0:128], X3[:], X1[:], ALU.mult, ['X3', 'X1'], ['KR'])
            tt(KR[:, :, 128:256], r_c, LW_[:], ALU.mult, ['rkv', 'LW'], ['KR'])
            wcb = wcs[:, :, 0:nb].unsqueeze(3).to_broadcast([128, 4, nb, blk])
            b4 = lambda t: t.rearrange("p a (b t) -> p a b t", t=blk)
            tt(X3[:], X4[:], X2[:], ALU.mult, ['X4', 'X2'], ['X3'])
            acopy(Bt[:], X3[:], ['X3'], ['Bt'])
            tt(b4(BWf[:]), b4(X3[:]), wcb, ALU.mult, ['X3', 'wcs'], ['BWf'])
            tt(X4[:], X5[:], X2[:], ALU.mult, ['X5', 'X2'], ['X4'])
            acopy(Kt[:], X4[:], ['X4'], ['Kt'])
            tt(b4(KWf[:]), b4(X4[:]), wcb, ALU.mult, ['X4', 'wcs'], ['KWf'])
            for (srcb, dstb, sk, dk) in ((v_c, Vtok, 'rkv', 'Vtok'), (BWf, BWtok, 'BWf', 'BWtok'), (KWf, KWtok, 'KWf', 'KWtok')):
                pbb, pkb = pbank()
                for p in range(4):
                    tr(pbb[:, p * 128:(p + 1) * 128], srcb[:, p, :], identS, [sk, 'cF', 'cB'], [pkb])
                acopy(f2(dstb[:]), pbb[:, 0:512], [pkb], [dk])

            if samp:
                DVE(lambda e: e.memset(S0g[:], 0.0), ['H0f'], ['H0f'])
                DVE(lambda e: e.memset(BKK[:], 0.0), ['LW'], ['LW'])
                DVE(lambda e: e.memset(BRR[:], 0.0), ['X3'], ['X3'])
            for hf in range(4 // NPI):
                for q in range(HP):
                    hd = hf * HP + q
                    p, hh = hd // 2, hd % 2
                    hs = slice(hh * 64, hh * 64 + 64)
                    pb, pk = pbank()
                    mm(pb[:, 0:256], Bt[hs, p, :], KR[hs, p, :], True, True, ['Bt', 'KR'], [pk])
                    mm(pb[:, 256:512], Kt[hs, p, :], KR[hs, p, :], True, True, ['Kt', 'KR'], [pk])
                    tt(MB[:, q, 128:256], pb[:, 0:128], cb('mA' + mset, 0, 128), ALU.mult, [pk, 'cB'], ['MB'])
                    tt(AT[:, q, :], pb[:, 128:512], cb('mA' + mset, 128, 512), ALU.mult, [pk, 'cB'], ['AT'])
                pbA, pkA = pbank()
                pbB, pkB = pbank()
                for q in range(HP):
                    hd = hf * HP + q
                    p, hh = hd // 2, hd % 2
                    hs = slice(hh * 64, hh * 64 + 64)
                    pbx, pkx = (pbA, pkA) if hh == 0 else (pbB, pkB)
                    mm(pbx[:, (q // 2) * 128:(q // 2 + 1) * 128], KR[hs, p, 0:128], Bt[hs, p, :], True, True, ['KR', 'Bt'], [pkx])
                for hh, (pbx, pkx) in enumerate(((pbA, pkA), (pbB, pkB))):
                    tt(MB[:, hh:HP:2, 0:128], pbx[:, 0:NPI * 128].rearrange("p (a b) -> p a b", a=NPI),
                       cb('mL' + mset).unsqueeze(1).to_broadcast([128, NPI, 128]), ALU.mult, [pkx, 'cB'], ['MB'])
                tt(PT[:], MB[:, :, 128:256], identF.unsqueeze(1).to_broadcast([128, HP, 128]), ALU.add, ['MB', 'cF'], ['PT'])
                if hf == 0 and not samp:
                    PA = bass.AP(tensor=X1.tensor, offset=X1.offset, ap=[list(X1.ap[0]), [144, 4], [1, 144]])
                    PBs = bass.AP(tensor=A_.tensor, offset=A_.offset, ap=[list(A_.ap[0]), [144, 4], [1, 144]])
                    ka, kb = ['X1', 'X2'], ['A', 'LW']
                    tt(PA[:, :, 1:143], ppb[:, :, 1:143], ppb[:, :, 0:142], ALU.add, ['ppb'], ka)
                    tt(PBs[:, 1:4, 3:143], PA[:, 1:4, 3:143], PA[:, 1:4, 1:141], ALU.add, ka, kb)
                    tt(PA[:, 2:4, 7:143], PBs[:, 2:4, 7:143], PBs[:, 2:4, 3:139], ALU.add, kb + ka, ka)
                    tt(PBs[:, 3:4, 15:143], PA[:, 3:4, 15:143], PA[:, 3:4, 7:135], ALU.add, ka + kb, kb)
                    psrc = [PA[:, 0, :], PBs[:, 1, :], PA[:, 2, :], PBs[:, 3, :]]
                    for g in range(4):
                        if first_chunk:
                            mg = WK['mg']
                            stt(mg[:], psrc[g][:, 15:143], 1.0 / WINS[g], ppb[:, g, 15:143], ALU.mult, ALU.subtract, ka + kb + ['ppb'], ['mg'])
                            tt(mg[:, 0:16], psrc[g][:, 15:31], cf('pcorr')[:, g * 16:(g + 1) * 16], ALU.mult, ka + kb + ['cF', 'mg'], ['mg'])
                            tt(mg[:, 0:16], mg[:, 0:16], ppb[:, g, 15:31], ALU.subtract, ['mg', 'ppb'], ['mg'])
                            vcopy(pz[:, g, :], mg[:], ['mg'], ['vbf'])
                        else:
                            stt(pz[:, g, :], psrc[g][:, 15:143], 1.0 / WINS[g], ppb[:, g, 15:143], ALU.mult, ALU.subtract, ka + kb + ['ppb'], ['vbf'])
                    if not last_chunk:
                        vcopy(PA[:, :, 0:15], ppb[:, :, 128:143], ['ppb'] + ka, ka)
                        vcopy(ppb[:, :, 0:15], PA[:, :, 0:15], ka + ['ppb'], ['ppb'])
                nlev = 2 if samp else 6
                curM = lambda q: MB[:, q, 0:128]
                curMT = lambda q: MB[:, q, 128:256]
                curk = ['MB']
                for lev in range(nlev):
                    nxt, nk = (MA, 'MA') if lev % 2 == 0 else (MB, 'MB')
                    lastlev = lev == nlev - 1
                    for h2 in range(NPI):
                        pb, pk = pbank()
                        for qq in range(2):
                            q = h2 * 2 + qq
                            mm(pb[:, qq * 256:qq * 256 + 128], curMT(q), curM(q), True, True, curk, [pk])
                            if not lastlev:
                                mm(pb[:, qq * 256 + 128:qq * 256 + 256], curM(q), curMT(q), True, True, curk, [pk])
                        acopy(nxt[:, h2 * 2:h2 * 2 + 2, :], pb[:, 0:512].rearrange("p (a b) -> p a b", a=2), [pk], [nk])
                    pb, pk = pbank()
                    for q in range(HP):
                        mm(pb[:, q * 128:(q + 1) * 128], nxt[:, q, 0:128], PT[:, q, :], True, True, [nk, 'PT'], [pk])
                    tt(PT[:], PT[:], pb[:, 0:HP * 128].rearrange("p (a b) -> p a b", a=HP), ALU.add, [pk, 'PT'], ['PT'])
                    curM = lambda q, nxt=nxt: nxt[:, q, 0:128]
                    curMT = lambda q, nxt=nxt: nxt[:, q, 128:256]
                    curk = [nk]

                if hf == 0 and not samp:
                    for g in range(4):
                        pb, pk = pbank()
                        mm(pb[:, 0:128], poolw[:, g, :], pz[:, g, :], True, True, ['lws', 'vbf'], [pk])
                        act(orp[:, 4 + g, :], pb[:, 0:128], AF.Identity, [pk, 'pcol'], ['orp'], scale=pc(l, "pool_scale", g))
                if not samp:
                    for pp_ in range(NPI):
                        p = hf * NPI + pp_
                        if not first_chunk:
                            mm(ZB[:, pp_ * 128:(pp_ + 1) * 128], KR[:, p, 0:128], H0b[:, p, :], True, False, ['KR', 'H0f'], [ZK])
                        for hh in range(2):
                            q = pp_ * 2 + hh
                            mm(ZB[:, pp_ * 128 + hh * 64:pp_ * 128 + hh * 64 + 64], AT[:, q, 128:256],
                               Vtok[:, p, hh * 64:hh * 64 + 64], first_chunk, True, ['AT', 'Vtok'], [ZK])
                    act(f2(Zn[:]), ZB[:, 0:NPI * 128], AF.Copy, [ZK], ['Zn'], scale=-1.0)
                    pbu, pku = pbank()
                    for q in range(HP):
                        pp_, hh = q // 2, q % 2
                        mm(pbu[:, q * 64:(q + 1) * 64], PT[:, q, :], Zn[:, pp_, hh * 64:hh * 64 + 64], True, True, ['PT', 'Zn'], [pku])
                    acopy(f2(Ut[:]), pbu[:, 0:NPI * 128], [pku], ['Ut'])
                    for pp_ in range(NPI):
                        p = hf * NPI + pp_
                        if not first_chunk:
                            mm(YB[:, pp_ * 128:(pp_ + 1) * 128], H0b[:, p, :], KR[:, p, 128:256], True, False, ['H0f', 'KR'], [YK])
                        for hh in range(2):
                            q = pp_ * 2 + hh
                            hs = slice(hh * 64, hh * 64 + 64)
                            mm(YB[hs, pp_ * 128:(pp_ + 1) * 128], Ut[:, pp_, hh * 64:hh * 64 + 64], AT[:, q, 0:128],
                               first_chunk, False, ['Ut', 'AT'], [YK])
                            mm(YB[hs, pp_ * 128:(pp_ + 1) * 128], Vtok[:, p, hh * 64:hh * 64 + 64], AT[:, q, 256:384],
                               False, True, ['Vtok', 'AT'], [YK])
                    acopy(f2(X5[:, hf * NPI:(hf + 1) * NPI, :]), YB[:, 0:NPI * 128], [YK], ['X5'])
                    pbh, pkh = pbank()
                    for pp_ in range(NPI):
                        p = hf * NPI + pp_
                        mm(pbh[:, pp_ * 128:(pp_ + 1) * 128], BWtok[:, p, :], Ut[:, pp_, :], True, False, ['BWtok', 'Ut'], [pkh])
                        mm(pbh[:, pp_ * 128:(pp_ + 1) * 128], KWtok[:, p, :], Vtok[:, p, :], False, True, ['KWtok', 'Vtok'], [pkh])
                    hsl = slice(hf * NPI, (hf + 1) * NPI)
                    tt(X3[:, 0:NPI, :], pbh[:, 0:NPI * 128].rearrange("p (a b) -> p a b", a=NPI),
                       bonesF.unsqueeze(1).to_broadcast([128, NPI, 128]), ALU.mult, [pkh, 'cF'], ['X3'])
                    if first_chunk:
                        vcopy(H0f[:, hsl, :], X3[:, 0:NPI, :], ['X3'], ['H0f'])
                    else:
                        tt(H0f[:, hsl, :], H0f[:, hsl, :], wcs[:, hsl, 0:1].to_broadcast([128, NPI, 128]), ALU.mult, ['H0f', 'wcs'], ['H0f'])
                        tt(H0f[:, hsl, :], H0f[:, hsl, :], X3[:, 0:NPI, :], ALU.add, ['H0f', 'X3'], ['H0f'])
                    if last_chunk:
                        pbs, pks = pbank()
                        for pp_ in range(NPI):
                            tr(pbs[:, pp_ * 128:(pp_ + 1) * 128], H0f[:, hf * NPI + pp_, :], identF, ['H0f', 'cF'], [pks])
                        for hh in range(2):
                            hs = slice(hh * 64, hh * 64 + 64)
                            acopy(SCo[hs, 0:NPI, :], pbs[hs, 0:NPI * 128].rearrange("p (a b) -> p a b", a=NPI)[:, :, hh * 64:hh * 64 + 64], [pks], ['X4'])
                        dma(nwkvp[l, hf * HP:hf * HP + HP].rearrange("(p hh) v n -> (hh v) p n", hh=2), SCo[:, 0:NPI, :], ['X4'], ['X4'])
                else:
                    for pp_ in range(NPI):
                        p = hf * NPI + pp_
                        for g in range(4):
                            for hh in range(2):
                                hs = slice(hh * 64, hh * 64 + 64)
                                dma(S0g[hs, :, hh * 64:hh * 64 + 64],
                                    swkv[l, g * 4:g * 4 + 4, 2 * p + hh, :, :].rearrange("b v k -> v b k"), ['H0f'], ['H0f'])
                            pb, pk = pbank()
                            for j in range(4):
                                tr(pb[:, j * 128:(j + 1) * 128], S0g[:, j, :], identF, ['H0f', 'cF'], [pk])
                            acopy(f2(HSb[:]), pb[:, 0:512], [pk], ['A'])
                            bkk_diag = bass.AP(tensor=BKK.tensor, offset=BKK.offset + 32 * g,
                                               ap=[list(BKK.ap[0]), [136, 4], [1, 8]])
                            vcopy(bkk_diag, KR[:, p, g * 32:g * 32 + 32].rearrange("q (b t) -> q b t", t=8), ['KR', 'LW'], ['LW'])
                            brr_diag = bass.AP(tensor=BRR.tensor, offset=BRR.offset + 32 * g,
                                               ap=[list(BRR.ap[0]), [136, 4], [1, 8]])
                            vcopy(brr_diag, KR[:, p, 128 + g * 32:128 + g * 32 + 32].rearrange("q (b t) -> q b t", t=8), ['KR', 'X3'], ['X3'])
                            for j in range(4):
                                b_ = g * 4 + j
                                mm(ZB[:, 0:128], BKK[:, j, :], HSb[:, j, :], b_ == 0, False, ['LW', 'A'], [ZK])
                                mm(YB[:, 0:128], HSb[:, j, :], BRR[:, j, :], b_ == 0, False, ['A', 'X3'], [YK])
                            DVE(lambda e, bkk_diag=bkk_diag: e.memset(bkk_diag, 0.0), ['LW'], ['LW'])
                            DVE(lambda e, brr_diag=brr_diag: e.memset(brr_diag, 0.0), ['X3'], ['X3'])
                        for hh in range(2):
                            q = pp_ * 2 + hh
                            mm(ZB[:, hh * 64:hh * 64 + 64], AT[:, q, 128:256], Vtok[:, p, hh * 64:hh * 64 + 64], False, True, ['AT', 'Vtok'], [ZK])
                        act(Zn[:, 0, :], ZB[:, 0:128], AF.Copy, [ZK], ['Zn'], scale=-1.0)
                        pbu, pku = pbank()
                        for hh in range(2):
                            q = pp_ * 2 + hh
                            mm(pbu[:, hh * 64:hh * 64 + 64], PT[:, q, :], Zn[:, 0, hh * 64:hh * 64 + 64], True, True, ['PT', 'Zn'], [pku])
                        acopy(Ut[:, 0, :], pbu[:, 0:128], [pku], ['Ut'])
                        for hh in range(2):
                            q = pp_ * 2 + hh
                            hs = slice(hh * 64, hh * 64 + 64)
                            mm(YB[hs, 0:128], Ut[:, 0, hh * 64:hh * 64 + 64], AT[:, q, 0:128], False, False, ['Ut', 'AT'], [YK])
                            mm(YB[hs, 0:128], Vtok[:, p, hh * 64:hh * 64 + 64], AT[:, q, 256:384], False, True, ['Vtok', 'AT'], [YK])
                        acopy(X5[:, p, :], YB[:, 0:128], [YK], ['X5'])
                        smk = cb('seqm')
                        for g in range(4):
                            for hh in range(2):
                                hs = slice(hh * 64, hh * 64 + 64)
                                dma(S0g[hs, :, hh * 64:hh * 64 + 64],
                                    swkv[l, g * 4:g * 4 + 4, 2 * p + hh, :, :].rearrange("b v k -> v b k"), ['H0f'], ['H0f'])
                            sm4 = smk[:, g * 4:g * 4 + 4].unsqueeze(2).to_broadcast([128, 4, 128])
                            tt(BIGB[:], BWtok[:, p, :].unsqueeze(1).to_broadcast([128, 4, 128]), sm4, ALU.mult, ['BWtok', 'cB'], ['BWf'])
                            tt(BIGK[:], KWtok[:, p, :].unsqueeze(1).to_broadcast([128, 4, 128]), sm4, ALU.mult, ['KWtok', 'cB'], ['KWf'])
                            tt(DIAG[:], identF.unsqueeze(1).to_broadcast([128, 4, 128]),
                               wcs[:, p, g * 4:g * 4 + 4].unsqueeze(2).to_broadcast([128, 4, 128]), ALU.mult, ['cF', 'wcs'], ['X1'])
                            pb, pk = pbank()
                            mm(pb[:, 0:512], onesF, f2(DIAG[:]), True, True, ['X1', 'cF'], [pk])
                            acopy(f2(WcBC[:]), pb[:, 0:512], [pk], ['X2'])
                            tt(WcBC[:], WcBC[:], S0g[:], ALU.mult, ['X2', 'H0f'], ['X2'])
                            pbs, pks = pbank()
                            mm(pbs[:, 0:512], Ut[:, 0, :], f2(BIGB[:]), True, False, ['Ut', 'BWf'], [pks])
                            mm(pbs[:, 0:512], Vtok[:, p, :], f2(BIGK[:]), False, True, ['Vtok', 'KWf'], [pks])
                            tt(WcBC[:], WcBC[:], v4(pbs[:, 0:512]), ALU.add, ['X2', pks], ['X2'])
                            for hh in range(2):
                                hs = slice(hh * 64, hh * 64 + 64)
                                vcopy(SCo[hs, :, :], WcBC[hs, :, hh * 64:hh * 64 + 64], ['X2', 'X4'], ['X4'])
                            dma(nwkvs[l, g * 4:g * 4 + 4, 2 * p:2 * p + 2, :, :].rearrange("b hh v n -> (hh v) b n"), SCo[:], ['X4'], ['X4'])

            pb, pk = pbank()
            mm(pb[:, 0:512], bonesF, f2(X5[:]), True, True, ['X5', 'cF'], [pk])
            ts(X1[:], v4(pb[:, 0:512]), 1.0 / 64, None, ALU.mult, ALU.bypass, [pk], ['X1'])
            tt(X5[:], X5[:], X1[:], ALU.subtract, ['X5', 'X1'], ['X5'])
            tt(X2[:], X5[:], X5[:], ALU.mult, ['X5'], ['X2'])
            pb, pk = pbank()
            mm(pb[:, 0:512], bonesF, f2(X2[:]), True, True, ['X2', 'cF'], [pk])
            rsq(X1[:], v4(pb[:, 0:512]), 1.0 / 64, GN_EPS, [pk], ['X1'])
            tt(X5[:], X5[:], X1[:], ALU.mult, ['X5', 'X1'], ['X5'])
            tt(X5[:], X5[:], bc4(pcs(l, "ln_w", 4)), ALU.mult, ['X5', 'pcol'], ['X5'])
            tt(X5[:], X5[:], bc4(pcs(l, "ln_b", 4)), ALU.add, ['X5', 'pcol'], ['X5'])
            tt(X5[:], X5[:], X6[:], ALU.add, ['X5', 'X6'], ['X5'])
            tt(orp[:, 0:4, :], X5[:], gbf[:], ALU.mult, ['X5', 'gbf'], ['orp'])

            if not samp:
                if last_chunk:
                    pb, pk = pbank()
                    for g in range(4):
                        tr(pb[0:15, g * 128:(g + 1) * 128], ppb[:, g, 128:143], identF, ['ppb', 'cF'], [pk])
                    acopy(f2(X1[0:15, :, :]), pb[0:15, 0:512], [pk], ['X1'])
                    dma(npoolp[l], f2(X1[0:15, :, :]), ['X1'], ['X1'])
                pass
            else:
                spv = spool[l].rearrange("b r c -> (b r) c")
                for half in range(2):
                    r0, rn = (0, 128) if half == 0 else (128, 112)
                    stg = [X1, X2][half]
                    dma(f2(stg[0:rn, :, :]), spv[r0:r0 + rn, :], [], [['X1', 'X2'][half]])
                    pb, pk = pbank()
                    for g in range(4):
                        tr(pb[:, g * 128:g * 128 + rn], f2(stg[0:rn, :, :])[:, g * 128:(g + 1) * 128], identF[0:rn, 0:rn],
                           [['X1', 'X2'][half], 'cF'], [pk])
                    acopy(f2(X3[:]) if half == 0 else f2(X4[:]), pb[:, 0:512], [pk], [['X3', 'X4'][half]])
                for g in range(4):
                    vcopy(ppbS[:, g, 0:8, 0:15], X3[:, g, 0:120].rearrange("p (b r) -> p b r", r=15), ['X3'], ['ppbS', 'ppb'])
                    vcopy(ppbS[:, g, 8, 0:8], X3[:, g, 120:128], ['X3'], ['ppbS', 'ppb'])
                    vcopy(ppbS[:, g, 8, 8:15], X4[:, g, 0:7], ['X4'], ['ppbS', 'ppb'])
                    vcopy(ppbS[:, g, 9:16, 0:15], X4[:, g, 7:112].rearrange("p (b r) -> p b r", r=15), ['X4'], ['ppbS', 'ppb'])
                for half in range(2):
                    r0, rn = (0, 128) if half == 0 else (128, 112)
                    stg = [X3, X4][half]
                    for g in range(4):
                        if half == 0:
                            vcopy(stg[:, g, 0:120].rearrange("p (b r) -> p b r", r=15), ppbS[:, g, 0:8, 8:23], ['ppbS'], [['X3', 'X4'][half]])
                            vcopy(stg[:, g, 120:128], ppbS[:, g, 8, 8:16], ['ppbS'], [['X3', 'X4'][half]])
                        else:
                            vcopy(stg[:, g, 0:7], ppbS[:, g, 8, 16:23], ['ppbS'], [['X3', 'X4'][half]])
                            vcopy(stg[:, g, 7:112].rearrange("p (b r) -> p b r", r=15), ppbS[:, g, 9:16, 8:23], ['ppbS'], [['X3', 'X4'][half]])
                    pb, pk = pbank()
                    for g in range(4):
                        tr(pb[0:rn, g * 128:(g + 1) * 128], stg[:, g, 0:rn], identF, [['X3', 'X4'][half], 'cF'], [pk])
                    ob = [X1, X2][half]
                    acopy(f2(ob[0:rn, :, :]), pb[0:rn, 0:512], [pk], [['X1', 'X2'][half]])
                    dma(npools[l, r0:r0 + rn, :], f2(ob[0:rn, :, :]), [['X1', 'X2'][half]], [['X1', 'X2'][half]])
                for g in range(4):
                    A0 = ppbS[:, g, :, :]
                    cur = X3[:].rearrange("p a b -> p (a b)")[:, 0:368].rearrange("p (b r) -> p b r", r=23)
                    oth = X4[:].rearrange("p a b -> p (a b)")[:, 0:368].rearrange("p (b r) -> p b r", r=23)
                    tt(cur[:, :, 1:23], A0[:, :, 1:23], A0[:, :, 0:22], ALU.add, ['ppbS', 'X3', 'X4'], ['X3', 'X4'])
                    span = 2
                    while span < WINS[g]:
                        lo = 2 * span - 1
                        tt(oth[:, :, lo:23], cur[:, :, lo:23], cur[:, :, lo - span:23 - span], ALU.add, ['X3', 'X4'], ['X3', 'X4'])
                        cur, oth = oth, cur
                        span *= 2
                    mg = WK['mg']
                    stt(mg[:].rearrange("p (b t) -> p b t", t=8), cur[:, :, 15:23], 1.0 / WINS[g], ppbS[:, g, :, 15:23],
                        ALU.mult, ALU.subtract, ['X3', 'X4', 'ppbS'], ['mg'])
                    acopy(pz[:, g, :], mg[:], ['mg'], ['vbf'])
            if samp:
                for g in range(4):
                    pb, pk = pbank()
                    mm(pb[:, 0:128], poolw[:, g, :], pz[:, g, :], True, True, ['lws', 'vbf'], [pk])
                    act(orp[:, 4 + g, :], pb[:, 0:128], AF.Identity, [pk, 'pcol'], ['orp'], scale=pc(l, "pool_scale", g))
            dma(orpd[:, :, t0:t0 + 128], orp[:], ['orp'], [('orpd', ci)])

        S.barrier()
        ptr[0] = UBASE
        T2 = 512
        WK['sq'] = alloc([2, T2], BF16); WK['rstd'] = alloc([T2]); WK['tmpn'] = alloc([2, T2]); WK['mg'] = alloc([128])
        hT2 = alloc([NKC, T2], BF16)
        mrg = alloc([NKC, NTOK], BF16)
        orp2 = alloc([1, 8, T2], BF16)
        g16 = alloc([16, T2], BF16)
        tmpb = WK['tmpn'][:, 0:1, :]
        W2 = [wl_kc(W["w_in"][l, :, 2304 + j * 512:2304 + (j + 1) * 512], 512) for j in range(4)]
        wbr, wbrk = wl_kc(W["w_br_rwkv"][l], 1024)
        wbp, wbpk = wl_kc(W["w_br_pool"][l], 1024)
        tl2 = tiles(T2)
        norm_mod_t(tl2[0][0], tl2[0][1], tl2[0][2], hT2, ['hT2'])
        for ti, (t0, n, samp) in enumerate(tl2):
            ob = orp2[:, 0, :, 0:n]
            ok = ('orp2', 0)
            dma(ob, orpd[:, :, t0:t0 + n], [('orpd', c) for c in range(t0 // 128, (t0 + n) // 128)], [ok])
            for cg in range(16):
                wt, wkey = W2[cg // 4]
                q = cg % 4
                pb, pk = pbank()
                for kc in range(NKC):
                    mm(pb[:, 0:n], wt[:, kc, q * 128:(q + 1) * 128], hT2[:, kc, 0:n], kc == 0, kc == NKC - 1, [wkey, 'hT2'], [pk])
                act(g16[:, cg, 0:n], pb[:, 0:n], AF.Sigmoid, [pk], ['g16'])
            if ti + 1 < len(tl2):
                norm_mod_t(tl2[ti + 1][0], tl2[ti + 1][1], tl2[ti + 1][2], hT2, ['hT2'])
            for c in range(8):
                pb, pk = pbank()
                for kc in range(4):
                    mm(pb[:, 0:n], wbr[:, kc, c * 128:(c + 1) * 128], ob[:, kc, :], kc == 0, kc == 3, [wbrk, ok], [pk])
                tt(tmpb[:, 0, 0:n], pb[:, 0:n], g16[:, c, 0:n], ALU.mult, [pk, 'g16'], [('tmpn', 0)])
                pb2, pk2 = pbank()
                for kc in range(4):
                    mm(pb2[:, 0:n], wbp[:, kc, c * 128:(c + 1) * 128], ob[:, 4 + kc, :], kc == 0, kc == 3, [wbpk, ok], [pk2])
                tt(mrg[:, c, t0:t0 + n], pb2[:, 0:n], g16[:, 8 + c, 0:n], ALU.mult, [pk2, 'g16'], [('mrg', t0)])
                tt(mrg[:, c, t0:t0 + n], mrg[:, c, t0:t0 + n], tmpb[:, 0, 0:n], ALU.add, [('tmpn', 0), ('mrg', t0)], [('mrg', t0)])
        wo = [wl_kc(W["w_out"][l, :, j * 512:(j + 1) * 512], 512) for j in range(2)]
        for (t0, n, samp) in tiles(T2):
            for c in range(8):
                wt, wkey = wo[c // 4]
                q = c % 4
                pb, pk = pbank()
                for kc in range(NKC):
                    mm(pb[:, 0:n], wt[:, kc, q * 128:(q + 1) * 128], mrg[:, kc, t0:t0 + n], kc == 0, kc == NKC - 1, [wkey, ('mrg', t0)], [pk])
                resid_update_t(t0, n, samp, c, pb, pk)

        S.barrier()
        ptr[0] = UBASE
        T3 = 512
        WK['sq'] = alloc([2, T3], BF16); WK['rstd'] = alloc([T3]); WK['tmpn'] = alloc([2, T3]); WK['mg'] = alloc([128])
        hTm = alloc([NKC, NTOK], BF16)
        rl = alloc([2, T3]); r2 = alloc([8, T3], BF16)
        ada(l, "mlp")
        tl3 = tiles(T3)
        norm_mod_t(tl3[0][0], tl3[0][1], tl3[0][2], hTm[:, :, tl3[0][0]:tl3[0][0] + tl3[0][1]], [('hTm', tl3[0][0])])
        for qd in range(4):
            w1 = [wl_kc(W["w_ff1"][l, :, qd * 1024 + j * 512:qd * 1024 + (j + 1) * 512], 512) for j in range(2)]
            w2 = [wl_kc(W["w_ff2"][l, qd * 1024:(qd + 1) * 1024, j * 512:(j + 1) * 512], 512) for j in range(2)]
            for ti3, (t0, n, samp) in enumerate(tl3):
                if qd == 0 and ti3 + 1 < len(tl3):
                    nt0, nn, nsamp = tl3[ti3 + 1]
                    norm_mod_t(nt0, nn, nsamp, hTm[:, :, nt0:nt0 + nn], [('hTm', nt0)])
                for c in range(8):
                    wt, wkey = w1[c // 4]
                    q = c % 4
                    pb, pk = pbank()
                    for kc in range(NKC):
                        mm(pb[:, 0:n], wt[:, kc, q * 128:(q + 1) * 128], hTm[:, kc, t0:t0 + n], kc == 0, kc == NKC - 1, [wkey, ('hTm', t0)], [pk])
                    act(rl[:, c % 2, 0:n], pb[:, 0:n], AF.Relu, [pk], [('rl', c % 2)])
                    tt(r2[:, c, 0:n], rl[:, c % 2, 0:n], rl[:, c % 2, 0:n], ALU.mult, [('rl', c % 2)], [('r2', c)])
                for c in range(8):
                    wt, wkey = w2[c // 4]
                    q = c % 4
                    pb, pk = pbank()
                    for kc in range(NKC):
                        mm(pb[:, 0:n], wt[:, kc, q * 128:(q + 1) * 128], r2[:, kc, 0:n], kc == 0, kc == NKC - 1, [wkey, ('r2', kc)], [pk])
                    resid_update_t(t0, n, samp, c, pb, pk)

    S.barrier()
    ptr[0] = UBASE
    sqr = alloc([2, 128], BF16); rstd = alloc([128]); tmpo = alloc([NKC, 128]); yout = alloc([2, D])
    ts(G32f[:], pcol[:, 0, 120:128].unsqueeze(2), 32.0, None, ALU.mult, ALU.bypass, ['pcol'], ['G32f'])
    for ci in range(NCH):
        t0 = ci * 128
        pb, pk = pbank()
        for c in range(NKC):
            act(sqr[:, c % 2, :], xT[:, c, t0:t0 + 128], AF.Square, xk(ci), [('sq', c % 2)])
            mm(pb[:, 0:128], onesB, sqr[:, c % 2, :], c == 0, c == NKC - 1, [('sq', c % 2), 'cB'], [pk])
        rsq(rstd[:], pb[:, 0:128], 1.0, D * EPS, [pk], ['rstd'])
        for c in range(NKC):
            stt(tmpo[:, c, :], xT[:, c, t0:t0 + 128], G32f[:, c, 0:1], rstd[:], ALU.mult, ALU.mult, xk(ci) + ['G32f', 'rstd'], ['tmpo'])
        yo = yout[:, ci % 2, :]
        for half in range(2):
            pb, pk = pbank()
            for q in range(4):
                c = half * 4 + q
                tr(pb[:, q * 128:(q + 1) * 128], tmpo[:, c, :], identF, ['tmpo', 'cF'], [pk])
            acopy(yo[:, half * 512:(half + 1) * 512], pb[:, 0:512], [pk], [('yout', ci % 2)])
        dst = yp[t0:t0 + 128, :] if ci < NPC else ys[:, :]
        dma(dst, yo, [('yout', ci % 2)], [('yout', ci % 2)])
    return nc, S, st


CSTF = {}
CSTB = {}
CF_COLS = 0
CB_COLS = 0


def _layout_consts():
    global CF_COLS, CB_COLS
    o = 0
    for nm, n in [('ident', 128), ('ones', 128), ('bones', 128), ('rmS', 128), ('pcorr', 64)]:
        CSTF[nm] = (o, n)
        o += n
    CF_COLS = o
    o = 0
    for nm, n in [('ident', 128), ('ones', 128), ('mAP', 512), ('mAS', 512), ('mLP', 128), ('mLS', 128), ('seqm', 16)]:
        CSTB[nm] = (o, n)
        o += n
    CB_COLS = o


_layout_consts()


def make_consts():
    c = np.zeros((128, CF_COLS + CB_COLS), np.float32)

    def putf(nm, a):
        o, n = CSTF[nm]
        c[:, o:o + n] = a

    def putb(nm, a):
        o, n = CSTB[nm]
        c[:, CF_COLS + o:CF_COLS + o + n] = a
    i = np.arange(128)
    putf('ident', np.eye(128)); putb('ident', np.eye(128))
    putf('ones', np.ones((128, 128))); putb('ones', np.ones((128, 128)))
    putf('bones', (i[:, None] // 64 == i[None, :] // 64).astype(np.float32))
    s, t = i[:, None], i[None, :]
    for tag, same in (('P', np.ones((128, 128), bool)), ('S', (s // 8) == (t // 8))):
        lt = ((s < t) & same).astype(np.float32)
        le = ((s <= t) & same).astype(np.float32)
        gtm = ((s > t) & same).astype(np.float32)
        putb('mA' + tag, np.concatenate([-lt, le, lt, le], axis=1))
        putb('mL' + tag, -gtm)
    rmS = np.ones((128, 128), np.float32)
    rmS[:, ::8] = 0
    putf('rmS', rmS)
    putb('seqm', (i[:, None] // 8 == np.arange(16)[None, :]).astype(np.float32))
    pc_ = np.zeros((128, 64), np.float32)
    for g, w in enumerate(WINS):
        tt_ = np.arange(16)
        pc_[:, g * 16:(g + 1) * 16] = (1.0 / np.minimum(tt_ + 1, w))[None, :]
    putf('pcorr', pc_)
    return c


def emit(nc, S, st):
    sems = {name: st.enter_context(nc.semaphore(name)) for name in S.cnt}
    block = st.enter_context(nc.Block())

    def run(stream, eng):
        for waits, fn, sem, inc in S.ops[stream]:
            for (s, v) in waits:
                eng.wait_ge(sems[s], v)
            if fn is not None:
                fn(eng).then_inc(sems[sem], inc)

    @block.sync
    def _(e):
        run('sp', e)
        for nm in S.cnt:
            if nm.startswith('sp'):
                e.wait_ge(sems[nm], S.cnt[nm])

    @block.gpsimd
    def _(e):
        run('pool', e)

    @block.tensor
    def _(e):
        run('pe', e)

    @block.vector
    def _(e):
        run('dve', e)

    @block.scalar
    def _(e):
        run('act', e)
    st.close()
    return nc


_WNAMES = ["w_ada_mix", "b_ada_mix", "norm_mix", "w_in", "mu_shift", "w0", "w2", "a0", "a2", "g2", "v0", "v1", "v2",
           "k_k", "k_a", "r_k", "ln_w", "ln_b", "pool_w", "pool_scale", "w_br_rwkv", "w_br_pool", "w_out",
           "w_ada_mlp", "b_ada_mlp", "norm_mlp", "w_ff1", "w_ff2", "norm_final"]


def make_in_maps(inputs, ncores, L):
    consts = make_consts()
    f = lambda a: np.ascontiguousarray(np.asarray(a, dtype=np.float32))
    shared = {}
    for nm in _WNAMES:
        a = f(inputs[nm])
        if nm == "r_k":
            a = a.reshape(L, MIX)
        if nm == "norm_final":
            a = a.reshape(1, D)
        shared[nm] = a
    shared["cst"] = consts
    maps = []
    for i in range(ncores):
        m = dict(shared)
        m["xp"] = f(inputs["x_prompt"][i])
        m["xs"] = f(inputs["x_sample"][16 * i:16 * i + 16]).reshape(128, D)
        m["cc"] = f(np.concatenate([np.asarray(inputs["c_prompt"])[i:i + 1], np.asarray(inputs["c_sample"])[16 * i:16 * i + 16]], axis=0))
        m["sshift"] = f(np.asarray(inputs["state_shift"])[:, 16 * i:16 * i + 16])
        m["spool"] = f(np.asarray(inputs["state_pool"])[:, 16 * i:16 * i + 16])
        m["swkv"] = f(np.asarray(inputs["state_wkv"])[:, 16 * i:16 * i + 16])
        maps.append(m)
    return maps


def gather(R, ncores, L):
    y_p = np.stack([R[i]["yp"] for i in range(ncores)], 0)
    y_s = np.concatenate([R[i]["ys"].reshape(16, 8, D) for i in range(ncores)], 0)
    sh_p = np.stack([R[i]["nshp"] for i in range(ncores)], 1)
    pool_p = np.stack([R[i]["npoolp"] for i in range(ncores)], 1)
    wkv_p = np.stack([R[i]["nwkvp"] for i in range(ncores)], 1)
    sh_s = np.concatenate([R[i]["nshs"] for i in range(ncores)], 1)
    pool_s = np.concatenate([R[i]["npools"].reshape(L, 16, 15, MIX) for i in range(ncores)], 1)
    wkv_s = np.concatenate([R[i]["nwkvs"] for i in range(ncores)], 1)
    return tuple(np.ascontiguousarray(a, dtype=np.float32) for a in (y_p, y_s, sh_p, pool_p, wkv_p, sh_s, pool_s, wkv_s))


def kernel(**inputs):
    ncores = 8
    L = 4
    nc, S, st = build(TP=2048, L=L)
    emit(nc, S, st)
    maps = make_in_maps(inputs, ncores, L)
    res = run_bass_kernel_spmd(nc, maps, core_ids=list(range(ncores)))
    return gather(res.results, ncores, L)
```

```python
import numpy as np
from contextlib import ExitStack
import concourse.bass as bass
import concourse.mybir as mybir
from concourse.bass_utils import run_bass_kernel_spmd

F32 = mybir.dt.float32
BF16 = mybir.dt.bfloat16
AF = mybir.ActivationFunctionType
ALU = mybir.AluOpType

D = 1024
NKC = 8
MIX = 512
RW = 1792
INC = 4352
DFF = 4096
EPS = 1e-6
GN_EPS = 64e-5
DEC_C = -float(np.exp(-0.5))
WINS = (2, 4, 8, 16)


class Sched:
    STREAMS = ('pe', 'act', 'dve', 'pool', 'sp')

    def __init__(self):
        self.ops = {s: [] for s in self.STREAMS}
        self.cnt = {}
        self.known = {s: {} for s in self.STREAMS}
        self.lastw = {}
        self.readers = {}
        self.dma_i = {}

    def op(self, stream, fn, r=(), w=(), sem=None, inc=1, nsem=1):
        sem = sem or stream
        if nsem > 1:
            i = self.dma_i.get(sem, 0)
            self.dma_i[sem] = i + 1
            sem = "%s%d" % (sem, i % nsem)
        need = {}
        if nsem > 1 and self.cnt.get(sem, 0):
            need[sem] = self.cnt[sem]

        def add(s, v):
            if need.get(s, 0) < v:
                need[s] = v
        for b in r:
            if b in self.lastw:
                add(*self.lastw[b])
        for b in w:
            if b in self.lastw:
                add(*self.lastw[b])
            for s, v in self.readers.get(b, {}).items():
                add(s, v)
        waits = []
        kn = self.known[stream]
        for s, v in need.items():
            if stream == 'pe' and s == 'pe':
                continue
            if kn.get(s, 0) < v:
                waits.append((s, v))
                kn[s] = v
        self.cnt[sem] = self.cnt.get(sem, 0) + inc
        val = self.cnt[sem]
        self.ops[stream].append((waits, fn, sem, inc))
        for b in r:
            d = self.readers.setdefault(b, {})
            if d.get(sem, 0) < val:
                d[sem] = val
        for b in w:
            self.lastw[b] = (sem, val)
            self.readers[b] = {}

    def barrier(self, streams=('pe', 'act', 'dve', 'sp')):
        for s in streams:
            waits = []
            for sem in list(self.cnt):
                if sem.startswith('pq'):
                    continue
                v = self.cnt.get(sem, 0)
                if s == 'pe' and sem == 'pe':
                    continue
                if v and self.known[s].get(sem, 0) < v:
                    waits.append((sem, v))
                    self.known[s][sem] = v
            if waits:
                self.ops[s].append((waits, None, None, 0))


def build(TP=2048, L=4):
    NTOK = TP + 128
    NCH = NTOK // 128
    NPC = TP // 128
    nc = bass.Bass("TRN2", target_bir_lowering=False)
    S = Sched()
    st = ExitStack()

    def din(name, shape, dt=F32):
        return nc.dram_tensor(name, list(shape), dt, kind="ExternalInput").ap()

    def dout(name, shape, dt=F32):
        return nc.dram_tensor(name, list(shape), dt, kind="ExternalOutput").ap()

    xp = din("xp", [TP, D]); xs = din("xs", [128, D]); cc = din("cc", [17, D])
    sshift = din("sshift", [L, 16, RW]); spool = din("spool", [L, 16, 15, MIX])
    swkv = din("swkv", [L, 16, 8, 64, 64])
    W = {}
    LV = max(L - 1, 1)
    for nm, shp in [("w_ada_mix", [L, D, 3 * D]), ("b_ada_mix", [L, 3 * D]), ("norm_mix", [L, D]),
                    ("w_in", [L, D, INC]), ("mu_shift", [L, RW]), ("w0", [L, MIX]), ("w2", [L, 64, MIX]),
                    ("a0", [L, MIX]), ("a2", [L, 64, MIX]), ("g2", [L, 128, MIX]), ("v0", [LV, MIX]),
                    ("v1", [LV, MIX, 32]), ("v2", [LV, 32, MIX]), ("k_k", [L, MIX]),
                    ("k_a", [L, MIX]), ("r_k", [L, MIX]), ("ln_w", [L, MIX]), ("ln_b", [L, MIX]),
                    ("pool_w", [L, 4, 128, 128]), ("pool_scale", [L, MIX]), ("w_br_rwkv", [L, MIX, D]),
                    ("w_br_pool", [L, MIX, D]), ("w_out", [L, D, D]), ("w_ada_mlp", [L, D, 3 * D]),
                    ("b_ada_mlp", [L, 3 * D]), ("norm_mlp", [L, D]), ("w_ff1", [L, D, DFF]),
                    ("w_ff2", [L, DFF, D]), ("norm_final", [1, D])]:
        W[nm] = din(nm, shp)
    cst = din("cst", [128, CF_COLS + CB_COLS])
    yp = dout("yp", [TP, D]); ys = dout("ys", [128, D])
    nshp = dout("nshp", [L, RW]); npoolp = dout("npoolp", [L, 15, MIX]); nwkvp = dout("nwkvp", [L, 8, 64, 64])
    nshs = dout("nshs", [L, 16, RW]); npools = dout("npools", [L, 16 * 15, MIX])
    nwkvs = dout("nwkvs", [L, 16, 8, 64, 64])
    vfd = nc.dram_tensor("vfirst_scr", [128, 4, NTOK], BF16, kind="Internal").ap()
    orpd = nc.dram_tensor("orp_scr", [128, 8, NTOK], BF16, kind="Internal").ap()

    NW = 53200
    big = st.enter_context(nc.sbuf_tensor("big", [128, NW], F32))
    ptr = [0]

    def alloc(shape, dt=F32):
        n = int(np.prod(shape))
        words = n if dt == F32 else (n + 1) // 2
        words = (words + 7) // 8 * 8
        o = ptr[0]
        ptr[0] += words
        assert ptr[0] <= NW, ("SBUF arena overflow", ptr[0], NW)
        v = big[:, o:o + words]
        if dt != F32:
            v = v.bitcast(dt)
        v = v[:, 0:n]
        if len(shape) == 1:
            return v
        names = " ".join("d%d" % i for i in range(len(shape)))
        kw = {"d%d" % i: int(shape[i]) for i in range(len(shape) - 1)}
        return v.rearrange("p (%s) -> p %s" % (names, names), **kw)

    SD = F32
    NPI = 2
    HP = 2 * NPI
    xT = alloc([NKC, NTOK])
    ring = alloc([6, 4096], BF16)
    cF = alloc([CF_COLS]); cB = alloc([CB_COLS], BF16)
    pcol = alloc([L, 128])
    omm = alloc([14]); omka = alloc([4])
    siluT = alloc([NKC, 17], BF16)
    modv = alloc([24, 17]); G32 = alloc([NKC, 17]); G32f = alloc([NKC, 1])
    lw_small = alloc([2176], BF16)
    H0f = alloc([4, 128]); H0b = H0f
    prcarry = alloc([14, 1])
    UBASE = ptr[0]

    PB = [st.enter_context(nc.psum_tensor("pb%d" % i, [128, 512], F32)) for i in range(8)]
    NROT = 6
    pbi = [0]
    pbt_i = [0]

    def pbank():
        i = pbi[0] % NROT
        pbi[0] += 1
        return PB[i], ('pb', i)
    ZB, ZK = PB[6], ('pb', 6)
    YB, YK = PB[7], ('pb', 7)

    def PE(fn, r, w): S.op('pe', fn, r, w)
    def ACT(fn, r, w): S.op('act', fn, r, w)
    def DVE(fn, r, w): S.op('dve', fn, r, w)
    def SPD(fn, r, w): S.op('sp', fn, r, w, sem='sp', inc=16, nsem=16)
    def PQD(fn, r, w): S.op('pool', fn, r, w, sem='pq', inc=16, nsem=8)

    def mm(out, lhsT, rhs, start, stop, r, w):
        PE(lambda e: e.matmul(out, lhsT, rhs, start=start, stop=stop, skip_group_check=True), r, w)

    def tr(out, in_, ident, r, w):
        PE(lambda e: e.transpose(out, in_, ident), r, w)

    def act(out, in_, func, r, w, bias=0.0, scale=1.0):
        ACT(lambda e: e.activation(out, in_, func, bias=bias, scale=scale), r, w)

    def acopy(out, in_, r, w):
        ACT(lambda e: e.copy(out, in_), r, w)

    def vcopy(out, in_, r, w):
        DVE(lambda e: e.tensor_copy(out, in_), r, w)

    def tt(out, a, b, op, r, w):
        DVE(lambda e: e.tensor_tensor(out, a, b, op), r, w)

    def ts(out, a, s1, s2, op0, op1, r, w):
        DVE(lambda e: e.tensor_scalar(out, a, s1, s2, op0, op1), r, w)

    def stt(out, a, s, b, op0, op1, r, w):
        DVE(lambda e: e.scalar_tensor_tensor(out, a, s, b, op0, op1), r, w)

    def dma(out, in_, r, w):
        SPD(lambda e: e.dma_start(out=out, in_=in_), r, w)

    def rsq(out, in_, mulc, addc, r, w):
        ts(out, in_, mulc, addc, ALU.mult, ALU.add, r, w)
        act(out, out, AF.Ln, w, w)
        act(out, out, AF.Exp, w, w, scale=-0.5)

    def xk(ci): return [('x', ci)]
    f2 = lambda t: t.rearrange("p a b -> p (a b)")
    v4 = lambda t: t.rearrange("p (a b) -> p a b", a=4)

    def cf(name, lo=0, hi=None):
        o, n = CSTF[name]
        return cF[:, o + lo:o + (n if hi is None else hi)]

    def cb(name, lo=0, hi=None):
        o, n = CSTB[name]
        return cB[:, o + lo:o + (n if hi is None else hi)]
    SD = F32
    identF = cf('ident'); identB = cb('ident'); identS = identF if SD == F32 else identB; onesB = cb('ones'); bonesF = cf('bones'); onesF = cf('ones')

    ptr[0] = UBASE
    pstage = alloc([L, 128]); cst17 = alloc([D]); xin = alloc([2, D])
    dma(cF[:], cst[:, 0:CF_COLS], [], ['cF'])
    PQD(lambda e: e.dma_start(out=cB[:], in_=cst[:, CF_COLS:CF_COLS + CB_COLS]), [], ['cB'])
    DVE(lambda e: e.memset(pstage[:], 0.0), [], ['pstage'])
    PROW = {}
    ro = 0
    for nm, nchk in [("norm_mix", 8), ("norm_mlp", 8), ("mu_shift", 14), ("w0", 4), ("a0", 4), ("v0", 4),
                     ("k_k", 4), ("k_a", 4), ("r_k", 4), ("ln_w", 4), ("ln_b", 4), ("pool_scale", 4),
                     ("b_ada_mix", 24), ("b_ada_mlp", 24)]:
        PROW[nm] = ro
        src = W[nm]
        if nm == "v0":
            if L > 1:
                dma(pstage[ro:ro + nchk, 1:L, :], src[0:L - 1, :].rearrange("l (c p) -> c l p", p=128), ['pstage'], ['pstage'])
        else:
            dma(pstage[ro:ro + nchk, 0:L, :], src.rearrange("l (c p) -> c l p", p=128), ['pstage'], ['pstage'])
        ro += nchk
    assert ro <= 120
    dma(pstage[120:128, 0, :], W["norm_final"].rearrange("o (c p) -> (o c) p", p=128), ['pstage'], ['pstage'])
    for l in range(L):
        pb, pk = pbank()
        tr(pb[:, 0:128], pstage[:, l, :], identF, ['pstage', 'cF'], [pk])
        acopy(pcol[:, l, :], pb[:, 0:128], [pk], ['pcol'])

    def pc(l, nm, c): return pcol[:, l, PROW[nm] + c:PROW[nm] + c + 1]
    def pcs(l, nm, n): return pcol[:, l, PROW[nm]:PROW[nm] + n]

    dma(cst17[0:17, :], cc[:, :], [], ['cst17'])
    pb, pk = pbank()
    for kc in range(NKC):
        tr(pb[:, kc * 17:(kc + 1) * 17], cst17[0:17, kc * 128:(kc + 1) * 128], identF[0:17, 0:17], ['cst17', 'cF'], [pk])
    act(f2(siluT[:]), pb[:, 0:NKC * 17], AF.Silu, [pk], ['siluT'])

    for ci in range(NCH):
        src = xp[ci * 128:(ci + 1) * 128, :] if ci < NPC else xs[:, :]
        xb_ = xin[:, ci % 2, :]
        dma(xb_, src, [], [('xin', ci % 2)])
        for half in range(2):
            pb, pk = pbank()
            for q in range(4):
                c = half * 4 + q
                tr(pb[:, q * 128:(q + 1) * 128], xb_[:, c * 128:(c + 1) * 128], identF, [('xin', ci % 2), 'cF'], [pk])
            acopy(xT[:, half * 4:half * 4 + 4, ci * 128:(ci + 1) * 128], v4(pb[:, 0:512]), [pk], xk(ci))

    ring_i = [0]

    def wload(src3, a, b):
        i = ring_i[0] % 6
        ring_i[0] += 1
        dst = ring[:, i, 0:a * b].rearrange("p (a b) -> p a b", a=a)
        PQD(lambda e: e.dma_start(out=dst, in_=src3), [], [('ring', i)])
        return dst, ('ring', i)

    def wl_kc(src2, ncol):
        return wload(src2.rearrange("(kc p) n -> p kc n", p=128), src2.shape[0] // 128, ncol)

    def ada(l, which):
        wsrc = W["w_ada_" + which]
        pbm, pkm = pbank()
        for j in range(6):
            wt, wkey = wl_kc(wsrc[l, :, j * 512:(j + 1) * 512], 512)
            for q in range(4):
                ch = j * 4 + q
                for kc in range(NKC):
                    mm(pbm[:, ch * 17:(ch + 1) * 17], wt[:, kc, q * 128:(q + 1) * 128], siluT[:, kc, :],
                       kc == 0, kc == NKC - 1, [wkey, 'siluT'], [pkm])
        tt(modv[:], pbm[:, 0:408].rearrange("p (a b) -> p a b", b=17),
           pcs(l, "b_ada_" + which, 24).unsqueeze(2).to_broadcast([128, 24, 17]), ALU.add, [pkm, 'pcol'], ['mod'])
        ts(G32[:], modv[:, 8:16, :], 1.0, 32.0, ALU.add, ALU.mult, ['mod'], ['mod'])
        tt(G32[:], G32[:], pcs(l, "norm_" + which, 8).unsqueeze(2).to_broadcast([128, 8, 17]), ALU.mult, ['mod', 'pcol'], ['mod'])

    WK = {}

    def xks(t0, n): return [('x', c) for c in range(t0 // 128, (t0 + n) // 128)]

    def tiles(size):
        out = []
        t = 0
        while t < TP:
            n = min(size, TP - t)
            out.append((t, n, False))
            t += n
        out.append((TP, 128, True))
        return out

    def norm_mod_t(t0, n, samp, hdst, hkeys):
        sqr, rstd, tmpn = WK['sq'], WK['rstd'], WK['tmpn']
        pb, pk = pbank()
        for c in range(NKC):
            act(sqr[:, c % 2, 0:n], xT[:, c, t0:t0 + n], AF.Square, xks(t0, n), [('sq', c % 2)])
            mm(pb[:, 0:n], onesB, sqr[:, c % 2, 0:n], c == 0, c == NKC - 1, [('sq', c % 2), 'cB'], [pk])
        rsq(rstd[:, 0:n], pb[:, 0:n], 1.0, D * EPS, [pk], ['rstd'])
        for c in range(NKC):
            tb = tmpn[:, c % 2, 0:n]
            tk = ('tmpn', c % 2)
            if not samp:
                stt(tb, xT[:, c, t0:t0 + n], G32[:, c, 0:1], rstd[:, 0:n], ALU.mult, ALU.mult, xks(t0, n) + ['mod', 'rstd'], [tk])
                act(hdst[:, c, 0:n], tb, AF.Identity, [tk, 'mod'], hkeys, bias=modv[:, c, 0:1])
            else:
                tt(tb, xT[:, c, t0:t0 + n], rstd[:, 0:n], ALU.mult, xks(t0, n) + ['rstd'], [tk])
                t3 = tb.rearrange("p (b t) -> p b t", t=8)
                tt(t3, t3, G32[:, c, 1:17].unsqueeze(2).to_broadcast([128, 16, 8]), ALU.mult, [tk, 'mod'], [tk])
                tt(hdst[:, c, 0:n].rearrange("p (b t) -> p b t", t=8), t3,
                   modv[:, c, 1:17].unsqueeze(2).to_broadcast([128, 16, 8]), ALU.add, [tk, 'mod'], hkeys)

    def norm_mod(ci, samp, hdst, hkeys):
        norm_mod_t(ci * 128, 128, samp, hdst, hkeys)

    def resid_update_t(t0, n, samp, c, pb, pk):
        xv = xT[:, c, t0:t0 + n]
        mg = WK['mg']
        if not samp:
            stt(xv, pb[:, 0:n], modv[:, 16 + c, 0:1], xv, ALU.mult, ALU.add, [pk, 'mod'] + xks(t0, n), xks(t0, n))
        else:
            tt(mg[:, 0:128].rearrange("p (b t) -> p b t", t=8), pb[:, 0:128].rearrange("p (b t) -> p b t", t=8),
               modv[:, 16 + c, 1:17].unsqueeze(2).to_broadcast([128, 16, 8]), ALU.mult, [pk, 'mod'], ['mg'])
            tt(xv, xv, mg[:, 0:128], ALU.add, ['mg'] + xks(t0, n), xks(t0, n))

    for l in range(L):
        S.barrier()
        ptr[0] = UBASE
        WK['sq'] = alloc([2, 128], BF16); WK['rstd'] = alloc([128]); WK['tmpn'] = alloc([2, 128]); WK['mg'] = alloc([128])
        hTc = alloc([NKC, 129], BF16); hT = hTc[:, :, 1:129]
        rkv = alloc([12, 128])
        twa = alloc([128], BF16); sgg = alloc([128], BF16); t1b = alloc([128], BF16)
        gbf = alloc([4, 128], BF16); vbf = alloc([4, 128], BF16); vfb = alloc([4, 128], BF16)
        FMB = [alloc([4, 128]) for _ in range(8)]
        KR = alloc([4, 256], SD)
        Bt = alloc([4, 128], SD); Kt = alloc([4, 128], SD); BWf = alloc([4, 128], SD); KWf = alloc([4, 128], SD)
        Vtok = alloc([4, 128], SD); BWtok = alloc([4, 128], SD); KWtok = alloc([4, 128], SD)
        AT = alloc([HP, 384], SD); MA = alloc([HP, 256], SD); MB = alloc([HP, 256], SD)
        PT = alloc([HP, 128], SD)
        Zn = alloc([NPI, 128], SD); Ut = alloc([NPI, 128], SD)
        wcs = alloc([4, 16])
        pz = vbf; orp = alloc([8, 128], BF16)
        shiftT = alloc([14, 16]); ppbS = alloc([4, 16, 23])
        ppb = ppbS.rearrange("p a b c -> p (a b c)")[:, 0:576].rearrange("p (a b) -> p a b", a=4)
        A_, LW_, X1, X2, X3, X4, X5, X6 = FMB
        DIAG = X1; WcBC = X2; SCo = X4[:, :, 0:64]
        S0g = H0f; HSb = A_; BKK = LW_; BRR = X3; BIGB = BWf; BIGK = KWf
        tsh = f2(X1[:])[:, 0:272].rearrange("p (a b) -> p a b", a=2)
        xl12 = X2[:, 0:2, :]
        pq2 = f2(X1[:])[:, 0:144]; pq4 = f2(X2[:])[:, 0:144]

        ts(omm[:], pcs(l, "mu_shift", 14), -1.0, 1.0, ALU.mult, ALU.add, ['pcol'], ['omm'])
        ts(omka[:], pcs(l, "k_a", 4), -1.0, 1.0, ALU.mult, ALU.add, ['pcol'], ['omm'])
        PQD(lambda e, l=l: e.dma_start(out=lw_small[0:64, 0:512], in_=W["w2"][l]), [], ['lws'])
        PQD(lambda e, l=l: e.dma_start(out=lw_small[64:128, 0:512], in_=W["a2"][l]), [], ['lws'])
        PQD(lambda e, l=l: e.dma_start(out=lw_small[:, 512:1024], in_=W["g2"][l]), [], ['lws'])
        if l > 0:
            PQD(lambda e, l=l: e.dma_start(out=lw_small[:, 1024:1152].rearrange("p (a b) -> p a b", a=4),
                                           in_=W["v1"][l - 1].rearrange("(kc p) n -> p kc n", p=128)), [], ['lws'])
            PQD(lambda e, l=l: e.dma_start(out=lw_small[0:32, 1152:1664], in_=W["v2"][l - 1]), [], ['lws'])
        PQD(lambda e, l=l: e.dma_start(out=lw_small[:, 1664:2176].rearrange("p (g d) -> p g d", g=4),
                                       in_=W["pool_w"][l].rearrange("g c d -> c g d")), [], ['lws'])
        w2a2 = lw_small[:, 0:512]; g2w = lw_small[:, 512:1024]
        v1w = lw_small[:, 1024:1152].rearrange("p (a b) -> p a b", a=4); v2w = lw_small[0:32, 1152:1664]
        poolw = lw_small[:, 1664:2176].rearrange("p (g d) -> p g d", g=4)

        ada(l, "mix")
        DVE(lambda e: e.memset(ppb[:, :, 0:15], 0.0), [], ['ppb'])
        DVE(lambda e: e.memset(hTc[:, :, 0:1], 0.0), ['hT'], ['hT'])

        W1 = []
        for j in range(5):
            ncol = 512 if j < 4 else 256
            W1.append(wl_kc(W["w_in"][l, :, j * 512:j * 512 + ncol], ncol))

        for ci in range(NCH):
            samp = ci >= NPC
            t0 = ci * 128
            first_chunk = ci == 0
            last_chunk = ci == NPC - 1
            nb = 16 if samp else 1
            blk = 128 // nb
            mset = 'S' if samp else 'P'
            norm_mod(ci, samp, hT, ['hT'])
            if samp:
                pbs_, pks_ = pbank()
                for g0 in range(0, 14, 4):
                    gn = min(4, 14 - g0)
                    dma(X6[0:16, :, :].rearrange("p a b -> p (a b)")[:, 0:gn * 128], sshift[l, :, g0 * 128:(g0 + gn) * 128], [], ['X6'])
                    for c in range(g0, g0 + gn):
                        tr(pbs_[:, c * 16:(c + 1) * 16], f2(X6[0:16, :, :])[:, (c - g0) * 128:(c - g0 + 1) * 128],
                           identF[0:16, 0:16], ['X6', 'cF'], [pks_])
                acopy(f2(shiftT[:]), pbs_[:, 0:224], [pks_], ['shiftT'])
            for cidx in range(18):
                wt, wkey = W1[cidx // 4]
                q = cidx % 4
                pb, pk = pbank()
                for kc in range(NKC):
                    if samp:
                        mm(pb[:, 0:128], wt[:, kc, q * 128:(q + 1) * 128], hT[:, kc, :], kc == 0, kc == NKC - 1, [wkey, 'hT'], [pk])
                    else:
                        mm(pb[:, 0:129], wt[:, kc, q * 128:(q + 1) * 128], hTc[:, kc, 0:129], kc == 0, kc == NKC - 1, [wkey, 'hT'], [pk])
                if cidx >= 14:
                    g = cidx - 14
                    if not samp:
                        acopy(ppb[:, g, 15:143], pb[:, 1:129], [pk], ['ppb'])
                    else:
                        acopy(ppbS[:, g, :, 15:23], pb[:, 0:128].rearrange("p (b t) -> p b t", t=8), [pk], ['ppbS', 'ppb'])
                    continue
                c = cidx
                tb = tsh[:, c % 2, :]
                tk = 'X1'
                mu = pc(l, "mu_shift", c)
                dst = rkv[:, c, :] if c < 12 else xl12[:, c - 12, :]
                dkey = 'rkv' if c < 12 else 'X2'
                if not samp:
                    act(tb[:, 0:129], pb[:, 0:129], AF.Identity, [pk, 'pcol'], [tk], scale=mu)
                    if last_chunk:
                        acopy(prcarry[:, c, :], pb[:, 128:129], [pk], ['prcarry'])
                    stt(dst, pb[:, 1:129], omm[:, c:c + 1], tb[:, 0:128], ALU.mult, ALU.add, [pk, 'omm', tk], [dkey])
                else:
                    p3 = pb[:, 0:128].rearrange("p (b t) -> p b t", t=8)
                    t3 = tb[:, 0:128].rearrange("p (b t) -> p b t", t=8)
                    act(t3[:, :, 1:8], p3[:, :, 0:7], AF.Identity, [pk, 'pcol'], [tk], scale=mu)
                    act(t3[:, :, 0:1], shiftT[:, c, :].unsqueeze(2), AF.Identity, ['shiftT', 'pcol'], [tk], scale=mu)
                    acopy(f2(X5[:])[:, c * 16:(c + 1) * 16].unsqueeze(2), p3[:, :, 7:8], [pk], ['X5'])
                    stt(dst, pb[:, 0:128], omm[:, c:c + 1], tb[:, 0:128], ALU.mult, ALU.add, [pk, 'omm', tk], [dkey])
            if not samp and not last_chunk:
                acopy(hTc[:, :, 0:1], hTc[:, :, 128:129], ['hT'], ['hT'])
            if last_chunk:
                pb, pk = pbank()
                tr(pb[0:14, 0:128], f2(prcarry[:]), identF, ['prcarry', 'cF'], [pk])
                acopy(f2(X6[0:14, :, :])[:, 0:128], pb[0:14, 0:128], [pk], ['X6'])
                dma(nshp[l].rearrange("(c p) -> c p", p=128), f2(X6[0:14, :, :])[:, 0:128], ['X6'], [])
            if samp:
                for g0 in range(0, 14, 4):
                    gn = min(4, 14 - g0)
                    pbx, pkx = pbank()
                    for c in range(g0, g0 + gn):
                        tr(pbx[0:16, (c - g0) * 128:(c - g0 + 1) * 128], f2(X5[:])[:, c * 16:(c + 1) * 16], identF, ['X5', 'cF'], [pkx])
                    ob = [X3, X4][(g0 // 4) % 2]
                    okey = ['X3', 'X4'][(g0 // 4) % 2]
                    acopy(f2(ob[0:16, :, :])[:, 0:gn * 128], pbx[0:16, 0:gn * 128], [pkx], [okey])
                    dma(nshs[l, :, g0 * 128:(g0 + gn) * 128], f2(ob[0:16, :, :])[:, 0:gn * 128], [okey], [])
            act(twa[0:64, :], xl12[0:64, 0, :], AF.Tanh, ['X2'], ['twa'])
            acopy(twa[64:128, :], xl12[64:128, 0, :], ['X2'], ['twa'])
            act(sgg[:], xl12[:, 1, :], AF.Sigmoid, ['X2'], ['sgg'])
            r_c = rkv[:, 0:4, :]; k_c = rkv[:, 4:8, :]; v_c = rkv[:, 8:12, :]
            for p in range(4):
                pb, pk = pbank()
                mm(pb[:, 0:128], w2a2[0:64, p * 128:(p + 1) * 128], twa[0:64, :], True, True, ['lws', 'twa'], [pk])
                act(LW_[:, p, :], pb[:, 0:128], AF.Sigmoid, [pk, 'pcol'], ['LW'], bias=pc(l, "w0", p))
                pb, pk = pbank()
                mm(pb[:, 0:128], w2a2[64:128, p * 128:(p + 1) * 128], twa[64:128, :], True, True, ['lws', 'twa'], [pk])
                act(A_[:, p, :], pb[:, 0:128], AF.Sigmoid, [pk, 'pcol'], ['A'], bias=pc(l, "a0", p))
                pb, pk = pbank()
                mm(pb[:, 0:128], g2w[:, p * 128:(p + 1) * 128], sgg[:], True, True, ['lws', 'sgg'], [pk])
                acopy(gbf[:, p, :], pb[:, 0:128], [pk], ['gbf'])
            ts(LW_[:], LW_[:], DEC_C, None, ALU.mult, ALU.bypass, ['LW'], ['LW'])
            bc4 = lambda col: col.unsqueeze(2).to_broadcast([128, 4, 128])
            d0 = cf('rmS') if samp else onesF
            for p in range(4):
                DVE(lambda e, p=p, d0=d0: e.tensor_tensor_scan(X2[:, p, :], d0, LW_[:, p, :], 0.0, ALU.mult, ALU.add),
                    ['LW', 'cF'], ['X2'])
            tt(X1[:], X2[:], LW_[:], ALU.subtract, ['X2', 'LW'], ['X1'])
            act(X1[:], X1[:], AF.Exp, ['X1'], ['X1'])
            act(LW_[:], X2[:], AF.Exp, ['X2'], ['LW'])
            ein4 = LW_[:].rearrange("p a (b t) -> p a b t", t=blk)
            vcopy(wcs[:, :, 0:nb].unsqueeze(3), ein4[:, :, :, blk - 1:blk], ['LW'], ['wcs'])
            act(X2[:], X2[:], AF.Exp, ['X2'], ['X2'], scale=-1.0)
            if l == 0:
                acopy(vbf[:], v_c, ['rkv'], ['vbf'])
                dma(vfd[:, :, t0:t0 + 128], vbf[:], ['vbf'], [('vfd', ci)])
            else:
                dma(vfb[:], vfd[:, :, t0:t0 + 128], [('vfd', ci)], ['vfb'])
                acopy(vbf[:], v_c, ['rkv'], ['vbf'])
                pb, pk = pbank()
                for p in range(4):
                    mm(pb[0:32, 0:128], v1w[:, p, :], vbf[:, p, :], p == 0, p == 3, ['lws', 'vbf'], [pk])
                acopy(t1b[0:32, :], pb[0:32, 0:128], [pk], ['t1b'])
                for p in range(4):
                    pb, pk = pbank()
                    mm(pb[:, 0:128], v2w[:, p * 128:(p + 1) * 128], t1b[0:32, :], True, True, ['lws', 't1b'], [pk])
                    act(X3[:, p, :], pb[:, 0:128], AF.Sigmoid, [pk, 'pcol'], ['X3'], bias=pc(l, "v0", p))
                tt(X4[:], vfb[:], v_c, ALU.subtract, ['vfb', 'rkv'], ['X4'])
                tt(X4[:], X4[:], X3[:], ALU.mult, ['X4', 'X3'], ['X4'])
                tt(v_c, v_c, X4[:], ALU.add, ['rkv', 'X4'], ['rkv'])

            tt(BWf[:], k_c, bc4(pcs(l, "k_k", 4)), ALU.mult, ['rkv', 'pcol'], ['BWf'])
            tt(KWf[:], BWf[:], BWf[:], ALU.mult, ['BWf'], ['KWf'])
            pb, pk = pbank()
            mm(pb[:, 0:512], bonesF, f2(KWf[:]), True, True, ['KWf', 'cF'], [pk])
            rsq(KWf[:], v4(pb[:, 0:512]), 1.0, 1e-12, [pk], ['KWf'])
            tt(X3[:], BWf[:], KWf[:], ALU.mult, ['BWf', 'KWf'], ['X3'])
            tt(X4[:], X3[:], A_[:], ALU.mult, ['X3', 'A'], ['X4'])
            tt(BWf[:], A_[:], bc4(pcs(l, "k_a", 4)), ALU.mult, ['A', 'pcol'], ['BWf'])
            tt(BWf[:], BWf[:], bc4(omka[:, 0:4]), ALU.add, ['BWf', 'omm'], ['BWf'])
            tt(X5[:], k_c, BWf[:], ALU.mult, ['rkv', 'BWf'], ['X5'])
            tt(BWf[:], r_c, X5[:], ALU.mult, ['rkv', 'X5'], ['BWf'])
            tt(BWf[:], BWf[:], bc4(pcs(l, "r_k", 4)), ALU.mult, ['BWf', 'pcol'], ['BWf'])
            pb, pk = pbank()
            mm(pb[:, 0:512], bonesF, f2(BWf[:]), True, True, ['BWf', 'cF'], [pk])
            tt(X6[:], v4(pb[:, 0:512]), v_c, ALU.mult, [pk, 'rkv'], ['X6'])
            tt(KR[:, :, 0:128], X3[:], X1[:], ALU.mult, ['X3', 'X1'], ['KR'])
            tt(KR[:, :, 128:256], r_c, LW_[:], ALU.mult, ['rkv', 'LW'], ['KR'])
            wcb = wcs[:, :, 0:nb].unsqueeze(3).to_broadcast([128, 4, nb, blk])
            b4 = lambda t: t.rearrange("p a (b t) -> p a b t", t=blk)
            tt(X3[:], X4[:], X2[:], ALU.mult, ['X4', 'X2'], ['X3'])
            acopy(Bt[:], X3[:], ['X3'], ['Bt'])
            tt(b4(BWf[:]), b4(X3[:]), wcb, ALU.mult, ['X3', 'wcs'], ['BWf'])
            tt(X4[:], X5[:], X2[:], ALU.mult, ['X5', 'X2'], ['X4'])
            acopy(Kt[:], X4[:], ['X4'], ['Kt'])
            tt(b4(KWf[:]), b4(X4[:]), wcb, ALU.mult, ['X4', 'wcs'], ['KWf'])
            for (srcb, dstb, sk, dk) in ((v_c, Vtok, 'rkv', 'Vtok'), (BWf, BWtok, 'BWf', 'BWtok'), (KWf, KWtok, 'KWf', 'KWtok')):
                pbb, pkb = pbank()
                for p in range(4):
                    tr(pbb[:, p * 128:(p + 1) * 128], srcb[:, p, :], identS, [sk, 'cF', 'cB'], [pkb])
                acopy(f2(dstb[:]), pbb[:, 0:512], [pkb], [dk])

            if samp:
                DVE(lambda e: e.memset(S0g[:], 0.0), ['H0f'], ['H0f'])
                DVE(lambda e: e.memset(BKK[:], 0.0), ['LW'], ['LW'])
                DVE(lambda e: e.memset(BRR[:], 0.0), ['X3'], ['X3'])
            for hf in range(4 // NPI):
                for q in range(HP):
                    hd = hf * HP + q
                    p, hh = hd // 2, hd % 2
                    hs = slice(hh * 64, hh * 64 + 64)
                    pb, pk = pbank()
                    mm(pb[:, 0:256], Bt[hs, p, :], KR[hs, p, :], True, True, ['Bt', 'KR'], [pk])
                    mm(pb[:, 256:512], Kt[hs, p, :], KR[hs, p, :], True, True, ['Kt', 'KR'], [pk])
                    tt(MB[:, q, 128:256], pb[:, 0:128], cb('mA' + mset, 0, 128), ALU.mult, [pk, 'cB'], ['MB'])
                    tt(AT[:, q, :], pb[:, 128:512], cb('mA' + mset, 128, 512), ALU.mult, [pk, 'cB'], ['AT'])
                pbA, pkA = pbank()
                pbB, pkB = pbank()
                for q in range(HP):
                    hd = hf * HP + q
                    p, hh = hd // 2, hd % 2
                    hs = slice(hh * 64, hh * 64 + 64)
                    pbx, pkx = (pbA, pkA) if hh == 0 else (pbB, pkB)
                    mm(pbx[:, (q // 2) * 128:(q // 2 + 1) * 128], KR[hs, p, 0:128], Bt[hs, p, :], True, True, ['KR', 'Bt'], [pkx])
                for hh, (pbx, pkx) in enumerate(((pbA, pkA), (pbB, pkB))):
                    tt(MB[:, hh:HP:2, 0:128], pbx[:, 0:NPI * 128].rearrange("p (a b) -> p a b", a=NPI),
                       cb('mL' + mset).unsqueeze(1).to_broadcast([128, NPI, 128]), ALU.mult, [pkx, 'cB'], ['MB'])
                tt(PT[:], MB[:, :, 128:256], identF.unsqueeze(1).to_broadcast([128, HP, 128]), ALU.add, ['MB', 'cF'], ['PT'])
                if hf == 0 and not samp:
                    PA = bass.AP(tensor=X1.tensor, offset=X1.offset, ap=[list(X1.ap[0]), [144, 4], [1, 144]])
                    PBs = bass.AP(tensor=A_.tensor, offset=A_.offset, ap=[list(A_.ap[0]), [144, 4], [1, 144]])
                    ka, kb = ['X1', 'X2'], ['A', 'LW']
                    tt(PA[:, :, 1:143], ppb[:, :, 1:143], ppb[:, :, 0:142], ALU.add, ['ppb'], ka)
                    tt(PBs[:, 1:4, 3:143], PA[:, 1:4, 3:143], PA[:, 1:4, 1:141], ALU.add, ka, kb)
                    tt(PA[:, 2:4, 7:143], PBs[:, 2:4, 7:143], PBs[:, 2:4, 3:139], ALU.add, kb + ka, ka)
                    tt(PBs[:, 3:4, 15:143], PA[:, 3:4, 15:143], PA[:, 3:4, 7:135], ALU.add, ka + kb, kb)
                    psrc = [PA[:, 0, :], PBs[:, 1, :], PA[:, 2, :], PBs[:, 3, :]]
                    for g in range(4):
                        if first_chunk:
                            mg = WK['mg']
                            stt(mg[:], psrc[g][:, 15:143], 1.0 / WINS[g], ppb[:, g, 15:143], ALU.mult, ALU.subtract, ka + kb + ['ppb'], ['mg'])
                            tt(mg[:, 0:16], psrc[g][:, 15:31], cf('pcorr')[:, g * 16:(g + 1) * 16], ALU.mult, ka + kb + ['cF', 'mg'], ['mg'])
                            tt(mg[:, 0:16], mg[:, 0:16], ppb[:, g, 15:31], ALU.subtract, ['mg', 'ppb'], ['mg'])
                            vcopy(pz[:, g, :], mg[:], ['mg'], ['vbf'])
                        else:
                            stt(pz[:, g, :], psrc[g][:, 15:143], 1.0 / WINS[g], ppb[:, g, 15:143], ALU.mult, ALU.subtract, ka + kb + ['ppb'], ['vbf'])
                    if not last_chunk:
                        vcopy(PA[:, :, 0:15], ppb[:, :, 128:143], ['ppb'] + ka, ka)
                        vcopy(ppb[:, :, 0:15], PA[:, :, 0:15], ka + ['ppb'], ['ppb'])
                nlev = 2 if samp else 6
                curM = lambda q: MB[:, q, 0:128]
                curMT = lambda q: MB[:, q, 128:256]
                curk = ['MB']
                for lev in range(nlev):
                    nxt, nk = (MA, 'MA') if lev % 2 == 0 else (MB, 'MB')
                    lastlev = lev == nlev - 1
                    for h2 in range(NPI):
                        pb, pk = pbank()
                        for qq in range(2):
                            q = h2 * 2 + qq
                            mm(pb[:, qq * 256:qq * 256 + 128], curMT(q), curM(q), True, True, curk, [pk])
                            if not lastlev:
                                mm(pb[:, qq * 256 + 128:qq * 256 + 256], curM(q), curMT(q), True, True, curk, [pk])
                        acopy(nxt[:, h2 * 2:h2 * 2 + 2, :], pb[:, 0:512].rearrange("p (a b) -> p a b", a=2), [pk], [nk])
                    pb, pk = pbank()
                    for q in range(HP):
                        mm(pb[:, q * 128:(q + 1) * 128], nxt[:, q, 0:128], PT[:, q, :], True, True, [nk, 'PT'], [pk])
                    tt(PT[:], PT[:], pb[:, 0:HP * 128].rearrange("p (a b) -> p a b", a=HP), ALU.add, [pk, 'PT'], ['PT'])
                    curM = lambda q, nxt=nxt: nxt[:, q, 0:128]
                    curMT = lambda q, nxt=nxt: nxt[:, q, 128:256]
                    curk = [nk]

                if hf == 0 and not samp:
                    for g in range(4):
                        pb, pk = pbank()
                        mm(pb[:, 0:128], poolw[:, g, :], pz[:, g, :], True, True, ['lws', 'vbf'], [pk])
                        act(orp[:, 4 + g, :], pb[:, 0:128], AF.Identity, [pk, 'pcol'], ['orp'], scale=pc(l, "pool_scale", g))
                if not samp:
                    for pp_ in range(NPI):
                        p = hf * NPI + pp_
                        if not first_chunk:
                            mm(ZB[:, pp_ * 128:(pp_ + 1) * 128], KR[:, p, 0:128], H0b[:, p, :], True, False, ['KR', 'H0f'], [ZK])
                        for hh in range(2):
                            q = pp_ * 2 + hh
                            mm(ZB[:, pp_ * 128 + hh * 64:pp_ * 128 + hh * 64 + 64], AT[:, q, 128:256],
                               Vtok[:, p, hh * 64:hh * 64 + 64], first_chunk, True, ['AT', 'Vtok'], [ZK])
                    act(f2(Zn[:]), ZB[:, 0:NPI * 128], AF.Copy, [ZK], ['Zn'], scale=-1.0)
                    pbu, pku = pbank()
                    for q in range(HP):
                        pp_, hh = q // 2, q % 2
                        mm(pbu[:, q * 64:(q + 1) * 64], PT[:, q, :], Zn[:, pp_, hh * 64:hh * 64 + 64], True, True, ['PT', 'Zn'], [pku])
                    acopy(f2(Ut[:]), pbu[:, 0:NPI * 128], [pku], ['Ut'])
                    for pp_ in range(NPI):
                        p = hf * NPI + pp_
                        if not first_chunk:
                            mm(YB[:, pp_ * 128:(pp_ + 1) * 128], H0b[:, p, :], KR[:, p, 128:256], True, False, ['H0f', 'KR'], [YK])
                        for hh in range(2):
                            q = pp_ * 2 + hh
                            hs = slice(hh * 64, hh * 64 + 64)
                            mm(YB[hs, pp_ * 128:(pp_ + 1) * 128], Ut[:, pp_, hh * 64:hh * 64 + 64], AT[:, q, 0:128],
                               first_chunk, False, ['Ut', 'AT'], [YK])
                            mm(YB[hs, pp_ * 128:(pp_ + 1) * 128], Vtok[:, p, hh * 64:hh * 64 + 64], AT[:, q, 256:384],
                               False, True, ['Vtok', 'AT'], [YK])
                    acopy(f2(X5[:, hf * NPI:(hf + 1) * NPI, :]), YB[:, 0:NPI * 128], [YK], ['X5'])
                    pbh, pkh = pbank()
                    for pp_ in range(NPI):
                        p = hf * NPI + pp_
                        mm(pbh[:, pp_ * 128:(pp_ + 1) * 128], BWtok[:, p, :], Ut[:, pp_, :], True, False, ['BWtok', 'Ut'], [pkh])
                        mm(pbh[:, pp_ * 128:(pp_ + 1) * 128], KWtok[:, p, :], Vtok[:, p, :], False, True, ['KWtok', 'Vtok'], [pkh])
                    hsl = slice(hf * NPI, (hf + 1) * NPI)
                    tt(X3[:, 0:NPI, :], pbh[:, 0:NPI * 128].rearrange("p (a b) -> p a b", a=NPI),
                       bonesF.unsqueeze(1).to_broadcast([128, NPI, 128]), ALU.mult, [pkh, 'cF'], ['X3'])
                    if first_chunk:
                        vcopy(H0f[:, hsl, :], X3[:, 0:NPI, :], ['X3'], ['H0f'])
                    else:
                        tt(H0f[:, hsl, :], H0f[:, hsl, :], wcs[:, hsl, 0:1].to_broadcast([128, NPI, 128]), ALU.mult, ['H0f', 'wcs'], ['H0f'])
                        tt(H0f[:, hsl, :], H0f[:, hsl, :], X3[:, 0:NPI, :], ALU.add, ['H0f', 'X3'], ['H0f'])
                    if last_chunk:
                        pbs, pks = pbank()
                        for pp_ in range(NPI):
                            tr(pbs[:, pp_ * 128:(pp_ + 1) * 128], H0f[:, hf * NPI + pp_, :], identF, ['H0f', 'cF'], [pks])
                        for hh in range(2):
                            hs = slice(hh * 64, hh * 64 + 64)
                            acopy(SCo[hs, 0:NPI, :], pbs[hs, 0:NPI * 128].rearrange("p (a b) -> p a b", a=NPI)[:, :, hh * 64:hh * 64 + 64], [pks], ['X4'])
                        dma(nwkvp[l, hf * HP:hf * HP + HP].rearrange("(p hh) v n -> (hh v) p n", hh=2), SCo[:, 0:NPI, :], ['X4'], ['X4'])
                else:
                    for pp_ in range(NPI):
                        p = hf * NPI + pp_
                        for g in range(4):
                            for hh in range(2):
                                hs = slice(hh * 64, hh * 64 + 64)
                                dma(S0g[hs, :, hh * 64:hh * 64 + 64],
                                    swkv[l, g * 4:g * 4 + 4, 2 * p + hh, :, :].rearrange("b v k -> v b k"), ['H0f'], ['H0f'])
                            pb, pk = pbank()
                            for j in range(4):
                                tr(pb[:, j * 128:(j + 1) * 128], S0g[:, j, :], identF, ['H0f', 'cF'], [pk])
                            acopy(f2(HSb[:]), pb[:, 0:512], [pk], ['A'])
                            bkk_diag = bass.AP(tensor=BKK.tensor, offset=BKK.offset + 32 * g,
                                               ap=[list(BKK.ap[0]), [136, 4], [1, 8]])
                            vcopy(bkk_diag, KR[:, p, g * 32:g * 32 + 32].rearrange("q (b t) -> q b t", t=8), ['KR', 'LW'], ['LW'])
                            brr_diag = bass.AP(tensor=BRR.tensor, offset=BRR.offset + 32 * g,
                                               ap=[list(BRR.ap[0]), [136, 4], [1, 8]])
                            vcopy(brr_diag, KR[:, p, 128 + g * 32:128 + g * 32 + 32].rearrange("q (b t) -> q b t", t=8), ['KR', 'X3'], ['X3'])
                            for j in range(4):
                                b_ = g * 4 + j
                                mm(ZB[:, 0:128], BKK[:, j, :], HSb[:, j, :], b_ == 0, False, ['LW', 'A'], [ZK])
                                mm(YB[:, 0:128], HSb[:, j, :], BRR[:, j, :], b_ == 0, False, ['A', 'X3'], [YK])
                            DVE(lambda e, bkk_diag=bkk_diag: e.memset(bkk_diag, 0.0), ['LW'], ['LW'])
                            DVE(lambda e, brr_diag=brr_diag: e.memset(brr_diag, 0.0), ['X3'], ['X3'])
                        for hh in range(2):
                            q = pp_ * 2 + hh
                            mm(ZB[:, hh * 64:hh * 64 + 64], AT[:, q, 128:256], Vtok[:, p, hh * 64:hh * 64 + 64], False, True, ['AT', 'Vtok'], [ZK])
                        act(Zn[:, 0, :], ZB[:, 0:128], AF.Copy, [ZK], ['Zn'], scale=-1.0)
                        pbu, pku = pbank()
                        for hh in range(2):
                            q = pp_ * 2 + hh
                            mm(pbu[:, hh * 64:hh * 64 + 64], PT[:, q, :], Zn[:, 0, hh * 64:hh * 64 + 64], True, True, ['PT', 'Zn'], [pku])
                        acopy(Ut[:, 0, :], pbu[:, 0:128], [pku], ['Ut'])
                        for hh in range(2):
                            q = pp_ * 2 + hh
                            hs = slice(hh * 64, hh * 64 + 64)
                            mm(YB[hs, 0:128], Ut[:, 0, hh * 64:hh * 64 + 64], AT[:, q, 0:128], False, False, ['Ut', 'AT'], [YK])
                            mm(YB[hs, 0:128], Vtok[:, p, hh * 64:hh * 64 + 64], AT[:, q, 256:384], False, True, ['Vtok', 'AT'], [YK])
                        acopy(X5[:, p, :], YB[:, 0:128], [YK], ['X5'])
                        smk = cb('seqm')
                        for g in range(4):
                            for hh in range(2):
                                hs = slice(hh * 64, hh * 64 + 64)
                                dma(S0g[hs, :, hh * 64:hh * 64 + 64],
                                    swkv[l, g * 4:g * 4 + 4, 2 * p + hh, :, :].rearrange("b v k -> v b k"), ['H0f'], ['H0f'])
                            sm4 = smk[:, g * 4:g * 4 + 4].unsqueeze(2).to_broadcast([128, 4, 128])
                            tt(BIGB[:], BWtok[:, p, :].unsqueeze(1).to_broadcast([128, 4, 128]), sm4, ALU.mult, ['BWtok', 'cB'], ['BWf'])
                            tt(BIGK[:], KWtok[:, p, :].unsqueeze(1).to_broadcast([128, 4, 128]), sm4, ALU.mult, ['KWtok', 'cB'], ['KWf'])
                            tt(DIAG[:], identF.unsqueeze(1).to_broadcast([128, 4, 128]),
                               wcs[:, p, g * 4:g * 4 + 4].unsqueeze(2).to_broadcast([128, 4, 128]), ALU.mult, ['cF', 'wcs'], ['X1'])
                            pb, pk = pbank()
                            mm(pb[:, 0:512], onesF, f2(DIAG[:]), True, True, ['X1', 'cF'], [pk])
                            acopy(f2(WcBC[:]), pb[:, 0:512], [pk], ['X2'])
                            tt(WcBC[:], WcBC[:], S0g[:], ALU.mult, ['X2', 'H0f'], ['X2'])
                            pbs, pks = pbank()
                            mm(pbs[:, 0:512], Ut[:, 0, :], f2(BIGB[:]), True, False, ['Ut', 'BWf'], [pks])
                            mm(pbs[:, 0:512], Vtok[:, p, :], f2(BIGK[:]), False, True, ['Vtok', 'KWf'], [pks])
                            tt(WcBC[:], WcBC[:], v4(pbs[:, 0:512]), ALU.add, ['X2', pks], ['X2'])
                            for hh in range(2):
                                hs = slice(hh * 64, hh * 64 + 64)
                                vcopy(SCo[hs, :, :], WcBC[hs, :, hh * 64:hh * 64 + 64], ['X2', 'X4'], ['X4'])
                            dma(nwkvs[l, g * 4:g * 4 + 4, 2 * p:2 * p + 2, :, :].rearrange("b hh v n -> (hh v) b n"), SCo[:], ['X4'], ['X4'])

            pb, pk = pbank()
            mm(pb[:, 0:512], bonesF, f2(X5[:]), True, True, ['X5', 'cF'], [pk])
            ts(X1[:], v4(pb[:, 0:512]), 1.0 / 64, None, ALU.mult, ALU.bypass, [pk], ['X1'])
            tt(X5[:], X5[:], X1[:], ALU.subtract, ['X5', 'X1'], ['X5'])
            tt(X2[:], X5[:], X5[:], ALU.mult, ['X5'], ['X2'])
            pb, pk = pbank()
            mm(pb[:, 0:512], bonesF, f2(X2[:]), True, True, ['X2', 'cF'], [pk])
            rsq(X1[:], v4(pb[:, 0:512]), 1.0 / 64, GN_EPS, [pk], ['X1'])
            tt(X5[:], X5[:], X1[:], ALU.mult, ['X5', 'X1'], ['X5'])
            tt(X5[:], X5[:], bc4(pcs(l, "ln_w", 4)), ALU.mult, ['X5', 'pcol'], ['X5'])
            tt(X5[:], X5[:], bc4(pcs(l, "ln_b", 4)), ALU.add, ['X5', 'pcol'], ['X5'])
            tt(X5[:], X5[:], X6[:], ALU.add, ['X5', 'X6'], ['X5'])
            tt(orp[:, 0:4, :], X5[:], gbf[:], ALU.mult, ['X5', 'gbf'], ['orp'])

            if not samp:
                if last_chunk:
                    pb, pk = pbank()
                    for g in range(4):
                        tr(pb[0:15, g * 128:(g + 1) * 128], ppb[:, g, 128:143], identF, ['ppb', 'cF'], [pk])
                    acopy(f2(X1[0:15, :, :]), pb[0:15, 0:512], [pk], ['X1'])
                    dma(npoolp[l], f2(X1[0:15, :, :]), ['X1'], ['X1'])
                pass
            else:
                spv = spool[l].rearrange("b r c -> (b r) c")
                for half in range(2):
                    r0, rn = (0, 128) if half == 0 else (128, 112)
                    stg = [X1, X2][half]
                    dma(f2(stg[0:rn, :, :]), spv[r0:r0 + rn, :], [], [['X1', 'X2'][half]])
                    pb, pk = pbank()
                    for g in range(4):
                        tr(pb[:, g * 128:g * 128 + rn], f2(stg[0:rn, :, :])[:, g * 128:(g + 1) * 128], identF[0:rn, 0:rn],
                           [['X1', 'X2'][half], 'cF'], [pk])
                    acopy(f2(X3[:]) if half == 0 else f2(X4[:]), pb[:, 0:512], [pk], [['X3', 'X4'][half]])
                for g in range(4):
                    vcopy(ppbS[:, g, 0:8, 0:15], X3[:, g, 0:120].rearrange("p (b r) -> p b r", r=15), ['X3'], ['ppbS', 'ppb'])
                    vcopy(ppbS[:, g, 8, 0:8], X3[:, g, 120:128], ['X3'], ['ppbS', 'ppb'])
                    vcopy(ppbS[:, g, 8, 8:15], X4[:, g, 0:7], ['X4'], ['ppbS', 'ppb'])
                    vcopy(ppbS[:, g, 9:16, 0:15], X4[:, g, 7:112].rearrange("p (b r) -> p b r", r=15), ['X4'], ['ppbS', 'ppb'])
                for half in range(2):
                    r0, rn = (0, 128) if half == 0 else (128, 112)
                    stg = [X3, X4][half]
                    for g in range(4):
                        if half == 0:
                            vcopy(stg[:, g, 0:120].rearrange("p (b r) -> p b r", r=15), ppbS[:, g, 0:8, 8:23], ['ppbS'], [['X3', 'X4'][half]])
                            vcopy(stg[:, g, 120:128], ppbS[:, g, 8, 8:16], ['ppbS'], [['X3', 'X4'][half]])
                        else:
                            vcopy(stg[:, g, 0:7], ppbS[:, g, 8, 16:23], ['ppbS'], [['X3', 'X4'][half]])
                            vcopy(stg[:, g, 7:112].rearrange("p (b r) -> p b r", r=15), ppbS[:, g, 9:16, 8:23], ['ppbS'], [['X3', 'X4'][half]])
                    pb, pk = pbank()
                    for g in range(4):
                        tr(pb[0:rn, g * 128:(g + 1) * 128], stg[:, g, 0:rn], identF, [['X3', 'X4'][half], 'cF'], [pk])
                    ob = [X1, X2][half]
                    acopy(f2(ob[0:rn, :, :]), pb[0:rn, 0:512], [pk], [['X1', 'X2'][half]])
                    dma(npools[l, r0:r0 + rn, :], f2(ob[0:rn, :, :]), [['X1', 'X2'][half]], [['X1', 'X2'][half]])
                for g in range(4):
                    A0 = ppbS[:, g, :, :]
                    cur = X3[:].rearrange("p a b -> p (a b)")[:, 0:368].rearrange("p (b r) -> p b r", r=23)
                    oth = X4[:].rearrange("p a b -> p (a b)")[:, 0:368].rearrange("p (b r) -> p b r", r=23)
                    tt(cur[:, :, 1:23], A0[:, :, 1:23], A0[:, :, 0:22], ALU.add, ['ppbS', 'X3', 'X4'], ['X3', 'X4'])
                    span = 2
                    while span < WINS[g]:
                        lo = 2 * span - 1
                        tt(oth[:, :, lo:23], cur[:, :, lo:23], cur[:, :, lo - span:23 - span], ALU.add, ['X3', 'X4'], ['X3', 'X4'])
                        cur, oth = oth, cur
                        span *= 2
                    mg = WK['mg']
                    stt(mg[:].rearrange("p (b t) -> p b t", t=8), cur[:, :, 15:23], 1.0 / WINS[g], ppbS[:, g, :, 15:23],
                        ALU.mult, ALU.subtract, ['X3', 'X4', 'ppbS'], ['mg'])
                    acopy(pz[:, g, :], mg[:], ['mg'], ['vbf'])
            if samp:
                for g in range(4):
                    pb, pk = pbank()
                    mm(pb[:, 0:128], poolw[:, g, :], pz[:, g, :], True, True, ['lws', 'vbf'], [pk])
                    act(orp[:, 4 + g, :], pb[:, 0:128], AF.Identity, [pk, 'pcol'], ['orp'], scale=pc(l, "pool_scale", g))
            dma(orpd[:, :, t0:t0 + 128], orp[:], ['orp'], [('orpd', ci)])

        S.barrier()
        ptr[0] = UBASE
        T2 = 512
        WK['sq'] = alloc([2, T2], BF16); WK['rstd'] = alloc([T2]); WK['tmpn'] = alloc([2, T2]); WK['mg'] = alloc([128])
        hT2 = alloc([NKC, T2], BF16)
        mrg = alloc([NKC, NTOK], BF16)
        orp2 = alloc([1, 8, T2], BF16)
        g16 = alloc([16, T2], BF16)
        tmpb = WK['tmpn'][:, 0:1, :]
        W2 = [wl_kc(W["w_in"][l, :, 2304 + j * 512:2304 + (j + 1) * 512], 512) for j in range(4)]
        wbr, wbrk = wl_kc(W["w_br_rwkv"][l], 1024)
        wbp, wbpk = wl_kc(W["w_br_pool"][l], 1024)
        tl2 = tiles(T2)
        norm_mod_t(tl2[0][0], tl2[0][1], tl2[0][2], hT2, ['hT2'])
        for ti, (t0, n, samp) in enumerate(tl2):
            ob = orp2[:, 0, :, 0:n]
            ok = ('orp2', 0)
            dma(ob, orpd[:, :, t0:t0 + n], [('orpd', c) for c in range(t0 // 128, (t0 + n) // 128)], [ok])
            for cg in range(16):
                wt, wkey = W2[cg // 4]
                q = cg % 4
                pb, pk = pbank()
                for kc in range(NKC):
                    mm(pb[:, 0:n], wt[:, kc, q * 128:(q + 1) * 128], hT2[:, kc, 0:n], kc == 0, kc == NKC - 1, [wkey, 'hT2'], [pk])
                act(g16[:, cg, 0:n], pb[:, 0:n], AF.Sigmoid, [pk], ['g16'])
            if ti + 1 < len(tl2):
                norm_mod_t(tl2[ti + 1][0], tl2[ti + 1][1], tl2[ti + 1][2], hT2, ['hT2'])
            for c in range(8):
                pb, pk = pbank()
                for kc in range(4):
                    mm(pb[:, 0:n], wbr[:, kc, c * 128:(c + 1) * 128], ob[:, kc, :], kc == 0, kc == 3, [wbrk, ok], [pk])
                tt(tmpb[:, 0, 0:n], pb[:, 0:n], g16[:, c, 0:n], ALU.mult, [pk, 'g16'], [('tmpn', 0)])
                pb2, pk2 = pbank()
                for kc in range(4):
                    mm(pb2[:, 0:n], wbp[:, kc, c * 128:(c + 1) * 128], ob[:, 4 + kc, :], kc == 0, kc == 3, [wbpk, ok], [pk2])
                tt(mrg[:, c, t0:t0 + n], pb2[:, 0:n], g16[:, 8 + c, 0:n], ALU.mult, [pk2, 'g16'], [('mrg', t0)])
                tt(mrg[:, c, t0:t0 + n], mrg[:, c, t0:t0 + n], tmpb[:, 0, 0:n], ALU.add, [('tmpn', 0), ('mrg', t0)], [('mrg', t0)])
        wo = [wl_kc(W["w_out"][l, :, j * 512:(j + 1) * 512], 512) for j in range(2)]
        for (t0, n, samp) in tiles(T2):
            for c in range(8):
                wt, wkey = wo[c // 4]
                q = c % 4
                pb, pk = pbank()
                for kc in range(NKC):
                    mm(pb[:, 0:n], wt[:, kc, q * 128:(q + 1) * 128], mrg[:, kc, t0:t0 + n], kc == 0, kc == NKC - 1, [wkey, ('mrg', t0)], [pk])
                resid_update_t(t0, n, samp, c, pb, pk)

        S.barrier()
        ptr[0] = UBASE
        T3 = 512
        WK['sq'] = alloc([2, T3], BF16); WK['rstd'] = alloc([T3]); WK['tmpn'] = alloc([2, T3]); WK['mg'] = alloc([128])
        hTm = alloc([NKC, NTOK], BF16)
        rl = alloc([2, T3]); r2 = alloc([8, T3], BF16)
        ada(l, "mlp")
        tl3 = tiles(T3)
        norm_mod_t(tl3[0][0], tl3[0][1], tl3[0][2], hTm[:, :, tl3[0][0]:tl3[0][0] + tl3[0][1]], [('hTm', tl3[0][0])])
        for qd in range(4):
            w1 = [wl_kc(W["w_ff1"][l, :, qd * 1024 + j * 512:qd * 1024 + (j + 1) * 512], 512) for j in range(2)]
            w2 = [wl_kc(W["w_ff2"][l, qd * 1024:(qd + 1) * 1024, j * 512:(j + 1) * 512], 512) for j in range(2)]
            for ti3, (t0, n, samp) in enumerate(tl3):
                if qd == 0 and ti3 + 1 < len(tl3):
                    nt0, nn, nsamp = tl3[ti3 + 1]
                    norm_mod_t(nt0, nn, nsamp, hTm[:, :, nt0:nt0 + nn], [('hTm', nt0)])
                for c in range(8):
                    wt, wkey = w1[c // 4]
                    q = c % 4
                    pb, pk = pbank()
                    for kc in range(NKC):
                        mm(pb[:, 0:n], wt[:, kc, q * 128:(q + 1) * 128], hTm[:, kc, t0:t0 + n], kc == 0, kc == NKC - 1, [wkey, ('hTm', t0)], [pk])
                    act(rl[:, c % 2, 0:n], pb[:, 0:n], AF.Relu, [pk], [('rl', c % 2)])
                    tt(r2[:, c, 0:n], rl[:, c % 2, 0:n], rl[:, c % 2, 0:n], ALU.mult, [('rl', c % 2)], [('r2', c)])
                for c in range(8):
                    wt, wkey = w2[c // 4]
                    q = c % 4
                    pb, pk = pbank()
                    for kc in range(NKC):
                        mm(pb[:, 0:n], wt[:, kc, q * 128:(q + 1) * 128], r2[:, kc, 0:n], kc == 0, kc == NKC - 1, [wkey, ('r2', kc)], [pk])
                    resid_update_t(t0, n, samp, c, pb, pk)

    S.barrier()
    ptr[0] = UBASE
    sqr = alloc([2, 128], BF16); rstd2 = alloc([2, 128]); tmpo2 = alloc([2, NKC, 128]); yout = alloc([2, D])
    ts(G32f[:], pcol[:, 0, 120:128].unsqueeze(2), 32.0, None, ALU.mult, ALU.bypass, ['pcol'], ['G32f'])

    def fin_stats(ci):
        t0 = ci * 128
        pb, pk = pbank()
        for c in range(NKC):
            act(sqr[:, c % 2, :], xT[:, c, t0:t0 + 128], AF.Square, xk(ci), [('sq', c % 2)])
            mm(pb[:, 0:128], onesB, sqr[:, c % 2, :], c == 0, c == NKC - 1, [('sq', c % 2), 'cB'], [pk])
        rsq(rstd2[:, ci % 2, :], pb[:, 0:128], 1.0, D * EPS, [pk], [('rstd', ci % 2)])

    fin_stats(0)
    for ci in range(NCH):
        t0 = ci * 128
        tmpo = tmpo2[:, ci % 2, :, :]
        tok = ('tmpo', ci % 2)
        for c in range(NKC):
            stt(tmpo[:, c, :], xT[:, c, t0:t0 + 128], G32f[:, c, 0:1], rstd2[:, ci % 2, :], ALU.mult, ALU.mult,
                xk(ci) + ['G32f', ('rstd', ci % 2)], [tok])
        if ci + 1 < NCH:
            fin_stats(ci + 1)
        yo = yout[:, ci % 2, :]
        for half in range(2):
            pb, pk = pbank()
            for q in range(4):
                c = half * 4 + q
                tr(pb[:, q * 128:(q + 1) * 128], tmpo[:, c, :], identF, [tok, 'cF'], [pk])
            acopy(yo[:, half * 512:(half + 1) * 512], pb[:, 0:512], [pk], [('yout', ci % 2)])
        dst = yp[t0:t0 + 128, :] if ci < NPC else ys[:, :]
        dma(dst, yo, [('yout', ci % 2)], [('yout', ci % 2)])
    return nc, S, st


CSTF = {}
CSTB = {}
CF_COLS = 0
CB_COLS = 0


def _layout_consts():
    global CF_COLS, CB_COLS
    o = 0
    for nm, n in [('ident', 128), ('ones', 128), ('bones', 128), ('rmS', 128), ('pcorr', 64)]:
        CSTF[nm] = (o, n)
        o += n
    CF_COLS = o
    o = 0
    for nm, n in [('ident', 128), ('ones', 128), ('mAP', 512), ('mAS', 512), ('mLP', 128), ('mLS', 128), ('seqm', 16)]:
        CSTB[nm] = (o, n)
        o += n
    CB_COLS = o


_layout_consts()


def make_consts():
    c = np.zeros((128, CF_COLS + CB_COLS), np.float32)

    def putf(nm, a):
        o, n = CSTF[nm]
        c[:, o:o + n] = a

    def putb(nm, a):
        o, n = CSTB[nm]
        c[:, CF_COLS + o:CF_COLS + o + n] = a
    i = np.arange(128)
    putf('ident', np.eye(128)); putb('ident', np.eye(128))
    putf('ones', np.ones((128, 128))); putb('ones', np.ones((128, 128)))
    putf('bones', (i[:, None] // 64 == i[None, :] // 64).astype(np.float32))
    s, t = i[:, None], i[None, :]
    for tag, same in (('P', np.ones((128, 128), bool)), ('S', (s // 8) == (t // 8))):
        lt = ((s < t) & same).astype(np.float32)
        le = ((s <= t) & same).astype(np.float32)
        gtm = ((s > t) & same).astype(np.float32)
        putb('mA' + tag, np.concatenate([-lt, le, lt, le], axis=1))
        putb('mL' + tag, -gtm)
    rmS = np.ones((128, 128), np.float32)
    rmS[:, ::8] = 0
    putf('rmS', rmS)
    putb('seqm', (i[:, None] // 8 == np.arange(16)[None, :]).astype(np.float32))
    pc_ = np.zeros((128, 64), np.float32)
    for g, w in enumerate(WINS):
        tt_ = np.arange(16)
        pc_[:, g * 16:(g + 1) * 16] = (1.0 / np.minimum(tt_ + 1, w))[None, :]
    putf('pcorr', pc_)
    return c


def emit(nc, S, st):
    sems = {name: st.enter_context(nc.semaphore(name)) for name in S.cnt}
    block = st.enter_context(nc.Block())

    def run(stream, eng):
        for waits, fn, sem, inc in S.ops[stream]:
            for (s, v) in waits:
                eng.wait_ge(sems[s], v)
            if fn is not None:
                fn(eng).then_inc(sems[sem], inc)

    @block.sync
    def _(e):
        run('sp', e)
        for nm in S.cnt:
            if nm.startswith('sp'):
                e.wait_ge(sems[nm], S.cnt[nm])

    @block.gpsimd
    def _(e):
        run('pool', e)

    @block.tensor
    def _(e):
        run('pe', e)

    @block.vector
    def _(e):
        run('dve', e)

    @block.scalar
    def _(e):
        run('act', e)
    st.close()
    return nc


_WNAMES = ["w_ada_mix", "b_ada_mix", "norm_mix", "w_in", "mu_shift", "w0", "w2", "a0", "a2", "g2", "v0", "v1", "v2",
           "k_k", "k_a", "r_k", "ln_w", "ln_b", "pool_w", "pool_scale", "w_br_rwkv", "w_br_pool", "w_out",
           "w_ada_mlp", "b_ada_mlp", "norm_mlp", "w_ff1", "w_ff2", "norm_final"]


def make_in_maps(inputs, ncores, L):
    consts = make_consts()
    f = lambda a: np.ascontiguousarray(np.asarray(a, dtype=np.float32))
    shared = {}
    for nm in _WNAMES:
        a = f(inputs[nm])
        if nm == "r_k":
            a = a.reshape(L, MIX)
        if nm == "norm_final":
            a = a.reshape(1, D)
        shared[nm] = a
    shared["cst"] = consts
    maps = []
    for i in range(ncores):
        m = dict(shared)
        m["xp"] = f(inputs["x_prompt"][i])
        m["xs"] = f(inputs["x_sample"][16 * i:16 * i + 16]).reshape(128, D)
        m["cc"] = f(np.concatenate([np.asarray(inputs["c_prompt"])[i:i + 1], np.asarray(inputs["c_sample"])[16 * i:16 * i + 16]], axis=0))
        m["sshift"] = f(np.asarray(inputs["state_shift"])[:, 16 * i:16 * i + 16])
        m["spool"] = f(np.asarray(inputs["state_pool"])[:, 16 * i:16 * i + 16])
        m["swkv"] = f(np.asarray(inputs["state_wkv"])[:, 16 * i:16 * i + 16])
        maps.append(m)
    return maps


def gather(R, ncores, L):
    y_p = np.stack([R[i]["yp"] for i in range(ncores)], 0)
    y_s = np.concatenate([R[i]["ys"].reshape(16, 8, D) for i in range(ncores)], 0)
    sh_p = np.stack([R[i]["nshp"] for i in range(ncores)], 1)
    pool_p = np.stack([R[i]["npoolp"] for i in range(ncores)], 1)
    wkv_p = np.stack([R[i]["nwkvp"] for i in range(ncores)], 1)
    sh_s = np.concatenate([R[i]["nshs"] for i in range(ncores)], 1)
    pool_s = np.concatenate([R[i]["npools"].reshape(L, 16, 15, MIX) for i in range(ncores)], 1)
    wkv_s = np.concatenate([R[i]["nwkvs"] for i in range(ncores)], 1)
    return tuple(np.ascontiguousarray(a, dtype=np.float32) for a in (y_p, y_s, sh_p, pool_p, wkv_p, sh_s, pool_s, wkv_s))


def kernel(**inputs):
    ncores = 8
    L = 4
    nc, S, st = build(TP=2048, L=L)
    emit(nc, S, st)
    maps = make_in_maps(inputs, ncores, L)
    res = run_bass_kernel_spmd(nc, maps, core_ids=list(range(ncores)))
    return gather(res.results, ncores, L)
```

```python
import numpy as np
from contextlib import ExitStack
import concourse.bass as bass
import concourse.mybir as mybir
from concourse.bass_utils import run_bass_kernel_spmd

F32 = mybir.dt.float32
BF16 = mybir.dt.bfloat16
AF = mybir.ActivationFunctionType
ALU = mybir.AluOpType

D = 1024
NKC = 8
MIX = 512
RW = 1792
INC = 4352
DFF = 4096
EPS = 1e-6
GN_EPS = 64e-5
DEC_C = -float(np.exp(-0.5))
WINS = (2, 4, 8, 16)


class Sched:
    STREAMS = ('pe', 'act', 'dve', 'pool', 'sp')

    def __init__(self):
        self.ops = {s: [] for s in self.STREAMS}
        self.cnt = {}
        self.known = {s: {} for s in self.STREAMS}
        self.lastw = {}
        self.readers = {}
        self.dma_i = {}

    def op(self, stream, fn, r=(), w=(), sem=None, inc=1, nsem=1):
        sem = sem or stream
        if nsem > 1:
            i = self.dma_i.get(sem, 0)
            self.dma_i[sem] = i + 1
            sem = "%s%d" % (sem, i % nsem)
        need = {}
        if nsem > 1 and self.cnt.get(sem, 0):
            need[sem] = self.cnt[sem]

        def add(s, v):
            if need.get(s, 0) < v:
                need[s] = v
        for b in r:
            if b in self.lastw:
                add(*self.lastw[b])
        for b in w:
            if b in self.lastw:
                add(*self.lastw[b])
            for s, v in self.readers.get(b, {}).items():
                add(s, v)
        waits = []
        kn = self.known[stream]
        for s, v in need.items():
            if stream == 'pe' and s == 'pe':
                continue
            if kn.get(s, 0) < v:
                waits.append((s, v))
                kn[s] = v
        self.cnt[sem] = self.cnt.get(sem, 0) + inc
        val = self.cnt[sem]
        self.ops[stream].append((waits, fn, sem, inc))
        for b in r:
            d = self.readers.setdefault(b, {})
            if d.get(sem, 0) < val:
                d[sem] = val
        for b in w:
            self.lastw[b] = (sem, val)
            self.readers[b] = {}

    def barrier(self, streams=('pe', 'act', 'dve', 'sp')):
        for s in streams:
            waits = []
            for sem in list(self.cnt):
                if sem.startswith('pq'):
                    continue
                v = self.cnt.get(sem, 0)
                if s == 'pe' and sem == 'pe':
                    continue
                if v and self.known[s].get(sem, 0) < v:
                    waits.append((sem, v))
                    self.known[s][sem] = v
            if waits:
                self.ops[s].append((waits, None, None, 0))


def build(TP=2048, L=4):
    NTOK = TP + 128
    NCH = NTOK // 128
    NPC = TP // 128
    nc = bass.Bass("TRN2", target_bir_lowering=False)
    S = Sched()
    st = ExitStack()

    def din(name, shape, dt=F32):
        return nc.dram_tensor(name, list(shape), dt, kind="ExternalInput").ap()

    def dout(name, shape, dt=F32):
        return nc.dram_tensor(name, list(shape), dt, kind="ExternalOutput").ap()

    xp = din("xp", [TP, D]); xs = din("xs", [128, D]); cc = din("cc", [17, D])
    sshift = din("sshift", [L, 16, RW]); spool = din("spool", [L, 16, 15, MIX])
    swkv = din("swkv", [L, 16, 8, 64, 64])
    W = {}
    LV = max(L - 1, 1)
    for nm, shp in [("w_ada_mix", [L, D, 3 * D]), ("b_ada_mix", [L, 3 * D]), ("norm_mix", [L, D]),
                    ("w_in", [L, D, INC]), ("mu_shift", [L, RW]), ("w0", [L, MIX]), ("w2", [L, 64, MIX]),
                    ("a0", [L, MIX]), ("a2", [L, 64, MIX]), ("g2", [L, 128, MIX]), ("v0", [LV, MIX]),
                    ("v1", [LV, MIX, 32]), ("v2", [LV, 32, MIX]), ("k_k", [L, MIX]),
                    ("k_a", [L, MIX]), ("r_k", [L, MIX]), ("ln_w", [L, MIX]), ("ln_b", [L, MIX]),
                    ("pool_w", [L, 4, 128, 128]), ("pool_scale", [L, MIX]), ("w_br_rwkv", [L, MIX, D]),
                    ("w_br_pool", [L, MIX, D]), ("w_out", [L, D, D]), ("w_ada_mlp", [L, D, 3 * D]),
                    ("b_ada_mlp", [L, 3 * D]), ("norm_mlp", [L, D]), ("w_ff1", [L, D, DFF]),
                    ("w_ff2", [L, DFF, D]), ("norm_final", [1, D])]:
        W[nm] = din(nm, shp)
    cst = din("cst", [128, CF_COLS + CB_COLS])
    yp = dout("yp", [TP, D]); ys = dout("ys", [128, D])
    nshp = dout("nshp", [L, RW]); npoolp = dout("npoolp", [L, 15, MIX]); nwkvp = dout("nwkvp", [L, 8, 64, 64])
    nshs = dout("nshs", [L, 16, RW]); npools = dout("npools", [L, 16 * 15, MIX])
    nwkvs = dout("nwkvs", [L, 16, 8, 64, 64])
    vfd = nc.dram_tensor("vfirst_scr", [128, 4, NTOK], BF16, kind="Internal").ap()
    orpd = nc.dram_tensor("orp_scr", [128, 8, NTOK], BF16, kind="Internal").ap()

    NW = 53200
    big = st.enter_context(nc.sbuf_tensor("big", [128, NW], F32))
    ptr = [0]

    def alloc(shape, dt=F32):
        n = int(np.prod(shape))
        words = n if dt == F32 else (n + 1) // 2
        words = (words + 7) // 8 * 8
        o = ptr[0]
        ptr[0] += words
        assert ptr[0] <= NW, ("SBUF arena overflow", ptr[0], NW)
        v = big[:, o:o + words]
        if dt != F32:
            v = v.bitcast(dt)
        v = v[:, 0:n]
        if len(shape) == 1:
            return v
        names = " ".join("d%d" % i for i in range(len(shape)))
        kw = {"d%d" % i: int(shape[i]) for i in range(len(shape) - 1)}
        return v.rearrange("p (%s) -> p %s" % (names, names), **kw)

    SD = F32
    NPI = 2
    HP = 2 * NPI
    xT = alloc([NKC, NTOK])
    ring = alloc([6, 4096], BF16)
    cF = alloc([CF_COLS]); cB = alloc([CB_COLS], BF16)
    pcol = alloc([L, 128])
    omm = alloc([14]); omka = alloc([4])
    siluT = alloc([NKC, 17], BF16)
    modv = alloc([24, 17]); G32 = alloc([NKC, 17]); G32f = alloc([NKC, 1])
    lw_small = alloc([2176], BF16)
    H0f = alloc([4, 128]); H0b = H0f
    prcarry = alloc([14, 1])
    UBASE = ptr[0]

    PB = [st.enter_context(nc.psum_tensor("pb%d" % i, [128, 512], F32)) for i in range(8)]
    NROT = 6
    pbi = [0]
    pbt_i = [0]

    def pbank():
        i = pbi[0] % NROT
        pbi[0] += 1
        return PB[i], ('pb', i)
    ZB, ZK = PB[6], ('pb', 6)
    YB, YK = PB[7], ('pb', 7)

    def PE(fn, r, w): S.op('pe', fn, r, w)
    def ACT(fn, r, w): S.op('act', fn, r, w)
    def DVE(fn, r, w): S.op('dve', fn, r, w)
    def SPD(fn, r, w): S.op('sp', fn, r, w, sem='sp', inc=16, nsem=16)
    def PQD(fn, r, w): S.op('pool', fn, r, w, sem='pq', inc=16, nsem=8)

    def mm(out, lhsT, rhs, start, stop, r, w):
        PE(lambda e: e.matmul(out, lhsT, rhs, start=start, stop=stop, skip_group_check=True), r, w)

    def tr(out, in_, ident, r, w):
        PE(lambda e: e.transpose(out, in_, ident), r, w)

    def act(out, in_, func, r, w, bias=0.0, scale=1.0):
        ACT(lambda e: e.activation(out, in_, func, bias=bias, scale=scale), r, w)

    def acopy(out, in_, r, w):
        ACT(lambda e: e.copy(out, in_), r, w)

    def vcopy(out, in_, r, w):
        DVE(lambda e: e.tensor_copy(out, in_), r, w)

    def tt(out, a, b, op, r, w):
        DVE(lambda e: e.tensor_tensor(out, a, b, op), r, w)

    def ts(out, a, s1, s2, op0, op1, r, w):
        DVE(lambda e: e.tensor_scalar(out, a, s1, s2, op0, op1), r, w)

    def stt(out, a, s, b, op0, op1, r, w):
        DVE(lambda e: e.scalar_tensor_tensor(out, a, s, b, op0, op1), r, w)

    def dma(out, in_, r, w):
        SPD(lambda e: e.dma_start(out=out, in_=in_), r, w)

    def rsq(out, in_, mulc, addc, r, w):
        ts(out, in_, mulc, addc, ALU.mult, ALU.add, r, w)
        act(out, out, AF.Ln, w, w)
        act(out, out, AF.Exp, w, w, scale=-0.5)

    def xk(ci): return [('x', ci)]
    f2 = lambda t: t.rearrange("p a b -> p (a b)")
    v4 = lambda t: t.rearrange("p (a b) -> p a b", a=4)

    def cf(name, lo=0, hi=None):
        o, n = CSTF[name]
        return cF[:, o + lo:o + (n if hi is None else hi)]

    def cb(name, lo=0, hi=None):
        o, n = CSTB[name]
        return cB[:, o + lo:o + (n if hi is None else hi)]
    SD = F32
    identF = cf('ident'); identB = cb('ident'); identS = identF if SD == F32 else identB; onesB = cb('ones'); bonesF = cf('bones'); onesF = cf('ones')

    ptr[0] = UBASE
    pstage = alloc([L, 128]); cst17 = alloc([D]); xin = alloc([2, D])
    dma(cF[:], cst[:, 0:CF_COLS], [], ['cF'])
    PQD(lambda e: e.dma_start(out=cB[:], in_=cst[:, CF_COLS:CF_COLS + CB_COLS]), [], ['cB'])
    DVE(lambda e: e.memset(pstage[:], 0.0), [], ['pstage'])
    PROW = {}
    ro = 0
    for nm, nchk in [("norm_mix", 8), ("norm_mlp", 8), ("mu_shift", 14), ("w0", 4), ("a0", 4), ("v0", 4),
                     ("k_k", 4), ("k_a", 4), ("r_k", 4), ("ln_w", 4), ("ln_b", 4), ("pool_scale", 4),
                     ("b_ada_mix", 24), ("b_ada_mlp", 24)]:
        PROW[nm] = ro
        src = W[nm]
        if nm == "v0":
            if L > 1:
                dma(pstage[ro:ro + nchk, 1:L, :], src[0:L - 1, :].rearrange("l (c p) -> c l p", p=128), ['pstage'], ['pstage'])
        else:
            dma(pstage[ro:ro + nchk, 0:L, :], src.rearrange("l (c p) -> c l p", p=128), ['pstage'], ['pstage'])
        ro += nchk
    assert ro <= 120
    dma(pstage[120:128, 0, :], W["norm_final"].rearrange("o (c p) -> (o c) p", p=128), ['pstage'], ['pstage'])
    for l in range(L):
        pb, pk = pbank()
        tr(pb[:, 0:128], pstage[:, l, :], identF, ['pstage', 'cF'], [pk])
        acopy(pcol[:, l, :], pb[:, 0:128], [pk], ['pcol'])

    def pc(l, nm, c): return pcol[:, l, PROW[nm] + c:PROW[nm] + c + 1]
    def pcs(l, nm, n): return pcol[:, l, PROW[nm]:PROW[nm] + n]

    dma(cst17[0:17, :], cc[:, :], [], ['cst17'])
    pb, pk = pbank()
    for kc in range(NKC):
        tr(pb[:, kc * 17:(kc + 1) * 17], cst17[0:17, kc * 128:(kc + 1) * 128], identF[0:17, 0:17], ['cst17', 'cF'], [pk])
    act(f2(siluT[:]), pb[:, 0:NKC * 17], AF.Silu, [pk], ['siluT'])

    for ci in range(NCH):
        src = xp[ci * 128:(ci + 1) * 128, :] if ci < NPC else xs[:, :]
        xb_ = xin[:, ci % 2, :]
        dma(xb_, src, [], [('xin', ci % 2)])
        for half in range(2):
            pb, pk = pbank()
            for q in range(4):
                c = half * 4 + q
                tr(pb[:, q * 128:(q + 1) * 128], xb_[:, c * 128:(c + 1) * 128], identF, [('xin', ci % 2), 'cF'], [pk])
            acopy(xT[:, half * 4:half * 4 + 4, ci * 128:(ci + 1) * 128], v4(pb[:, 0:512]), [pk], xk(ci))

    ring_i = [0]

    def wload(src3, a, b):
        i = ring_i[0] % 6
        ring_i[0] += 1
        dst = ring[:, i, 0:a * b].rearrange("p (a b) -> p a b", a=a)
        PQD(lambda e: e.dma_start(out=dst, in_=src3), [], [('ring', i)])
        return dst, ('ring', i)

    def wl_kc(src2, ncol):
        return wload(src2.rearrange("(kc p) n -> p kc n", p=128), src2.shape[0] // 128, ncol)

    def ada(l, which):
        wsrc = W["w_ada_" + which]
        pbm, pkm = pbank()
        for j in range(6):
            wt, wkey = wl_kc(wsrc[l, :, j * 512:(j + 1) * 512], 512)
            for q in range(4):
                ch = j * 4 + q
                for kc in range(NKC):
                    mm(pbm[:, ch * 17:(ch + 1) * 17], wt[:, kc, q * 128:(q + 1) * 128], siluT[:, kc, :],
                       kc == 0, kc == NKC - 1, [wkey, 'siluT'], [pkm])
        tt(modv[:], pbm[:, 0:408].rearrange("p (a b) -> p a b", b=17),
           pcs(l, "b_ada_" + which, 24).unsqueeze(2).to_broadcast([128, 24, 17]), ALU.add, [pkm, 'pcol'], ['mod'])
        ts(G32[:], modv[:, 8:16, :], 1.0, 32.0, ALU.add, ALU.mult, ['mod'], ['mod'])
        tt(G32[:], G32[:], pcs(l, "norm_" + which, 8).unsqueeze(2).to_broadcast([128, 8, 17]), ALU.mult, ['mod', 'pcol'], ['mod'])

    WK = {}

    def xks(t0, n): return [('x', c) for c in range(t0 // 128, (t0 + n) // 128)]

    def tiles(size):
        out = []
        t = 0
        while t < TP:
            n = min(size, TP - t)
            out.append((t, n, False))
            t += n
        out.append((TP, 128, True))
        return out

    def norm_mod_t(t0, n, samp, hdst, hkeys):
        sqr, rstd, tmpn = WK['sq'], WK['rstd'], WK['tmpn']
        pb, pk = pbank()
        for c in range(NKC):
            act(sqr[:, c % 2, 0:n], xT[:, c, t0:t0 + n], AF.Square, xks(t0, n), [('sq', c % 2)])
            mm(pb[:, 0:n], onesB, sqr[:, c % 2, 0:n], c == 0, c == NKC - 1, [('sq', c % 2), 'cB'], [pk])
        rsq(rstd[:, 0:n], pb[:, 0:n], 1.0, D * EPS, [pk], ['rstd'])
        for c in range(NKC):
            tb = tmpn[:, c % 2, 0:n]
            tk = ('tmpn', c % 2)
            if not samp:
                stt(tb, xT[:, c, t0:t0 + n], G32[:, c, 0:1], rstd[:, 0:n], ALU.mult, ALU.mult, xks(t0, n) + ['mod', 'rstd'], [tk])
                act(hdst[:, c, 0:n], tb, AF.Identity, [tk, 'mod'], hkeys, bias=modv[:, c, 0:1])
            else:
                tt(tb, xT[:, c, t0:t0 + n], rstd[:, 0:n], ALU.mult, xks(t0, n) + ['rstd'], [tk])
                t3 = tb.rearrange("p (b t) -> p b t", t=8)
                tt(t3, t3, G32[:, c, 1:17].unsqueeze(2).to_broadcast([128, 16, 8]), ALU.mult, [tk, 'mod'], [tk])
                tt(hdst[:, c, 0:n].rearrange("p (b t) -> p b t", t=8), t3,
                   modv[:, c, 1:17].unsqueeze(2).to_broadcast([128, 16, 8]), ALU.add, [tk, 'mod'], hkeys)

    def norm_mod(ci, samp, hdst, hkeys):
        norm_mod_t(ci * 128, 128, samp, hdst, hkeys)

    def resid_update_t(t0, n, samp, c, pb, pk):
        xv = xT[:, c, t0:t0 + n]
        mg = WK['mg']
        if not samp:
            stt(xv, pb[:, 0:n], modv[:, 16 + c, 0:1], xv, ALU.mult, ALU.add, [pk, 'mod'] + xks(t0, n), xks(t0, n))
        else:
            tt(mg[:, 0:128].rearrange("p (b t) -> p b t", t=8), pb[:, 0:128].rearrange("p (b t) -> p b t", t=8),
               modv[:, 16 + c, 1:17].unsqueeze(2).to_broadcast([128, 16, 8]), ALU.mult, [pk, 'mod'], ['mg'])
            tt(xv, xv, mg[:, 0:128], ALU.add, ['mg'] + xks(t0, n), xks(t0, n))

    for l in range(L):
        S.barrier()
        ptr[0] = UBASE
        WK['sq'] = alloc([2, 128], BF16); WK['rstd'] = alloc([128]); WK['tmpn'] = alloc([2, 128]); WK['mg'] = alloc([128])
        hTc = alloc([NKC, 129], BF16); hT = hTc[:, :, 1:129]
        rkv = alloc([12, 128])
        twa = alloc([128], BF16); sgg = alloc([128], BF16); t1b = alloc([128], BF16)
        gbf = alloc([4, 128], BF16); vbf = alloc([4, 128], BF16); vfb = alloc([4, 128], BF16)
        FMB = [alloc([4, 128]) for _ in range(8)]
        KR = alloc([4, 256], SD)
        Bt = alloc([4, 128], SD); Kt = alloc([4, 128], SD); BWf = alloc([4, 128], SD); KWf = alloc([4, 128], SD)
        Vtok = alloc([4, 128], SD); BWtok = alloc([4, 128], SD); KWtok = alloc([4, 128], SD)
        AT = alloc([HP, 384], SD); MA = alloc([HP, 256], SD); MB = alloc([HP, 256], SD)
        PT = alloc([HP, 128], SD)
        Zn = alloc([NPI, 128], SD); Ut = alloc([NPI, 128], SD)
        wcs = alloc([4, 16])
        pz = vbf; orp = alloc([8, 128], BF16)
        shiftT = alloc([14, 16]); ppbS = alloc([4, 16, 23])
        ppb = ppbS.rearrange("p a b c -> p (a b c)")[:, 0:576].rearrange("p (a b) -> p a b", a=4)
        A_, LW_, X1, X2, X3, X4, X5, X6 = FMB
        DIAG = X1; WcBC = X2; SCo = X4[:, :, 0:64]
        S0g = H0f; HSb = A_; BKK = LW_; BRR = X3; BIGB = BWf; BIGK = KWf
        tsh = f2(X1[:])[:, 0:272].rearrange("p (a b) -> p a b", a=2)
        xl12 = X2[:, 0:2, :]
        pq2 = f2(X1[:])[:, 0:144]; pq4 = f2(X2[:])[:, 0:144]

        ts(omm[:], pcs(l, "mu_shift", 14), -1.0, 1.0, ALU.mult, ALU.add, ['pcol'], ['omm'])
        ts(omka[:], pcs(l, "k_a", 4), -1.0, 1.0, ALU.mult, ALU.add, ['pcol'], ['omm'])
        PQD(lambda e, l=l: e.dma_start(out=lw_small[0:64, 0:512], in_=W["w2"][l]), [], ['lws'])
        PQD(lambda e, l=l: e.dma_start(out=lw_small[64:128, 0:512], in_=W["a2"][l]), [], ['lws'])
        PQD(lambda e, l=l: e.dma_start(out=lw_small[:, 512:1024], in_=W["g2"][l]), [], ['lws'])
        if l > 0:
            PQD(lambda e, l=l: e.dma_start(out=lw_small[:, 1024:1152].rearrange("p (a b) -> p a b", a=4),
                                           in_=W["v1"][l - 1].rearrange("(kc p) n -> p kc n", p=128)), [], ['lws'])
            PQD(lambda e, l=l: e.dma_start(out=lw_small[0:32, 1152:1664], in_=W["v2"][l - 1]), [], ['lws'])
        PQD(lambda e, l=l: e.dma_start(out=lw_small[:, 1664:2176].rearrange("p (g d) -> p g d", g=4),
                                       in_=W["pool_w"][l].rearrange("g c d -> c g d")), [], ['lws'])
        w2a2 = lw_small[:, 0:512]; g2w = lw_small[:, 512:1024]
        v1w = lw_small[:, 1024:1152].rearrange("p (a b) -> p a b", a=4); v2w = lw_small[0:32, 1152:1664]
        poolw = lw_small[:, 1664:2176].rearrange("p (g d) -> p g d", g=4)

        ada(l, "mix")
        DVE(lambda e: e.memset(ppb[:, :, 0:15], 0.0), [], ['ppb'])
        DVE(lambda e: e.memset(hTc[:, :, 0:1], 0.0), ['hT'], ['hT'])

        W1 = []
        for j in range(5):
            ncol = 512 if j < 4 else 256
            W1.append(wl_kc(W["w_in"][l, :, j * 512:j * 512 + ncol], ncol))

        for ci in range(NCH):
            samp = ci >= NPC
            t0 = ci * 128
            first_chunk = ci == 0
            last_chunk = ci == NPC - 1
            nb = 16 if samp else 1
            blk = 128 // nb
            mset = 'S' if samp else 'P'
            norm_mod(ci, samp, hT, ['hT'])
            if samp:
                pbs_, pks_ = pbank()
                for g0 in range(0, 14, 4):
                    gn = min(4, 14 - g0)
                    dma(X6[0:16, :, :].rearrange("p a b -> p (a b)")[:, 0:gn * 128], sshift[l, :, g0 * 128:(g0 + gn) * 128], [], ['X6'])
                    for c in range(g0, g0 + gn):
                        tr(pbs_[:, c * 16:(c + 1) * 16], f2(X6[0:16, :, :])[:, (c - g0) * 128:(c - g0 + 1) * 128],
                           identF[0:16, 0:16], ['X6', 'cF'], [pks_])
                acopy(f2(shiftT[:]), pbs_[:, 0:224], [pks_], ['shiftT'])
            for cidx in range(18):
                wt, wkey = W1[cidx // 4]
                q = cidx % 4
                pb, pk = pbank()
                for kc in range(NKC):
                    if samp:
                        mm(pb[:, 0:128], wt[:, kc, q * 128:(q + 1) * 128], hT[:, kc, :], kc == 0, kc == NKC - 1, [wkey, 'hT'], [pk])
                    else:
                        mm(pb[:, 0:129], wt[:, kc, q * 128:(q + 1) * 128], hTc[:, kc, 0:129], kc == 0, kc == NKC - 1, [wkey, 'hT'], [pk])
                if cidx >= 14:
                    g = cidx - 14
                    if not samp:
                        acopy(ppb[:, g, 15:143], pb[:, 1:129], [pk], ['ppb'])
                    else:
                        acopy(ppbS[:, g, :, 15:23], pb[:, 0:128].rearrange("p (b t) -> p b t", t=8), [pk], ['ppbS', 'ppb'])
                    continue
                c = cidx
                tb = tsh[:, c % 2, :]
                tk = 'X1'
                mu = pc(l, "mu_shift", c)
                dst = rkv[:, c, :] if c < 12 else xl12[:, c - 12, :]
                dkey = 'rkv' if c < 12 else 'X2'
                if not samp:
                    act(tb[:, 0:129], pb[:, 0:129], AF.Identity, [pk, 'pcol'], [tk], scale=mu)
                    if last_chunk:
                        acopy(prcarry[:, c, :], pb[:, 128:129], [pk], ['prcarry'])
                    stt(dst, pb[:, 1:129], omm[:, c:c + 1], tb[:, 0:128], ALU.mult, ALU.add, [pk, 'omm', tk], [dkey])
                else:
                    p3 = pb[:, 0:128].rearrange("p (b t) -> p b t", t=8)
                    t3 = tb[:, 0:128].rearrange("p (b t) -> p b t", t=8)
                    act(t3[:, :, 1:8], p3[:, :, 0:7], AF.Identity, [pk, 'pcol'], [tk], scale=mu)
                    act(t3[:, :, 0:1], shiftT[:, c, :].unsqueeze(2), AF.Identity, ['shiftT', 'pcol'], [tk], scale=mu)
                    acopy(f2(X5[:])[:, c * 16:(c + 1) * 16].unsqueeze(2), p3[:, :, 7:8], [pk], ['X5'])
                    stt(dst, pb[:, 0:128], omm[:, c:c + 1], tb[:, 0:128], ALU.mult, ALU.add, [pk, 'omm', tk], [dkey])
            if not samp and not last_chunk:
                acopy(hTc[:, :, 0:1], hTc[:, :, 128:129], ['hT'], ['hT'])
            if last_chunk:
                pb, pk = pbank()
                tr(pb[0:14, 0:128], f2(prcarry[:]), identF, ['prcarry', 'cF'], [pk])
                acopy(f2(X6[0:14, :, :])[:, 0:128], pb[0:14, 0:128], [pk], ['X6'])
                dma(nshp[l].rearrange("(c p) -> c p", p=128), f2(X6[0:14, :, :])[:, 0:128], ['X6'], [])
            if samp:
                for g0 in range(0, 14, 4):
                    gn = min(4, 14 - g0)
                    pbx, pkx = pbank()
                    for c in range(g0, g0 + gn):
                        tr(pbx[0:16, (c - g0) * 128:(c - g0 + 1) * 128], f2(X5[:])[:, c * 16:(c + 1) * 16], identF, ['X5', 'cF'], [pkx])
                    ob = [X3, X4][(g0 // 4) % 2]
                    okey = ['X3', 'X4'][(g0 // 4) % 2]
                    acopy(f2(ob[0:16, :, :])[:, 0:gn * 128], pbx[0:16, 0:gn * 128], [pkx], [okey])
                    dma(nshs[l, :, g0 * 128:(g0 + gn) * 128], f2(ob[0:16, :, :])[:, 0:gn * 128], [okey], [])
            act(twa[0:64, :], xl12[0:64, 0, :], AF.Tanh, ['X2'], ['twa'])
            acopy(twa[64:128, :], xl12[64:128, 0, :], ['X2'], ['twa'])
            act(sgg[:], xl12[:, 1, :], AF.Sigmoid, ['X2'], ['sgg'])
            r_c = rkv[:, 0:4, :]; k_c = rkv[:, 4:8, :]; v_c = rkv[:, 8:12, :]
            for p in range(4):
                pb, pk = pbank()
                mm(pb[:, 0:128], w2a2[0:64, p * 128:(p + 1) * 128], twa[0:64, :], True, True, ['lws', 'twa'], [pk])
                act(LW_[:, p, :], pb[:, 0:128], AF.Sigmoid, [pk, 'pcol'], ['LW'], bias=pc(l, "w0", p))
                pb, pk = pbank()
                mm(pb[:, 0:128], w2a2[64:128, p * 128:(p + 1) * 128], twa[64:128, :], True, True, ['lws', 'twa'], [pk])
                act(A_[:, p, :], pb[:, 0:128], AF.Sigmoid, [pk, 'pcol'], ['A'], bias=pc(l, "a0", p))
                pb, pk = pbank()
                mm(pb[:, 0:128], g2w[:, p * 128:(p + 1) * 128], sgg[:], True, True, ['lws', 'sgg'], [pk])
                acopy(gbf[:, p, :], pb[:, 0:128], [pk], ['gbf'])
            ts(LW_[:], LW_[:], DEC_C, None, ALU.mult, ALU.bypass, ['LW'], ['LW'])
            bc4 = lambda col: col.unsqueeze(2).to_broadcast([128, 4, 128])
            d0 = cf('rmS') if samp else onesF
            for p in range(4):
                DVE(lambda e, p=p, d0=d0: e.tensor_tensor_scan(X2[:, p, :], d0, LW_[:, p, :], 0.0, ALU.mult, ALU.add),
                    ['LW', 'cF'], ['X2'])
            tt(X1[:], X2[:], LW_[:], ALU.subtract, ['X2', 'LW'], ['X1'])
            act(X1[:], X1[:], AF.Exp, ['X1'], ['X1'])
            act(LW_[:], X2[:], AF.Exp, ['X2'], ['LW'])
            ein4 = LW_[:].rearrange("p a (b t) -> p a b t", t=blk)
            vcopy(wcs[:, :, 0:nb].unsqueeze(3), ein4[:, :, :, blk - 1:blk], ['LW'], ['wcs'])
            act(X2[:], X2[:], AF.Exp, ['X2'], ['X2'], scale=-1.0)
            if l == 0:
                acopy(vbf[:], v_c, ['rkv'], ['vbf'])
                dma(vfd[:, :, t0:t0 + 128], vbf[:], ['vbf'], [('vfd', ci)])
            else:
                dma(vfb[:], vfd[:, :, t0:t0 + 128], [('vfd', ci)], ['vfb'])
                acopy(vbf[:], v_c, ['rkv'], ['vbf'])
                pb, pk = pbank()
                for p in range(4):
                    mm(pb[0:32, 0:128], v1w[:, p, :], vbf[:, p, :], p == 0, p == 3, ['lws', 'vbf'], [pk])
                acopy(t1b[0:32, :], pb[0:32, 0:128], [pk], ['t1b'])
                for p in range(4):
                    pb, pk = pbank()
                    mm(pb[:, 0:128], v2w[:, p * 128:(p + 1) * 128], t1b[0:32, :], True, True, ['lws', 't1b'], [pk])
                    act(X3[:, p, :], pb[:, 0:128], AF.Sigmoid, [pk, 'pcol'], ['X3'], bias=pc(l, "v0", p))
                tt(X4[:], vfb[:], v_c, ALU.subtract, ['vfb', 'rkv'], ['X4'])
                tt(X4[:], X4[:], X3[:], ALU.mult, ['X4', 'X3'], ['X4'])
                tt(v_c, v_c, X4[:], ALU.add, ['rkv', 'X4'], ['rkv'])

            tt(BWf[:], k_c, bc4(pcs(l, "k_k", 4)), ALU.mult, ['rkv', 'pcol'], ['BWf'])
            tt(KWf[:], BWf[:], BWf[:], ALU.mult, ['BWf'], ['KWf'])
            pb, pk = pbank()
            mm(pb[:, 0:512], bonesF, f2(KWf[:]), True, True, ['KWf', 'cF'], [pk])
            rsq(KWf[:], v4(pb[:, 0:512]), 1.0, 1e-12, [pk], ['KWf'])
            tt(X4[:], A_[:], bc4(pcs(l, "k_a", 4)), ALU.mult, ['A', 'pcol'], ['X4'])
            tt(X4[:], X4[:], bc4(omka[:, 0:4]), ALU.add, ['X4', 'omm'], ['X4'])
            tt(X5[:], k_c, X4[:], ALU.mult, ['rkv', 'X4'], ['X5'])
            tt(X4[:], r_c, X5[:], ALU.mult, ['rkv', 'X5'], ['X4'])
            tt(X4[:], X4[:], bc4(pcs(l, "r_k", 4)), ALU.mult, ['X4', 'pcol'], ['X4'])
            pb, pk = pbank()
            mm(pb[:, 0:512], bonesF, f2(X4[:]), True, True, ['X4', 'cF'], [pk])
            tt(X3[:], BWf[:], KWf[:], ALU.mult, ['BWf', 'KWf'], ['X3'])
            tt(X6[:], v4(pb[:, 0:512]), v_c, ALU.mult, [pk, 'rkv'], ['X6'])
            tt(X4[:], X3[:], A_[:], ALU.mult, ['X3', 'A'], ['X4'])
            tt(KR[:, :, 0:128], X3[:], X1[:], ALU.mult, ['X3', 'X1'], ['KR'])
            tt(KR[:, :, 128:256], r_c, LW_[:], ALU.mult, ['rkv', 'LW'], ['KR'])
            wcb = wcs[:, :, 0:nb].unsqueeze(3).to_broadcast([128, 4, nb, blk])
            b4 = lambda t: t.rearrange("p a (b t) -> p a b t", t=blk)
            tt(X3[:], X4[:], X2[:], ALU.mult, ['X4', 'X2'], ['X3'])
            acopy(Bt[:], X3[:], ['X3'], ['Bt'])
            tt(b4(BWf[:]), b4(X3[:]), wcb, ALU.mult, ['X3', 'wcs'], ['BWf'])
            tt(X4[:], X5[:], X2[:], ALU.mult, ['X5', 'X2'], ['X4'])
            acopy(Kt[:], X4[:], ['X4'], ['Kt'])
            tt(b4(KWf[:]), b4(X4[:]), wcb, ALU.mult, ['X4', 'wcs'], ['KWf'])
            for (srcb, dstb, sk, dk) in ((v_c, Vtok, 'rkv', 'Vtok'), (BWf, BWtok, 'BWf', 'BWtok'), (KWf, KWtok, 'KWf', 'KWtok')):
                pbb, pkb = pbank()
                for p in range(4):
                    tr(pbb[:, p * 128:(p + 1) * 128], srcb[:, p, :], identS, [sk, 'cF', 'cB'], [pkb])
                acopy(f2(dstb[:]), pbb[:, 0:512], [pkb], [dk])

            if samp:
                DVE(lambda e: e.memset(S0g[:], 0.0), ['H0f'], ['H0f'])
                DVE(lambda e: e.memset(BKK[:], 0.0), ['LW'], ['LW'])
                DVE(lambda e: e.memset(BRR[:], 0.0), ['X3'], ['X3'])
            for hf in range(4 // NPI):
                for q in range(HP):
                    hd = hf * HP + q
                    p, hh = hd // 2, hd % 2
                    hs = slice(hh * 64, hh * 64 + 64)
                    pb, pk = pbank()
                    mm(pb[:, 0:256], Bt[hs, p, :], KR[hs, p, :], True, True, ['Bt', 'KR'], [pk])
                    mm(pb[:, 256:512], Kt[hs, p, :], KR[hs, p, :], True, True, ['Kt', 'KR'], [pk])
                    tt(MB[:, q, 128:256], pb[:, 0:128], cb('mA' + mset, 0, 128), ALU.mult, [pk, 'cB'], ['MB'])
                    tt(AT[:, q, :], pb[:, 128:512], cb('mA' + mset, 128, 512), ALU.mult, [pk, 'cB'], ['AT'])
                pbA, pkA = pbank()
                pbB, pkB = pbank()
                for q in range(HP):
                    hd = hf * HP + q
                    p, hh = hd // 2, hd % 2
                    hs = slice(hh * 64, hh * 64 + 64)
                    pbx, pkx = (pbA, pkA) if hh == 0 else (pbB, pkB)
                    mm(pbx[:, (q // 2) * 128:(q // 2 + 1) * 128], KR[hs, p, 0:128], Bt[hs, p, :], True, True, ['KR', 'Bt'], [pkx])
                for hh, (pbx, pkx) in enumerate(((pbA, pkA), (pbB, pkB))):
                    tt(MB[:, hh:HP:2, 0:128], pbx[:, 0:NPI * 128].rearrange("p (a b) -> p a b", a=NPI),
                       cb('mL' + mset).unsqueeze(1).to_broadcast([128, NPI, 128]), ALU.mult, [pkx, 'cB'], ['MB'])
                tt(PT[:], MB[:, :, 128:256], identF.unsqueeze(1).to_broadcast([128, HP, 128]), ALU.add, ['MB', 'cF'], ['PT'])
                if hf == 0 and not samp:
                    PA = bass.AP(tensor=X1.tensor, offset=X1.offset, ap=[list(X1.ap[0]), [144, 4], [1, 144]])
                    PBs = bass.AP(tensor=A_.tensor, offset=A_.offset, ap=[list(A_.ap[0]), [144, 4], [1, 144]])
                    ka, kb = ['X1', 'X2'], ['A', 'LW']
                    tt(PA[:, :, 1:143], ppb[:, :, 1:143], ppb[:, :, 0:142], ALU.add, ['ppb'], ka)
                    tt(PBs[:, 1:4, 3:143], PA[:, 1:4, 3:143], PA[:, 1:4, 1:141], ALU.add, ka, kb)
                    tt(PA[:, 2:4, 7:143], PBs[:, 2:4, 7:143], PBs[:, 2:4, 3:139], ALU.add, kb + ka, ka)
                    tt(PBs[:, 3:4, 15:143], PA[:, 3:4, 15:143], PA[:, 3:4, 7:135], ALU.add, ka + kb, kb)
                    psrc = [PA[:, 0, :], PBs[:, 1, :], PA[:, 2, :], PBs[:, 3, :]]
                    for g in range(4):
                        if first_chunk:
                            mg = WK['mg']
                            stt(mg[:], psrc[g][:, 15:143], 1.0 / WINS[g], ppb[:, g, 15:143], ALU.mult, ALU.subtract, ka + kb + ['ppb'], ['mg'])
                            tt(mg[:, 0:16], psrc[g][:, 15:31], cf('pcorr')[:, g * 16:(g + 1) * 16], ALU.mult, ka + kb + ['cF', 'mg'], ['mg'])
                            tt(mg[:, 0:16], mg[:, 0:16], ppb[:, g, 15:31], ALU.subtract, ['mg', 'ppb'], ['mg'])
                            vcopy(pz[:, g, :], mg[:], ['mg'], ['vbf'])
                        else:
                            stt(pz[:, g, :], psrc[g][:, 15:143], 1.0 / WINS[g], ppb[:, g, 15:143], ALU.mult, ALU.subtract, ka + kb + ['ppb'], ['vbf'])
                    if not last_chunk:
                        vcopy(PA[:, :, 0:15], ppb[:, :, 128:143], ['ppb'] + ka, ka)
                        vcopy(ppb[:, :, 0:15], PA[:, :, 0:15], ka + ['ppb'], ['ppb'])
                nlev = 2 if samp else 6
                curM = lambda q: MB[:, q, 0:128]
                curMT = lambda q: MB[:, q, 128:256]
                curk = ['MB']
                for lev in range(nlev):
                    nxt, nk = (MA, 'MA') if lev % 2 == 0 else (MB, 'MB')
                    lastlev = lev == nlev - 1
                    for h2 in range(NPI):
                        pb, pk = pbank()
                        for qq in range(2):
                            q = h2 * 2 + qq
                            mm(pb[:, qq * 256:qq * 256 + 128], curMT(q), curM(q), True, True, curk, [pk])
                            if not lastlev:
                                mm(pb[:, qq * 256 + 128:qq * 256 + 256], curM(q), curMT(q), True, True, curk, [pk])
                        acopy(nxt[:, h2 * 2:h2 * 2 + 2, :], pb[:, 0:512].rearrange("p (a b) -> p a b", a=2), [pk], [nk])
                    pb, pk = pbank()
                    for q in range(HP):
                        mm(pb[:, q * 128:(q + 1) * 128], nxt[:, q, 0:128], PT[:, q, :], True, True, [nk, 'PT'], [pk])
                    tt(PT[:], PT[:], pb[:, 0:HP * 128].rearrange("p (a b) -> p a b", a=HP), ALU.add, [pk, 'PT'], ['PT'])
                    curM = lambda q, nxt=nxt: nxt[:, q, 0:128]
                    curMT = lambda q, nxt=nxt: nxt[:, q, 128:256]
                    curk = [nk]

                if hf == 0 and not samp:
                    for g in range(4):
                        pb, pk = pbank()
                        mm(pb[:, 0:128], poolw[:, g, :], pz[:, g, :], True, True, ['lws', 'vbf'], [pk])
                        act(orp[:, 4 + g, :], pb[:, 0:128], AF.Identity, [pk, 'pcol'], ['orp'], scale=pc(l, "pool_scale", g))
                if not samp:
                    for pp_ in range(NPI):
                        p = hf * NPI + pp_
                        if not first_chunk:
                            mm(ZB[:, pp_ * 128:(pp_ + 1) * 128], KR[:, p, 0:128], H0b[:, p, :], True, False, ['KR', 'H0f'], [ZK])
                        for hh in range(2):
                            q = pp_ * 2 + hh
                            mm(ZB[:, pp_ * 128 + hh * 64:pp_ * 128 + hh * 64 + 64], AT[:, q, 128:256],
                               Vtok[:, p, hh * 64:hh * 64 + 64], first_chunk, True, ['AT', 'Vtok'], [ZK])
                    act(f2(Zn[:]), ZB[:, 0:NPI * 128], AF.Copy, [ZK], ['Zn'], scale=-1.0)
                    pbu, pku = pbank()
                    for q in range(HP):
                        pp_, hh = q // 2, q % 2
                        mm(pbu[:, q * 64:(q + 1) * 64], PT[:, q, :], Zn[:, pp_, hh * 64:hh * 64 + 64], True, True, ['PT', 'Zn'], [pku])
                    acopy(f2(Ut[:]), pbu[:, 0:NPI * 128], [pku], ['Ut'])
                    for pp_ in range(NPI):
                        p = hf * NPI + pp_
                        if not first_chunk:
                            mm(YB[:, pp_ * 128:(pp_ + 1) * 128], H0b[:, p, :], KR[:, p, 128:256], True, False, ['H0f', 'KR'], [YK])
                        for hh in range(2):
                            q = pp_ * 2 + hh
                            hs = slice(hh * 64, hh * 64 + 64)
                            mm(YB[hs, pp_ * 128:(pp_ + 1) * 128], Ut[:, pp_, hh * 64:hh * 64 + 64], AT[:, q, 0:128],
                               first_chunk, False, ['Ut', 'AT'], [YK])
                            mm(YB[hs, pp_ * 128:(pp_ + 1) * 128], Vtok[:, p, hh * 64:hh * 64 + 64], AT[:, q, 256:384],
                               False, True, ['Vtok', 'AT'], [YK])
                    acopy(f2(X5[:, hf * NPI:(hf + 1) * NPI, :]), YB[:, 0:NPI * 128], [YK], ['X5'])
                    pbh, pkh = pbank()
                    for pp_ in range(NPI):
                        p = hf * NPI + pp_
                        mm(pbh[:, pp_ * 128:(pp_ + 1) * 128], BWtok[:, p, :], Ut[:, pp_, :], True, False, ['BWtok', 'Ut'], [pkh])
                        mm(pbh[:, pp_ * 128:(pp_ + 1) * 128], KWtok[:, p, :], Vtok[:, p, :], False, True, ['KWtok', 'Vtok'], [pkh])
                    hsl = slice(hf * NPI, (hf + 1) * NPI)
                    tt(X3[:, 0:NPI, :], pbh[:, 0:NPI * 128].rearrange("p (a b) -> p a b", a=NPI),
                       bonesF.unsqueeze(1).to_broadcast([128, NPI, 128]), ALU.mult, [pkh, 'cF'], ['X3'])
                    if first_chunk:
                        vcopy(H0f[:, hsl, :], X3[:, 0:NPI, :], ['X3'], ['H0f'])
                    else:
                        tt(H0f[:, hsl, :], H0f[:, hsl, :], wcs[:, hsl, 0:1].to_broadcast([128, NPI, 128]), ALU.mult, ['H0f', 'wcs'], ['H0f'])
                        tt(H0f[:, hsl, :], H0f[:, hsl, :], X3[:, 0:NPI, :], ALU.add, ['H0f', 'X3'], ['H0f'])
                    if last_chunk:
                        pbs, pks = pbank()
                        for pp_ in range(NPI):
                            tr(pbs[:, pp_ * 128:(pp_ + 1) * 128], H0f[:, hf * NPI + pp_, :], identF, ['H0f', 'cF'], [pks])
                        for hh in range(2):
                            hs = slice(hh * 64, hh * 64 + 64)
                            acopy(SCo[hs, 0:NPI, :], pbs[hs, 0:NPI * 128].rearrange("p (a b) -> p a b", a=NPI)[:, :, hh * 64:hh * 64 + 64], [pks], ['X4'])
                        dma(nwkvp[l, hf * HP:hf * HP + HP].rearrange("(p hh) v n -> (hh v) p n", hh=2), SCo[:, 0:NPI, :], ['X4'], ['X4'])
                else:
                    for pp_ in range(NPI):
                        p = hf * NPI + pp_
                        for g in range(4):
                            for hh in range(2):
                                hs = slice(hh * 64, hh * 64 + 64)
                                dma(S0g[hs, :, hh * 64:hh * 64 + 64],
                                    swkv[l, g * 4:g * 4 + 4, 2 * p + hh, :, :].rearrange("b v k -> v b k"), ['H0f'], ['H0f'])
                            pb, pk = pbank()
                            for j in range(4):
                                tr(pb[:, j * 128:(j + 1) * 128], S0g[:, j, :], identF, ['H0f', 'cF'], [pk])
                            acopy(f2(HSb[:]), pb[:, 0:512], [pk], ['A'])
                            bkk_diag = bass.AP(tensor=BKK.tensor, offset=BKK.offset + 32 * g,
                                               ap=[list(BKK.ap[0]), [136, 4], [1, 8]])
                            vcopy(bkk_diag, KR[:, p, g * 32:g * 32 + 32].rearrange("q (b t) -> q b t", t=8), ['KR', 'LW'], ['LW'])
                            brr_diag = bass.AP(tensor=BRR.tensor, offset=BRR.offset + 32 * g,
                                               ap=[list(BRR.ap[0]), [136, 4], [1, 8]])
                            vcopy(brr_diag, KR[:, p, 128 + g * 32:128 + g * 32 + 32].rearrange("q (b t) -> q b t", t=8), ['KR', 'X3'], ['X3'])
                            for j in range(4):
                                b_ = g * 4 + j
                                mm(ZB[:, 0:128], BKK[:, j, :], HSb[:, j, :], b_ == 0, False, ['LW', 'A'], [ZK])
                                mm(YB[:, 0:128], HSb[:, j, :], BRR[:, j, :], b_ == 0, False, ['A', 'X3'], [YK])
                            DVE(lambda e, bkk_diag=bkk_diag: e.memset(bkk_diag, 0.0), ['LW'], ['LW'])
                            DVE(lambda e, brr_diag=brr_diag: e.memset(brr_diag, 0.0), ['X3'], ['X3'])
                        for hh in range(2):
                            q = pp_ * 2 + hh
                            mm(ZB[:, hh * 64:hh * 64 + 64], AT[:, q, 128:256], Vtok[:, p, hh * 64:hh * 64 + 64], False, True, ['AT', 'Vtok'], [ZK])
                        act(Zn[:, 0, :], ZB[:, 0:128], AF.Copy, [ZK], ['Zn'], scale=-1.0)
                        pbu, pku = pbank()
                        for hh in range(2):
                            q = pp_ * 2 + hh
                            mm(pbu[:, hh * 64:hh * 64 + 64], PT[:, q, :], Zn[:, 0, hh * 64:hh * 64 + 64], True, True, ['PT', 'Zn'], [pku])
                        acopy(Ut[:, 0, :], pbu[:, 0:128], [pku], ['Ut'])
                        for hh in range(2):
                            q = pp_ * 2 + hh
                            hs = slice(hh * 64, hh * 64 + 64)
                            mm(YB[hs, 0:128], Ut[:, 0, hh * 64:hh * 64 + 64], AT[:, q, 0:128], False, False, ['Ut', 'AT'], [YK])
                            mm(YB[hs, 0:128], Vtok[:, p, hh * 64:hh * 64 + 64], AT[:, q, 256:384], False, True, ['Vtok', 'AT'], [YK])
                        acopy(X5[:, p, :], YB[:, 0:128], [YK], ['X5'])
                        smk = cb('seqm')
                        for g in range(4):
                            for hh in range(2):
                                hs = slice(hh * 64, hh * 64 + 64)
                                dma(S0g[hs, :, hh * 64:hh * 64 + 64],
                                    swkv[l, g * 4:g * 4 + 4, 2 * p + hh, :, :].rearrange("b v k -> v b k"), ['H0f'], ['H0f'])
                            sm4 = smk[:, g * 4:g * 4 + 4].unsqueeze(2).to_broadcast([128, 4, 128])
                            tt(BIGB[:], BWtok[:, p, :].unsqueeze(1).to_broadcast([128, 4, 128]), sm4, ALU.mult, ['BWtok', 'cB'], ['BWf'])
                            tt(BIGK[:], KWtok[:, p, :].unsqueeze(1).to_broadcast([128, 4, 128]), sm4, ALU.mult, ['KWtok', 'cB'], ['KWf'])
                            tt(DIAG[:], identF.unsqueeze(1).to_broadcast([128, 4, 128]),
                               wcs[:, p, g * 4:g * 4 + 4].unsqueeze(2).to_broadcast([128, 4, 128]), ALU.mult, ['cF', 'wcs'], ['X1'])
                            pb, pk = pbank()
                            mm(pb[:, 0:512], onesF, f2(DIAG[:]), True, True, ['X1', 'cF'], [pk])
                            acopy(f2(WcBC[:]), pb[:, 0:512], [pk], ['X2'])
                            tt(WcBC[:], WcBC[:], S0g[:], ALU.mult, ['X2', 'H0f'], ['X2'])
                            pbs, pks = pbank()
                            mm(pbs[:, 0:512], Ut[:, 0, :], f2(BIGB[:]), True, False, ['Ut', 'BWf'], [pks])
                            mm(pbs[:, 0:512], Vtok[:, p, :], f2(BIGK[:]), False, True, ['Vtok', 'KWf'], [pks])
                            tt(WcBC[:], WcBC[:], v4(pbs[:, 0:512]), ALU.add, ['X2', pks], ['X2'])
                            for hh in range(2):
                                hs = slice(hh * 64, hh * 64 + 64)
                                vcopy(SCo[hs, :, :], WcBC[hs, :, hh * 64:hh * 64 + 64], ['X2', 'X4'], ['X4'])
                            dma(nwkvs[l, g * 4:g * 4 + 4, 2 * p:2 * p + 2, :, :].rearrange("b hh v n -> (hh v) b n"), SCo[:], ['X4'], ['X4'])

            pb, pk = pbank()
            mm(pb[:, 0:512], bonesF, f2(X5[:]), True, True, ['X5', 'cF'], [pk])
            ts(X1[:], v4(pb[:, 0:512]), 1.0 / 64, None, ALU.mult, ALU.bypass, [pk], ['X1'])
            tt(X5[:], X5[:], X1[:], ALU.subtract, ['X5', 'X1'], ['X5'])
            tt(X2[:], X5[:], X5[:], ALU.mult, ['X5'], ['X2'])
            pb, pk = pbank()
            mm(pb[:, 0:512], bonesF, f2(X2[:]), True, True, ['X2', 'cF'], [pk])
            rsq(X1[:], v4(pb[:, 0:512]), 1.0 / 64, GN_EPS, [pk], ['X1'])
            tt(X5[:], X5[:], X1[:], ALU.mult, ['X5', 'X1'], ['X5'])
            tt(X5[:], X5[:], bc4(pcs(l, "ln_w", 4)), ALU.mult, ['X5', 'pcol'], ['X5'])
            tt(X5[:], X5[:], bc4(pcs(l, "ln_b", 4)), ALU.add, ['X5', 'pcol'], ['X5'])
            tt(X5[:], X5[:], X6[:], ALU.add, ['X5', 'X6'], ['X5'])
            tt(orp[:, 0:4, :], X5[:], gbf[:], ALU.mult, ['X5', 'gbf'], ['orp'])

            if not samp:
                if last_chunk:
                    pb, pk = pbank()
                    for g in range(4):
                        tr(pb[0:15, g * 128:(g + 1) * 128], ppb[:, g, 128:143], identF, ['ppb', 'cF'], [pk])
                    acopy(f2(X1[0:15, :, :]), pb[0:15, 0:512], [pk], ['X1'])
                    dma(npoolp[l], f2(X1[0:15, :, :]), ['X1'], ['X1'])
                pass
            else:
                spv = spool[l].rearrange("b r c -> (b r) c")
                for half in range(2):
                    r0, rn = (0, 128) if half == 0 else (128, 112)
                    stg = [X1, X2][half]
                    dma(f2(stg[0:rn, :, :]), spv[r0:r0 + rn, :], [], [['X1', 'X2'][half]])
                    pb, pk = pbank()
                    for g in range(4):
                        tr(pb[:, g * 128:g * 128 + rn], f2(stg[0:rn, :, :])[:, g * 128:(g + 1) * 128], identF[0:rn, 0:rn],
                           [['X1', 'X2'][half], 'cF'], [pk])
                    acopy(f2(X3[:]) if half == 0 else f2(X4[:]), pb[:, 0:512], [pk], [['X3', 'X4'][half]])
                for g in range(4):
                    vcopy(ppbS[:, g, 0:8, 0:15], X3[:, g, 0:120].rearrange("p (b r) -> p b r", r=15), ['X3'], ['ppbS', 'ppb'])
                    vcopy(ppbS[:, g, 8, 0:8], X3[:, g, 120:128], ['X3'], ['ppbS', 'ppb'])
                    vcopy(ppbS[:, g, 8, 8:15], X4[:, g, 0:7], ['X4'], ['ppbS', 'ppb'])
                    vcopy(ppbS[:, g, 9:16, 0:15], X4[:, g, 7:112].rearrange("p (b r) -> p b r", r=15), ['X4'], ['ppbS', 'ppb'])
                for half in range(2):
                    r0, rn = (0, 128) if half == 0 else (128, 112)
                    stg = [X3, X4][half]
                    for g in range(4):
                        if half == 0:
                            vcopy(stg[:, g, 0:120].rearrange("p (b r) -> p b r", r=15), ppbS[:, g, 0:8, 8:23], ['ppbS'], [['X3', 'X4'][half]])
                            vcopy(stg[:, g, 120:128], ppbS[:, g, 8, 8:16], ['ppbS'], [['X3', 'X4'][half]])
                        else:
                            vcopy(stg[:, g, 0:7], ppbS[:, g, 8, 16:23], ['ppbS'], [['X3', 'X4'][half]])
                            vcopy(stg[:, g, 7:112].rearrange("p (b r) -> p b r", r=15), ppbS[:, g, 9:16, 8:23], ['ppbS'], [['X3', 'X4'][half]])
                    pb, pk = pbank()
                    for g in range(4):
                        tr(pb[0:rn, g * 128:(g + 1) * 128], stg[:, g, 0:rn], identF, [['X3', 'X4'][half], 'cF'], [pk])
                    ob = [X1, X2][half]
                    acopy(f2(ob[0:rn, :, :]), pb[0:rn, 0:512], [pk], [['X1', 'X2'][half]])
                    dma(npools[l, r0:r0 + rn, :], f2(ob[0:rn, :, :]), [['X1', 'X2'][half]], [['X1', 'X2'][half]])
                for g in range(4):
                    A0 = ppbS[:, g, :, :]
                    cur = X3[:].rearrange("p a b -> p (a b)")[:, 0:368].rearrange("p (b r) -> p b r", r=23)
                    oth = X4[:].rearrange("p a b -> p (a b)")[:, 0:368].rearrange("p (b r) -> p b r", r=23)
                    tt(cur[:, :, 1:23], A0[:, :, 1:23], A0[:, :, 0:22], ALU.add, ['ppbS', 'X3', 'X4'], ['X3', 'X4'])
                    span = 2
                    while span < WINS[g]:
                        lo = 2 * span - 1
                        tt(oth[:, :, lo:23], cur[:, :, lo:23], cur[:, :, lo - span:23 - span], ALU.add, ['X3', 'X4'], ['X3', 'X4'])
                        cur, oth = oth, cur
                        span *= 2
                    mg = WK['mg']
                    stt(mg[:].rearrange("p (b t) -> p b t", t=8), cur[:, :, 15:23], 1.0 / WINS[g], ppbS[:, g, :, 15:23],
                        ALU.mult, ALU.subtract, ['X3', 'X4', 'ppbS'], ['mg'])
                    acopy(pz[:, g, :], mg[:], ['mg'], ['vbf'])
            if samp:
                for g in range(4):
                    pb, pk = pbank()
                    mm(pb[:, 0:128], poolw[:, g, :], pz[:, g, :], True, True, ['lws', 'vbf'], [pk])
                    act(orp[:, 4 + g, :], pb[:, 0:128], AF.Identity, [pk, 'pcol'], ['orp'], scale=pc(l, "pool_scale", g))
            dma(orpd[:, :, t0:t0 + 128], orp[:], ['orp'], [('orpd', ci)])

        S.barrier()
        ptr[0] = UBASE
        T2 = 512
        WK['sq'] = alloc([2, T2], BF16); WK['rstd'] = alloc([T2]); WK['tmpn'] = alloc([2, T2]); WK['mg'] = alloc([128])
        hT2 = alloc([NKC, T2], BF16)
        mrg = alloc([NKC, NTOK], BF16)
        orp2 = alloc([1, 8, T2], BF16)
        g16 = alloc([16, T2], BF16)
        tmpb = WK['tmpn'][:, 0:1, :]
        W2 = [wl_kc(W["w_in"][l, :, 2304 + j * 512:2304 + (j + 1) * 512], 512) for j in range(4)]
        wbr, wbrk = wl_kc(W["w_br_rwkv"][l], 1024)
        wbp, wbpk = wl_kc(W["w_br_pool"][l], 1024)
        tl2 = tiles(T2)
        norm_mod_t(tl2[0][0], tl2[0][1], tl2[0][2], hT2, ['hT2'])
        for ti, (t0, n, samp) in enumerate(tl2):
            ob = orp2[:, 0, :, 0:n]
            ok = ('orp2', 0)
            dma(ob, orpd[:, :, t0:t0 + n], [('orpd', c) for c in range(t0 // 128, (t0 + n) // 128)], [ok])
            for cg in range(16):
                wt, wkey = W2[cg // 4]
                q = cg % 4
                pb, pk = pbank()
                for kc in range(NKC):
                    mm(pb[:, 0:n], wt[:, kc, q * 128:(q + 1) * 128], hT2[:, kc, 0:n], kc == 0, kc == NKC - 1, [wkey, 'hT2'], [pk])
                act(g16[:, cg, 0:n], pb[:, 0:n], AF.Sigmoid, [pk], ['g16'])
            if ti + 1 < len(tl2):
                norm_mod_t(tl2[ti + 1][0], tl2[ti + 1][1], tl2[ti + 1][2], hT2, ['hT2'])
            for c in range(8):
                pb, pk = pbank()
                for kc in range(4):
                    mm(pb[:, 0:n], wbr[:, kc, c * 128:(c + 1) * 128], ob[:, kc, :], kc == 0, kc == 3, [wbrk, ok], [pk])
                tt(tmpb[:, 0, 0:n], pb[:, 0:n], g16[:, c, 0:n], ALU.mult, [pk, 'g16'], [('tmpn', 0)])
                pb2, pk2 = pbank()
                for kc in range(4):
                    mm(pb2[:, 0:n], wbp[:, kc, c * 128:(c + 1) * 128], ob[:, 4 + kc, :], kc == 0, kc == 3, [wbpk, ok], [pk2])
                tt(mrg[:, c, t0:t0 + n], pb2[:, 0:n], g16[:, 8 + c, 0:n], ALU.mult, [pk2, 'g16'], [('mrg', t0)])
                tt(mrg[:, c, t0:t0 + n], mrg[:, c, t0:t0 + n], tmpb[:, 0, 0:n], ALU.add, [('tmpn', 0), ('mrg', t0)], [('mrg', t0)])
        wo = [wl_kc(W["w_out"][l, :, j * 512:(j + 1) * 512], 512) for j in range(2)]
        for (t0, n, samp) in tiles(T2):
            for c in range(8):
                wt, wkey = wo[c // 4]
                q = c % 4
                pb, pk = pbank()
                for kc in range(NKC):
                    mm(pb[:, 0:n], wt[:, kc, q * 128:(q + 1) * 128], mrg[:, kc, t0:t0 + n], kc == 0, kc == NKC - 1, [wkey, ('mrg', t0)], [pk])
                resid_update_t(t0, n, samp, c, pb, pk)

        S.barrier()
        ptr[0] = UBASE
        T3 = 512
        WK['sq'] = alloc([2, T3], BF16); WK['rstd'] = alloc([T3]); WK['tmpn'] = alloc([2, T3]); WK['mg'] = alloc([128])
        hTm = alloc([NKC, NTOK], BF16)
        rl = alloc([2, T3]); r2 = alloc([8, T3], BF16)
        ada(l, "mlp")
        tl3 = tiles(T3)
        norm_mod_t(tl3[0][0], tl3[0][1], tl3[0][2], hTm[:, :, tl3[0][0]:tl3[0][0] + tl3[0][1]], [('hTm', tl3[0][0])])
        for qd in range(4):
            w1 = [wl_kc(W["w_ff1"][l, :, qd * 1024 + j * 512:qd * 1024 + (j + 1) * 512], 512) for j in range(2)]
            w2 = [wl_kc(W["w_ff2"][l, qd * 1024:(qd + 1) * 1024, j * 512:(j + 1) * 512], 512) for j in range(2)]
            for ti3, (t0, n, samp) in enumerate(tl3):
                if qd == 0 and ti3 + 1 < len(tl3):
                    nt0, nn, nsamp = tl3[ti3 + 1]
                    norm_mod_t(nt0, nn, nsamp, hTm[:, :, nt0:nt0 + nn], [('hTm', nt0)])
                for c in range(8):
                    wt, wkey = w1[c // 4]
                    q = c % 4
                    pb, pk = pbank()
                    for kc in range(NKC):
                        mm(pb[:, 0:n], wt[:, kc, q * 128:(q + 1) * 128], hTm[:, kc, t0:t0 + n], kc == 0, kc == NKC - 1, [wkey, ('hTm', t0)], [pk])
                    act(rl[:, c % 2, 0:n], pb[:, 0:n], AF.Relu, [pk], [('rl', c % 2)])
                    tt(r2[:, c, 0:n], rl[:, c % 2, 0:n], rl[:, c % 2, 0:n], ALU.mult, [('rl', c % 2)], [('r2', c)])
                for c in range(8):
                    wt, wkey = w2[c // 4]
                    q = c % 4
                    pb, pk = pbank()
                    for kc in range(NKC):
                        mm(pb[:, 0:n], wt[:, kc, q * 128:(q + 1) * 128], r2[:, kc, 0:n], kc == 0, kc == NKC - 1, [wkey, ('r2', kc)], [pk])
                    resid_update_t(t0, n, samp, c, pb, pk)

    S.barrier()
    ptr[0] = UBASE
    sqr = alloc([2, 128], BF16); rstd2 = alloc([2, 128]); tmpo2 = alloc([2, NKC, 128]); yout = alloc([2, D])
    ts(G32f[:], pcol[:, 0, 120:128].unsqueeze(2), 32.0, None, ALU.mult, ALU.bypass, ['pcol'], ['G32f'])

    def fin_stats(ci):
        t0 = ci * 128
        pb, pk = pbank()
        for c in range(NKC):
            act(sqr[:, c % 2, :], xT[:, c, t0:t0 + 128], AF.Square, xk(ci), [('sq', c % 2)])
            mm(pb[:, 0:128], onesB, sqr[:, c % 2, :], c == 0, c == NKC - 1, [('sq', c % 2), 'cB'], [pk])
        rsq(rstd2[:, ci % 2, :], pb[:, 0:128], 1.0, D * EPS, [pk], [('rstd', ci % 2)])

    fin_stats(0)
    for ci in range(NCH):
        t0 = ci * 128
        tmpo = tmpo2[:, ci % 2, :, :]
        tok = ('tmpo', ci % 2)
        for c in range(NKC):
            stt(tmpo[:, c, :], xT[:, c, t0:t0 + 128], G32f[:, c, 0:1], rstd2[:, ci % 2, :], ALU.mult, ALU.mult,
                xk(ci) + ['G32f', ('rstd', ci % 2)], [tok])
        if ci + 1 < NCH:
            fin_stats(ci + 1)
        yo = yout[:, ci % 2, :]
        for half in range(2):
            pb, pk = pbank()
            for q in range(4):
                c = half * 4 + q
                tr(pb[:, q * 128:(q + 1) * 128], tmpo[:, c, :], identF, [tok, 'cF'], [pk])
            acopy(yo[:, half * 512:(half + 1) * 512], pb[:, 0:512], [pk], [('yout', ci % 2)])
        dst = yp[t0:t0 + 128, :] if ci < NPC else ys[:, :]
        dma(dst, yo, [('yout', ci % 2)], [('yout', ci % 2)])
    return nc, S, st


CSTF = {}
CSTB = {}
CF_COLS = 0
CB_COLS = 0


def _layout_consts():
    global CF_COLS, CB_COLS
    o = 0
    for nm, n in [('ident', 128), ('ones', 128), ('bones', 128), ('rmS', 128), ('pcorr', 64)]:
        CSTF[nm] = (o, n)
        o += n
    CF_COLS = o
    o = 0
    for nm, n in [('ident', 128), ('ones', 128), ('mAP', 512), ('mAS', 512), ('mLP', 128), ('mLS', 128), ('seqm', 16)]:
        CSTB[nm] = (o, n)
        o += n
    CB_COLS = o


_layout_consts()


def make_consts():
    c = np.zeros((128, CF_COLS + CB_COLS), np.float32)

    def putf(nm, a):
        o, n = CSTF[nm]
        c[:, o:o + n] = a

    def putb(nm, a):
        o, n = CSTB[nm]
        c[:, CF_COLS + o:CF_COLS + o + n] = a
    i = np.arange(128)
    putf('ident', np.eye(128)); putb('ident', np.eye(128))
    putf('ones', np.ones((128, 128))); putb('ones', np.ones((128, 128)))
    putf('bones', (i[:, None] // 64 == i[None, :] // 64).astype(np.float32))
    s, t = i[:, None], i[None, :]
    for tag, same in (('P', np.ones((128, 128), bool)), ('S', (s // 8) == (t // 8))):
        lt = ((s < t) & same).astype(np.float32)
        le = ((s <= t) & same).astype(np.float32)
        gtm = ((s > t) & same).astype(np.float32)
        putb('mA' + tag, np.concatenate([-lt, le, lt, le], axis=1))
        putb('mL' + tag, -gtm)
    rmS = np.ones((128, 128), np.float32)
    rmS[:, ::8] = 0
    putf('rmS', rmS)
    putb('seqm', (i[:, None] // 8 == np.arange(16)[None, :]).astype(np.float32))
    pc_ = np.zeros((128, 64), np.float32)
    for g, w in enumerate(WINS):
        tt_ = np.arange(16)
        pc_[:, g * 16:(g + 1) * 16] = (1.0 / np.minimum(tt_ + 1, w))[None, :]
    putf('pcorr', pc_)
    return c


def emit(nc, S, st):
    sems = {name: st.enter_context(nc.semaphore(name)) for name in S.cnt}
    block = st.enter_context(nc.Block())

    def run(stream, eng):
        for waits, fn, sem, inc in S.ops[stream]:
            for (s, v) in waits:
                eng.wait_ge(sems[s], v)
            if fn is not None:
                fn(eng).then_inc(sems[sem], inc)

    @block.sync
    def _(e):
        run('sp', e)
        for nm in S.cnt:
            if nm.startswith('sp'):
                e.wait_ge(sems[nm], S.cnt[nm])

    @block.gpsimd
    def _(e):
        run('pool', e)

    @block.tensor
    def _(e):
        run('pe', e)

    @block.vector
    def _(e):
        run('dve', e)

    @block.scalar
    def _(e):
        run('act', e)
    st.close()
    return nc


_WNAMES = ["w_ada_mix", "b_ada_mix", "norm_mix", "w_in", "mu_shift", "w0", "w2", "a0", "a2", "g2", "v0", "v1", "v2",
           "k_k", "k_a", "r_k", "ln_w", "ln_b", "pool_w", "pool_scale", "w_br_rwkv", "w_br_pool", "w_out",
           "w_ada_mlp", "b_ada_mlp", "norm_mlp", "w_ff1", "w_ff2", "norm_final"]


def make_in_maps(inputs, ncores, L):
    consts = make_consts()
    f = lambda a: np.ascontiguousarray(np.asarray(a, dtype=np.float32))
    shared = {}
    for nm in _WNAMES:
        a = f(inputs[nm])
        if nm == "r_k":
            a = a.reshape(L, MIX)
        if nm == "norm_final":
            a = a.reshape(1, D)
        shared[nm] = a
    shared["cst"] = consts
    maps = []
    for i in range(ncores):
        m = dict(shared)
        m["xp"] = f(inputs["x_prompt"][i])
        m["xs"] = f(inputs["x_sample"][16 * i:16 * i + 16]).reshape(128, D)
        m["cc"] = f(np.concatenate([np.asarray(inputs["c_prompt"])[i:i + 1], np.asarray(inputs["c_sample"])[16 * i:16 * i + 16]], axis=0))
        m["sshift"] = f(np.asarray(inputs["state_shift"])[:, 16 * i:16 * i + 16])
        m["spool"] = f(np.asarray(inputs["state_pool"])[:, 16 * i:16 * i + 16])
        m["swkv"] = f(np.asarray(inputs["state_wkv"])[:, 16 * i:16 * i + 16])
        maps.append(m)
    return maps


def gather(R, ncores, L):
    y_p = np.stack([R[i]["yp"] for i in range(ncores)], 0)
    y_s = np.concatenate([R[i]["ys"].reshape(16, 8, D) for i in range(ncores)], 0)
    sh_p = np.stack([R[i]["nshp"] for i in range(ncores)], 1)
    pool_p = np.stack([R[i]["npoolp"] for i in range(ncores)], 1)
    wkv_p = np.stack([R[i]["nwkvp"] for i in range(ncores)], 1)
    sh_s = np.concatenate([R[i]["nshs"] for i in range(ncores)], 1)
    pool_s = np.concatenate([R[i]["npools"].reshape(L, 16, 15, MIX) for i in range(ncores)], 1)
    wkv_s = np.concatenate([R[i]["nwkvs"] for i in range(ncores)], 1)
    return tuple(np.ascontiguousarray(a, dtype=np.float32) for a in (y_p, y_s, sh_p, pool_p, wkv_p, sh_s, pool_s, wkv_s))


def kernel(**inputs):
    ncores = 8
    L = 4
    nc, S, st = build(TP=2048, L=L)
    emit(nc, S, st)
    maps = make_in_maps(inputs, ncores, L)
    res = run_bass_kernel_spmd(nc, maps, core_ids=list(range(ncores)))
    return gather(res.results, ncores, L)
```

```python
import numpy as np
from contextlib import ExitStack
import concourse.bass as bass
import concourse.mybir as mybir
from concourse.bass_utils import run_bass_kernel_spmd

F32 = mybir.dt.float32
BF16 = mybir.dt.bfloat16
AF = mybir.ActivationFunctionType
ALU = mybir.AluOpType

D = 1024
NKC = 8
MIX = 512
RW = 1792
INC = 4352
DFF = 4096
EPS = 1e-6
GN_EPS = 64e-5
DEC_C = -float(np.exp(-0.5))
WINS = (2, 4, 8, 16)


class Sched:
    STREAMS = ('pe', 'act', 'dve', 'pool', 'sp')

    def __init__(self):
        self.ops = {s: [] for s in self.STREAMS}
        self.cnt = {}
        self.known = {s: {} for s in self.STREAMS}
        self.lastw = {}
        self.readers = {}
        self.dma_i = {}

    def op(self, stream, fn, r=(), w=(), sem=None, inc=1, nsem=1):
        sem = sem or stream
        if nsem > 1:
            i = self.dma_i.get(sem, 0)
            self.dma_i[sem] = i + 1
            sem = "%s%d" % (sem, i % nsem)
        need = {}
        if nsem > 1 and self.cnt.get(sem, 0):
            need[sem] = self.cnt[sem]

        def add(s, v):
            if need.get(s, 0) < v:
                need[s] = v
        for b in r:
            if b in self.lastw:
                add(*self.lastw[b])
        for b in w:
            if b in self.lastw:
                add(*self.lastw[b])
            for s, v in self.readers.get(b, {}).items():
                add(s, v)
        waits = []
        kn = self.known[stream]
        for s, v in need.items():
            if stream == 'pe' and s == 'pe':
                continue
            if kn.get(s, 0) < v:
                waits.append((s, v))
                kn[s] = v
        self.cnt[sem] = self.cnt.get(sem, 0) + inc
        val = self.cnt[sem]
        self.ops[stream].append((waits, fn, sem, inc))
        for b in r:
            d = self.readers.setdefault(b, {})
            if d.get(sem, 0) < val:
                d[sem] = val
        for b in w:
            self.lastw[b] = (sem, val)
            self.readers[b] = {}

    def barrier(self, streams=('pe', 'act', 'dve', 'sp')):
        for s in streams:
            waits = []
            for sem in list(self.cnt):
                if sem.startswith('pq'):
                    continue
                v = self.cnt.get(sem, 0)
                if s == 'pe' and sem == 'pe':
                    continue
                if v and self.known[s].get(sem, 0) < v:
                    waits.append((sem, v))
                    self.known[s][sem] = v
            if waits:
                self.ops[s].append((waits, None, None, 0))


def build(TP=2048, L=4):
    NTOK = TP + 128
    NCH = NTOK // 128
    NPC = TP // 128
    nc = bass.Bass("TRN2", target_bir_lowering=False)
    S = Sched()
    st = ExitStack()

    def din(name, shape, dt=F32):
        return nc.dram_tensor(name, list(shape), dt, kind="ExternalInput").ap()

    def dout(name, shape, dt=F32):
        return nc.dram_tensor(name, list(shape), dt, kind="ExternalOutput").ap()

    xp = din("xp", [TP, D]); xs = din("xs", [128, D]); cc = din("cc", [17, D])
    sshift = din("sshift", [L, 16, RW]); spool = din("spool", [L, 16, 15, MIX])
    swkv = din("swkv", [L, 16, 8, 64, 64])
    W = {}
    LV = max(L - 1, 1)
    for nm, shp in [("w_ada_mix", [L, D, 3 * D]), ("b_ada_mix", [L, 3 * D]), ("norm_mix", [L, D]),
                    ("w_in", [L, D, INC]), ("mu_shift", [L, RW]), ("w0", [L, MIX]), ("w2", [L, 64, MIX]),
                    ("a0", [L, MIX]), ("a2", [L, 64, MIX]), ("g2", [L, 128, MIX]), ("v0", [LV, MIX]),
                    ("v1", [LV, MIX, 32]), ("v2", [LV, 32, MIX]), ("k_k", [L, MIX]),
                    ("k_a", [L, MIX]), ("r_k", [L, MIX]), ("ln_w", [L, MIX]), ("ln_b", [L, MIX]),
                    ("pool_w", [L, 4, 128, 128]), ("pool_scale", [L, MIX]), ("w_br_rwkv", [L, MIX, D]),
                    ("w_br_pool", [L, MIX, D]), ("w_out", [L, D, D]), ("w_ada_mlp", [L, D, 3 * D]),
                    ("b_ada_mlp", [L, 3 * D]), ("norm_mlp", [L, D]), ("w_ff1", [L, D, DFF]),
                    ("w_ff2", [L, DFF, D]), ("norm_final", [1, D])]:
        W[nm] = din(nm, shp)
    cst = din("cst", [128, CF_COLS + CB_COLS])
    yp = dout("yp", [TP, D]); ys = dout("ys", [128, D])
    nshp = dout("nshp", [L, RW]); npoolp = dout("npoolp", [L, 15, MIX]); nwkvp = dout("nwkvp", [L, 8, 64, 64])
    nshs = dout("nshs", [L, 16, RW]); npools = dout("npools", [L, 16 * 15, MIX])
    nwkvs = dout("nwkvs", [L, 16, 8, 64, 64])
    vfd = nc.dram_tensor("vfirst_scr", [128, 4, NTOK], BF16, kind="Internal").ap()
    orpd = nc.dram_tensor("orp_scr", [128, 8, NTOK], BF16, kind="Internal").ap()

    NW = 53200
    big = st.enter_context(nc.sbuf_tensor("big", [128, NW], F32))
    ptr = [0]

    def alloc(shape, dt=F32):
        n = int(np.prod(shape))
        words = n if dt == F32 else (n + 1) // 2
        words = (words + 7) // 8 * 8
        o = ptr[0]
        ptr[0] += words
        assert ptr[0] <= NW, ("SBUF arena overflow", ptr[0], NW)
        v = big[:, o:o + words]
        if dt != F32:
            v = v.bitcast(dt)
        v = v[:, 0:n]
        if len(shape) == 1:
            return v
        names = " ".join("d%d" % i for i in range(len(shape)))
        kw = {"d%d" % i: int(shape[i]) for i in range(len(shape) - 1)}
        return v.rearrange("p (%s) -> p %s" % (names, names), **kw)

    SD = F32
    NPI = 2
    HP = 2 * NPI
    xT = alloc([NKC, NTOK])
    ring = alloc([6, 4096], BF16)
    cF = alloc([CF_COLS]); cB = alloc([CB_COLS], BF16)
    pcol = alloc([L, 128])
    omm = alloc([14]); omka = alloc([4])
    siluT = alloc([NKC, 17], BF16)
    modv = alloc([24, 17]); G32 = alloc([NKC, 17]); G32f = alloc([NKC, 1])
    lw_small = alloc([2176], BF16)
    H0f = alloc([4, 128]); H0b = H0f
    prcarry = alloc([14, 1])
    UBASE = ptr[0]

    PB = [st.enter_context(nc.psum_tensor("pb%d" % i, [128, 512], F32)) for i in range(8)]
    NROT = 6
    pbi = [0]
    pbt_i = [0]

    def pbank():
        i = pbi[0] % NROT
        pbi[0] += 1
        return PB[i], ('pb', i)
    ZB, ZK = PB[6], ('pb', 6)
    YB, YK = PB[7], ('pb', 7)

    def PE(fn, r, w): S.op('pe', fn, r, w)
    def ACT(fn, r, w): S.op('act', fn, r, w)
    def DVE(fn, r, w): S.op('dve', fn, r, w)
    def SPD(fn, r, w): S.op('sp', fn, r, w, sem='sp', inc=16, nsem=16)
    def PQD(fn, r, w): S.op('pool', fn, r, w, sem='pq', inc=16, nsem=8)

    def mm(out, lhsT, rhs, start, stop, r, w):
        PE(lambda e: e.matmul(out, lhsT, rhs, start=start, stop=stop, skip_group_check=True), r, w)

    def tr(out, in_, ident, r, w):
        PE(lambda e: e.transpose(out, in_, ident), r, w)

    def act(out, in_, func, r, w, bias=0.0, scale=1.0):
        ACT(lambda e: e.activation(out, in_, func, bias=bias, scale=scale), r, w)

    def acopy(out, in_, r, w):
        ACT(lambda e: e.copy(out, in_), r, w)

    def vcopy(out, in_, r, w):
        DVE(lambda e: e.tensor_copy(out, in_), r, w)

    def tt(out, a, b, op, r, w):
        DVE(lambda e: e.tensor_tensor(out, a, b, op), r, w)

    def ts(out, a, s1, s2, op0, op1, r, w):
        DVE(lambda e: e.tensor_scalar(out, a, s1, s2, op0, op1), r, w)

    def stt(out, a, s, b, op0, op1, r, w):
        DVE(lambda e: e.scalar_tensor_tensor(out, a, s, b, op0, op1), r, w)

    def dma(out, in_, r, w):
        SPD(lambda e: e.dma_start(out=out, in_=in_), r, w)

    def rsq(out, in_, mulc, addc, r, w):
        ts(out, in_, mulc, addc, ALU.mult, ALU.add, r, w)
        act(out, out, AF.Ln, w, w)
        act(out, out, AF.Exp, w, w, scale=-0.5)

    def xk(ci): return [('x', ci)]
    f2 = lambda t: t.rearrange("p a b -> p (a b)")
    v4 = lambda t: t.rearrange("p (a b) -> p a b", a=4)

    def cf(name, lo=0, hi=None):
        o, n = CSTF[name]
        return cF[:, o + lo:o + (n if hi is None else hi)]

    def cb(name, lo=0, hi=None):
        o, n = CSTB[name]
        return cB[:, o + lo:o + (n if hi is None else hi)]
    SD = F32
    identF = cf('ident'); identB = cb('ident'); identS = identF if SD == F32 else identB; onesB = cb('ones'); bonesF = cf('bones'); onesF = cf('ones')

    ptr[0] = UBASE
    pstage = alloc([L, 128]); cst17 = alloc([D]); xin = alloc([2, D])
    dma(cF[:], cst[:, 0:CF_COLS], [], ['cF'])
    PQD(lambda e: e.dma_start(out=cB[:], in_=cst[:, CF_COLS:CF_COLS + CB_COLS]), [], ['cB'])
    DVE(lambda e: e.memset(pstage[:], 0.0), [], ['pstage'])
    PROW = {}
    ro = 0
    for nm, nchk in [("norm_mix", 8), ("norm_mlp", 8), ("mu_shift", 14), ("w0", 4), ("a0", 4), ("v0", 4),
                     ("k_k", 4), ("k_a", 4), ("r_k", 4), ("ln_w", 4), ("ln_b", 4), ("pool_scale", 4),
                     ("b_ada_mix", 24), ("b_ada_mlp", 24)]:
        PROW[nm] = ro
        src = W[nm]
        if nm == "v0":
            if L > 1:
                dma(pstage[ro:ro + nchk, 1:L, :], src[0:L - 1, :].rearrange("l (c p) -> c l p", p=128), ['pstage'], ['pstage'])
        else:
            dma(pstage[ro:ro + nchk, 0:L, :], src.rearrange("l (c p) -> c l p", p=128), ['pstage'], ['pstage'])
        ro += nchk
    assert ro <= 120
    dma(pstage[120:128, 0, :], W["norm_final"].rearrange("o (c p) -> (o c) p", p=128), ['pstage'], ['pstage'])
    for l in range(L):
        pb, pk = pbank()
        tr(pb[:, 0:128], pstage[:, l, :], identF, ['pstage', 'cF'], [pk])
        acopy(pcol[:, l, :], pb[:, 0:128], [pk], ['pcol'])

    def pc(l, nm, c): return pcol[:, l, PROW[nm] + c:PROW[nm] + c + 1]
    def pcs(l, nm, n): return pcol[:, l, PROW[nm]:PROW[nm] + n]

    dma(cst17[0:17, :], cc[:, :], [], ['cst17'])
    pb, pk = pbank()
    for kc in range(NKC):
        tr(pb[:, kc * 17:(kc + 1) * 17], cst17[0:17, kc * 128:(kc + 1) * 128], identF[0:17, 0:17], ['cst17', 'cF'], [pk])
    act(f2(siluT[:]), pb[:, 0:NKC * 17], AF.Silu, [pk], ['siluT'])

    for ci in range(NCH):
        src = xp[ci * 128:(ci + 1) * 128, :] if ci < NPC else xs[:, :]
        xb_ = xin[:, ci % 2, :]
        dma(xb_, src, [], [('xin', ci % 2)])
        for half in range(2):
            pb, pk = pbank()
            for q in range(4):
                c = half * 4 + q
                tr(pb[:, q * 128:(q + 1) * 128], xb_[:, c * 128:(c + 1) * 128], identF, [('xin', ci % 2), 'cF'], [pk])
            acopy(xT[:, half * 4:half * 4 + 4, ci * 128:(ci + 1) * 128], v4(pb[:, 0:512]), [pk], xk(ci))

    ring_i = [0]

    def wload(src3, a, b):
        i = ring_i[0] % 6
        ring_i[0] += 1
        dst = ring[:, i, 0:a * b].rearrange("p (a b) -> p a b", a=a)
        PQD(lambda e: e.dma_start(out=dst, in_=src3), [], [('ring', i)])
        return dst, ('ring', i)

    def wl_kc(src2, ncol):
        return wload(src2.rearrange("(kc p) n -> p kc n", p=128), src2.shape[0] // 128, ncol)

    def ada(l, which):
        wsrc = W["w_ada_" + which]
        pbm, pkm = pbank()
        for j in range(6):
            wt, wkey = wl_kc(wsrc[l, :, j * 512:(j + 1) * 512], 512)
            for q in range(4):
                ch = j * 4 + q
                for kc in range(NKC):
                    mm(pbm[:, ch * 17:(ch + 1) * 17], wt[:, kc, q * 128:(q + 1) * 128], siluT[:, kc, :],
                       kc == 0, kc == NKC - 1, [wkey, 'siluT'], [pkm])
        tt(modv[:], pbm[:, 0:408].rearrange("p (a b) -> p a b", b=17),
           pcs(l, "b_ada_" + which, 24).unsqueeze(2).to_broadcast([128, 24, 17]), ALU.add, [pkm, 'pcol'], ['mod'])
        ts(G32[:], modv[:, 8:16, :], 1.0, 32.0, ALU.add, ALU.mult, ['mod'], ['mod'])
        tt(G32[:], G32[:], pcs(l, "norm_" + which, 8).unsqueeze(2).to_broadcast([128, 8, 17]), ALU.mult, ['mod', 'pcol'], ['mod'])

    WK = {}

    def xks(t0, n): return [('x', c) for c in range(t0 // 128, (t0 + n) // 128)]

    def tiles(size):
        out = []
        t = 0
        while t < TP:
            n = min(size, TP - t)
            out.append((t, n, False))
            t += n
        out.append((TP, 128, True))
        return out

    def norm_mod_t(t0, n, samp, hdst, hkeys):
        sqr, rstd, tmpn = WK['sq'], WK['rstd'], WK['tmpn']
        pb, pk = pbank()
        for c in range(NKC):
            act(sqr[:, c % 2, 0:n], xT[:, c, t0:t0 + n], AF.Square, xks(t0, n), [('sq', c % 2)])
            mm(pb[:, 0:n], onesB, sqr[:, c % 2, 0:n], c == 0, c == NKC - 1, [('sq', c % 2), 'cB'], [pk])
        rsq(rstd[:, 0:n], pb[:, 0:n], 1.0, D * EPS, [pk], ['rstd'])
        for c in range(NKC):
            tb = tmpn[:, c % 2, 0:n]
            tk = ('tmpn', c % 2)
            if not samp:
                stt(tb, xT[:, c, t0:t0 + n], G32[:, c, 0:1], rstd[:, 0:n], ALU.mult, ALU.mult, xks(t0, n) + ['mod', 'rstd'], [tk])
                act(hdst[:, c, 0:n], tb, AF.Identity, [tk, 'mod'], hkeys, bias=modv[:, c, 0:1])
            else:
                tt(tb, xT[:, c, t0:t0 + n], rstd[:, 0:n], ALU.mult, xks(t0, n) + ['rstd'], [tk])
                t3 = tb.rearrange("p (b t) -> p b t", t=8)
                tt(t3, t3, G32[:, c, 1:17].unsqueeze(2).to_broadcast([128, 16, 8]), ALU.mult, [tk, 'mod'], [tk])
                tt(hdst[:, c, 0:n].rearrange("p (b t) -> p b t", t=8), t3,
                   modv[:, c, 1:17].unsqueeze(2).to_broadcast([128, 16, 8]), ALU.add, [tk, 'mod'], hkeys)

    def norm_mod(ci, samp, hdst, hkeys):
        norm_mod_t(ci * 128, 128, samp, hdst, hkeys)

    def resid_update_t(t0, n, samp, c, pb, pk):
        xv = xT[:, c, t0:t0 + n]
        mg = WK['mg']
        if not samp:
            stt(xv, pb[:, 0:n], modv[:, 16 + c, 0:1], xv, ALU.mult, ALU.add, [pk, 'mod'] + xks(t0, n), xks(t0, n))
        else:
            tt(mg[:, 0:128].rearrange("p (b t) -> p b t", t=8), pb[:, 0:128].rearrange("p (b t) -> p b t", t=8),
               modv[:, 16 + c, 1:17].unsqueeze(2).to_broadcast([128, 16, 8]), ALU.mult, [pk, 'mod'], ['mg'])
            tt(xv, xv, mg[:, 0:128], ALU.add, ['mg'] + xks(t0, n), xks(t0, n))

    for l in range(L):
        S.barrier()
        ptr[0] = UBASE
        WK['sq'] = alloc([2, 128], BF16); WK['rstd'] = alloc([128]); WK['tmpn'] = alloc([2, 128]); WK['mg'] = alloc([128])
        hTc = alloc([NKC, 129], BF16); hT = hTc[:, :, 1:129]
        rkv = alloc([12, 128])
        twa = alloc([128], BF16); sgg = alloc([128], BF16); t1b = alloc([128], BF16)
        gbf = alloc([4, 128], BF16); vbf = alloc([4, 128], BF16); vfb = alloc([4, 128], BF16)
        FMB = [alloc([4, 128]) for _ in range(8)]
        KR = alloc([4, 256], SD)
        Bt = alloc([4, 128], SD); Kt = alloc([4, 128], SD); BWf = alloc([4, 128], SD); KWf = alloc([4, 128], SD)
        Vtok = alloc([4, 128], SD); BWtok = alloc([4, 128], SD); KWtok = alloc([4, 128], SD)
        AT = alloc([HP, 384], SD); MA = alloc([HP, 256], SD); MB = alloc([HP, 256], SD)
        PT = alloc([HP, 128], SD)
        Zn = alloc([NPI, 128], SD); Ut = alloc([NPI, 128], SD)
        wcs = alloc([4, 16])
        pz = vbf; orp = alloc([8, 128], BF16)
        shiftT = alloc([14, 16]); ppbS = alloc([4, 16, 23])
        ppb = ppbS.rearrange("p a b c -> p (a b c)")[:, 0:576].rearrange("p (a b) -> p a b", a=4)
        A_, LW_, X1, X2, X3, X4, X5, X6 = FMB
        DIAG = X1; WcBC = X2; SCo = X4[:, :, 0:64]
        S0g = H0f; HSb = A_; BKK = LW_; BRR = X3; BIGB = BWf; BIGK = KWf
        tsh = f2(X1[:])[:, 0:272].rearrange("p (a b) -> p a b", a=2)
        xl12 = X2[:, 0:2, :]
        pq2 = f2(X1[:])[:, 0:144]; pq4 = f2(X2[:])[:, 0:144]

        ts(omm[:], pcs(l, "mu_shift", 14), -1.0, 1.0, ALU.mult, ALU.add, ['pcol'], ['omm'])
        ts(omka[:], pcs(l, "k_a", 4), -1.0, 1.0, ALU.mult, ALU.add, ['pcol'], ['omm'])
        PQD(lambda e, l=l: e.dma_start(out=lw_small[0:64, 0:512], in_=W["w2"][l]), [], ['lws'])
        PQD(lambda e, l=l: e.dma_start(out=lw_small[64:128, 0:512], in_=W["a2"][l]), [], ['lws'])
        PQD(lambda e, l=l: e.dma_start(out=lw_small[:, 512:1024], in_=W["g2"][l]), [], ['lws'])
        if l > 0:
            PQD(lambda e, l=l: e.dma_start(out=lw_small[:, 1024:1152].rearrange("p (a b) -> p a b", a=4),
                                           in_=W["v1"][l - 1].rearrange("(kc p) n -> p kc n", p=128)), [], ['lws'])
            PQD(lambda e, l=l: e.dma_start(out=lw_small[0:32, 1152:1664], in_=W["v2"][l - 1]), [], ['lws'])
        PQD(lambda e, l=l: e.dma_start(out=lw_small[:, 1664:2176].rearrange("p (g d) -> p g d", g=4),
                                       in_=W["pool_w"][l].rearrange("g c d -> c g d")), [], ['lws'])
        w2a2 = lw_small[:, 0:512]; g2w = lw_small[:, 512:1024]
        v1w = lw_small[:, 1024:1152].rearrange("p (a b) -> p a b", a=4); v2w = lw_small[0:32, 1152:1664]
        poolw = lw_small[:, 1664:2176].rearrange("p (g d) -> p g d", g=4)

        ada(l, "mix")
        DVE(lambda e: e.memset(ppb[:, :, 0:15], 0.0), [], ['ppb'])
        DVE(lambda e: e.memset(hTc[:, :, 0:1], 0.0), ['hT'], ['hT'])

        W1 = []
        for j in range(5):
            ncol = 512 if j < 4 else 256
            W1.append(wl_kc(W["w_in"][l, :, j * 512:j * 512 + ncol], ncol))

        for ci in range(NCH):
            samp = ci >= NPC
            t0 = ci * 128
            first_chunk = ci == 0
            last_chunk = ci == NPC - 1
            nb = 16 if samp else 1
            blk = 128 // nb
            mset = 'S' if samp else 'P'
            norm_mod(ci, samp, hT, ['hT'])
            if samp:
                pbs_, pks_ = pbank()
                for g0 in range(0, 14, 4):
                    gn = min(4, 14 - g0)
                    dma(X6[0:16, :, :].rearrange("p a b -> p (a b)")[:, 0:gn * 128], sshift[l, :, g0 * 128:(g0 + gn) * 128], [], ['X6'])
                    for c in range(g0, g0 + gn):
                        tr(pbs_[:, c * 16:(c + 1) * 16], f2(X6[0:16, :, :])[:, (c - g0) * 128:(c - g0 + 1) * 128],
                           identF[0:16, 0:16], ['X6', 'cF'], [pks_])
                acopy(f2(shiftT[:]), pbs_[:, 0:224], [pks_], ['shiftT'])
            for cidx in range(18):
                wt, wkey = W1[cidx // 4]
                q = cidx % 4
                pb, pk = pbank()
                for kc in range(NKC):
                    if samp:
                        mm(pb[:, 0:128], wt[:, kc, q * 128:(q + 1) * 128], hT[:, kc, :], kc == 0, kc == NKC - 1, [wkey, 'hT'], [pk])
                    else:
                        mm(pb[:, 0:129], wt[:, kc, q * 128:(q + 1) * 128], hTc[:, kc, 0:129], kc == 0, kc == NKC - 1, [wkey, 'hT'], [pk])
                if cidx >= 14:
                    g = cidx - 14
                    if not samp:
                        acopy(ppb[:, g, 15:143], pb[:, 1:129], [pk], ['ppb'])
                    else:
                        acopy(ppbS[:, g, :, 15:23], pb[:, 0:128].rearrange("p (b t) -> p b t", t=8), [pk], ['ppbS', 'ppb'])
                    continue
                c = cidx
                tb = tsh[:, c % 2, :]
                tk = 'X1'
                mu = pc(l, "mu_shift", c)
                dst = rkv[:, c, :] if c < 12 else xl12[:, c - 12, :]
                dkey = 'rkv' if c < 12 else 'X2'
                if not samp:
                    act(tb[:, 0:129], pb[:, 0:129], AF.Identity, [pk, 'pcol'], [tk], scale=mu)
                    if last_chunk:
                        acopy(prcarry[:, c, :], pb[:, 128:129], [pk], ['prcarry'])
                    stt(dst, pb[:, 1:129], omm[:, c:c + 1], tb[:, 0:128], ALU.mult, ALU.add, [pk, 'omm', tk], [dkey])
                else:
                    p3 = pb[:, 0:128].rearrange("p (b t) -> p b t", t=8)
                    t3 = tb[:, 0:128].rearrange("p (b t) -> p b t", t=8)
                    act(t3[:, :, 1:8], p3[:, :, 0:7], AF.Identity, [pk, 'pcol'], [tk], scale=mu)
                    act(t3[:, :, 0:1], shiftT[:, c, :].unsqueeze(2), AF.Identity, ['shiftT', 'pcol'], [tk], scale=mu)
                    acopy(f2(X5[:])[:, c * 16:(c + 1) * 16].unsqueeze(2), p3[:, :, 7:8], [pk], ['X5'])
                    stt(dst, pb[:, 0:128], omm[:, c:c + 1], tb[:, 0:128], ALU.mult, ALU.add, [pk, 'omm', tk], [dkey])
            if not samp and not last_chunk:
                acopy(hTc[:, :, 0:1], hTc[:, :, 128:129], ['hT'], ['hT'])
            if last_chunk:
                pb, pk = pbank()
                tr(pb[0:14, 0:128], f2(prcarry[:]), identF, ['prcarry', 'cF'], [pk])
                acopy(f2(X6[0:14, :, :])[:, 0:128], pb[0:14, 0:128], [pk], ['X6'])
                dma(nshp[l].rearrange("(c p) -> c p", p=128), f2(X6[0:14, :, :])[:, 0:128], ['X6'], [])
            if samp:
                for g0 in range(0, 14, 4):
                    gn = min(4, 14 - g0)
                    pbx, pkx = pbank()
                    for c in range(g0, g0 + gn):
                        tr(pbx[0:16, (c - g0) * 128:(c - g0 + 1) * 128], f2(X5[:])[:, c * 16:(c + 1) * 16], identF, ['X5', 'cF'], [pkx])
                    ob = [X3, X4][(g0 // 4) % 2]
                    okey = ['X3', 'X4'][(g0 // 4) % 2]
                    acopy(f2(ob[0:16, :, :])[:, 0:gn * 128], pbx[0:16, 0:gn * 128], [pkx], [okey])
                    dma(nshs[l, :, g0 * 128:(g0 + gn) * 128], f2(ob[0:16, :, :])[:, 0:gn * 128], [okey], [])
            act(twa[0:64, :], xl12[0:64, 0, :], AF.Tanh, ['X2'], ['twa'])
            acopy(twa[64:128, :], xl12[64:128, 0, :], ['X2'], ['twa'])
            act(sgg[:], xl12[:, 1, :], AF.Sigmoid, ['X2'], ['sgg'])
            r_c = rkv[:, 0:4, :]; k_c = rkv[:, 4:8, :]; v_c = rkv[:, 8:12, :]
            for p in range(4):
                pb, pk = pbank()
                mm(pb[:, 0:128], w2a2[0:64, p * 128:(p + 1) * 128], twa[0:64, :], True, True, ['lws', 'twa'], [pk])
                act(LW_[:, p, :], pb[:, 0:128], AF.Sigmoid, [pk, 'pcol'], ['LW'], bias=pc(l, "w0", p))
                pb, pk = pbank()
                mm(pb[:, 0:128], w2a2[64:128, p * 128:(p + 1) * 128], twa[64:128, :], True, True, ['lws', 'twa'], [pk])
                act(A_[:, p, :], pb[:, 0:128], AF.Sigmoid, [pk, 'pcol'], ['A'], bias=pc(l, "a0", p))
                pb, pk = pbank()
                mm(pb[:, 0:128], g2w[:, p * 128:(p + 1) * 128], sgg[:], True, True, ['lws', 'sgg'], [pk])
                acopy(gbf[:, p, :], pb[:, 0:128], [pk], ['gbf'])
            ts(LW_[:], LW_[:], DEC_C, None, ALU.mult, ALU.bypass, ['LW'], ['LW'])
            bc4 = lambda col: col.unsqueeze(2).to_broadcast([128, 4, 128])
            d0 = cf('rmS') if samp else onesF
            for p in range(4):
                DVE(lambda e, p=p, d0=d0: e.tensor_tensor_scan(X2[:, p, :], d0, LW_[:, p, :], 0.0, ALU.mult, ALU.add),
                    ['LW', 'cF'], ['X2'])
            tt(X1[:], X2[:], LW_[:], ALU.subtract, ['X2', 'LW'], ['X1'])
            act(X1[:], X1[:], AF.Exp, ['X1'], ['X1'])
            act(LW_[:], X2[:], AF.Exp, ['X2'], ['LW'])
            ein4 = LW_[:].rearrange("p a (b t) -> p a b t", t=blk)
            vcopy(wcs[:, :, 0:nb].unsqueeze(3), ein4[:, :, :, blk - 1:blk], ['LW'], ['wcs'])
            act(X2[:], X2[:], AF.Exp, ['X2'], ['X2'], scale=-1.0)
            if l == 0:
                acopy(vbf[:], v_c, ['rkv'], ['vbf'])
                dma(vfd[:, :, t0:t0 + 128], vbf[:], ['vbf'], [('vfd', ci)])
            else:
                dma(vfb[:], vfd[:, :, t0:t0 + 128], [('vfd', ci)], ['vfb'])
                acopy(vbf[:], v_c, ['rkv'], ['vbf'])
                pb, pk = pbank()
                for p in range(4):
                    mm(pb[0:32, 0:128], v1w[:, p, :], vbf[:, p, :], p == 0, p == 3, ['lws', 'vbf'], [pk])
                acopy(t1b[0:32, :], pb[0:32, 0:128], [pk], ['t1b'])
                for p in range(4):
                    pb, pk = pbank()
                    mm(pb[:, 0:128], v2w[:, p * 128:(p + 1) * 128], t1b[0:32, :], True, True, ['lws', 't1b'], [pk])
                    act(X6[:, p, :], pb[:, 0:128], AF.Sigmoid, [pk, 'pcol'], ['X6'], bias=pc(l, "v0", p))

            tt(BWf[:], k_c, bc4(pcs(l, "k_k", 4)), ALU.mult, ['rkv', 'pcol'], ['BWf'])
            tt(KWf[:], BWf[:], BWf[:], ALU.mult, ['BWf'], ['KWf'])
            pb, pk = pbank()
            mm(pb[:, 0:512], bonesF, f2(KWf[:]), True, True, ['KWf', 'cF'], [pk])
            rsq(KWf[:], v4(pb[:, 0:512]), 1.0, 1e-12, [pk], ['KWf'])
            tt(X4[:], A_[:], bc4(pcs(l, "k_a", 4)), ALU.mult, ['A', 'pcol'], ['X4'])
            tt(X4[:], X4[:], bc4(omka[:, 0:4]), ALU.add, ['X4', 'omm'], ['X4'])
            tt(X5[:], k_c, X4[:], ALU.mult, ['rkv', 'X4'], ['X5'])
            tt(X4[:], r_c, X5[:], ALU.mult, ['rkv', 'X5'], ['X4'])
            tt(X4[:], X4[:], bc4(pcs(l, "r_k", 4)), ALU.mult, ['X4', 'pcol'], ['X4'])
            pb, pk = pbank()
            mm(pb[:, 0:512], bonesF, f2(X4[:]), True, True, ['X4', 'cF'], [pk])
            if l > 0:
                tt(X3[:], vfb[:], v_c, ALU.subtract, ['vfb', 'rkv'], ['X3'])
                tt(X3[:], X3[:], X6[:], ALU.mult, ['X3', 'X6'], ['X3'])
                tt(v_c, v_c, X3[:], ALU.add, ['rkv', 'X3'], ['rkv'])
            tt(X3[:], BWf[:], KWf[:], ALU.mult, ['BWf', 'KWf'], ['X3'])
            tt(X6[:], v4(pb[:, 0:512]), v_c, ALU.mult, [pk, 'rkv'], ['X6'])
            tt(X4[:], X3[:], A_[:], ALU.mult, ['X3', 'A'], ['X4'])
            tt(KR[:, :, 0:128], X3[:], X1[:], ALU.mult, ['X3', 'X1'], ['KR'])
            tt(KR[:, :, 128:256], r_c, LW_[:], ALU.mult, ['rkv', 'LW'], ['KR'])
            wcb = wcs[:, :, 0:nb].unsqueeze(3).to_broadcast([128, 4, nb, blk])
            b4 = lambda t: t.rearrange("p a (b t) -> p a b t", t=blk)
            tt(X3[:], X4[:], X2[:], ALU.mult, ['X4', 'X2'], ['X3'])
            acopy(Bt[:], X3[:], ['X3'], ['Bt'])
            tt(b4(BWf[:]), b4(X3[:]), wcb, ALU.mult, ['X3', 'wcs'], ['BWf'])
            tt(X4[:], X5[:], X2[:], ALU.mult, ['X5', 'X2'], ['X4'])
            acopy(Kt[:], X4[:], ['X4'], ['Kt'])
            tt(b4(KWf[:]), b4(X4[:]), wcb, ALU.mult, ['X4', 'wcs'], ['KWf'])
            for (srcb, dstb, sk, dk) in ((v_c, Vtok, 'rkv', 'Vtok'), (BWf, BWtok, 'BWf', 'BWtok'), (KWf, KWtok, 'KWf', 'KWtok')):
                pbb, pkb = pbank()
                for p in range(4):
                    tr(pbb[:, p * 128:(p + 1) * 128], srcb[:, p, :], identS, [sk, 'cF', 'cB'], [pkb])
                acopy(f2(dstb[:]), pbb[:, 0:512], [pkb], [dk])

            if samp:
                DVE(lambda e: e.memset(S0g[:], 0.0), ['H0f'], ['H0f'])
                DVE(lambda e: e.memset(BKK[:], 0.0), ['LW'], ['LW'])
                DVE(lambda e: e.memset(BRR[:], 0.0), ['X3'], ['X3'])
            for hf in range(4 // NPI):
                for q in range(HP):
                    hd = hf * HP + q
                    p, hh = hd // 2, hd % 2
                    hs = slice(hh * 64, hh * 64 + 64)
                    pb, pk = pbank()
                    mm(pb[:, 0:256], Bt[hs, p, :], KR[hs, p, :], True, True, ['Bt', 'KR'], [pk])
                    mm(pb[:, 256:512], Kt[hs, p, :], KR[hs, p, :], True, True, ['Kt', 'KR'], [pk])
                    tt(MB[:, q, 128:256], pb[:, 0:128], cb('mA' + mset, 0, 128), ALU.mult, [pk, 'cB'], ['MB'])
                    tt(AT[:, q, :], pb[:, 128:512], cb('mA' + mset, 128, 512), ALU.mult, [pk, 'cB'], ['AT'])
                pbA, pkA = pbank()
                pbB, pkB = pbank()
                for q in range(HP):
                    hd = hf * HP + q
                    p, hh = hd // 2, hd % 2
                    hs = slice(hh * 64, hh * 64 + 64)
                    pbx, pkx = (pbA, pkA) if hh == 0 else (pbB, pkB)
                    mm(pbx[:, (q // 2) * 128:(q // 2 + 1) * 128], KR[hs, p, 0:128], Bt[hs, p, :], True, True, ['KR', 'Bt'], [pkx])
                for hh, (pbx, pkx) in enumerate(((pbA, pkA), (pbB, pkB))):
                    tt(MB[:, hh:HP:2, 0:128], pbx[:, 0:NPI * 128].rearrange("p (a b) -> p a b", a=NPI),
                       cb('mL' + mset).unsqueeze(1).to_broadcast([128, NPI, 128]), ALU.mult, [pkx, 'cB'], ['MB'])
                tt(PT[:], MB[:, :, 128:256], identF.unsqueeze(1).to_broadcast([128, HP, 128]), ALU.add, ['MB', 'cF'], ['PT'])
                if hf == 0 and not samp:
                    PA = bass.AP(tensor=X1.tensor, offset=X1.offset, ap=[list(X1.ap[0]), [144, 4], [1, 144]])
                    PBs = bass.AP(tensor=A_.tensor, offset=A_.offset, ap=[list(A_.ap[0]), [144, 4], [1, 144]])
                    ka, kb = ['X1', 'X2'], ['A', 'LW']
                    tt(PA[:, :, 1:143], ppb[:, :, 1:143], ppb[:, :, 0:142], ALU.add, ['ppb'], ka)
                    tt(PBs[:, 1:4, 3:143], PA[:, 1:4, 3:143], PA[:, 1:4, 1:141], ALU.add, ka, kb)
                    tt(PA[:, 2:4, 7:143], PBs[:, 2:4, 7:143], PBs[:, 2:4, 3:139], ALU.add, kb + ka, ka)
                    tt(PBs[:, 3:4, 15:143], PA[:, 3:4, 15:143], PA[:, 3:4, 7:135], ALU.add, ka + kb, kb)
                    psrc = [PA[:, 0, :], PBs[:, 1, :], PA[:, 2, :], PBs[:, 3, :]]
                    for g in range(4):
                        if first_chunk:
                            mg = WK['mg']
                            stt(mg[:], psrc[g][:, 15:143], 1.0 / WINS[g], ppb[:, g, 15:143], ALU.mult, ALU.subtract, ka + kb + ['ppb'], ['mg'])
                            tt(mg[:, 0:16], psrc[g][:, 15:31], cf('pcorr')[:, g * 16:(g + 1) * 16], ALU.mult, ka + kb + ['cF', 'mg'], ['mg'])
                            tt(mg[:, 0:16], mg[:, 0:16], ppb[:, g, 15:31], ALU.subtract, ['mg', 'ppb'], ['mg'])
                            vcopy(pz[:, g, :], mg[:], ['mg'], ['vbf'])
                        else:
                            stt(pz[:, g, :], psrc[g][:, 15:143], 1.0 / WINS[g], ppb[:, g, 15:143], ALU.mult, ALU.subtract, ka + kb + ['ppb'], ['vbf'])
                    if not last_chunk:
                        vcopy(PA[:, :, 0:15], ppb[:, :, 128:143], ['ppb'] + ka, ka)
                        vcopy(ppb[:, :, 0:15], PA[:, :, 0:15], ka + ['ppb'], ['ppb'])
                nlev = 2 if samp else 6
                curM = lambda q: MB[:, q, 0:128]
                curMT = lambda q: MB[:, q, 128:256]
                curk = ['MB']
                for lev in range(nlev):
                    nxt, nk = (MA, 'MA') if lev % 2 == 0 else (MB, 'MB')
                    lastlev = lev == nlev - 1
                    for h2 in range(NPI):
                        pb, pk = pbank()
                        for qq in range(2):
                            q = h2 * 2 + qq
                            mm(pb[:, qq * 256:qq * 256 + 128], curMT(q), curM(q), True, True, curk, [pk])
                            if not lastlev:
                                mm(pb[:, qq * 256 + 128:qq * 256 + 256], curM(q), curMT(q), True, True, curk, [pk])
                        acopy(nxt[:, h2 * 2:h2 * 2 + 2, :], pb[:, 0:512].rearrange("p (a b) -> p a b", a=2), [pk], [nk])
                    pb, pk = pbank()
                    for q in range(HP):
                        mm(pb[:, q * 128:(q + 1) * 128], nxt[:, q, 0:128], PT[:, q, :], True, True, [nk, 'PT'], [pk])
                    tt(PT[:], PT[:], pb[:, 0:HP * 128].rearrange("p (a b) -> p a b", a=HP), ALU.add, [pk, 'PT'], ['PT'])
                    curM = lambda q, nxt=nxt: nxt[:, q, 0:128]
                    curMT = lambda q, nxt=nxt: nxt[:, q, 128:256]
                    curk = [nk]

                if hf == 0 and not samp:
                    for g in range(4):
                        pb, pk = pbank()
                        mm(pb[:, 0:128], poolw[:, g, :], pz[:, g, :], True, True, ['lws', 'vbf'], [pk])
                        act(orp[:, 4 + g, :], pb[:, 0:128], AF.Identity, [pk, 'pcol'], ['orp'], scale=pc(l, "pool_scale", g))
                if not samp:
                    for pp_ in range(NPI):
                        p = hf * NPI + pp_
                        if not first_chunk:
                            mm(ZB[:, pp_ * 128:(pp_ + 1) * 128], KR[:, p, 0:128], H0b[:, p, :], True, False, ['KR', 'H0f'], [ZK])
                        for hh in range(2):
                            q = pp_ * 2 + hh
                            mm(ZB[:, pp_ * 128 + hh * 64:pp_ * 128 + hh * 64 + 64], AT[:, q, 128:256],
                               Vtok[:, p, hh * 64:hh * 64 + 64], first_chunk, True, ['AT', 'Vtok'], [ZK])
                    act(f2(Zn[:]), ZB[:, 0:NPI * 128], AF.Copy, [ZK], ['Zn'], scale=-1.0)
                    pbu, pku = pbank()
                    for q in range(HP):
                        pp_, hh = q // 2, q % 2
                        mm(pbu[:, q * 64:(q + 1) * 64], PT[:, q, :], Zn[:, pp_, hh * 64:hh * 64 + 64], True, True, ['PT', 'Zn'], [pku])
                    acopy(f2(Ut[:]), pbu[:, 0:NPI * 128], [pku], ['Ut'])
                    for pp_ in range(NPI):
                        p = hf * NPI + pp_
                        if not first_chunk:
                            mm(YB[:, pp_ * 128:(pp_ + 1) * 128], H0b[:, p, :], KR[:, p, 128:256], True, False, ['H0f', 'KR'], [YK])
                        for hh in range(2):
                            q = pp_ * 2 + hh
                            hs = slice(hh * 64, hh * 64 + 64)
                            mm(YB[hs, pp_ * 128:(pp_ + 1) * 128], Ut[:, pp_, hh * 64:hh * 64 + 64], AT[:, q, 0:128],
                               first_chunk, False, ['Ut', 'AT'], [YK])
                            mm(YB[hs, pp_ * 128:(pp_ + 1) * 128], Vtok[:, p, hh * 64:hh * 64 + 64], AT[:, q, 256:384],
                               False, True, ['Vtok', 'AT'], [YK])
                    acopy(f2(X5[:, hf * NPI:(hf + 1) * NPI, :]), YB[:, 0:NPI * 128], [YK], ['X5'])
                    pbh, pkh = pbank()
                    for pp_ in range(NPI):
                        p = hf * NPI + pp_
                        mm(pbh[:, pp_ * 128:(pp_ + 1) * 128], BWtok[:, p, :], Ut[:, pp_, :], True, False, ['BWtok', 'Ut'], [pkh])
                        mm(pbh[:, pp_ * 128:(pp_ + 1) * 128], KWtok[:, p, :], Vtok[:, p, :], False, True, ['KWtok', 'Vtok'], [pkh])
                    hsl = slice(hf * NPI, (hf + 1) * NPI)
                    tt(X3[:, 0:NPI, :], pbh[:, 0:NPI * 128].rearrange("p (a b) -> p a b", a=NPI),
                       bonesF.unsqueeze(1).to_broadcast([128, NPI, 128]), ALU.mult, [pkh, 'cF'], ['X3'])
                    if first_chunk:
                        vcopy(H0f[:, hsl, :], X3[:, 0:NPI, :], ['X3'], ['H0f'])
                    else:
                        tt(H0f[:, hsl, :], H0f[:, hsl, :], wcs[:, hsl, 0:1].to_broadcast([128, NPI, 128]), ALU.mult, ['H0f', 'wcs'], ['H0f'])
                        tt(H0f[:, hsl, :], H0f[:, hsl, :], X3[:, 0:NPI, :], ALU.add, ['H0f', 'X3'], ['H0f'])
                    if last_chunk:
                        pbs, pks = pbank()
                        for pp_ in range(NPI):
                            tr(pbs[:, pp_ * 128:(pp_ + 1) * 128], H0f[:, hf * NPI + pp_, :], identF, ['H0f', 'cF'], [pks])
                        for hh in range(2):
                            hs = slice(hh * 64, hh * 64 + 64)
                            acopy(SCo[hs, 0:NPI, :], pbs[hs, 0:NPI * 128].rearrange("p (a b) -> p a b", a=NPI)[:, :, hh * 64:hh * 64 + 64], [pks], ['X4'])
                        dma(nwkvp[l, hf * HP:hf * HP + HP].rearrange("(p hh) v n -> (hh v) p n", hh=2), SCo[:, 0:NPI, :], ['X4'], ['X4'])
                else:
                    for pp_ in range(NPI):
                        p = hf * NPI + pp_
                        for g in range(4):
                            for hh in range(2):
                                hs = slice(hh * 64, hh * 64 + 64)
                                dma(S0g[hs, :, hh * 64:hh * 64 + 64],
                                    swkv[l, g * 4:g * 4 + 4, 2 * p + hh, :, :].rearrange("b v k -> v b k"), ['H0f'], ['H0f'])
                            pb, pk = pbank()
                            for j in range(4):
                                tr(pb[:, j * 128:(j + 1) * 128], S0g[:, j, :], identF, ['H0f', 'cF'], [pk])
                            acopy(f2(HSb[:]), pb[:, 0:512], [pk], ['A'])
                            bkk_diag = bass.AP(tensor=BKK.tensor, offset=BKK.offset + 32 * g,
                                               ap=[list(BKK.ap[0]), [136, 4], [1, 8]])
                            vcopy(bkk_diag, KR[:, p, g * 32:g * 32 + 32].rearrange("q (b t) -> q b t", t=8), ['KR', 'LW'], ['LW'])
                            brr_diag = bass.AP(tensor=BRR.tensor, offset=BRR.offset + 32 * g,
                                               ap=[list(BRR.ap[0]), [136, 4], [1, 8]])
                            vcopy(brr_diag, KR[:, p, 128 + g * 32:128 + g * 32 + 32].rearrange("q (b t) -> q b t", t=8), ['KR', 'X3'], ['X3'])
                            for j in range(4):
                                b_ = g * 4 + j
                                mm(ZB[:, 0:128], BKK[:, j, :], HSb[:, j, :], b_ == 0, False, ['LW', 'A'], [ZK])
                                mm(YB[:, 0:128], HSb[:, j, :], BRR[:, j, :], b_ == 0, False, ['A', 'X3'], [YK])
                            DVE(lambda e, bkk_diag=bkk_diag: e.memset(bkk_diag, 0.0), ['LW'], ['LW'])
                            DVE(lambda e, brr_diag=brr_diag: e.memset(brr_diag, 0.0), ['X3'], ['X3'])
                        for hh in range(2):
                            q = pp_ * 2 + hh
                            mm(ZB[:, hh * 64:hh * 64 + 64], AT[:, q, 128:256], Vtok[:, p, hh * 64:hh * 64 + 64], False, True, ['AT', 'Vtok'], [ZK])
                        act(Zn[:, 0, :], ZB[:, 0:128], AF.Copy, [ZK], ['Zn'], scale=-1.0)
                        pbu, pku = pbank()
                        for hh in range(2):
                            q = pp_ * 2 + hh
                            mm(pbu[:, hh * 64:hh * 64 + 64], PT[:, q, :], Zn[:, 0, hh * 64:hh * 64 + 64], True, True, ['PT', 'Zn'], [pku])
                        acopy(Ut[:, 0, :], pbu[:, 0:128], [pku], ['Ut'])
                        for hh in range(2):
                            q = pp_ * 2 + hh
                            hs = slice(hh * 64, hh * 64 + 64)
                            mm(YB[hs, 0:128], Ut[:, 0, hh * 64:hh * 64 + 64], AT[:, q, 0:128], False, False, ['Ut', 'AT'], [YK])
                            mm(YB[hs, 0:128], Vtok[:, p, hh * 64:hh * 64 + 64], AT[:, q, 256:384], False, True, ['Vtok', 'AT'], [YK])
                        acopy(X5[:, p, :], YB[:, 0:128], [YK], ['X5'])
                        smk = cb('seqm')
                        for g in range(4):
                            for hh in range(2):
                                hs = slice(hh * 64, hh * 64 + 64)
                                dma(S0g[hs, :, hh * 64:hh * 64 + 64],
                                    swkv[l, g * 4:g * 4 + 4, 2 * p + hh, :, :].rearrange("b v k -> v b k"), ['H0f'], ['H0f'])
                            sm4 = smk[:, g * 4:g * 4 + 4].unsqueeze(2).to_broadcast([128, 4, 128])
                            tt(BIGB[:], BWtok[:, p, :].unsqueeze(1).to_broadcast([128, 4, 128]), sm4, ALU.mult, ['BWtok', 'cB'], ['BWf'])
                            tt(BIGK[:], KWtok[:, p, :].unsqueeze(1).to_broadcast([128, 4, 128]), sm4, ALU.mult, ['KWtok', 'cB'], ['KWf'])
                            tt(DIAG[:], identF.unsqueeze(1).to_broadcast([128, 4, 128]),
                               wcs[:, p, g * 4:g * 4 + 4].unsqueeze(2).to_broadcast([128, 4, 128]), ALU.mult, ['cF', 'wcs'], ['X1'])
                            pb, pk = pbank()
                            mm(pb[:, 0:512], onesF, f2(DIAG[:]), True, True, ['X1', 'cF'], [pk])
                            acopy(f2(WcBC[:]), pb[:, 0:512], [pk], ['X2'])
                            tt(WcBC[:], WcBC[:], S0g[:], ALU.mult, ['X2', 'H0f'], ['X2'])
                            pbs, pks = pbank()
                            mm(pbs[:, 0:512], Ut[:, 0, :], f2(BIGB[:]), True, False, ['Ut', 'BWf'], [pks])
                            mm(pbs[:, 0:512], Vtok[:, p, :], f2(BIGK[:]), False, True, ['Vtok', 'KWf'], [pks])
                            tt(WcBC[:], WcBC[:], v4(pbs[:, 0:512]), ALU.add, ['X2', pks], ['X2'])
                            for hh in range(2):
                                hs = slice(hh * 64, hh * 64 + 64)
                                vcopy(SCo[hs, :, :], WcBC[hs, :, hh * 64:hh * 64 + 64], ['X2', 'X4'], ['X4'])
                            dma(nwkvs[l, g * 4:g * 4 + 4, 2 * p:2 * p + 2, :, :].rearrange("b hh v n -> (hh v) b n"), SCo[:], ['X4'], ['X4'])

            pb, pk = pbank()
            mm(pb[:, 0:512], bonesF, f2(X5[:]), True, True, ['X5', 'cF'], [pk])
            ts(X1[:], v4(pb[:, 0:512]), 1.0 / 64, None, ALU.mult, ALU.bypass, [pk], ['X1'])
            tt(X5[:], X5[:], X1[:], ALU.subtract, ['X5', 'X1'], ['X5'])
            tt(X2[:], X5[:], X5[:], ALU.mult, ['X5'], ['X2'])
            pb, pk = pbank()
            mm(pb[:, 0:512], bonesF, f2(X2[:]), True, True, ['X2', 'cF'], [pk])
            rsq(X1[:], v4(pb[:, 0:512]), 1.0 / 64, GN_EPS, [pk], ['X1'])
            tt(X5[:], X5[:], X1[:], ALU.mult, ['X5', 'X1'], ['X5'])
            tt(X5[:], X5[:], bc4(pcs(l, "ln_w", 4)), ALU.mult, ['X5', 'pcol'], ['X5'])
            tt(X5[:], X5[:], bc4(pcs(l, "ln_b", 4)), ALU.add, ['X5', 'pcol'], ['X5'])
            tt(X5[:], X5[:], X6[:], ALU.add, ['X5', 'X6'], ['X5'])
            tt(orp[:, 0:4, :], X5[:], gbf[:], ALU.mult, ['X5', 'gbf'], ['orp'])

            if not samp:
                if last_chunk:
                    pb, pk = pbank()
                    for g in range(4):
                        tr(pb[0:15, g * 128:(g + 1) * 128], ppb[:, g, 128:143], identF, ['ppb', 'cF'], [pk])
                    acopy(f2(X1[0:15, :, :]), pb[0:15, 0:512], [pk], ['X1'])
                    dma(npoolp[l], f2(X1[0:15, :, :]), ['X1'], ['X1'])
                pass
            else:
                spv = spool[l].rearrange("b r c -> (b r) c")
                for half in range(2):
                    r0, rn = (0, 128) if half == 0 else (128, 112)
                    stg = [X1, X2][half]
                    dma(f2(stg[0:rn, :, :]), spv[r0:r0 + rn, :], [], [['X1', 'X2'][half]])
                    pb, pk = pbank()
                    for g in range(4):
                        tr(pb[:, g * 128:g * 128 + rn], f2(stg[0:rn, :, :])[:, g * 128:(g + 1) * 128], identF[0:rn, 0:rn],
                           [['X1', 'X2'][half], 'cF'], [pk])
                    acopy(f2(X3[:]) if half == 0 else f2(X4[:]), pb[:, 0:512], [pk], [['X3', 'X4'][half]])
                for g in range(4):
                    vcopy(ppbS[:, g, 0:8, 0:15], X3[:, g, 0:120].rearrange("p (b r) -> p b r", r=15), ['X3'], ['ppbS', 'ppb'])
                    vcopy(ppbS[:, g, 8, 0:8], X3[:, g, 120:128], ['X3'], ['ppbS', 'ppb'])
                    vcopy(ppbS[:, g, 8, 8:15], X4[:, g, 0:7], ['X4'], ['ppbS', 'ppb'])
                    vcopy(ppbS[:, g, 9:16, 0:15], X4[:, g, 7:112].rearrange("p (b r) -> p b r", r=15), ['X4'], ['ppbS', 'ppb'])
                for half in range(2):
                    r0, rn = (0, 128) if half == 0 else (128, 112)
                    stg = [X3, X4][half]
                    for g in range(4):
                        if half == 0:
                            vcopy(stg[:, g, 0:120].rearrange("p (b r) -> p b r", r=15), ppbS[:, g, 0:8, 8:23], ['ppbS'], [['X3', 'X4'][half]])
                            vcopy(stg[:, g, 120:128], ppbS[:, g, 8, 8:16], ['ppbS'], [['X3', 'X4'][half]])
                        else:
                            vcopy(stg[:, g, 0:7], ppbS[:, g, 8, 16:23], ['ppbS'], [['X3', 'X4'][half]])
                            vcopy(stg[:, g, 7:112].rearrange("p (b r) -> p b r", r=15), ppbS[:, g, 9:16, 8:23], ['ppbS'], [['X3', 'X4'][half]])
                    pb, pk = pbank()
                    for g in range(4):
                        tr(pb[0:rn, g * 128:(g + 1) * 128], stg[:, g, 0:rn], identF, [['X3', 'X4'][half], 'cF'], [pk])
                    ob = [X1, X2][half]
                    acopy(f2(ob[0:rn, :, :]), pb[0:rn, 0:512], [pk], [['X1', 'X2'][half]])
                    dma(npools[l, r0:r0 + rn, :], f2(ob[0:rn, :, :]), [['X1', 'X2'][half]], [['X1', 'X2'][half]])
                for g in range(4):
                    A0 = ppbS[:, g, :, :]
                    cur = X3[:].rearrange("p a b -> p (a b)")[:, 0:368].rearrange("p (b r) -> p b r", r=23)
                    oth = X4[:].rearrange("p a b -> p (a b)")[:, 0:368].rearrange("p (b r) -> p b r", r=23)
                    tt(cur[:, :, 1:23], A0[:, :, 1:23], A0[:, :, 0:22], ALU.add, ['ppbS', 'X3', 'X4'], ['X3', 'X4'])
                    span = 2
                    while span < WINS[g]:
                        lo = 2 * span - 1
                        tt(oth[:, :, lo:23], cur[:, :, lo:23], cur[:, :, lo - span:23 - span], ALU.add, ['X3', 'X4'], ['X3', 'X4'])
                        cur, oth = oth, cur
                        span *= 2
                    mg = WK['mg']
                    stt(mg[:].rearrange("p (b t) -> p b t", t=8), cur[:, :, 15:23], 1.0 / WINS[g], ppbS[:, g, :, 15:23],
                        ALU.mult, ALU.subtract, ['X3', 'X4', 'ppbS'], ['mg'])
                    acopy(pz[:, g, :], mg[:], ['mg'], ['vbf'])
            if samp:
                for g in range(4):
                    pb, pk = pbank()
                    mm(pb[:, 0:128], poolw[:, g, :], pz[:, g, :], True, True, ['lws', 'vbf'], [pk])
                    act(orp[:, 4 + g, :], pb[:, 0:128], AF.Identity, [pk, 'pcol'], ['orp'], scale=pc(l, "pool_scale", g))
            dma(orpd[:, :, t0:t0 + 128], orp[:], ['orp'], [('orpd', ci)])

        S.barrier()
        ptr[0] = UBASE
        T2 = 512
        WK['sq'] = alloc([2, T2], BF16); WK['rstd'] = alloc([T2]); WK['tmpn'] = alloc([2, T2]); WK['mg'] = alloc([128])
        hT2 = alloc([NKC, T2], BF16)
        mrg = alloc([NKC, NTOK], BF16)
        orp2 = alloc([1, 8, T2], BF16)
        g16 = alloc([16, T2], BF16)
        tmpb = WK['tmpn'][:, 0:1, :]
        W2 = [wl_kc(W["w_in"][l, :, 2304 + j * 512:2304 + (j + 1) * 512], 512) for j in range(4)]
        wbr, wbrk = wl_kc(W["w_br_rwkv"][l], 1024)
        wbp, wbpk = wl_kc(W["w_br_pool"][l], 1024)
        tl2 = tiles(T2)
        norm_mod_t(tl2[0][0], tl2[0][1], tl2[0][2], hT2, ['hT2'])
        for ti, (t0, n, samp) in enumerate(tl2):
            ob = orp2[:, 0, :, 0:n]
            ok = ('orp2', 0)
            dma(ob, orpd[:, :, t0:t0 + n], [('orpd', c) for c in range(t0 // 128, (t0 + n) // 128)], [ok])
            for cg in range(16):
                wt, wkey = W2[cg // 4]
                q = cg % 4
                pb, pk = pbank()
                for kc in range(NKC):
                    mm(pb[:, 0:n], wt[:, kc, q * 128:(q + 1) * 128], hT2[:, kc, 0:n], kc == 0, kc == NKC - 1, [wkey, 'hT2'], [pk])
                act(g16[:, cg, 0:n], pb[:, 0:n], AF.Sigmoid, [pk], ['g16'])
            if ti + 1 < len(tl2):
                norm_mod_t(tl2[ti + 1][0], tl2[ti + 1][1], tl2[ti + 1][2], hT2, ['hT2'])
            for c in range(8):
                pb, pk = pbank()
                for kc in range(4):
                    mm(pb[:, 0:n], wbr[:, kc, c * 128:(c + 1) * 128], ob[:, kc, :], kc == 0, kc == 3, [wbrk, ok], [pk])
                tt(tmpb[:, 0, 0:n], pb[:, 0:n], g16[:, c, 0:n], ALU.mult, [pk, 'g16'], [('tmpn', 0)])
                pb2, pk2 = pbank()
                for kc in range(4):
                    mm(pb2[:, 0:n], wbp[:, kc, c * 128:(c + 1) * 128], ob[:, 4 + kc, :], kc == 0, kc == 3, [wbpk, ok], [pk2])
                tt(mrg[:, c, t0:t0 + n], pb2[:, 0:n], g16[:, 8 + c, 0:n], ALU.mult, [pk2, 'g16'], [('mrg', t0)])
                tt(mrg[:, c, t0:t0 + n], mrg[:, c, t0:t0 + n], tmpb[:, 0, 0:n], ALU.add, [('tmpn', 0), ('mrg', t0)], [('mrg', t0)])
        wo = [wl_kc(W["w_out"][l, :, j * 512:(j + 1) * 512], 512) for j in range(2)]
        for (t0, n, samp) in tiles(T2):
            for c in range(8):
                wt, wkey = wo[c // 4]
                q = c % 4
                pb, pk = pbank()
                for kc in range(NKC):
                    mm(pb[:, 0:n], wt[:, kc, q * 128:(q + 1) * 128], mrg[:, kc, t0:t0 + n], kc == 0, kc == NKC - 1, [wkey, ('mrg', t0)], [pk])
                resid_update_t(t0, n, samp, c, pb, pk)

        S.barrier()
        ptr[0] = UBASE
        T3 = 512
        WK['sq'] = alloc([2, T3], BF16); WK['rstd'] = alloc([T3]); WK['tmpn'] = alloc([2, T3]); WK['mg'] = alloc([128])
        hTm = alloc([NKC, NTOK], BF16)
        rl = alloc([2, T3]); r2 = alloc([8, T3], BF16)
        ada(l, "mlp")
        tl3 = tiles(T3)
        norm_mod_t(tl3[0][0], tl3[0][1], tl3[0][2], hTm[:, :, tl3[0][0]:tl3[0][0] + tl3[0][1]], [('hTm', tl3[0][0])])
        for qd in range(4):
            w1 = [wl_kc(W["w_ff1"][l, :, qd * 1024 + j * 512:qd * 1024 + (j + 1) * 512], 512) for j in range(2)]
            w2 = [wl_kc(W["w_ff2"][l, qd * 1024:(qd + 1) * 1024, j * 512:(j + 1) * 512], 512) for j in range(2)]
            for ti3, (t0, n, samp) in enumerate(tl3):
                if qd == 0 and ti3 + 1 < len(tl3):
                    nt0, nn, nsamp = tl3[ti3 + 1]
                    norm_mod_t(nt0, nn, nsamp, hTm[:, :, nt0:nt0 + nn], [('hTm', nt0)])
                for c in range(8):
                    wt, wkey = w1[c // 4]
                    q = c % 4
                    pb, pk = pbank()
                    for kc in range(NKC):
                        mm(pb[:, 0:n], wt[:, kc, q * 128:(q + 1) * 128], hTm[:, kc, t0:t0 + n], kc == 0, kc == NKC - 1, [wkey, ('hTm', t0)], [pk])
                    act(rl[:, c % 2, 0:n], pb[:, 0:n], AF.Relu, [pk], [('rl', c % 2)])
                    tt(r2[:, c, 0:n], rl[:, c % 2, 0:n], rl[:, c % 2, 0:n], ALU.mult, [('rl', c % 2)], [('r2', c)])
                for c in range(8):
                    wt, wkey = w2[c // 4]
                    q = c % 4
                    pb, pk = pbank()
                    for kc in range(NKC):
                        mm(pb[:, 0:n], wt[:, kc, q * 128:(q + 1) * 128], r2[:, kc, 0:n], kc == 0, kc == NKC - 1, [wkey, ('r2', kc)], [pk])
                    resid_update_t(t0, n, samp, c, pb, pk)

    S.barrier()
    ptr[0] = UBASE
    sqr = alloc([2, 128], BF16); rstd2 = alloc([2, 128]); tmpo2 = alloc([2, NKC, 128]); yout = alloc([2, D])
    ts(G32f[:], pcol[:, 0, 120:128].unsqueeze(2), 32.0, None, ALU.mult, ALU.bypass, ['pcol'], ['G32f'])

    def fin_stats(ci):
        t0 = ci * 128
        pb, pk = pbank()
        for c in range(NKC):
            act(sqr[:, c % 2, :], xT[:, c, t0:t0 + 128], AF.Square, xk(ci), [('sq', c % 2)])
            mm(pb[:, 0:128], onesB, sqr[:, c % 2, :], c == 0, c == NKC - 1, [('sq', c % 2), 'cB'], [pk])
        rsq(rstd2[:, ci % 2, :], pb[:, 0:128], 1.0, D * EPS, [pk], [('rstd', ci % 2)])

    fin_stats(0)
    for ci in range(NCH):
        t0 = ci * 128
        tmpo = tmpo2[:, ci % 2, :, :]
        tok = ('tmpo', ci % 2)
        for c in range(NKC):
            stt(tmpo[:, c, :], xT[:, c, t0:t0 + 128], G32f[:, c, 0:1], rstd2[:, ci % 2, :], ALU.mult, ALU.mult,
                xk(ci) + ['G32f', ('rstd', ci % 2)], [tok])
        if ci + 1 < NCH:
            fin_stats(ci + 1)
        yo = yout[:, ci % 2, :]
        for half in range(2):
            pb, pk = pbank()
            for q in range(4):
                c = half * 4 + q
                tr(pb[:, q * 128:(q + 1) * 128], tmpo[:, c, :], identF, [tok, 'cF'], [pk])
            acopy(yo[:, half * 512:(half + 1) * 512], pb[:, 0:512], [pk], [('yout', ci % 2)])
        dst = yp[t0:t0 + 128, :] if ci < NPC else ys[:, :]
        dma(dst, yo, [('yout', ci % 2)], [('yout', ci % 2)])
    return nc, S, st


CSTF = {}
CSTB = {}
CF_COLS = 0
CB_COLS = 0


def _layout_consts():
    global CF_COLS, CB_COLS
    o = 0
    for nm, n in [('ident', 128), ('ones', 128), ('bones', 128), ('rmS', 128), ('pcorr', 64)]:
        CSTF[nm] = (o, n)
        o += n
    CF_COLS = o
    o = 0
    for nm, n in [('ident', 128), ('ones', 128), ('mAP', 512), ('mAS', 512), ('mLP', 128), ('mLS', 128), ('seqm', 16)]:
        CSTB[nm] = (o, n)
        o += n
    CB_COLS = o


_layout_consts()


def make_consts():
    c = np.zeros((128, CF_COLS + CB_COLS), np.float32)

    def putf(nm, a):
        o, n = CSTF[nm]
        c[:, o:o + n] = a

    def putb(nm, a):
        o, n = CSTB[nm]
        c[:, CF_COLS + o:CF_COLS + o + n] = a
    i = np.arange(128)
    putf('ident', np.eye(128)); putb('ident', np.eye(128))
    putf('ones', np.ones((128, 128))); putb('ones', np.ones((128, 128)))
    putf('bones', (i[:, None] // 64 == i[None, :] // 64).astype(np.float32))
    s, t = i[:, None], i[None, :]
    for tag, same in (('P', np.ones((128, 128), bool)), ('S', (s // 8) == (t // 8))):
        lt = ((s < t) & same).astype(np.float32)
        le = ((s <= t) & same).astype(np.float32)
        gtm = ((s > t) & same).astype(np.float32)
        putb('mA' + tag, np.concatenate([-lt, le, lt, le], axis=1))
        putb('mL' + tag, -gtm)
    rmS = np.ones((128, 128), np.float32)
    rmS[:, ::8] = 0
    putf('rmS', rmS)
    putb('seqm', (i[:, None] // 8 == np.arange(16)[None, :]).astype(np.float32))
    pc_ = np.zeros((128, 64), np.float32)
    for g, w in enumerate(WINS):
        tt_ = np.arange(16)
        pc_[:, g * 16:(g + 1) * 16] = (1.0 / np.minimum(tt_ + 1, w))[None, :]
    putf('pcorr', pc_)
    return c


def emit(nc, S, st):
    sems = {name: st.enter_context(nc.semaphore(name)) for name in S.cnt}
    block = st.enter_context(nc.Block())

    def run(stream, eng):
        for waits, fn, sem, inc in S.ops[stream]:
            for (s, v) in waits:
                eng.wait_ge(sems[s], v)
            if fn is not None:
                fn(eng).then_inc(sems[sem], inc)

    @block.sync
    def _(e):
        run('sp', e)
        for nm in S.cnt:
            if nm.startswith('sp'):
                e.wait_ge(sems[nm], S.cnt[nm])

    @block.gpsimd
    def _(e):
        run('pool', e)

    @block.tensor
    def _(e):
        run('pe', e)

    @block.vector
    def _(e):
        run('dve', e)

    @block.scalar
    def _(e):
        run('act', e)
    st.close()
    return nc


_WNAMES = ["w_ada_mix", "b_ada_mix", "norm_mix", "w_in", "mu_shift", "w0", "w2", "a0", "a2", "g2", "v0", "v1", "v2",
           "k_k", "k_a", "r_k", "ln_w", "ln_b", "pool_w", "pool_scale", "w_br_rwkv", "w_br_pool", "w_out",
           "w_ada_mlp", "b_ada_mlp", "norm_mlp", "w_ff1", "w_ff2", "norm_final"]


def make_in_maps(inputs, ncores, L):
    consts = make_consts()
    f = lambda a: np.ascontiguousarray(np.asarray(a, dtype=np.float32))
    shared = {}
    for nm in _WNAMES:
        a = f(inputs[nm])
        if nm == "r_k":
            a = a.reshape(L, MIX)
        if nm == "norm_final":
            a = a.reshape(1, D)
        shared[nm] = a
    shared["cst"] = consts
    maps = []
    for i in range(ncores):
        m = dict(shared)
        m["xp"] = f(inputs["x_prompt"][i])
        m["xs"] = f(inputs["x_sample"][16 * i:16 * i + 16]).reshape(128, D)
        m["cc"] = f(np.concatenate([np.asarray(inputs["c_prompt"])[i:i + 1], np.asarray(inputs["c_sample"])[16 * i:16 * i + 16]], axis=0))
        m["sshift"] = f(np.asarray(inputs["state_shift"])[:, 16 * i:16 * i + 16])
        m["spool"] = f(np.asarray(inputs["state_pool"])[:, 16 * i:16 * i + 16])
        m["swkv"] = f(np.asarray(inputs["state_wkv"])[:, 16 * i:16 * i + 16])
        maps.append(m)
    return maps


def gather(R, ncores, L):
    y_p = np.stack([R[i]["yp"] for i in range(ncores)], 0)
    y_s = np.concatenate([R[i]["ys"].reshape(16, 8, D) for i in range(ncores)], 0)
    sh_p = np.stack([R[i]["nshp"] for i in range(ncores)], 1)
    pool_p = np.stack([R[i]["npoolp"] for i in range(ncores)], 1)
    wkv_p = np.stack([R[i]["nwkvp"] for i in range(ncores)], 1)
    sh_s = np.concatenate([R[i]["nshs"] for i in range(ncores)], 1)
    pool_s = np.concatenate([R[i]["npools"].reshape(L, 16, 15, MIX) for i in range(ncores)], 1)
    wkv_s = np.concatenate([R[i]["nwkvs"] for i in range(ncores)], 1)
    return tuple(np.ascontiguousarray(a, dtype=np.float32) for a in (y_p, y_s, sh_p, pool_p, wkv_p, sh_s, pool_s, wkv_s))


def kernel(**inputs):
    ncores = 8
    L = 4
    nc, S, st = build(TP=2048, L=L)
    emit(nc, S, st)
    maps = make_in_maps(inputs, ncores, L)
    res = run_bass_kernel_spmd(nc, maps, core_ids=list(range(ncores)))
    return gather(res.results, ncores, L)
```
